# Optimizing a Trainium2 kernel written in Bass

```python
import jax, jax.numpy as jnp
from jax import lax
import numpy as np

D_MODEL = 1024
BATCH = 2
SEQ = 16384
DEPTH = 2

N_A = DEPTH // 2
N_B = DEPTH - N_A
PLE_DIM = 256
EPS = 1e-6
NEG = -1e30
BIG = 1e30

RET_HEADS = D_MODEL // 256
RET_QK_DIM = D_MODEL // RET_HEADS
RET_V_DIM = 2 * RET_QK_DIM
RET_QK_W = RET_HEADS * RET_QK_DIM
RET_V_W = RET_HEADS * RET_V_DIM
RET_CHUNK = 128
ROPE_BASE = 10000.0

NSA_DK = 128
NSA_DV = 128
NSA_W = 2 * D_MODEL
NSA_HEADS = NSA_W // NSA_DV
NSA_GROUPS = 4
NSA_HPG = NSA_HEADS // NSA_GROUPS
NSA_KV_W = NSA_GROUPS * NSA_DK
CMP_LEN = 32
CMP_STRIDE = 16
CMP_HIDDEN = 256
SEL_BLOCK = 64
N_SEL = 16
WIN = 512
Q_BLOCK = 128

kernel_name = "yoco_retnet_nsa_hybrid"


def rmsnorm(x, g):
    xf = x.astype(jnp.float32)
    y = xf * lax.rsqrt(jnp.mean(xf * xf, axis=-1, keepdims=True) + EPS)
    return (y * g.astype(jnp.float32)).astype(x.dtype)


def rotary(x, pos):
    half = x.shape[-1] // 2
    inv = ROPE_BASE ** (-jnp.arange(half, dtype=jnp.float32) / half)
    ang = pos.astype(jnp.float32)[:, None] * inv[None, :]
    cos, sin = jnp.cos(ang), jnp.sin(ang)
    x1, x2 = x[..., :half], x[..., half:]
    return jnp.concatenate([x1 * cos - x2 * sin, x1 * sin + x2 * cos], axis=-1)


def retention_chunkwise(q, k, v):
    b, h, t, _ = q.shape
    nc = t // RET_CHUNK
    lg = jnp.log1p(-(2.0 ** (-5.0 - jnp.arange(h, dtype=jnp.float32))))
    pos = jnp.arange(RET_CHUNK, dtype=jnp.float32)
    diff = pos[:, None] - pos[None, :]
    decay = jnp.where(diff[None] >= 0,
                      jnp.exp(jnp.maximum(diff, 0.0)[None] * lg[:, None, None]), 0.0)
    q_dec = jnp.exp((pos + 1.0)[None, :] * lg[:, None])
    k_dec = jnp.exp((RET_CHUNK - 1.0 - pos)[None, :] * lg[:, None])
    chunk_dec = jnp.exp(RET_CHUNK * lg)

    def to_chunks(a):
        return jnp.moveaxis(a.reshape(b, h, nc, RET_CHUNK, a.shape[-1]), 2, 0)

    def step(state, qkv):
        qc, kc, vc = qkv
        s = jnp.einsum('bhnd,bhmd->bhnm', qc, kc) * decay
        o = (jnp.einsum('bhnm,bhmv->bhnv', s, vc)
             + jnp.einsum('bhnd,bhdv->bhnv', qc, state) * q_dec[..., None])
        state = (state * chunk_dec[:, None, None]
                 + jnp.einsum('bhmd,bhmv->bhdv', kc * k_dec[..., None], vc))
        return state, o

    state0 = jnp.zeros((b, h, q.shape[-1], v.shape[-1]), jnp.float32)
    _, o = lax.scan(step, state0, (to_chunks(q), to_chunks(k), to_chunks(v)))
    return jnp.moveaxis(o, 0, 2).reshape(b, h, t, v.shape[-1])


def retention_layer(h, g_norm, w_in, gn_gain, w_out):
    b, t, _ = h.shape
    proj = rmsnorm(h, g_norm) @ w_in
    q, k, v, gate = jnp.split(proj, [RET_QK_W, 2 * RET_QK_W, 2 * RET_QK_W + RET_V_W], axis=-1)

    def heads(a, d):
        return a.reshape(b, t, RET_HEADS, d).transpose(0, 2, 1, 3).astype(jnp.float32)

    pos = jnp.arange(t)
    q = rotary(heads(q, RET_QK_DIM), pos)
    k = rotary(heads(k, RET_QK_DIM), pos) * (RET_QK_DIM ** -0.5)
    o = retention_chunkwise(q, k, heads(v, RET_V_DIM))
    mu = jnp.mean(o, axis=-1, keepdims=True)
    var = jnp.mean(jnp.square(o - mu), axis=-1, keepdims=True)
    o = ((o - mu) * lax.rsqrt(var + EPS)).transpose(0, 2, 1, 3).reshape(b, t, RET_V_W)
    o = o * gn_gain.astype(jnp.float32)
    y = (jax.nn.silu(gate.astype(jnp.float32)) * o).astype(h.dtype) @ w_out
    return h + y


def per_layer_embedding(h, p_i, g, w_gate, w_emb):
    gate = jax.nn.sigmoid(rmsnorm(h, g) @ w_gate)
    return h + gate * (p_i @ w_emb)


def compress_blocks(a, pe, w1, w2):
    b, g, t, d = a.shape
    lc = CMP_LEN // CMP_STRIDE
    n_cmp = (t - CMP_LEN) // CMP_STRIDE + 1
    sub = a.reshape(b, g, t // CMP_STRIDE, CMP_STRIDE, d)
    blocks = jnp.concatenate([sub[:, :, o:o + n_cmp] for o in range(lc)], axis=3)
    blocks = (blocks + pe).reshape(b, g, n_cmp, CMP_LEN * d)
    return jax.nn.gelu(blocks @ w1) @ w2


def nsa_shared_kv(h, g_kv, w_kv, pe_k, w1_k, w2_k, pe_v, w1_v, w2_v):
    b, t, _ = h.shape
    proj = rmsnorm(h, g_kv) @ w_kv
    parts = proj.reshape(b, t, 6, NSA_GROUPS, NSA_DK).transpose(2, 0, 3, 1, 4)
    kc, vc, ks, vs, kw, vw = parts[0], parts[1], parts[2], parts[3], parts[4], parts[5]
    k_cmp = compress_blocks(kc, pe_k, w1_k, w2_k)
    v_cmp = compress_blocks(vc, pe_v, w1_v, w2_v)
    nslc = t // SEL_BLOCK
    k_sel = ks.reshape(b, NSA_GROUPS, nslc, SEL_BLOCK * NSA_DK)
    v_sel = vs.reshape(b, NSA_GROUPS, nslc, SEL_BLOCK * NSA_DV)
    pad = ((0, 0), (0, 0), (WIN, 0), (0, 0))
    return (k_cmp, v_cmp, k_sel, v_sel, jnp.pad(kw, pad), jnp.pad(vw, pad))


def selection_importance(p, nslc):
    f = SEL_BLOCK // CMP_STRIDE
    lc = CMP_LEN // CMP_STRIDE
    left = lc - 1
    right = f * nslc - p.shape[-1]
    pp = jnp.pad(p, ((0, 0), (0, 0), (0, 0), (left, right)))
    terms = []
    for o in range(-(lc - 1), f):
        w = float(min(o + lc, f) - max(o, 0))
        s = left + o
        terms.append(w * pp[..., s:s + f * (nslc - 1) + 1:f])
    return sum(terms[1:], terms[0])


def nsa_query_block(q, gates, blk, k_cmp, v_cmp, k_sel, v_sel, k_win, v_win):
    b = q.shape[0]
    scale = NSA_DK ** -0.5
    t = blk * Q_BLOCK + jnp.arange(Q_BLOCK)

    n_cmp = k_cmp.shape[2]
    cmp_end = jnp.arange(n_cmp) * CMP_STRIDE + CMP_LEN - 1
    valid_c = cmp_end[None, :] <= t[:, None]
    s_c = jnp.einsum('bgrqd,bgnd->bgrqn', q, k_cmp).astype(jnp.float32) * scale
    p_c = jnp.where(valid_c, jax.nn.softmax(jnp.where(valid_c, s_c, NEG), axis=-1), 0.0)
    o_c = jnp.einsum('bgrqn,bgnd->bgrqd', p_c, v_cmp.astype(jnp.float32))

    nslc = k_sel.shape[2]
    n_top = min(N_SEL, nslc)
    imp = selection_importance(jnp.sum(p_c, axis=2), nslc)
    cur = (t // SEL_BLOCK)[:, None]
    j = jnp.arange(nslc)[None, :]
    forced = (j == 0) | (j == cur) | (j == cur - 1)
    score = jnp.where(j <= cur, jnp.where(forced, BIG, imp), -BIG)
    top_v, top_i = lax.top_k(score, n_top)
    sel_ok = top_v > -0.5 * BIG
    idx = top_i.reshape(b, NSA_GROUPS, Q_BLOCK * n_top)
    bi = jnp.arange(b)[:, None, None]
    gi = jnp.arange(NSA_GROUPS)[None, :, None]
    ks = k_sel[bi, gi, idx].reshape(b, NSA_GROUPS, Q_BLOCK, n_top, SEL_BLOCK, NSA_DK)
    vs = v_sel[bi, gi, idx].reshape(b, NSA_GROUPS, Q_BLOCK, n_top, SEL_BLOCK, NSA_DV)
    kpos = top_i[..., None] * SEL_BLOCK + jnp.arange(SEL_BLOCK)
    valid_s = (sel_ok[..., None] & (kpos <= t[:, None, None]))[:, :, None]
    s_s = jnp.einsum('bgrqd,bgqnkd->bgrqnk', q, ks).astype(jnp.float32) * scale
    p_s = jnp.where(valid_s, jax.nn.softmax(jnp.where(valid_s, s_s, NEG), axis=(-2, -1)), 0.0)
    o_s = jnp.einsum('bgrqnk,bgqnkd->bgrqd', p_s, vs.astype(jnp.float32))

    start = blk * Q_BLOCK
    kw = lax.dynamic_slice_in_dim(k_win, start, WIN + Q_BLOCK, axis=2)
    vw = lax.dynamic_slice_in_dim(v_win, start, WIN + Q_BLOCK, axis=2)
    kwpos = start - WIN + jnp.arange(WIN + Q_BLOCK)
    valid_w = ((kwpos[None, :] <= t[:, None]) & (kwpos[None, :] > t[:, None] - WIN)
               & (kwpos[None, :] >= 0))
    s_w = jnp.einsum('bgrqd,bgkd->bgrqk', q, kw).astype(jnp.float32) * scale
    p_w = jax.nn.softmax(jnp.where(valid_w, s_w, NEG), axis=-1)
    o_w = jnp.einsum('bgrqk,bgkd->bgrqd', p_w, vw.astype(jnp.float32))

    return gates[..., 0:1] * o_c + gates[..., 1:2] * o_s + gates[..., 2:3] * o_w


def nsa_layer(h, g_norm, w_in, w_out, k_cmp, v_cmp, k_sel, v_sel, k_win, v_win):
    b, t, _ = h.shape
    proj = rmsnorm(h, g_norm) @ w_in
    q_w = NSA_HEADS * NSA_DK
    q, gate, bgate = jnp.split(proj, [q_w, q_w + NSA_W], axis=-1)
    nqb = t // Q_BLOCK
    qb = q.reshape(b, nqb, Q_BLOCK, NSA_GROUPS, NSA_HPG, NSA_DK).transpose(1, 0, 3, 4, 2, 5)
    gb = jax.nn.sigmoid(bgate.astype(jnp.float32)).reshape(
        b, nqb, Q_BLOCK, NSA_GROUPS, NSA_HPG, 3).transpose(1, 0, 3, 4, 2, 5)

    def body(args):
        q_blk, g_blk, blk = args
        return nsa_query_block(q_blk, g_blk, blk, k_cmp, v_cmp, k_sel, v_sel, k_win, v_win)

    o = lax.map(body, (qb, gb, jnp.arange(nqb)))
    o = o.transpose(1, 0, 4, 2, 3, 5).reshape(b, t, NSA_W)
    y = (jax.nn.silu(gate.astype(jnp.float32)) * o).astype(h.dtype) @ w_out
    return h + y


def setup_inputs(seed: int = 0) -> dict:
    key = jax.random.key(seed)
    ks = jax.random.split(key, 24)

    def nrm(k, shape, scale):
        return jax.random.normal(k, shape, jnp.float32) * scale

    def gain(k, shape):
        return 1.0 + 0.02 * jax.random.normal(k, shape, jnp.float32)

    ret_in_w = 2 * RET_QK_W + 2 * RET_V_W
    nsa_in_w = NSA_HEADS * NSA_DK + NSA_W + 3 * NSA_HEADS
    return {
        "x": nrm(ks[0], (BATCH, SEQ, D_MODEL), 1.0),
        "p": nrm(ks[1], (DEPTH, BATCH, SEQ, PLE_DIM), 1.0),
        "ret_norm": gain(ks[2], (N_A, D_MODEL)),
        "ret_w_in": nrm(ks[3], (N_A, D_MODEL, ret_in_w), D_MODEL ** -0.5),
        "ret_gn": gain(ks[4], (N_A, RET_V_W)),
        "ret_w_out": nrm(ks[5], (N_A, RET_V_W, D_MODEL), RET_V_W ** -0.5),
        "kv_norm": gain(ks[6], (D_MODEL,)),
        "kv_w": nrm(ks[7], (D_MODEL, 6 * NSA_KV_W), D_MODEL ** -0.5),
        "cmp_pe_k": nrm(ks[8], (CMP_LEN, NSA_DK), 0.02),
        "cmp_w1_k": nrm(ks[9], (CMP_LEN * NSA_DK, CMP_HIDDEN), (CMP_LEN * NSA_DK) ** -0.5),
        "cmp_w2_k": nrm(ks[10], (CMP_HIDDEN, NSA_DK), CMP_HIDDEN ** -0.5),
        "cmp_pe_v": nrm(ks[11], (CMP_LEN, NSA_DV), 0.02),
        "cmp_w1_v": nrm(ks[12], (CMP_LEN * NSA_DV, CMP_HIDDEN), (CMP_LEN * NSA_DV) ** -0.5),
        "cmp_w2_v": nrm(ks[13], (CMP_HIDDEN, NSA_DV), CMP_HIDDEN ** -0.5),
        "nsa_norm": gain(ks[14], (N_B, D_MODEL)),
        "nsa_w_in": nrm(ks[15], (N_B, D_MODEL, nsa_in_w), D_MODEL ** -0.5),
        "nsa_w_out": nrm(ks[16], (N_B, NSA_W, D_MODEL), NSA_W ** -0.5),
        "ple_norm": gain(ks[17], (DEPTH, D_MODEL)),
        "ple_w_gate": nrm(ks[18], (DEPTH, D_MODEL, D_MODEL), D_MODEL ** -0.5),
        "ple_w_emb": nrm(ks[19], (DEPTH, PLE_DIM, D_MODEL), PLE_DIM ** -0.5),
        "final_norm": gain(ks[20], (D_MODEL,)),
    }


def reference(x, p, ret_norm, ret_w_in, ret_gn, ret_w_out, kv_norm, kv_w,
              cmp_pe_k, cmp_w1_k, cmp_w2_k, cmp_pe_v, cmp_w1_v, cmp_w2_v,
              nsa_norm, nsa_w_in, nsa_w_out, ple_norm, ple_w_gate, ple_w_emb, final_norm):
    h = x
    shared = None
    for i in range(DEPTH):
        if i < N_A:
            h = retention_layer(h, ret_norm[i], ret_w_in[i], ret_gn[i], ret_w_out[i])
        else:
            if i == N_A:
                shared = nsa_shared_kv(h, kv_norm, kv_w, cmp_pe_k, cmp_w1_k, cmp_w2_k,
                                       cmp_pe_v, cmp_w1_v, cmp_w2_v)
            j = i - N_A
            h = nsa_layer(h, nsa_norm[j], nsa_w_in[j], nsa_w_out[j], *shared)
        h = per_layer_embedding(h, p[i], ple_norm[i], ple_w_gate[i], ple_w_emb[i])
    return rmsnorm(h, final_norm)
```

```python
import contextlib
import numpy as np
import concourse.bass as bass
import concourse.mybir as mybir
from concourse.bass_utils import run_bass_kernel_spmd

F32 = mybir.dt.float32
BF16 = mybir.dt.bfloat16
AF = mybir.ActivationFunctionType
ALU = mybir.AluOpType
AX = mybir.AxisListType
ENGS = ("tensor", "vector", "scalar", "gpsimd", "sync")
EPS = 1e-6
SCALE = 128 ** -0.5


class Buf:
    __slots__ = ("name", "t", "w", "r")

    def __init__(self, name, t=None):
        self.name = name
        self.t = t
        self.w = None
        self.r = []

    def __getitem__(self, idx):
        return self.t[idx]


class _Rec:
    def __init__(self):
        self.call = None

    def __getattr__(self, name):
        def f(*a, **kw):
            assert self.call is None
            self.call = (name, a, kw)
            return self
        return f


class Sched:
    def __init__(self, nc, n_dma_sems=12):
        self.nc = nc
        self.sems = {}
        self.cnt = {}
        for e in ENGS:
            self.sems[e] = nc.alloc_semaphore("s_" + e)
            self.cnt[e] = 0
        self.sems["cc"] = nc.alloc_semaphore("s_cc")
        self.cnt["cc"] = 0
        self.dq = {}
        for q in ("sync", "gpsimd", "scalar"):
            lst = []
            for i in range(n_dma_sems):
                k = "d_%s_%d" % (q, i)
                self.sems[k] = nc.alloc_semaphore(k)
                self.cnt[k] = 0
                lst.append(k)
            self.dq[q] = [lst, 0]
        self.known = {e: {} for e in ENGS}
        self.E = {e: getattr(nc, e) for e in ENGS}
        self.ninstr = 0
        self.uid = 0
        self.stacks = [contextlib.ExitStack()]

    def push(self):
        self.stacks.append(contextlib.ExitStack())

    def pop(self):
        self.barrier()
        self.stacks.pop().close()

    def _nm(self, name):
        self.uid += 1
        return "%s_%d" % (name, self.uid)

    def sb(self, name, shape, dtype):
        nm = self._nm(name)
        return Buf(nm, self.stacks[-1].enter_context(self.nc.sbuf_tensor(nm, list(shape), dtype)))

    def ps(self, name, shape, dtype=F32):
        nm = self._nm(name)
        return Buf(nm, self.stacks[-1].enter_context(self.nc.psum_tensor(nm, list(shape), dtype)))

    def dr(self, name, shape, dtype):
        return self.nc.dram_tensor(self._nm(name), list(shape), dtype)

    def _need(self, eng, deps):
        kn = self.known[eng]
        best = {}
        for d in deps:
            if d is None:
                continue
            k, v = d
            if k == eng and eng == "tensor":
                continue
            if kn.get(k, 0) >= v:
                continue
            if best.get(k, 0) < v:
                best[k] = v
        return best

    def _emit_waits(self, eng, best):
        for k, v in best.items():
            self.E[eng].wait_ge(self.sems[k], v)
            self.known[eng][k] = v
            self.ninstr += 1

    @staticmethod
    def _deps(reads, writes):
        deps = []
        for b in reads:
            deps.append(b.w)
        for b in writes:
            deps.append(b.w)
            deps.extend(b.r)
        return deps

    @staticmethod
    def _mark(ev, reads, writes):
        for b in reads:
            b.r.append(ev)
        for b in writes:
            b.w = ev
            b.r = []

    def op(self, eng, fn, reads=(), writes=()):
        self._emit_waits(eng, self._need(eng, self._deps(reads, writes)))
        self.cnt[eng] += 1
        rec = _Rec()
        fn(rec)
        name, a, kw = rec.call
        getattr(self.E[eng], name)(*a, **kw).then_inc(self.sems[eng], 1)
        self.ninstr += 1
        ev = (eng, self.cnt[eng])
        self._mark(ev, reads, writes)
        return ev

    def dma(self, q, out, in_, reads=(), writes=(), **kw):
        lst, idx = self.dq[q]
        k = lst[idx % len(lst)]
        self.dq[q][1] = idx + 1
        deps = self._deps(reads, writes)
        if self.cnt[k] > 0:
            deps.append((k, self.cnt[k]))
        self._emit_waits(q, self._need(q, deps))
        self.cnt[k] += 16
        self.E[q].dma_start(out=out, in_=in_, **kw).then_inc(self.sems[k], 16)
        self.ninstr += 1
        ev = (k, self.cnt[k])
        self._mark(ev, reads, writes)
        return ev

    def collective(self, kind, op, groups, in_ap, out_ap, reads=(), writes=()):
        deps = self._deps(reads, writes)
        if self.cnt["cc"] > 0:
            deps.append(("cc", self.cnt["cc"]))
        self._emit_waits("gpsimd", self._need("gpsimd", deps))
        self.cnt["cc"] += 1
        self.E["gpsimd"].collective_compute(kind, op, replica_groups=groups, ins=[in_ap], outs=[out_ap]).then_inc(
            self.sems["cc"], 1)
        self.ninstr += 1
        ev = ("cc", self.cnt["cc"])
        self._mark(ev, reads, writes)
        return ev

    def _all_events(self):
        return [(k, v) for k, v in self.cnt.items() if v > 0]

    def barrier(self):
        ev = self._all_events()
        for e in ENGS:
            self._emit_waits(e, self._need(e, [d for d in ev if d[0] != e]))

    def finish(self):
        deps = [(k, v) for k, v in self.cnt.items() if v > 0 and (k.startswith("d_") or k == "cc")]
        self._emit_waits("sync", self._need("sync", deps))
        self.barrier()
        while self.stacks:
            self.stacks.pop().close()


def load_w_bf16(S, q, dst, dst_fn, src_ap, stage, kc, ncols, gain=None):
    for k in range(kc):
        c0 = 0
        while c0 < ncols:
            cw = min(2048, ncols - c0)
            S.dma(q, stage[:, 0:cw], src_ap[:, k, c0:c0 + cw], writes=[stage])
            if gain is None:
                S.op("gpsimd", lambda e: e.tensor_copy(out=dst_fn(k, c0, cw), in_=stage[:, 0:cw]),
                     reads=[stage], writes=[dst])
            else:
                S.op("gpsimd", lambda e: e.tensor_scalar(out=dst_fn(k, c0, cw), in0=stage[:, 0:cw],
                                                         scalar1=gain[:, k:k + 1], scalar2=None, op0=ALU.mult),
                     reads=[stage, gain], writes=[dst])
            c0 += cw


def load_const_bf16(S, q, dst, dst_ap, src_ap, stage, ncols):
    S.dma(q, stage[:, 0:ncols], src_ap, writes=[stage])
    S.op("gpsimd", lambda e: e.tensor_copy(out=dst_ap, in_=stage[:, 0:ncols]), reads=[stage], writes=[dst])


def rms_rstd(S, xt, D, sqs, ss, rs):
    S.op("scalar", lambda e: e.activation(out=sqs[:, :], in_=xt[:, :], func=AF.Square, accum_out=ss[:, :]),
         reads=[xt], writes=[sqs, ss])
    S.op("vector", lambda e: e.tensor_scalar(out=rs[:, :], in0=ss[:, :], scalar1=1.0 / D, scalar2=EPS,
                                             op0=ALU.mult, op1=ALU.add), reads=[ss], writes=[rs])
    S.op("scalar", lambda e: e.activation(out=rs[:, :], in_=rs[:, :], func=AF.Sqrt), reads=[rs], writes=[rs])
    S.op("vector", lambda e: e.reciprocal(out=rs[:, :], in_=rs[:, :]), reads=[rs], writes=[rs])


def rmsnorm_T(S, xt, ident, sqs, ss, rs, xn, pT, dstT, tok0):
    rms_rstd(S, xt, 1024, sqs, ss, rs)
    S.op("vector", lambda e: e.tensor_scalar(out=xn[:, :], in0=xt[:, :], scalar1=rs[:, 0:1], scalar2=None,
                                             op0=ALU.mult), reads=[xt, rs], writes=[xn])
    for k in range(8):
        S.op("tensor", lambda e: e.transpose(out=pT[:, k, :], in_=xn[:, k * 128:(k + 1) * 128], identity=ident[:, :]),
             reads=[xn, ident], writes=[pT])
    S.op("vector", lambda e: e.tensor_copy(out=dstT[:, :, tok0:tok0 + 128], in_=pT[:, :, :]), reads=[pT], writes=[dstT])


class PleCtx:
    def __init__(self, S):
        self.S = S
        self.Wg = S.sb("pleWg", [128, 8, 1024], BF16)
        self.We = S.sb("pleWe", [128, 2, 1024], BF16)
        self.gain = S.sb("pleGain", [128, 8], F32)
        self.sqs = S.sb("ple_sqs", [128, 1024], F32)
        self.ss = S.sb("ple_ss", [128, 1], F32)
        self.rs = S.sb("ple_rs", [128, 1], F32)
        self.xn = S.sb("ple_xn", [128, 1024], BF16)
        self.hnT = S.sb("ple_hnT", [128, 8, 128], BF16)
        self.pf = [S.sb("ple_pf", [128, 256], F32) for i in range(2)]
        self.pb = S.sb("ple_pb", [128, 256], BF16)
        self.pT = S.sb("ple_pT", [128, 2, 128], BF16)
        self.sig = S.sb("ple_sig", [128, 512], F32)
        self.prod = S.sb("ple_prod", [128, 512], F32)
        self.npf = 0

    def load_weights(self, q, g_ap, wg_ap, we_ap, stage):
        S = self.S
        S.dma(q, self.gain[:, :], g_ap, writes=[self.gain])
        load_w_bf16(S, q, self.Wg, lambda k, c0, cw: self.Wg[:, k, c0:c0 + cw],
                    wg_ap.rearrange("(k p) n -> p k n", p=128), stage, 8, 1024, gain=self.gain)
        load_w_bf16(S, q, self.We, lambda k, c0, cw: self.We[:, k, c0:c0 + cw],
                    we_ap.rearrange("(k p) n -> p k n", p=128), stage, 2, 1024)

    def apply(self, h, p_ap, ident, pA, pG, pE, dmaq="gpsimd"):
        S = self.S
        pf = self.pf[self.npf % 2]
        self.npf += 1
        S.dma(dmaq, pf[:, :], p_ap, writes=[pf])
        rmsnorm_T(S, h, ident, self.sqs, self.ss, self.rs, self.xn, pA, self.hnT, 0)
        S.op("gpsimd", lambda e: e.tensor_copy(out=self.pb[:, :], in_=pf[:, :]), reads=[pf], writes=[self.pb])
        for k in range(2):
            S.op("tensor", lambda e: e.transpose(out=pA[:, k, :], in_=self.pb[:, k * 128:(k + 1) * 128], identity=ident[:, :]),
                 reads=[self.pb, ident], writes=[pA])
        S.op("vector", lambda e: e.tensor_copy(out=self.pT[:, :, :], in_=pA[:, 0:2, :]), reads=[pA], writes=[self.pT])
        for hf in range(2):
            cs = slice(hf * 512, (hf + 1) * 512)
            for k in range(8):
                S.op("tensor", lambda e: e.matmul(pG[:, :], lhsT=self.hnT[:, k, :], rhs=self.Wg[:, k, cs],
                                                  start=(k == 0), stop=(k == 7)), reads=[self.hnT, self.Wg], writes=[pG])
            for k in range(2):
                S.op("tensor", lambda e: e.matmul(pE[:, :], lhsT=self.pT[:, k, :], rhs=self.We[:, k, cs],
                                                  start=(k == 0), stop=(k == 1)), reads=[self.pT, self.We], writes=[pE])
            S.op("scalar", lambda e: e.activation(out=self.sig[:, :], in_=pG[:, :], func=AF.Sigmoid), reads=[pG], writes=[self.sig])
            S.op("vector", lambda e: e.tensor_tensor(out=self.prod[:, :], in0=self.sig[:, :], in1=pE[:, :], op=ALU.mult),
                 reads=[self.sig, pE], writes=[self.prod])
            S.op("gpsimd", lambda e: e.tensor_add(out=h[:, cs], in0=h[:, cs], in1=self.prod[:, :]), reads=[h, self.prod], writes=[h])


def build_program(T, groups, upto=4):
    nc = bass.Bass("TRN2", target_bir_lowering=False)
    NT = T // 128
    NS = T // 512
    NCH = T // 1024
    NC16 = T // 2048
    TL = T // 4

    def din(name, shape):
        return nc.dram_tensor(name, list(shape), F32, kind="ExternalInput").ap()

    x = din("x", [T, 1024])
    p0 = din("p0", [T, 256])
    p1s = din("p1s", [TL, 256])
    identd = din("ident", [128, 128])
    r_w_in = din("r_w_in", [1024, 1536])
    r_g_in = din("r_g_in", [128, 8])
    r_gn = din("r_gn", [128, 512])
    r_w_out = din("r_w_out", [512, 1024])
    cosT = din("cosT", [128, T])
    sinT = din("sinT", [128, T])
    qdec = din("qdec", [128, 512])
    kdec = din("kdec", [128, 512])
    cdec = din("cdec", [128, 1])
    causT_d = din("causT", [128, 128])
    upT_d = din("upT", [128, 128])
    pg = [din("pg%d" % l, [128, 8]) for l in range(2)]
    wg = [din("wg%d" % l, [1024, 1024]) for l in range(2)]
    we = [din("we%d" % l, [256, 1024]) for l in range(2)]
    fng = din("fng", [128, 1024])
    kv_g = din("kv_g", [128, 8])
    kv_w = din("kv_w", [1024, 768])
    peT_k = din("peT_k", [128, 32])
    peT_v = din("peT_v", [128, 32])
    w1_k = din("w1_k", [4096, 256])
    w1_v = din("w1_v", [4096, 256])
    w2_k = din("w2_k", [256, 128])
    w2_v = din("w2_v", [256, 128])
    n_g = din("n_g", [128, 8])
    n_w_in = din("n_w_in", [1024, 1036])
    n_w_out = din("n_w_out", [512, 1024])
    ex_d = din("ex", [128, 64 * 128])
    maug_d = din("maug", [128, 8 * 257])
    cmpm_d = din("cmpm", [128, 33 * 128])
    onesel_d = din("onesel", [128, 9])
    out = nc.dram_tensor("out", [TL, 1024], F32, kind="ExternalOutput").ap()
    dbg = nc.dram_tensor("dbg", [T, 1024], F32, kind="ExternalOutput").ap() if upto < 4 else None

    def dbg_out(src, bufs):
        for c in range(T // 1024):
            S.dma("sync", dbg[c * 1024:(c + 1) * 1024, :], src[c * 1024:(c + 1) * 1024, :], reads=bufs[c * 8:(c + 1) * 8])
        S.finish()
        return nc, S.ninstr

    S = Sched(nc)
    Y1 = S.dr("Y1", [T, 1024], F32)
    Y1s = S.dr("Y1s", [T, 1024], F32)
    Y2 = S.dr("Y2", [T, 1024], F32)
    Y2s = S.dr("Y2s", [TL, 1024], F32)
    H1 = S.dr("H1", [T, 1024], F32)
    HT = S.dr("HT", [NT, 128, 1024], BF16)
    KW = S.dr("KW", [NT, 128, 128], BF16)
    VW = S.dr("VW", [NT, 128, 128], BF16)
    bY1 = [Buf("bY1_%d" % i) for i in range(NT)]
    bY1s = [Buf("bY1s_%d" % i) for i in range(NT)]
    bY2 = [Buf("bY2_%d" % i) for i in range(NT)]
    bY2s = [Buf("bY2s_%d" % i) for i in range(NCH)]
    bH1 = [Buf("bH1_%d" % i) for i in range(NT)]
    bHT = [Buf("bHT_%d" % i) for i in range(NT)]
    bKW = [Buf("bKW_%d" % i) for i in range(NT)]
    bVW = [Buf("bVW_%d" % i) for i in range(NT)]

    stage = S.sb("stage", [128, 2048], F32)
    idf = S.sb("idf", [128, 128], F32)
    ident = S.sb("identb", [128, 128], BF16)
    S.dma("sync", idf[:, :], identd[:, :], writes=[idf])
    S.op("vector", lambda e: e.tensor_copy(out=ident[:, :], in_=idf[:, :]), reads=[idf], writes=[ident])

    S.push()
    W = S.sb("W", [128, 8, 1536], BF16)
    Wo = S.sb("Wo", [128, 4, 1024], BF16)
    gin = S.sb("gin", [128, 8], F32)
    gnt = S.sb("gnt", [128, 512], F32)
    qd_t = S.sb("qd_t", [128, 512], F32)
    kd_t = S.sb("kd_t", [128, 512], F32)
    cd_t = S.sb("cd_t", [128, 1], F32)
    caus = S.sb("caus", [128, 128], F32)
    xts = [S.sb("xt", [128, 1024], F32) for i in range(2)]
    sqs = S.sb("sqs", [128, 1024], F32)
    ss = S.sb("ss", [128, 1], F32)
    rs = S.sb("rs", [128, 1], F32)
    xn = S.sb("xn", [128, 1024], BF16)
    xnT = S.sb("xnT", [128, 8, 512], BF16)
    cs = [S.sb("cs", [128, 512], F32) for i in range(2)]
    sn = [S.sb("sn", [128, 512], F32) for i in range(2)]
    tabs = [S.sb("tab", [128, 512], F32) for i in range(4)]
    raw = [S.sb("raw", [128, 512], F32) for i in range(4)]
    tmp = [S.sb("tmp", [128, 512], F32) for i in range(4)]
    qdT = S.sb("qdT", [128, 2, 512], BF16)
    kTp = S.sb("kTp", [128, 2, 512], BF16)
    vb = S.sb("vb", [128, 4, 512], BF16)
    gs = S.sb("gs", [128, 4, 512], F32)
    st_f = [S.sb("st_f", [128, 512], F32) for i in range(2)]
    st_b = [S.sb("st_b", [128, 512], BF16) for i in range(2)]
    kd = S.sb("kd", [128, 256], BF16)
    ST = S.sb("ST", [128, 128], BF16)
    osq = S.sb("osq", [128, 512], F32)
    stat = S.sb("stat", [128, 4], F32)
    on = S.sb("on", [128, 512], F32)
    og = S.sb("og", [128, 512], BF16)
    ogT = S.sb("ogT", [128, 4, 128], BF16)
    yo = [S.sb("yo", [128, 1024], F32) for i in range(2)]
    pA = S.ps("pA", [128, 8, 128], BF16)
    pB = [S.ps("pB", [128, 512], F32) for i in range(2)]
    pS = S.ps("pS", [128, 128], F32)
    pO = S.ps("pO", [128, 512], F32)
    pSt = [S.ps("pSt", [128, 512], F32) for i in range(2)]
    pY = S.ps("pY", [128, 512], F32)

    for (dst, src) in ((gin, r_g_in), (gnt, r_gn), (qd_t, qdec), (kd_t, kdec), (cd_t, cdec), (caus, causT_d)):
        S.dma("sync", dst[:, :], src[:, :], writes=[dst])
    load_w_bf16(S, "sync", W, lambda k, c0, cw: W[:, k, c0:c0 + cw],
                r_w_in.rearrange("(k p) n -> p k n", p=128), stage, 8, 1536, gain=gin)
    load_w_bf16(S, "sync", Wo, lambda k, c0, cw: Wo[:, k, c0:c0 + cw],
                r_w_out.rearrange("(k p) n -> p k n", p=128), stage, 4, 1024)
    for i in range(2):
        S.op("gpsimd", lambda e: e.memset(st_f[i][:, :], 0.0), writes=[st_f[i]])
        S.op("gpsimd", lambda e: e.memset(st_b[i][:, :], 0.0), writes=[st_b[i]])

    def allreduce_chunk(c):
        r0 = c * 1024
        tl = list(range(c * 8, c * 8 + 8))
        S.collective("AllReduce", ALU.add, groups, Y1[r0:r0 + 1024, :], Y1s[r0:r0 + 1024, :],
                     reads=[bY1[t] for t in tl], writes=[bY1s[t] for t in tl])

    for s in range(NS):
        t0 = s * 512
        cst, snt = cs[s % 2], sn[s % 2]
        S.dma("gpsimd", cst[:, :], cosT[:, t0:t0 + 512], writes=[cst])
        S.dma("gpsimd", snt[:, :], sinT[:, t0:t0 + 512], writes=[snt])
        S.op("gpsimd", lambda e: e.tensor_mul(out=tabs[0][:, :], in0=cst[:, :], in1=qd_t[:, :]), reads=[cst, qd_t], writes=[tabs[0]])
        S.op("gpsimd", lambda e: e.tensor_mul(out=tabs[1][:, :], in0=snt[:, :], in1=qd_t[:, :]), reads=[snt, qd_t], writes=[tabs[1]])
        S.op("gpsimd", lambda e: e.tensor_mul(out=tabs[2][:, :], in0=cst[:, :], in1=kd_t[:, :]), reads=[cst, kd_t], writes=[tabs[2]])
        S.op("gpsimd", lambda e: e.tensor_mul(out=tabs[3][:, :], in0=snt[:, :], in1=kd_t[:, :]), reads=[snt, kd_t], writes=[tabs[3]])
        for j in range(4):
            ti = s * 4 + j
            xt = xts[ti % 2]
            S.dma("sync", xt[:, :], x[ti * 128:(ti + 1) * 128, :], writes=[xt])
            rmsnorm_T(S, xt, ident, sqs, ss, rs, xn, pA, xnT, j * 128)
        for dc in range(4):
            pb = pB[dc % 2]
            for k in range(8):
                S.op("tensor", lambda e: e.matmul(pb[:, :], lhsT=W[:, k, dc * 128:(dc + 1) * 128], rhs=xnT[:, k, :],
                                                  start=(k == 0), stop=(k == 7)), reads=[W, xnT], writes=[pb])
            S.op("scalar", lambda e: e.copy(out=raw[dc][:, :], in_=pb[:, :]), reads=[pb], writes=[raw[dc]])
        for (eng, x1, x2, ct, st_, dst, ta, tb) in (("gpsimd", raw[0], raw[1], tabs[0], tabs[1], qdT, tmp[0], tmp[1]),
                                                    ("vector", raw[2], raw[3], tabs[2], tabs[3], kTp, tmp[2], tmp[3])):
            S.op(eng, lambda e: e.tensor_mul(out=ta[:, :], in0=x1[:, :], in1=ct[:, :]), reads=[x1, ct], writes=[ta])
            S.op(eng, lambda e: e.tensor_mul(out=tb[:, :], in0=x2[:, :], in1=st_[:, :]), reads=[x2, st_], writes=[tb])
            S.op(eng, lambda e: e.tensor_sub(out=dst[:, 0, :], in0=ta[:, :], in1=tb[:, :]), reads=[ta, tb], writes=[dst])
            S.op(eng, lambda e: e.tensor_mul(out=ta[:, :], in0=x1[:, :], in1=st_[:, :]), reads=[x1, st_], writes=[ta])
            S.op(eng, lambda e: e.tensor_mul(out=tb[:, :], in0=x2[:, :], in1=ct[:, :]), reads=[x2, ct], writes=[tb])
            S.op(eng, lambda e: e.tensor_add(out=dst[:, 1, :], in0=ta[:, :], in1=tb[:, :]), reads=[ta, tb], writes=[dst])
        for j in range(4):
            for (which, c0) in (("v", 512), ("g", 1024)):
                pb = pB[0] if which == "v" else pB[1]
                for k in range(8):
                    S.op("tensor", lambda e: e.matmul(pb[:, :], lhsT=xnT[:, k, j * 128:(j + 1) * 128], rhs=W[:, k, c0:c0 + 512],
                                                      start=(k == 0), stop=(k == 7)), reads=[W, xnT], writes=[pb])
                if which == "v":
                    S.op("scalar", lambda e: e.copy(out=vb[:, j, :], in_=pb[:, :]), reads=[pb], writes=[vb])
                else:
                    S.op("scalar", lambda e: e.activation(out=gs[:, j, :], in_=pb[:, :], func=AF.Silu), reads=[pb], writes=[gs])
        for j in range(4):
            ti = s * 4 + j
            tk = slice(j * 128, (j + 1) * 128)
            for dc in range(2):
                S.op("tensor", lambda e: e.transpose(out=pA[:, dc, :], in_=kTp[:, dc, tk], identity=ident[:, :]),
                     reads=[kTp, ident], writes=[pA])
            S.op("vector", lambda e: e.tensor_scalar(out=kd[:, :], in0=pA[:, 0:2, :].rearrange("p a b -> p (a b)"),
                                                     scalar1=cd_t[:, 0:1], scalar2=None, op0=ALU.mult),
                 reads=[pA, cd_t], writes=[kd])
            for dc in range(2):
                S.op("tensor", lambda e: e.matmul(pS[:, :], lhsT=kTp[:, dc, tk], rhs=qdT[:, dc, tk],
                                                  start=(dc == 0), stop=(dc == 1)), reads=[kTp, qdT], writes=[pS])
            S.op("vector", lambda e: e.tensor_tensor(out=ST[:, :], in0=pS[:, :], in1=caus[:, :], op=ALU.mult),
                 reads=[pS, caus], writes=[ST])
            S.op("tensor", lambda e: e.matmul(pO[:, :], lhsT=ST[:, :], rhs=vb[:, j, :], start=True, stop=False),
                 reads=[ST, vb], writes=[pO])
            for dc in range(2):
                S.op("tensor", lambda e: e.matmul(pO[:, :], lhsT=qdT[:, dc, tk], rhs=st_b[dc][:, :],
                                                  start=False, stop=(dc == 1)), reads=[qdT, st_b[dc]], writes=[pO])
            for dc in range(2):
                S.op("tensor", lambda e: e.matmul(pSt[dc][:, :], lhsT=kd[:, dc * 128:(dc + 1) * 128], rhs=vb[:, j, :],
                                                  start=True, stop=True), reads=[kd, vb], writes=[pSt[dc]])
                S.op("vector", lambda e: e.scalar_tensor_tensor(out=st_f[dc][:, :], in0=st_f[dc][:, :], scalar=cd_t[:, 0:1],
                                                                in1=pSt[dc][:, :], op0=ALU.mult, op1=ALU.add),
                     reads=[st_f[dc], cd_t, pSt[dc]], writes=[st_f[dc]])
                S.op("gpsimd", lambda e: e.tensor_copy(out=st_b[dc][:, :], in_=st_f[dc][:, :]), reads=[st_f[dc]], writes=[st_b[dc]])
            S.op("scalar", lambda e: e.activation(out=on[:, :], in_=pO[:, :], func=AF.Identity, accum_out=stat[:, 0:1]),
                 reads=[pO], writes=[on, stat])
            S.op("scalar", lambda e: e.activation(out=osq[:, :], in_=pO[:, :], func=AF.Square, accum_out=stat[:, 1:2]),
                 reads=[pO], writes=[osq, stat])
            S.op("vector", lambda e: e.tensor_scalar(out=stat[:, 0:2], in0=stat[:, 0:2], scalar1=1.0 / 512, scalar2=None,
                                                     op0=ALU.mult), reads=[stat], writes=[stat])
            S.op("vector", lambda e: e.tensor_tensor(out=stat[:, 2:3], in0=stat[:, 0:1], in1=stat[:, 0:1], op=ALU.mult),
                 reads=[stat], writes=[stat])
            S.op("vector", lambda e: e.tensor_tensor(out=stat[:, 2:3], in0=stat[:, 1:2], in1=stat[:, 2:3], op=ALU.subtract),
                 reads=[stat], writes=[stat])
            S.op("vector", lambda e: e.tensor_scalar(out=stat[:, 2:3], in0=stat[:, 2:3], scalar1=EPS, scalar2=None,
                                                     op0=ALU.add), reads=[stat], writes=[stat])
            S.op("scalar", lambda e: e.activation(out=stat[:, 2:3], in_=stat[:, 2:3], func=AF.Sqrt), reads=[stat], writes=[stat])
            S.op("vector", lambda e: e.reciprocal(out=stat[:, 3:4], in_=stat[:, 2:3]), reads=[stat], writes=[stat])
            S.op("vector", lambda e: e.tensor_scalar(out=on[:, :], in0=on[:, :], scalar1=stat[:, 0:1], scalar2=stat[:, 3:4],
                                                     op0=ALU.subtract, op1=ALU.mult), reads=[on, stat], writes=[on])
            S.op("gpsimd", lambda e: e.tensor_mul(out=on[:, :], in0=on[:, :], in1=gnt[:, :]), reads=[on, gnt], writes=[on])
            S.op("gpsimd", lambda e: e.tensor_mul(out=og[:, :], in0=on[:, :], in1=gs[:, j, :]), reads=[on, gs], writes=[og])
            for c in range(4):
                S.op("tensor", lambda e: e.transpose(out=pA[:, 2 + c, :], in_=og[:, c * 128:(c + 1) * 128], identity=ident[:, :]),
                     reads=[og, ident], writes=[pA])
            S.op("vector", lambda e: e.tensor_copy(out=ogT[:, :, :], in_=pA[:, 2:6, :]), reads=[pA], writes=[ogT])
            yt = yo[ti % 2]
            for hf in range(2):
                for c in range(4):
                    S.op("tensor", lambda e: e.matmul(pY[:, :], lhsT=ogT[:, c, :], rhs=Wo[:, c, hf * 512:(hf + 1) * 512],
                                                      start=(c == 0), stop=(c == 3)), reads=[ogT, Wo], writes=[pY])
                S.op("scalar", lambda e: e.copy(out=yt[:, hf * 512:(hf + 1) * 512], in_=pY[:, :]), reads=[pY], writes=[yt])
            S.dma("sync", Y1[ti * 128:(ti + 1) * 128, :], yt[:, :], reads=[yt], writes=[bY1[ti]])
            if ti >= 9 and (ti - 9) % 8 == 0:
                allreduce_chunk((ti - 9) // 8)
    allreduce_chunk(NCH - 1)
    if NCH >= 2 and (NT - 1) < 9 + 8 * (NCH - 2):
        pass
    issued = set([(ti - 9) // 8 for ti in range(NT) if ti >= 9 and (ti - 9) % 8 == 0] + [NCH - 1])
    for c in range(NCH):
        if c not in issued:
            allreduce_chunk(c)
    S.pop()
    if upto == 1:
        return dbg_out(Y1s, bY1s)

    S.push()
    KsT = S.sb("KsT", [128, T], BF16)
    Vs = S.sb("Vs", [128, NT, 128], BF16)
    KcT = S.sb("KcT", [128, NC16 * 128], BF16)
    Vc = S.sb("Vc", [128, NC16, 128], BF16)

    S.push()
    ple = PleCtx(S)
    ple.load_weights("sync", pg[0][:, :], wg[0], we[0], stage)
    kvg = S.sb("kvg", [128, 8], F32)
    Wkv = S.sb("Wkv", [128, 8, 768], BF16)
    S.dma("sync", kvg[:, :], kv_g[:, :], writes=[kvg])
    load_w_bf16(S, "sync", Wkv, lambda k, c0, cw: Wkv[:, k, c0:c0 + cw],
                kv_w.rearrange("(k p) n -> p k n", p=128), stage, 8, 768, gain=kvg)
    w1 = [S.sb("w1", [128, 32, 256], BF16) for i in range(2)]
    w2 = [S.sb("w2", [128, 2, 128], BF16) for i in range(2)]
    peT = [S.sb("peT", [128, 32], BF16) for i in range(2)]
    for i, (w1d, w2d, ped) in enumerate(((w1_k, w2_k, peT_k), (w1_v, w2_v, peT_v))):
        load_w_bf16(S, "sync", w1[i], lambda k, c0, cw: w1[i][:, k, c0:c0 + cw],
                    w1d.rearrange("(l d) h -> d l h", d=128), stage, 32, 256)
        load_w_bf16(S, "sync", w2[i], lambda k, c0, cw: w2[i][:, k, c0:c0 + cw],
                    w2d.rearrange("(k p) n -> p k n", p=128), stage, 2, 128)
        load_const_bf16(S, "sync", peT[i], peT[i][:, :], ped[:, :], stage, 32)
    cb = [S.sb("cb", [128, 2064], BF16) for i in range(2)]
    hs = [S.sb("h", [128, 1024], F32) for i in range(2)]
    ybs = [S.sb("yb", [128, 1024], F32) for i in range(2)]
    hTs = [S.sb("hT", [128, 8, 128], BF16) for i in range(2)]
    kwt = [S.sb("kwt", [128, 128], BF16) for i in range(2)]
    vwt = [S.sb("vwt", [128, 128], BF16) for i in range(2)]
    sqs = S.sb("sqs", [128, 1024], F32)
    ss = S.sb("ss", [128, 1], F32)
    rs = S.sb("rs", [128, 1], F32)
    xn = S.sb("xn", [128, 1024], BF16)
    ones1 = S.sb("ones1", [1, 128], BF16)
    bias_f = S.sb("bias_f", [1, 512], F32)
    bias_hi = S.sb("bias_hi", [1, 512], BF16)
    bias_hif = S.sb("bias_hif", [1, 512], F32)
    bias_lo = S.sb("bias_lo", [1, 512], BF16)
    xs_ = S.sb("xs_", [128, 256], F32)
    x2_ = S.sb("x2_", [128, 256], F32)
    sg_ = S.sb("sg_", [128, 256], F32)
    hid = S.sb("hid", [128, 256], BF16)
    hidT = S.sb("hidT", [128, 2, 128], BF16)
    pA = S.ps("pA", [128, 8, 128], BF16)
    pG = S.ps("pG", [128, 512], F32)
    pE = S.ps("pE", [128, 512], F32)
    pKT = S.ps("pKT", [128, 4, 128], F32)
    pKV = S.ps("pKV", [128, 256], F32)
    pH = S.ps("pH", [128, 256], F32)
    pC = S.ps("pC", [128, 128], F32)

    S.op("gpsimd", lambda e: e.memset(ones1[:, :], 1.0), writes=[ones1])
    for i in range(2):
        S.op("gpsimd", lambda e: e.memset(cb[i][:, 0:16], 0.0), writes=[cb[i]])
    for i in range(2):
        for l in range(32):
            S.op("tensor", lambda e: e.matmul(pH[0:1, :], lhsT=peT[i][:, l:l + 1], rhs=w1[i][:, l, :],
                                              start=(l == 0), stop=(l == 31)), reads=[peT[i], w1[i]], writes=[pH])
        S.op("scalar", lambda e: e.copy(out=bias_f[:, i * 256:(i + 1) * 256], in_=pH[0:1, :]), reads=[pH], writes=[bias_f])
    S.op("vector", lambda e: e.tensor_copy(out=bias_hi[:, :], in_=bias_f[:, :]), reads=[bias_f], writes=[bias_hi])
    S.op("vector", lambda e: e.tensor_copy(out=bias_hif[:, :], in_=bias_hi[:, :]), reads=[bias_hi], writes=[bias_hif])
    S.op("vector", lambda e: e.tensor_sub(out=bias_hif[:, :], in0=bias_f[:, :], in1=bias_hif[:, :]), reads=[bias_f, bias_hif], writes=[bias_hif])
    S.op("vector", lambda e: e.tensor_copy(out=bias_lo[:, :], in_=bias_hif[:, :]), reads=[bias_hif], writes=[bias_lo])

    for i in range(NT):
        rows = slice(i * 128, (i + 1) * 128)
        h = hs[i % 2]
        yb = ybs[i % 2]
        S.dma("sync", h[:, :], x[rows, :], writes=[h])
        S.dma("sync", yb[:, :], Y1s[rows, :], reads=[bY1s[i]], writes=[yb])
        S.op("vector", lambda e: e.tensor_add(out=h[:, :], in0=h[:, :], in1=yb[:, :]), reads=[h, yb], writes=[h])
        ple.apply(h, p0[rows, :], ident, pA, pG, pE)
        S.dma("sync", H1[rows, :], h[:, :], reads=[h], writes=[bH1[i]])
        hT = hTs[i % 2]
        rmsnorm_T(S, h, ident, sqs, ss, rs, xn, pA, hT, 0)
        S.dma("gpsimd", HT[i].rearrange("p (k t) -> p k t", k=8), hT[:, :, :], reads=[hT], writes=[bHT[i]])
        for a in range(4):
            for k in range(8):
                S.op("tensor", lambda e: e.matmul(pKT[:, a, :], lhsT=Wkv[:, k, a * 128:(a + 1) * 128], rhs=hT[:, k, :],
                                                  start=(k == 0), stop=(k == 7)), reads=[Wkv, hT], writes=[pKT])
        for k in range(8):
            S.op("tensor", lambda e: e.matmul(pKV[:, :], lhsT=hT[:, k, :], rhs=Wkv[:, k, 512:768],
                                              start=(k == 0), stop=(k == 7)), reads=[Wkv, hT], writes=[pKV])
        cc0 = 16 + (i % 16) * 128
        S.op("scalar", lambda e: e.copy(out=cb[0][:, cc0:cc0 + 128], in_=pKT[:, 0, :]), reads=[pKT], writes=[cb[0]])
        S.op("scalar", lambda e: e.copy(out=cb[1][:, cc0:cc0 + 128], in_=pKT[:, 1, :]), reads=[pKT], writes=[cb[1]])
        S.op("scalar", lambda e: e.copy(out=KsT[:, rows], in_=pKT[:, 2, :]), reads=[pKT], writes=[KsT])
        kw_, vw_ = kwt[i % 2], vwt[i % 2]
        S.op("scalar", lambda e: e.copy(out=kw_[:, :], in_=pKT[:, 3, :]), reads=[pKT], writes=[kw_])
        S.op("vector", lambda e: e.tensor_copy(out=Vs[:, i, :], in_=pKV[:, 0:128]), reads=[pKV], writes=[Vs])
        S.op("vector", lambda e: e.tensor_copy(out=vw_[:, :], in_=pKV[:, 128:256]), reads=[pKV], writes=[vw_])
        S.dma("gpsimd", KW[i], kw_[:, :], reads=[kw_], writes=[bKW[i]])
        S.dma("gpsimd", VW[i], vw_[:, :], reads=[vw_], writes=[bVW[i]])
        if i % 16 == 15:
            s16 = i // 16
            for X in range(2):
                bc = slice(X * 256, (X + 1) * 256)
                S.op("tensor", lambda e: e.matmul(pH[:, :], lhsT=ones1[0:1, :], rhs=bias_hi[0:1, bc], start=True, stop=False),
                     reads=[ones1, bias_hi], writes=[pH])
                S.op("tensor", lambda e: e.matmul(pH[:, :], lhsT=ones1[0:1, :], rhs=bias_lo[0:1, bc], start=False, stop=False),
                     reads=[ones1, bias_lo], writes=[pH])
                for l in range(32):
                    S.op("tensor", lambda e: e.matmul(pH[:, :], lhsT=cb[X][:, l:l + 2033:16], rhs=w1[X][:, l, :],
                                                      start=False, stop=(l == 31)), reads=[cb[X], w1[X]], writes=[pH])
                S.op("scalar", lambda e: e.copy(out=xs_[:, :], in_=pH[:, :]), reads=[pH], writes=[xs_])
                S.op("vector", lambda e: e.tensor_tensor(out=x2_[:, :], in0=xs_[:, :], in1=xs_[:, :], op=ALU.mult), reads=[xs_], writes=[x2_])
                S.op("vector", lambda e: e.tensor_scalar(out=x2_[:, :], in0=x2_[:, :], scalar1=0.044715, scalar2=1.0,
                                                         op0=ALU.mult, op1=ALU.add), reads=[x2_], writes=[x2_])
                S.op("vector", lambda e: e.tensor_tensor(out=x2_[:, :], in0=x2_[:, :], in1=xs_[:, :], op=ALU.mult), reads=[x2_, xs_], writes=[x2_])
                S.op("scalar", lambda e: e.activation(out=sg_[:, :], in_=x2_[:, :], func=AF.Sigmoid, scale=1.5957691216057308),
                     reads=[x2_], writes=[sg_])
                S.op("vector", lambda e: e.tensor_tensor(out=hid[:, :], in0=xs_[:, :], in1=sg_[:, :], op=ALU.mult), reads=[xs_, sg_], writes=[hid])
                for hc in range(2):
                    S.op("tensor", lambda e: e.transpose(out=pA[:, hc, :], in_=hid[:, hc * 128:(hc + 1) * 128], identity=ident[:, :]),
                         reads=[hid, ident], writes=[pA])
                S.op("vector", lambda e: e.tensor_copy(out=hidT[:, :, :], in_=pA[:, 0:2, :]), reads=[pA], writes=[hidT])
                if X == 0:
                    for hc in range(2):
                        S.op("tensor", lambda e: e.matmul(pC[:, :], lhsT=w2[0][:, hc, :], rhs=hidT[:, hc, :],
                                                          start=(hc == 0), stop=(hc == 1)), reads=[w2[0], hidT], writes=[pC])
                    S.op("scalar", lambda e: e.copy(out=KcT[:, s16 * 128:(s16 + 1) * 128], in_=pC[:, :]), reads=[pC], writes=[KcT])
                else:
                    for hc in range(2):
                        S.op("tensor", lambda e: e.matmul(pC[:, :], lhsT=hidT[:, hc, :], rhs=w2[1][:, hc, :],
                                                          start=(hc == 0), stop=(hc == 1)), reads=[w2[1], hidT], writes=[pC])
                    S.op("scalar", lambda e: e.copy(out=Vc[:, s16, :], in_=pC[:, :]), reads=[pC], writes=[Vc])
                S.op("vector", lambda e: e.tensor_copy(out=cb[X][:, 0:16], in_=cb[X][:, 2048:2064]), reads=[cb[X]], writes=[cb[X]])
    S.pop()

    if upto == 2:
        S.pop()
        return dbg_out(H1, bH1)
    S.push()
    ng = S.sb("ng", [128, 8], F32)
    Wn = S.sb("Wn", [128, 8, 1036], BF16)
    Wo2 = S.sb("Wo2", [128, 4, 1024], BF16)
    S.dma("sync", ng[:, :], n_g[:, :], writes=[ng])
    load_w_bf16(S, "sync", Wn, lambda k, c0, cw: Wn[:, k, c0:c0 + cw],
                n_w_in.rearrange("(k p) n -> p k n", p=128), stage, 8, 1036, gain=ng)
    load_w_bf16(S, "sync", Wo2, lambda k, c0, cw: Wo2[:, k, c0:c0 + cw],
                n_w_out.rearrange("(k p) n -> p k n", p=128), stage, 4, 1024)
    Ex = S.sb("Ex", [128, 64, 128], BF16)
    for c in range(4):
        load_const_bf16(S, "sync", Ex, Ex[:, c * 16:(c + 1) * 16, :].rearrange("p a b -> p (a b)"),
                        ex_d[:, c * 2048:(c + 1) * 2048], stage, 2048)
    Maug = S.sb("Maug", [128, 8, 257], BF16)
    load_const_bf16(S, "sync", Maug, Maug[:, 0:4, :].rearrange("p a b -> p (a b)"), maug_d[:, 0:1028], stage, 1028)
    load_const_bf16(S, "sync", Maug, Maug[:, 4:8, :].rearrange("p a b -> p (a b)"), maug_d[:, 1028:2056], stage, 1028)
    cmpm = S.sb("cmpm", [128, 33, 128], BF16)
    load_const_bf16(S, "sync", cmpm, cmpm[:, 0:16, :].rearrange("p a b -> p (a b)"), cmpm_d[:, 0:2048], stage, 2048)
    load_const_bf16(S, "sync", cmpm, cmpm[:, 16:32, :].rearrange("p a b -> p (a b)"), cmpm_d[:, 2048:4096], stage, 2048)
    load_const_bf16(S, "sync", cmpm, cmpm[:, 32, :], cmpm_d[:, 4096:4224], stage, 128)
    onesel = S.sb("onesel", [128, 3, 3], BF16)
    load_const_bf16(S, "sync", onesel, onesel[:, :, :].rearrange("p a b -> p (a b)"), onesel_d[:, :], stage, 9)
    causT = S.sb("causT", [128, 128], BF16)
    upT = S.sb("upT", [128, 128], BF16)
    load_const_bf16(S, "sync", causT, causT[:, :], causT_d[:, :], stage, 128)
    load_const_bf16(S, "sync", upT, upT[:, :], upT_d[:, :], stage, 128)

    hTs = [S.sb("hT", [128, 8, 128], BF16) for i in range(2)]
    h1s = [S.sb("h1", [128, 1024], F32) for i in range(2)]
    kwr = S.sb("kwr", [128, 6, 128], BF16)
    vwr = S.sb("vwr", [128, 6, 128], BF16)
    kwb = [Buf("kwb%d" % i, kwr.t) for i in range(6)]
    vwb = [Buf("vwb%d" % i, vwr.t) for i in range(6)]
    QTs = [S.sb("QT", [128, 512], BF16) for i in range(2)]
    gsils = [S.sb("gsil", [128, 512], F32) for i in range(2)]
    bgs = [S.sb("bg", [128, 12], F32) for i in range(2)]
    EmC = S.sb("EmC", [128, 8, 512], BF16)
    NBUF = 4
    Eb = [S.sb("Eb", [128, 512], BF16) for i in range(NBUF)]
    Emb = [S.sb("Emb", [128, 512], BF16) for i in range(NBUF)]
    imp = S.sb("imp", [128, 256], F32)
    score = S.sb("score", [128, 256], F32)
    sc2 = S.sb("sc2", [128, 256], F32)
    selF = S.sb("selF", [128, 256], F32)
    selT = S.sb("selT", [128, 2, 128], BF16)
    m8 = S.sb("m8", [128, 16], F32)
    rcol = S.sb("rcol", [128, 1], F32)
    sumsbs = [S.sb("sumsb", [3, 512], F32) for i in range(2)]
    coef = S.sb("coef", [128, 12], F32)
    o_ = S.sb("o_", [128, 512], F32)
    ogf = S.sb("ogf", [128, 512], F32)
    ogT2 = S.sb("ogT2", [128, 4, 128], BF16)
    yts = [S.sb("yt", [128, 1024], F32) for i in range(2)]
    ObTs = [[S.sb("ObT", [128, 512], F32) for i in range(3)] for p in range(2)]
    pSc = [S.ps("pSc", [128, 512], F32) for i in range(2)]
    pM = S.ps("pM", [128, 128], F32)
    pO2 = [S.ps("pOb", [128, 512], F32) for i in range(2)]
    pOb = [pO2[0], pO2[1], pO2[0]]
    pSums = [S.ps("pSum", [3, 512], F32) for i in range(2)]
    pX = S.ps("pX", [128, 512], F32)

    S.op("gpsimd", lambda e: e.memset(selF[:, :], 0.0), writes=[selF])
    ctr = {"e": 0, "m": 0, "s": 0}

    def stageA(u):
        rows = u["rows"]
        QT = u["QT"]
        ps = pSc[ctr["s"] % 2]
        ctr["s"] += 1
        S.op("tensor", lambda e: e.matmul(ps[0:rows, :], lhsT=u["ksrc"], rhs=QT[:, :], start=True, stop=True),
             reads=[u["ktrack"], QT], writes=[ps])
        E = Eb[ctr["e"] % NBUF]
        ctr["e"] += 1
        S.op("scalar", lambda e: e.activation(out=E[0:rows, :], in_=ps[0:rows, :], func=AF.Exp, scale=SCALE),
             reads=[ps], writes=[E])
        mask = u["mask"]
        if mask is None:
            Em = E
        else:
            Em = Emb[ctr["m"] % NBUF]
            ctr["m"] += 1
            if mask[0] == "sb":
                S.op("vector", lambda e: e.tensor_tensor(
                    out=Em[0:rows, :].rearrange("p (r q) -> p r q", r=4), in0=E[0:rows, :].rearrange("p (r q) -> p r q", r=4),
                    in1=mask[1].unsqueeze(1).broadcast_to([rows, 4, 128]), op=ALU.mult),
                     reads=[E] + mask[2], writes=[Em])
            else:
                t = mask[1]
                S.op("tensor", lambda e: e.matmul(pM[:, :], lhsT=Ex[:, t % 64, :], rhs=selT[:, t // 64, :],
                                                  start=True, stop=True), reads=[Ex, selT], writes=[pM])
                S.op("vector", lambda e: e.tensor_tensor(
                    out=Em[0:rows, :].rearrange("p (r q) -> p r q", r=4), in0=E[0:rows, :].rearrange("p (r q) -> p r q", r=4),
                    in1=pM[:, :].unsqueeze(1).broadcast_to([128, 4, 128]), op=ALU.mult),
                     reads=[E, pM], writes=[Em])
        u["Em"] = Em
        if u.get("emc") is not None:
            c = u["emc"]
            S.op("gpsimd", lambda e: e.tensor_copy(out=EmC[0:rows, c, :], in_=Em[0:rows, :]), reads=[Em], writes=[EmC])

    def stageB(u):
        rows = u["rows"]
        Em = u["Em"]
        b_idx = u["b"]
        q = u["q"]
        pSum = pSums[q["par"]]
        S.op("tensor", lambda e: e.matmul(pOb[b_idx][:, :], lhsT=u["vsrc"], rhs=Em[0:rows, :], start=u["first"], stop=u["last"]),
             reads=[u["vtrack"], Em], writes=[pOb[b_idx]])
        S.op("tensor", lambda e: e.matmul(pSum[:, :], lhsT=onesel[0:rows, b_idx, :], rhs=Em[0:rows, :],
                                          start=(q["nsum"] == 0), stop=(q["nsum"] == q["total_sum"] - 1)),
             reads=[onesel, Em], writes=[pSum])
        q["nsum"] += 1
        for f in u.get("post", ()):
            f()

    def rs_chunk(c):
        tl = list(range(c * 8, c * 8 + 8))
        S.collective("ReduceScatter", ALU.add, groups, Y2[c * 1024:(c + 1) * 1024, :], Y2s[c * 256:(c + 1) * 256, :],
                     reads=[bY2[t] for t in tl], writes=[bY2s[c]])

    def prologue(i):
        par = i % 2
        rows = slice(i * 128, (i + 1) * 128)
        hT, h1, QT, gsil, bg = hTs[par], h1s[par], QTs[par], gsils[par], bgs[par]
        S.dma("sync", hT[:, :, :], HT[i].rearrange("p (k t) -> p k t", k=8), reads=[bHT[i]], writes=[hT])
        S.dma("sync", h1[:, :], H1[rows, :], reads=[bH1[i]], writes=[h1])
        S.dma("gpsimd", kwr[:, i % 6, :], KW[i], reads=[bKW[i]], writes=[kwb[i % 6]])
        S.dma("gpsimd", vwr[:, i % 6, :], VW[i], reads=[bVW[i]], writes=[vwb[i % 6]])
        for r in range(4):
            for k in range(8):
                S.op("tensor", lambda e: e.matmul(pX[:, r * 128:(r + 1) * 128], lhsT=Wn[:, k, r * 128:(r + 1) * 128],
                                                  rhs=hT[:, k, :], start=(k == 0), stop=(k == 7)), reads=[Wn, hT], writes=[pX])
        S.op("scalar", lambda e: e.copy(out=QT[:, :], in_=pX[:, :]), reads=[pX], writes=[QT])
        for k in range(8):
            S.op("tensor", lambda e: e.matmul(pX[:, :], lhsT=hT[:, k, :], rhs=Wn[:, k, 512:1024],
                                              start=(k == 0), stop=(k == 7)), reads=[Wn, hT], writes=[pX])
        S.op("scalar", lambda e: e.activation(out=gsil[:, :], in_=pX[:, :], func=AF.Silu), reads=[pX], writes=[gsil])
        for k in range(8):
            S.op("tensor", lambda e: e.matmul(pX[:, 0:12], lhsT=hT[:, k, :], rhs=Wn[:, k, 1024:1036],
                                              start=(k == 0), stop=(k == 7)), reads=[Wn, hT], writes=[pX])
        S.op("scalar", lambda e: e.activation(out=bg[:, :], in_=pX[:, 0:12], func=AF.Sigmoid), reads=[pX], writes=[bg])

    def make_units(i):
        par = i % 2
        QT = QTs[par]
        Wp = 8 * (i + 1)
        nch = (Wp + 127) // 128
        wt = [t for t in range(i - 4, i + 1) if t >= 0]
        q = dict(i=i, par=par, nsum=0, total_sum=nch + (i + 1) + len(wt), topk_done=(i < 8))
        OT = ObTs[par]

        def evac(b):
            return lambda: S.op("scalar", lambda e: e.copy(out=OT[b][:, :], in_=pOb[b][:, :]), reads=[pOb[b]], writes=[OT[b]])

        def topk():
            ncol = 2 * i
            for r in range(4):
                for c in range(nch):
                    rws = min(128, Wp - c * 128)
                    S.op("tensor", lambda e: e.matmul(pX[:, 0:257], lhsT=EmC[0:rws, c, r * 128:(r + 1) * 128], rhs=Maug[0:rws, c, :],
                                                      start=(c == 0), stop=(c == nch - 1)), reads=[EmC, Maug], writes=[pX])
                S.op("vector", lambda e: e.tensor_scalar(out=rcol[:, :], in0=pX[:, 256:257], scalar1=1e-30, scalar2=None,
                                                         op0=ALU.add), reads=[pX], writes=[rcol])
                S.op("vector", lambda e: e.reciprocal(out=rcol[:, :], in_=rcol[:, :]), reads=[rcol], writes=[rcol])
                if r == 0:
                    S.op("vector", lambda e: e.tensor_scalar(out=imp[:, 0:ncol], in0=pX[:, 0:ncol], scalar1=rcol[:, 0:1],
                                                             scalar2=None, op0=ALU.mult), reads=[pX, rcol], writes=[imp])
                else:
                    S.op("vector", lambda e: e.scalar_tensor_tensor(out=imp[:, 0:ncol], in0=pX[:, 0:ncol], scalar=rcol[:, 0:1],
                                                                    in1=imp[:, 0:ncol], op0=ALU.mult, op1=ALU.add),
                         reads=[pX, rcol, imp], writes=[imp])
            S.op("vector", lambda e: e.tensor_copy(out=score[:, 0:ncol], in_=imp[:, 0:ncol]), reads=[imp], writes=[score])
            S.op("vector", lambda e: e.memset(score[:, 0:1], -1.0), writes=[score])
            S.op("vector", lambda e: e.memset(score[0:64, ncol - 1:ncol], -1.0), writes=[score])
            S.op("vector", lambda e: e.max(out=m8[:, 0:8], in_=score[:, 0:ncol]), reads=[score], writes=[m8])
            S.op("vector", lambda e: e.match_replace(out=sc2[:, 0:ncol], in_to_replace=m8[:, 0:8], in_values=score[:, 0:ncol],
                                                     imm_value=-2.0), reads=[m8, score], writes=[sc2])
            S.op("vector", lambda e: e.max(out=m8[:, 8:16], in_=sc2[:, 0:ncol]), reads=[sc2, m8], writes=[m8])
            S.op("vector", lambda e: e.tensor_scalar(out=selF[:, 0:ncol], in0=score[:, 0:ncol], scalar1=m8[:, 12:13], scalar2=None,
                                                     op0=ALU.is_ge), reads=[score, m8], writes=[selF])
            S.op("vector", lambda e: e.memset(selF[:, 0:1], 1.0), writes=[selF])
            S.op("vector", lambda e: e.memset(selF[0:64, ncol - 1:ncol], 1.0), writes=[selF])
            for c in range((ncol + 127) // 128):
                S.op("tensor", lambda e: e.transpose(out=pX[:, c * 128:(c + 1) * 128], in_=selF[:, c * 128:(c + 1) * 128],
                                                     identity=idf[:, :]), reads=[selF, idf], writes=[pX])
                S.op("vector", lambda e: e.tensor_copy(out=selT[:, c, :], in_=pX[:, c * 128:(c + 1) * 128]), reads=[pX], writes=[selT])
            q["topk_done"] = True

        units = []
        for c in range(nch):
            rws = min(128, Wp - c * 128)
            if c == nch - 1:
                mk = ("sb", cmpm[0:rws, (i % 16) + (0 if i < 16 else 16), :], [cmpm])
            elif c == 0:
                mk = ("sb", cmpm[0:rws, 32, :], [cmpm])
            else:
                mk = None
            u = dict(q=q, QT=QT, ksrc=KcT[:, c * 128:c * 128 + rws], ktrack=KcT, vsrc=Vc[0:rws, c, :], vtrack=Vc, rows=rws, mask=mk,
                     b=0, first=(c == 0), last=(c == nch - 1), emc=(c if i >= 8 else None), post=[])
            if c == nch - 1:
                u["post"].append(evac(0))
                if i >= 8:
                    u["post"].append(topk)
            units.append(u)
        for n, t in enumerate(wt):
            if t == i:
                mk = ("sb", causT[:, :], [causT])
            elif t == i - 4:
                mk = ("sb", upT[:, :], [upT])
            else:
                mk = None
            u = dict(q=q, QT=QT, ksrc=kwr[:, t % 6, :], ktrack=kwb[t % 6], vsrc=vwr[:, t % 6, :], vtrack=vwb[t % 6], rows=128, mask=mk,
                     b=2, first=(n == 0), last=(n == len(wt) - 1), post=[])
            if n == len(wt) - 1:
                u["post"].append(evac(2))
            units.append(u)
        for t in range(i + 1):
            if t == i:
                mk = ("sb", causT[:, :], [causT])
            elif i >= 8:
                mk = ("sel", t)
            else:
                mk = None
            u = dict(q=q, QT=QT, ksrc=KsT[:, t * 128:(t + 1) * 128], ktrack=KsT, vsrc=Vs[:, t, :], vtrack=Vs, rows=128, mask=mk,
                     b=1, first=(t == 0), last=(t == i), post=[], needs_sel=(mk is not None and mk[0] == "sel"))
            if t == i:
                u["post"].append(evac(1))
            units.append(u)
        return units, q

    def epilogue_parts(i):
        par = i % 2
        rows = slice(i * 128, (i + 1) * 128)
        h1, gsil, bg = h1s[par], gsils[par], bgs[par]
        OT = ObTs[par]
        pSum = pSums[par]
        sumsb = sumsbs[par]
        yt = yts[par]

        def combine(b):
            for r in range(4):
                S.op("tensor", lambda e: e.transpose(out=pX[:, r * 128:(r + 1) * 128], in_=OT[b][:, r * 128:(r + 1) * 128],
                                                     identity=idf[:, :]), reads=[OT[b], idf], writes=[pX])
            for r in range(4):
                cs_ = slice(r * 128, (r + 1) * 128)
                if b == 0:
                    S.op("vector", lambda e: e.tensor_scalar(out=o_[:, cs_], in0=pX[:, cs_], scalar1=coef[:, r * 3 + b:r * 3 + b + 1],
                                                             scalar2=None, op0=ALU.mult), reads=[pX, coef], writes=[o_])
                else:
                    S.op("vector", lambda e: e.scalar_tensor_tensor(out=o_[:, cs_], in0=pX[:, cs_], scalar=coef[:, r * 3 + b:r * 3 + b + 1],
                                                                    in1=o_[:, cs_], op0=ALU.mult, op1=ALU.add),
                         reads=[pX, coef, o_], writes=[o_])

        def part1():
            S.op("scalar", lambda e: e.copy(out=sumsb[:, :], in_=pSum[:, :]), reads=[pSum], writes=[sumsb])
            for r in range(4):
                S.op("tensor", lambda e: e.matmul(pX[:, r * 3:(r + 1) * 3], lhsT=sumsb[0:3, r * 128:(r + 1) * 128], rhs=idf[0:3, 0:3],
                                                  start=True, stop=True), reads=[sumsb, idf], writes=[pX])
            S.op("vector", lambda e: e.tensor_scalar(out=coef[:, :], in0=pX[:, 0:12], scalar1=1e-30, scalar2=None, op0=ALU.add),
                 reads=[pX], writes=[coef])
            S.op("vector", lambda e: e.reciprocal(out=coef[:, :], in_=coef[:, :]), reads=[coef], writes=[coef])
            S.op("vector", lambda e: e.tensor_tensor(out=coef[:, :], in0=coef[:, :], in1=bg[:, :], op=ALU.mult), reads=[coef, bg], writes=[coef])
            combine(0)

        def part2():
            combine(2)
            combine(1)
            S.op("gpsimd", lambda e: e.tensor_mul(out=ogf[:, :], in0=o_[:, :], in1=gsil[:, :]), reads=[o_, gsil], writes=[ogf])

        def part3():
            for c in range(4):
                S.op("tensor", lambda e: e.transpose(out=pX[:, c * 128:(c + 1) * 128], in_=ogf[:, c * 128:(c + 1) * 128],
                                                     identity=idf[:, :]), reads=[ogf, idf], writes=[pX])
            S.op("vector", lambda e: e.tensor_copy(out=ogT2[:, :, :].rearrange("p a b -> p (a b)"), in_=pX[:, :]), reads=[pX], writes=[ogT2])
            for hf in range(2):
                for c in range(4):
                    S.op("tensor", lambda e: e.matmul(pX[:, :], lhsT=ogT2[:, c, :], rhs=Wo2[:, c, hf * 512:(hf + 1) * 512],
                                                      start=(c == 0), stop=(c == 3)), reads=[ogT2, Wo2], writes=[pX])
                S.op("vector", lambda e: e.scalar_tensor_tensor(out=yt[:, hf * 512:(hf + 1) * 512], in0=h1[:, hf * 512:(hf + 1) * 512],
                                                                scalar=0.25, in1=pX[:, :], op0=ALU.mult, op1=ALU.add),
                     reads=[h1, pX], writes=[yt])
            S.dma("sync", Y2[rows, :], yt[:, :], reads=[yt], writes=[bY2[i]])
            if i % 8 == 7:
                rs_chunk(i // 8)

        return [part1, part2, part3]

    LOOK = 2
    prologue(0)
    pending = []
    for i in range(NT):
        units, q = make_units(i)
        nu = len(units)
        hooks = {}
        pos = [min(nu - 1, 1), min(nu - 1, 4), min(nu - 1, 7)]
        for k, f in enumerate(pending):
            hooks.setdefault(pos[k], []).append(f)
        if i + 1 < NT:
            hooks.setdefault(min(nu - 1, 9), []).append(lambda i=i: prologue(i + 1))
        for n in range(nu + LOOK):
            if n < nu:
                if units[n].get("needs_sel"):
                    assert q["topk_done"]
                stageA(units[n])
            if n - LOOK >= 0:
                stageB(units[n - LOOK])
            for f in hooks.get(n, ()):
                f()
        assert q["nsum"] == q["total_sum"]
        pending = epilogue_parts(i)
    for f in pending:
        f()
    S.pop()
    S.pop()
    if upto == 3:
        return dbg_out(Y2, bY2)

    S.push()
    ple = PleCtx(S)
    ple.load_weights("sync", pg[1][:, :], wg[1], we[1], stage)
    fn = S.sb("fn", [128, 1024], F32)
    S.dma("sync", fn[:, :], fng[:, :], writes=[fn])
    hs = [S.sb("h", [128, 1024], F32) for i in range(2)]
    ob = [S.sb("ob", [128, 1024], F32) for i in range(2)]
    sqs = S.sb("sqs", [128, 1024], F32)
    ss = S.sb("ss", [128, 1], F32)
    rs = S.sb("rs", [128, 1], F32)
    pA = S.ps("pA", [128, 8, 128], BF16)
    pG = S.ps("pG", [128, 512], F32)
    pE = S.ps("pE", [128, 512], F32)
    for u in range(TL // 128):
        rows = slice(u * 128, (u + 1) * 128)
        h = hs[u % 2]
        S.dma("sync", h[:, :], Y2s[rows, :], reads=[bY2s[u // 2]], writes=[h])
        ple.apply(h, p1s[rows, :], ident, pA, pG, pE)
        rms_rstd(S, h, 1024, sqs, ss, rs)
        o = ob[u % 2]
        S.op("vector", lambda e: e.scalar_tensor_tensor(out=o[:, :], in0=h[:, :], scalar=rs[:, 0:1], in1=fn[:, :],
                                                        op0=ALU.mult, op1=ALU.mult), reads=[h, rs, fn], writes=[o])
        S.dma("gpsimd", out[rows, :], o[:, :], reads=[o])
    S.finish()
    return nc, S.ninstr


def _colgain(g):
    return np.ascontiguousarray(np.asarray(g, np.float32).reshape(8, 128).T)


def _consts(T):
    half = 128
    inv = (10000.0 ** (-np.arange(half, dtype=np.float32) / np.float32(half))).astype(np.float32)
    pos = np.arange(T, dtype=np.float32)
    ang = (inv[:, None] * pos[None, :]).astype(np.float32)
    c = dict(cosT=np.cos(ang).astype(np.float32), sinT=np.sin(ang).astype(np.float32))
    p = np.arange(128)
    c["causT"] = (p[:, None] <= p[None, :]).astype(np.float32)
    c["upT"] = (p[:, None] > p[None, :]).astype(np.float32)
    c["ident"] = np.eye(128, dtype=np.float32)
    ex = np.zeros((128, 64, 128), np.float32)
    for tt in range(64):
        for hb in range(2):
            ex[(2 * tt + hb) % 128, tt, hb * 64:(hb + 1) * 64] = 1.0
    c["ex"] = ex.reshape(128, 64 * 128)
    m = np.zeros((1024, 257), np.float32)
    for j in range(256):
        for (off, w) in ((0, 1.0), (1, 2.0), (2, 2.0), (3, 2.0), (4, 1.0)):
            n = 4 * j + off
            if n < 1024:
                m[n, j] = w
    m[:, 256] = 1.0
    c["maug"] = np.ascontiguousarray(m.reshape(8, 128, 257).transpose(1, 0, 2)).reshape(128, 8 * 257)
    lane = np.arange(128)[:, None]
    ql = np.arange(128)[None, :]
    cm = np.zeros((128, 33, 128), np.float32)
    for res in range(16):
        v = (ql >= 16 * lane + 15 - 128 * res).astype(np.float32)
        b = v.copy()
        a = v.copy()
        a[0, :] = 0.0
        cm[:, res, :] = a
        cm[:, 16 + res, :] = b
    fm = np.ones((128, 128), np.float32)
    fm[0, :] = 0.0
    cm[:, 32, :] = fm
    c["cmpm"] = cm.reshape(128, 33 * 128)
    os_ = np.zeros((128, 3, 3), np.float32)
    for b in range(3):
        os_[:, b, b] = 1.0
    c["onesel"] = os_.reshape(128, 9)
    return c


def _head_consts(hd):
    lg = np.log1p(-(np.float32(2.0) ** np.float32(-5.0 - hd))).astype(np.float32)
    p = np.arange(128, dtype=np.float32)
    qd = np.exp((p + 1.0) * lg).astype(np.float32)
    kdv = (np.exp(-(p + 1.0) * lg) / 16.0).astype(np.float32)
    return dict(qdec=np.ascontiguousarray(np.broadcast_to(np.tile(qd, 4)[None, :], (128, 512))).astype(np.float32),
                kdec=np.ascontiguousarray(np.broadcast_to(np.tile(kdv, 4)[None, :], (128, 512))).astype(np.float32),
                cdec=np.full((128, 1), np.exp(np.float32(128.0) * lg), np.float32))


def _inmaps(T, B, I):
    C = _consts(T)
    TL = T // 4
    maps = []
    ca = np.ascontiguousarray
    for b in range(B):
        for g in range(4):
            m = dict(C)
            m.update(_head_consts(g))
            m["x"] = ca(I["x"][b, :T])
            m["p0"] = ca(I["p"][0, b, :T])
            p1 = I["p"][1, b, :T].reshape(T // 1024, 4, 256, 256)[:, g].reshape(TL, 256)
            m["p1s"] = ca(p1)
            wi = I["ret_w_in"][0]
            m["r_w_in"] = ca(np.concatenate([wi[:, g * 256:(g + 1) * 256], wi[:, 1024 + g * 256:1024 + (g + 1) * 256],
                                             wi[:, 2048 + g * 512:2048 + (g + 1) * 512], wi[:, 4096 + g * 512:4096 + (g + 1) * 512]], axis=1))
            m["r_g_in"] = _colgain(I["ret_norm"][0])
            m["r_gn"] = ca(np.broadcast_to(I["ret_gn"][0][g * 512:(g + 1) * 512][None, :], (128, 512)))
            m["r_w_out"] = ca(I["ret_w_out"][0][g * 512:(g + 1) * 512, :])
            for l in range(2):
                m["pg%d" % l] = _colgain(I["ple_norm"][l])
                m["wg%d" % l] = ca(I["ple_w_gate"][l])
                m["we%d" % l] = ca(I["ple_w_emb"][l])
            m["fng"] = ca(np.broadcast_to(I["final_norm"][None, :], (128, 1024)))
            m["kv_g"] = _colgain(I["kv_norm"])
            kw = I["kv_w"]
            order = [0, 1, 2, 4, 3, 5]
            m["kv_w"] = ca(np.concatenate([kw[:, pt * 512 + g * 128: pt * 512 + (g + 1) * 128] for pt in order], axis=1))
            m["peT_k"] = ca(I["cmp_pe_k"].T)
            m["peT_v"] = ca(I["cmp_pe_v"].T)
            m["w1_k"] = ca(I["cmp_w1_k"])
            m["w1_v"] = ca(I["cmp_w1_v"])
            m["w2_k"] = ca(I["cmp_w2_k"])
            m["w2_v"] = ca(I["cmp_w2_v"])
            m["n_g"] = _colgain(I["nsa_norm"][0])
            nw = I["nsa_w_in"][0]
            m["n_w_in"] = ca(np.concatenate([nw[:, g * 512:(g + 1) * 512], nw[:, 2048 + g * 512:2048 + (g + 1) * 512],
                                             nw[:, 4096 + g * 12:4096 + (g + 1) * 12]], axis=1))
            m["n_w_out"] = ca(I["nsa_w_out"][0][g * 512:(g + 1) * 512, :])
            maps.append({k: np.asarray(v, np.float32) for k, v in m.items()})
    return maps


_PROG = {}


def run_module(I, T, B):
    key = (T, B)
    if key not in _PROG:
        groups = [[b * 4 + g for g in range(4)] for b in range(B)]
        _PROG[key] = build_program(T, groups)[0]
    nc = _PROG[key]
    maps = _inmaps(T, B, I)
    res = run_bass_kernel_spmd(nc, maps, core_ids=list(range(4 * B)))
    outp = np.empty((B, T, 1024), np.float32)
    for b in range(B):
        for g in range(4):
            o = res.results[b * 4 + g]["out"].reshape(T // 1024, 256, 1024)
            outp[b].reshape(T // 1024, 4, 256, 1024)[:, g] = o
    return outp


def kernel(**inputs):
    I = {k: np.asarray(v) for k, v in inputs.items()}
    return run_module(I, 16384, 2)
```

```python
import contextlib
import numpy as np
import concourse.bass as bass
import concourse.mybir as mybir
from concourse.bass_utils import run_bass_kernel_spmd

F32 = mybir.dt.float32
BF16 = mybir.dt.bfloat16
AF = mybir.ActivationFunctionType
ALU = mybir.AluOpType
AX = mybir.AxisListType
ENGS = ("tensor", "vector", "scalar", "gpsimd", "sync")
EPS = 1e-6
SCALE = 128 ** -0.5


class Buf:
    __slots__ = ("name", "t", "w", "r")

    def __init__(self, name, t=None):
        self.name = name
        self.t = t
        self.w = None
        self.r = []

    def __getitem__(self, idx):
        return self.t[idx]


class _Rec:
    def __init__(self):
        self.call = None

    def __getattr__(self, name):
        def f(*a, **kw):
            assert self.call is None
            self.call = (name, a, kw)
            return self
        return f


class Sched:
    def __init__(self, nc, n_dma_sems=12):
        self.nc = nc
        self.sems = {}
        self.cnt = {}
        for e in ENGS:
            self.sems[e] = nc.alloc_semaphore("s_" + e)
            self.cnt[e] = 0
        self.sems["cc"] = nc.alloc_semaphore("s_cc")
        self.cnt["cc"] = 0
        self.dq = {}
        for q in ("sync", "gpsimd", "scalar"):
            lst = []
            for i in range(n_dma_sems):
                k = "d_%s_%d" % (q, i)
                self.sems[k] = nc.alloc_semaphore(k)
                self.cnt[k] = 0
                lst.append(k)
            self.dq[q] = [lst, 0]
        self.known = {e: {} for e in ENGS}
        self.E = {e: getattr(nc, e) for e in ENGS}
        self.ninstr = 0
        self.uid = 0
        self.stacks = [contextlib.ExitStack()]

    def push(self):
        self.stacks.append(contextlib.ExitStack())

    def pop(self):
        self.barrier()
        self.stacks.pop().close()

    def _nm(self, name):
        self.uid += 1
        return "%s_%d" % (name, self.uid)

    def sb(self, name, shape, dtype):
        nm = self._nm(name)
        return Buf(nm, self.stacks[-1].enter_context(self.nc.sbuf_tensor(nm, list(shape), dtype)))

    def ps(self, name, shape, dtype=F32):
        nm = self._nm(name)
        return Buf(nm, self.stacks[-1].enter_context(self.nc.psum_tensor(nm, list(shape), dtype)))

    def dr(self, name, shape, dtype):
        return self.nc.dram_tensor(self._nm(name), list(shape), dtype)

    def _need(self, eng, deps):
        kn = self.known[eng]
        best = {}
        for d in deps:
            if d is None:
                continue
            k, v = d
            if k == eng and eng == "tensor":
                continue
            if kn.get(k, 0) >= v:
                continue
            if best.get(k, 0) < v:
                best[k] = v
        return best

    def _emit_waits(self, eng, best):
        for k, v in best.items():
            self.E[eng].wait_ge(self.sems[k], v)
            self.known[eng][k] = v
            self.ninstr += 1

    @staticmethod
    def _deps(reads, writes):
        deps = []
        for b in reads:
            deps.append(b.w)
        for b in writes:
            deps.append(b.w)
            deps.extend(b.r)
        return deps

    @staticmethod
    def _mark(ev, reads, writes):
        for b in reads:
            b.r.append(ev)
        for b in writes:
            b.w = ev
            b.r = []

    def op(self, eng, fn, reads=(), writes=()):
        self._emit_waits(eng, self._need(eng, self._deps(reads, writes)))
        self.cnt[eng] += 1
        rec = _Rec()
        fn(rec)
        name, a, kw = rec.call
        getattr(self.E[eng], name)(*a, **kw).then_inc(self.sems[eng], 1)
        self.ninstr += 1
        ev = (eng, self.cnt[eng])
        self._mark(ev, reads, writes)
        return ev

    def dma(self, q, out, in_, reads=(), writes=(), **kw):
        lst, idx = self.dq[q]
        k = lst[idx % len(lst)]
        self.dq[q][1] = idx + 1
        deps = self._deps(reads, writes)
        if self.cnt[k] > 0:
            deps.append((k, self.cnt[k]))
        self._emit_waits(q, self._need(q, deps))
        self.cnt[k] += 16
        self.E[q].dma_start(out=out, in_=in_, **kw).then_inc(self.sems[k], 16)
        self.ninstr += 1
        ev = (k, self.cnt[k])
        self._mark(ev, reads, writes)
        return ev

    def collective(self, kind, op, groups, in_ap, out_ap, reads=(), writes=()):
        deps = self._deps(reads, writes)
        if self.cnt["cc"] > 0:
            deps.append(("cc", self.cnt["cc"]))
        self._emit_waits("gpsimd", self._need("gpsimd", deps))
        self.cnt["cc"] += 1
        self.E["gpsimd"].collective_compute(kind, op, replica_groups=groups, ins=[in_ap], outs=[out_ap]).then_inc(
            self.sems["cc"], 1)
        self.ninstr += 1
        ev = ("cc", self.cnt["cc"])
        self._mark(ev, reads, writes)
        return ev

    def _all_events(self):
        return [(k, v) for k, v in self.cnt.items() if v > 0]

    def barrier(self):
        ev = self._all_events()
        for e in ENGS:
            self._emit_waits(e, self._need(e, [d for d in ev if d[0] != e]))

    def finish(self):
        deps = [(k, v) for k, v in self.cnt.items() if v > 0 and (k.startswith("d_") or k == "cc")]
        self._emit_waits("sync", self._need("sync", deps))
        self.barrier()
        while self.stacks:
            self.stacks.pop().close()


def load_w_bf16(S, q, dst, dst_fn, src_ap, stage, kc, ncols, gain=None):
    for k in range(kc):
        c0 = 0
        while c0 < ncols:
            cw = min(2048, ncols - c0)
            S.dma(q, stage[:, 0:cw], src_ap[:, k, c0:c0 + cw], writes=[stage])
            if gain is None:
                S.op("gpsimd", lambda e: e.tensor_copy(out=dst_fn(k, c0, cw), in_=stage[:, 0:cw]),
                     reads=[stage], writes=[dst])
            else:
                S.op("gpsimd", lambda e: e.tensor_scalar(out=dst_fn(k, c0, cw), in0=stage[:, 0:cw],
                                                         scalar1=gain[:, k:k + 1], scalar2=None, op0=ALU.mult),
                     reads=[stage, gain], writes=[dst])
            c0 += cw


def load_const_bf16(S, q, dst, dst_ap, src_ap, stage, ncols):
    S.dma(q, stage[:, 0:ncols], src_ap, writes=[stage])
    S.op("gpsimd", lambda e: e.tensor_copy(out=dst_ap, in_=stage[:, 0:ncols]), reads=[stage], writes=[dst])


def rms_rstd(S, xt, D, sqs, ss, rs):
    S.op("scalar", lambda e: e.activation(out=sqs[:, :], in_=xt[:, :], func=AF.Square, accum_out=ss[:, :]),
         reads=[xt], writes=[sqs, ss])
    S.op("vector", lambda e: e.tensor_scalar(out=rs[:, :], in0=ss[:, :], scalar1=1.0 / D, scalar2=EPS,
                                             op0=ALU.mult, op1=ALU.add), reads=[ss], writes=[rs])
    S.op("scalar", lambda e: e.activation(out=rs[:, :], in_=rs[:, :], func=AF.Sqrt), reads=[rs], writes=[rs])
    S.op("vector", lambda e: e.reciprocal(out=rs[:, :], in_=rs[:, :]), reads=[rs], writes=[rs])


def rmsnorm_T(S, xt, ident, sqs, ss, rs, xn, pT, dstT, tok0):
    rms_rstd(S, xt, 1024, sqs, ss, rs)
    S.op("vector", lambda e: e.tensor_scalar(out=xn[:, :], in0=xt[:, :], scalar1=rs[:, 0:1], scalar2=None,
                                             op0=ALU.mult), reads=[xt, rs], writes=[xn])
    for k in range(8):
        S.op("tensor", lambda e: e.transpose(out=pT[:, k, :], in_=xn[:, k * 128:(k + 1) * 128], identity=ident[:, :]),
             reads=[xn, ident], writes=[pT])
    S.op("vector", lambda e: e.tensor_copy(out=dstT[:, :, tok0:tok0 + 128], in_=pT[:, :, :]), reads=[pT], writes=[dstT])


class PleCtx:
    def __init__(self, S):
        self.S = S
        self.Wg = S.sb("pleWg", [128, 8, 1024], BF16)
        self.We = S.sb("pleWe", [128, 2, 1024], BF16)
        self.gain = S.sb("pleGain", [128, 8], F32)
        self.sqs = S.sb("ple_sqs", [128, 1024], F32)
        self.ss = S.sb("ple_ss", [128, 1], F32)
        self.rs = S.sb("ple_rs", [128, 1], F32)
        self.xn = S.sb("ple_xn", [128, 1024], BF16)
        self.hnT = S.sb("ple_hnT", [128, 8, 128], BF16)
        self.pf = [S.sb("ple_pf", [128, 256], F32) for i in range(2)]
        self.pb = S.sb("ple_pb", [128, 256], BF16)
        self.pT = S.sb("ple_pT", [128, 2, 128], BF16)
        self.sig = S.sb("ple_sig", [128, 512], F32)
        self.prod = S.sb("ple_prod", [128, 512], F32)
        self.npf = 0

    def load_weights(self, q, g_ap, wg_ap, we_ap, stage):
        S = self.S
        S.dma(q, self.gain[:, :], g_ap, writes=[self.gain])
        load_w_bf16(S, q, self.Wg, lambda k, c0, cw: self.Wg[:, k, c0:c0 + cw],
                    wg_ap.rearrange("(k p) n -> p k n", p=128), stage, 8, 1024, gain=self.gain)
        load_w_bf16(S, q, self.We, lambda k, c0, cw: self.We[:, k, c0:c0 + cw],
                    we_ap.rearrange("(k p) n -> p k n", p=128), stage, 2, 1024)

    def apply(self, h, p_ap, ident, pA, pG, pE, dmaq="gpsimd"):
        S = self.S
        pf = self.pf[self.npf % 2]
        self.npf += 1
        S.dma(dmaq, pf[:, :], p_ap, writes=[pf])
        rmsnorm_T(S, h, ident, self.sqs, self.ss, self.rs, self.xn, pA, self.hnT, 0)
        S.op("gpsimd", lambda e: e.tensor_copy(out=self.pb[:, :], in_=pf[:, :]), reads=[pf], writes=[self.pb])
        for k in range(2):
            S.op("tensor", lambda e: e.transpose(out=pA[:, k, :], in_=self.pb[:, k * 128:(k + 1) * 128], identity=ident[:, :]),
                 reads=[self.pb, ident], writes=[pA])
        S.op("vector", lambda e: e.tensor_copy(out=self.pT[:, :, :], in_=pA[:, 0:2, :]), reads=[pA], writes=[self.pT])
        for hf in range(2):
            cs = slice(hf * 512, (hf + 1) * 512)
            for k in range(8):
                S.op("tensor", lambda e: e.matmul(pG[:, :], lhsT=self.hnT[:, k, :], rhs=self.Wg[:, k, cs],
                                                  start=(k == 0), stop=(k == 7)), reads=[self.hnT, self.Wg], writes=[pG])
            for k in range(2):
                S.op("tensor", lambda e: e.matmul(pE[:, :], lhsT=self.pT[:, k, :], rhs=self.We[:, k, cs],
                                                  start=(k == 0), stop=(k == 1)), reads=[self.pT, self.We], writes=[pE])
            S.op("scalar", lambda e: e.activation(out=self.sig[:, :], in_=pG[:, :], func=AF.Sigmoid), reads=[pG], writes=[self.sig])
            S.op("vector", lambda e: e.tensor_tensor(out=self.prod[:, :], in0=self.sig[:, :], in1=pE[:, :], op=ALU.mult),
                 reads=[self.sig, pE], writes=[self.prod])
            S.op("gpsimd", lambda e: e.tensor_add(out=h[:, cs], in0=h[:, cs], in1=self.prod[:, :]), reads=[h, self.prod], writes=[h])


def build_program(T, groups, upto=4):
    nc = bass.Bass("TRN2", target_bir_lowering=False)
    NT = T // 128
    NS = T // 512
    NCH = T // 1024
    NC16 = T // 2048
    TL = T // 4

    def din(name, shape):
        return nc.dram_tensor(name, list(shape), F32, kind="ExternalInput").ap()

    x = din("x", [T, 1024])
    p0 = din("p0", [T, 256])
    p1s = din("p1s", [TL, 256])
    identd = din("ident", [128, 128])
    r_w_in = din("r_w_in", [1024, 1536])
    r_g_in = din("r_g_in", [128, 8])
    r_gn = din("r_gn", [128, 512])
    r_w_out = din("r_w_out", [512, 1024])
    cosT = din("cosT", [128, T])
    sinT = din("sinT", [128, T])
    qdec = din("qdec", [128, 512])
    kdec = din("kdec", [128, 512])
    cdec = din("cdec", [128, 1])
    causT_d = din("causT", [128, 128])
    upT_d = din("upT", [128, 128])
    pg = [din("pg%d" % l, [128, 8]) for l in range(2)]
    wg = [din("wg%d" % l, [1024, 1024]) for l in range(2)]
    we = [din("we%d" % l, [256, 1024]) for l in range(2)]
    fng = din("fng", [128, 1024])
    kv_g = din("kv_g", [128, 8])
    kv_w = din("kv_w", [1024, 768])
    peT_k = din("peT_k", [128, 32])
    peT_v = din("peT_v", [128, 32])
    w1_k = din("w1_k", [4096, 256])
    w1_v = din("w1_v", [4096, 256])
    w2_k = din("w2_k", [256, 128])
    w2_v = din("w2_v", [256, 128])
    n_g = din("n_g", [128, 8])
    n_w_in = din("n_w_in", [1024, 1036])
    n_w_out = din("n_w_out", [512, 1024])
    ex_d = din("ex", [128, 64 * 128])
    maug_d = din("maug", [128, 8 * 257])
    cmpm_d = din("cmpm", [128, 33 * 128])
    onesel_d = din("onesel", [128, 9])
    out = nc.dram_tensor("out", [TL, 1024], F32, kind="ExternalOutput").ap()
    dbg = nc.dram_tensor("dbg", [T, 1024], F32, kind="ExternalOutput").ap() if upto < 4 else None

    def dbg_out(src, bufs):
        for c in range(T // 1024):
            S.dma("sync", dbg[c * 1024:(c + 1) * 1024, :], src[c * 1024:(c + 1) * 1024, :], reads=bufs[c * 8:(c + 1) * 8])
        S.finish()
        return nc, S.ninstr

    S = Sched(nc)
    Y1 = S.dr("Y1", [T, 1024], F32)
    Y1s = S.dr("Y1s", [T, 1024], F32)
    Y2 = S.dr("Y2", [T, 1024], F32)
    Y2s = S.dr("Y2s", [TL, 1024], F32)
    H1 = S.dr("H1", [T, 1024], F32)
    HT = S.dr("HT", [NT, 128, 1024], BF16)
    KW = S.dr("KW", [NT, 128, 128], BF16)
    VW = S.dr("VW", [NT, 128, 128], BF16)
    bY1 = [Buf("bY1_%d" % i) for i in range(NT)]
    bY1s = [Buf("bY1s_%d" % i) for i in range(NT)]
    bY2 = [Buf("bY2_%d" % i) for i in range(NT)]
    bY2s = [Buf("bY2s_%d" % i) for i in range(NCH)]
    bH1 = [Buf("bH1_%d" % i) for i in range(NT)]
    bHT = [Buf("bHT_%d" % i) for i in range(NT)]
    bKW = [Buf("bKW_%d" % i) for i in range(NT)]
    bVW = [Buf("bVW_%d" % i) for i in range(NT)]

    stage = S.sb("stage", [128, 2048], F32)
    idf = S.sb("idf", [128, 128], F32)
    ident = S.sb("identb", [128, 128], BF16)
    S.dma("sync", idf[:, :], identd[:, :], writes=[idf])
    S.op("vector", lambda e: e.tensor_copy(out=ident[:, :], in_=idf[:, :]), reads=[idf], writes=[ident])

    S.push()
    W = S.sb("W", [128, 8, 1536], BF16)
    Wo = S.sb("Wo", [128, 4, 1024], BF16)
    gin = S.sb("gin", [128, 8], F32)
    gnt = S.sb("gnt", [128, 512], F32)
    qd_t = S.sb("qd_t", [128, 512], F32)
    kd_t = S.sb("kd_t", [128, 512], F32)
    cd_t = S.sb("cd_t", [128, 1], F32)
    caus = S.sb("caus", [128, 128], F32)
    xts = [S.sb("xt", [128, 1024], F32) for i in range(2)]
    sqs = S.sb("sqs", [128, 1024], F32)
    ss = S.sb("ss", [128, 1], F32)
    rs = S.sb("rs", [128, 1], F32)
    xn = S.sb("xn", [128, 1024], BF16)
    xnT = S.sb("xnT", [128, 8, 512], BF16)
    cs = [S.sb("cs", [128, 512], F32) for i in range(2)]
    sn = [S.sb("sn", [128, 512], F32) for i in range(2)]
    tabs = [S.sb("tab", [128, 512], F32) for i in range(4)]
    raw = [S.sb("raw", [128, 512], F32) for i in range(4)]
    tmp = [S.sb("tmp", [128, 512], F32) for i in range(4)]
    qdT = S.sb("qdT", [128, 2, 512], BF16)
    kTp = S.sb("kTp", [128, 2, 512], BF16)
    vb = S.sb("vb", [128, 4, 512], BF16)
    gs = S.sb("gs", [128, 4, 512], F32)
    st_f = [S.sb("st_f", [128, 512], F32) for i in range(2)]
    st_b = [S.sb("st_b", [128, 512], BF16) for i in range(2)]
    kd = S.sb("kd", [128, 256], BF16)
    ST = S.sb("ST", [128, 128], BF16)
    osq = S.sb("osq", [128, 512], F32)
    stat = S.sb("stat", [128, 4], F32)
    on = S.sb("on", [128, 512], F32)
    og = S.sb("og", [128, 512], BF16)
    ogT = S.sb("ogT", [128, 4, 128], BF16)
    yo = [S.sb("yo", [128, 1024], F32) for i in range(2)]
    pA = S.ps("pA", [128, 8, 128], BF16)
    pB = [S.ps("pB", [128, 512], F32) for i in range(2)]
    pS = S.ps("pS", [128, 128], F32)
    pO = S.ps("pO", [128, 512], F32)
    pSt = [S.ps("pSt", [128, 512], F32) for i in range(2)]
    pY = S.ps("pY", [128, 512], F32)

    for (dst, src) in ((gin, r_g_in), (gnt, r_gn), (qd_t, qdec), (kd_t, kdec), (cd_t, cdec), (caus, causT_d)):
        S.dma("sync", dst[:, :], src[:, :], writes=[dst])
    load_w_bf16(S, "sync", W, lambda k, c0, cw: W[:, k, c0:c0 + cw],
                r_w_in.rearrange("(k p) n -> p k n", p=128), stage, 8, 1536, gain=gin)
    load_w_bf16(S, "sync", Wo, lambda k, c0, cw: Wo[:, k, c0:c0 + cw],
                r_w_out.rearrange("(k p) n -> p k n", p=128), stage, 4, 1024)
    for i in range(2):
        S.op("gpsimd", lambda e: e.memset(st_f[i][:, :], 0.0), writes=[st_f[i]])
        S.op("gpsimd", lambda e: e.memset(st_b[i][:, :], 0.0), writes=[st_b[i]])

    def allreduce_chunk(c):
        r0 = c * 1024
        tl = list(range(c * 8, c * 8 + 8))
        S.collective("AllReduce", ALU.add, groups, Y1[r0:r0 + 1024, :], Y1s[r0:r0 + 1024, :],
                     reads=[bY1[t] for t in tl], writes=[bY1s[t] for t in tl])

    for s in range(NS):
        t0 = s * 512
        cst, snt = cs[s % 2], sn[s % 2]
        S.dma("gpsimd", cst[:, :], cosT[:, t0:t0 + 512], writes=[cst])
        S.dma("gpsimd", snt[:, :], sinT[:, t0:t0 + 512], writes=[snt])
        S.op("gpsimd", lambda e: e.tensor_mul(out=tabs[0][:, :], in0=cst[:, :], in1=qd_t[:, :]), reads=[cst, qd_t], writes=[tabs[0]])
        S.op("gpsimd", lambda e: e.tensor_mul(out=tabs[1][:, :], in0=snt[:, :], in1=qd_t[:, :]), reads=[snt, qd_t], writes=[tabs[1]])
        S.op("gpsimd", lambda e: e.tensor_mul(out=tabs[2][:, :], in0=cst[:, :], in1=kd_t[:, :]), reads=[cst, kd_t], writes=[tabs[2]])
        S.op("gpsimd", lambda e: e.tensor_mul(out=tabs[3][:, :], in0=snt[:, :], in1=kd_t[:, :]), reads=[snt, kd_t], writes=[tabs[3]])
        for j in range(4):
            ti = s * 4 + j
            xt = xts[ti % 2]
            S.dma("sync", xt[:, :], x[ti * 128:(ti + 1) * 128, :], writes=[xt])
            rmsnorm_T(S, xt, ident, sqs, ss, rs, xn, pA, xnT, j * 128)
        for dc in range(4):
            pb = pB[dc % 2]
            for k in range(8):
                S.op("tensor", lambda e: e.matmul(pb[:, :], lhsT=W[:, k, dc * 128:(dc + 1) * 128], rhs=xnT[:, k, :],
                                                  start=(k == 0), stop=(k == 7)), reads=[W, xnT], writes=[pb])
            S.op("scalar", lambda e: e.copy(out=raw[dc][:, :], in_=pb[:, :]), reads=[pb], writes=[raw[dc]])
        for (eng, x1, x2, ct, st_, dst, ta, tb) in (("gpsimd", raw[0], raw[1], tabs[0], tabs[1], qdT, tmp[0], tmp[1]),
                                                    ("vector", raw[2], raw[3], tabs[2], tabs[3], kTp, tmp[2], tmp[3])):
            S.op(eng, lambda e: e.tensor_mul(out=ta[:, :], in0=x1[:, :], in1=ct[:, :]), reads=[x1, ct], writes=[ta])
            S.op(eng, lambda e: e.tensor_mul(out=tb[:, :], in0=x2[:, :], in1=st_[:, :]), reads=[x2, st_], writes=[tb])
            S.op(eng, lambda e: e.tensor_sub(out=dst[:, 0, :], in0=ta[:, :], in1=tb[:, :]), reads=[ta, tb], writes=[dst])
            S.op(eng, lambda e: e.tensor_mul(out=ta[:, :], in0=x1[:, :], in1=st_[:, :]), reads=[x1, st_], writes=[ta])
            S.op(eng, lambda e: e.tensor_mul(out=tb[:, :], in0=x2[:, :], in1=ct[:, :]), reads=[x2, ct], writes=[tb])
            S.op(eng, lambda e: e.tensor_add(out=dst[:, 1, :], in0=ta[:, :], in1=tb[:, :]), reads=[ta, tb], writes=[dst])
        for j in range(4):
            for (which, c0) in (("v", 512), ("g", 1024)):
                pb = pB[0] if which == "v" else pB[1]
                for k in range(8):
                    S.op("tensor", lambda e: e.matmul(pb[:, :], lhsT=xnT[:, k, j * 128:(j + 1) * 128], rhs=W[:, k, c0:c0 + 512],
                                                      start=(k == 0), stop=(k == 7)), reads=[W, xnT], writes=[pb])
                if which == "v":
                    S.op("scalar", lambda e: e.copy(out=vb[:, j, :], in_=pb[:, :]), reads=[pb], writes=[vb])
                else:
                    S.op("scalar", lambda e: e.activation(out=gs[:, j, :], in_=pb[:, :], func=AF.Silu), reads=[pb], writes=[gs])
        for j in range(4):
            ti = s * 4 + j
            tk = slice(j * 128, (j + 1) * 128)
            for dc in range(2):
                S.op("tensor", lambda e: e.transpose(out=pA[:, dc, :], in_=kTp[:, dc, tk], identity=ident[:, :]),
                     reads=[kTp, ident], writes=[pA])
            S.op("vector", lambda e: e.tensor_scalar(out=kd[:, :], in0=pA[:, 0:2, :].rearrange("p a b -> p (a b)"),
                                                     scalar1=cd_t[:, 0:1], scalar2=None, op0=ALU.mult),
                 reads=[pA, cd_t], writes=[kd])
            for dc in range(2):
                S.op("tensor", lambda e: e.matmul(pS[:, :], lhsT=kTp[:, dc, tk], rhs=qdT[:, dc, tk],
                                                  start=(dc == 0), stop=(dc == 1)), reads=[kTp, qdT], writes=[pS])
            S.op("vector", lambda e: e.tensor_tensor(out=ST[:, :], in0=pS[:, :], in1=caus[:, :], op=ALU.mult),
                 reads=[pS, caus], writes=[ST])
            S.op("tensor", lambda e: e.matmul(pO[:, :], lhsT=ST[:, :], rhs=vb[:, j, :], start=True, stop=False),
                 reads=[ST, vb], writes=[pO])
            for dc in range(2):
                S.op("tensor", lambda e: e.matmul(pO[:, :], lhsT=qdT[:, dc, tk], rhs=st_b[dc][:, :],
                                                  start=False, stop=(dc == 1)), reads=[qdT, st_b[dc]], writes=[pO])
            for dc in range(2):
                S.op("tensor", lambda e: e.matmul(pSt[dc][:, :], lhsT=kd[:, dc * 128:(dc + 1) * 128], rhs=vb[:, j, :],
                                                  start=True, stop=True), reads=[kd, vb], writes=[pSt[dc]])
                S.op("vector", lambda e: e.scalar_tensor_tensor(out=st_f[dc][:, :], in0=st_f[dc][:, :], scalar=cd_t[:, 0:1],
                                                                in1=pSt[dc][:, :], op0=ALU.mult, op1=ALU.add),
                     reads=[st_f[dc], cd_t, pSt[dc]], writes=[st_f[dc]])
                S.op("gpsimd", lambda e: e.tensor_copy(out=st_b[dc][:, :], in_=st_f[dc][:, :]), reads=[st_f[dc]], writes=[st_b[dc]])
            S.op("scalar", lambda e: e.activation(out=on[:, :], in_=pO[:, :], func=AF.Identity, accum_out=stat[:, 0:1]),
                 reads=[pO], writes=[on, stat])
            S.op("scalar", lambda e: e.activation(out=osq[:, :], in_=pO[:, :], func=AF.Square, accum_out=stat[:, 1:2]),
                 reads=[pO], writes=[osq, stat])
            S.op("vector", lambda e: e.tensor_scalar(out=stat[:, 0:2], in0=stat[:, 0:2], scalar1=1.0 / 512, scalar2=None,
                                                     op0=ALU.mult), reads=[stat], writes=[stat])
            S.op("vector", lambda e: e.tensor_tensor(out=stat[:, 2:3], in0=stat[:, 0:1], in1=stat[:, 0:1], op=ALU.mult),
                 reads=[stat], writes=[stat])
            S.op("vector", lambda e: e.tensor_tensor(out=stat[:, 2:3], in0=stat[:, 1:2], in1=stat[:, 2:3], op=ALU.subtract),
                 reads=[stat], writes=[stat])
            S.op("vector", lambda e: e.tensor_scalar(out=stat[:, 2:3], in0=stat[:, 2:3], scalar1=EPS, scalar2=None,
                                                     op0=ALU.add), reads=[stat], writes=[stat])
            S.op("scalar", lambda e: e.activation(out=stat[:, 2:3], in_=stat[:, 2:3], func=AF.Sqrt), reads=[stat], writes=[stat])
            S.op("vector", lambda e: e.reciprocal(out=stat[:, 3:4], in_=stat[:, 2:3]), reads=[stat], writes=[stat])
            S.op("vector", lambda e: e.tensor_scalar(out=on[:, :], in0=on[:, :], scalar1=stat[:, 0:1], scalar2=stat[:, 3:4],
                                                     op0=ALU.subtract, op1=ALU.mult), reads=[on, stat], writes=[on])
            S.op("gpsimd", lambda e: e.tensor_mul(out=on[:, :], in0=on[:, :], in1=gnt[:, :]), reads=[on, gnt], writes=[on])
            S.op("gpsimd", lambda e: e.tensor_mul(out=og[:, :], in0=on[:, :], in1=gs[:, j, :]), reads=[on, gs], writes=[og])
            for c in range(4):
                S.op("tensor", lambda e: e.transpose(out=pA[:, 2 + c, :], in_=og[:, c * 128:(c + 1) * 128], identity=ident[:, :]),
                     reads=[og, ident], writes=[pA])
            S.op("vector", lambda e: e.tensor_copy(out=ogT[:, :, :], in_=pA[:, 2:6, :]), reads=[pA], writes=[ogT])
            yt = yo[ti % 2]
            for hf in range(2):
                for c in range(4):
                    S.op("tensor", lambda e: e.matmul(pY[:, :], lhsT=ogT[:, c, :], rhs=Wo[:, c, hf * 512:(hf + 1) * 512],
                                                      start=(c == 0), stop=(c == 3)), reads=[ogT, Wo], writes=[pY])
                S.op("scalar", lambda e: e.copy(out=yt[:, hf * 512:(hf + 1) * 512], in_=pY[:, :]), reads=[pY], writes=[yt])
            S.dma("sync", Y1[ti * 128:(ti + 1) * 128, :], yt[:, :], reads=[yt], writes=[bY1[ti]])
            if ti >= 9 and (ti - 9) % 8 == 0:
                allreduce_chunk((ti - 9) // 8)
    allreduce_chunk(NCH - 1)
    if NCH >= 2 and (NT - 1) < 9 + 8 * (NCH - 2):
        pass
    issued = set([(ti - 9) // 8 for ti in range(NT) if ti >= 9 and (ti - 9) % 8 == 0] + [NCH - 1])
    for c in range(NCH):
        if c not in issued:
            allreduce_chunk(c)
    S.pop()
    if upto == 1:
        return dbg_out(Y1s, bY1s)

    S.push()
    KsT = S.sb("KsT", [128, T], BF16)
    Vs = S.sb("Vs", [128, NT, 128], BF16)
    KcT = S.sb("KcT", [128, NC16 * 128], BF16)
    Vc = S.sb("Vc", [128, NC16, 128], BF16)

    S.push()
    ple = PleCtx(S)
    ple.load_weights("sync", pg[0][:, :], wg[0], we[0], stage)
    kvg = S.sb("kvg", [128, 8], F32)
    Wkv = S.sb("Wkv", [128, 8, 768], BF16)
    S.dma("sync", kvg[:, :], kv_g[:, :], writes=[kvg])
    load_w_bf16(S, "sync", Wkv, lambda k, c0, cw: Wkv[:, k, c0:c0 + cw],
                kv_w.rearrange("(k p) n -> p k n", p=128), stage, 8, 768, gain=kvg)
    w1 = [S.sb("w1", [128, 32, 256], BF16) for i in range(2)]
    w2 = [S.sb("w2", [128, 2, 128], BF16) for i in range(2)]
    peT = [S.sb("peT", [128, 32], BF16) for i in range(2)]
    for i, (w1d, w2d, ped) in enumerate(((w1_k, w2_k, peT_k), (w1_v, w2_v, peT_v))):
        load_w_bf16(S, "sync", w1[i], lambda k, c0, cw: w1[i][:, k, c0:c0 + cw],
                    w1d.rearrange("(l d) h -> d l h", d=128), stage, 32, 256)
        load_w_bf16(S, "sync", w2[i], lambda k, c0, cw: w2[i][:, k, c0:c0 + cw],
                    w2d.rearrange("(k p) n -> p k n", p=128), stage, 2, 128)
        load_const_bf16(S, "sync", peT[i], peT[i][:, :], ped[:, :], stage, 32)
    cb = [S.sb("cb", [128, 2064], BF16) for i in range(2)]
    hs = [S.sb("h", [128, 1024], F32) for i in range(2)]
    ybs = [S.sb("yb", [128, 1024], F32) for i in range(2)]
    hTs = [S.sb("hT", [128, 8, 128], BF16) for i in range(2)]
    kwt = [S.sb("kwt", [128, 128], BF16) for i in range(2)]
    vwt = [S.sb("vwt", [128, 128], BF16) for i in range(2)]
    sqs = S.sb("sqs", [128, 1024], F32)
    ss = S.sb("ss", [128, 1], F32)
    rs = S.sb("rs", [128, 1], F32)
    xn = S.sb("xn", [128, 1024], BF16)
    ones1 = S.sb("ones1", [1, 128], BF16)
    bias_f = S.sb("bias_f", [1, 512], F32)
    bias_hi = S.sb("bias_hi", [1, 512], BF16)
    bias_hif = S.sb("bias_hif", [1, 512], F32)
    bias_lo = S.sb("bias_lo", [1, 512], BF16)
    xs_ = S.sb("xs_", [128, 256], F32)
    x2_ = S.sb("x2_", [128, 256], F32)
    sg_ = S.sb("sg_", [128, 256], F32)
    hid = S.sb("hid", [128, 256], BF16)
    hidT = S.sb("hidT", [128, 2, 128], BF16)
    pA = S.ps("pA", [128, 8, 128], BF16)
    pG = S.ps("pG", [128, 512], F32)
    pE = S.ps("pE", [128, 512], F32)
    pKT = S.ps("pKT", [128, 4, 128], F32)
    pKV = S.ps("pKV", [128, 256], F32)
    pH = S.ps("pH", [128, 256], F32)
    pC = S.ps("pC", [128, 128], F32)

    S.op("gpsimd", lambda e: e.memset(ones1[:, :], 1.0), writes=[ones1])
    for i in range(2):
        S.op("gpsimd", lambda e: e.memset(cb[i][:, 0:16], 0.0), writes=[cb[i]])
    for i in range(2):
        for l in range(32):
            S.op("tensor", lambda e: e.matmul(pH[0:1, :], lhsT=peT[i][:, l:l + 1], rhs=w1[i][:, l, :],
                                              start=(l == 0), stop=(l == 31)), reads=[peT[i], w1[i]], writes=[pH])
        S.op("scalar", lambda e: e.copy(out=bias_f[:, i * 256:(i + 1) * 256], in_=pH[0:1, :]), reads=[pH], writes=[bias_f])
    S.op("vector", lambda e: e.tensor_copy(out=bias_hi[:, :], in_=bias_f[:, :]), reads=[bias_f], writes=[bias_hi])
    S.op("vector", lambda e: e.tensor_copy(out=bias_hif[:, :], in_=bias_hi[:, :]), reads=[bias_hi], writes=[bias_hif])
    S.op("vector", lambda e: e.tensor_sub(out=bias_hif[:, :], in0=bias_f[:, :], in1=bias_hif[:, :]), reads=[bias_f, bias_hif], writes=[bias_hif])
    S.op("vector", lambda e: e.tensor_copy(out=bias_lo[:, :], in_=bias_hif[:, :]), reads=[bias_hif], writes=[bias_lo])

    for i in range(NT):
        rows = slice(i * 128, (i + 1) * 128)
        h = hs[i % 2]
        yb = ybs[i % 2]
        S.dma("sync", h[:, :], x[rows, :], writes=[h])
        S.dma("sync", yb[:, :], Y1s[rows, :], reads=[bY1s[i]], writes=[yb])
        S.op("vector", lambda e: e.tensor_add(out=h[:, :], in0=h[:, :], in1=yb[:, :]), reads=[h, yb], writes=[h])
        ple.apply(h, p0[rows, :], ident, pA, pG, pE)
        S.dma("sync", H1[rows, :], h[:, :], reads=[h], writes=[bH1[i]])
        hT = hTs[i % 2]
        rmsnorm_T(S, h, ident, sqs, ss, rs, xn, pA, hT, 0)
        S.dma("gpsimd", HT[i].rearrange("p (k t) -> p k t", k=8), hT[:, :, :], reads=[hT], writes=[bHT[i]])
        for a in range(4):
            for k in range(8):
                S.op("tensor", lambda e: e.matmul(pKT[:, a, :], lhsT=Wkv[:, k, a * 128:(a + 1) * 128], rhs=hT[:, k, :],
                                                  start=(k == 0), stop=(k == 7)), reads=[Wkv, hT], writes=[pKT])
        for k in range(8):
            S.op("tensor", lambda e: e.matmul(pKV[:, :], lhsT=hT[:, k, :], rhs=Wkv[:, k, 512:768],
                                              start=(k == 0), stop=(k == 7)), reads=[Wkv, hT], writes=[pKV])
        cc0 = 16 + (i % 16) * 128
        S.op("scalar", lambda e: e.copy(out=cb[0][:, cc0:cc0 + 128], in_=pKT[:, 0, :]), reads=[pKT], writes=[cb[0]])
        S.op("scalar", lambda e: e.copy(out=cb[1][:, cc0:cc0 + 128], in_=pKT[:, 1, :]), reads=[pKT], writes=[cb[1]])
        S.op("scalar", lambda e: e.copy(out=KsT[:, rows], in_=pKT[:, 2, :]), reads=[pKT], writes=[KsT])
        kw_, vw_ = kwt[i % 2], vwt[i % 2]
        S.op("scalar", lambda e: e.copy(out=kw_[:, :], in_=pKT[:, 3, :]), reads=[pKT], writes=[kw_])
        S.op("vector", lambda e: e.tensor_copy(out=Vs[:, i, :], in_=pKV[:, 0:128]), reads=[pKV], writes=[Vs])
        S.op("vector", lambda e: e.tensor_copy(out=vw_[:, :], in_=pKV[:, 128:256]), reads=[pKV], writes=[vw_])
        S.dma("gpsimd", KW[i], kw_[:, :], reads=[kw_], writes=[bKW[i]])
        S.dma("gpsimd", VW[i], vw_[:, :], reads=[vw_], writes=[bVW[i]])
        if i % 16 == 15:
            s16 = i // 16
            for X in range(2):
                bc = slice(X * 256, (X + 1) * 256)
                S.op("tensor", lambda e: e.matmul(pH[:, :], lhsT=ones1[0:1, :], rhs=bias_hi[0:1, bc], start=True, stop=False),
                     reads=[ones1, bias_hi], writes=[pH])
                S.op("tensor", lambda e: e.matmul(pH[:, :], lhsT=ones1[0:1, :], rhs=bias_lo[0:1, bc], start=False, stop=False),
                     reads=[ones1, bias_lo], writes=[pH])
                for l in range(32):
                    S.op("tensor", lambda e: e.matmul(pH[:, :], lhsT=cb[X][:, l:l + 2033:16], rhs=w1[X][:, l, :],
                                                      start=False, stop=(l == 31)), reads=[cb[X], w1[X]], writes=[pH])
                S.op("scalar", lambda e: e.copy(out=xs_[:, :], in_=pH[:, :]), reads=[pH], writes=[xs_])
                S.op("vector", lambda e: e.tensor_tensor(out=x2_[:, :], in0=xs_[:, :], in1=xs_[:, :], op=ALU.mult), reads=[xs_], writes=[x2_])
                S.op("vector", lambda e: e.tensor_scalar(out=x2_[:, :], in0=x2_[:, :], scalar1=0.044715, scalar2=1.0,
                                                         op0=ALU.mult, op1=ALU.add), reads=[x2_], writes=[x2_])
                S.op("vector", lambda e: e.tensor_tensor(out=x2_[:, :], in0=x2_[:, :], in1=xs_[:, :], op=ALU.mult), reads=[x2_, xs_], writes=[x2_])
                S.op("scalar", lambda e: e.activation(out=sg_[:, :], in_=x2_[:, :], func=AF.Sigmoid, scale=1.5957691216057308),
                     reads=[x2_], writes=[sg_])
                S.op("vector", lambda e: e.tensor_tensor(out=hid[:, :], in0=xs_[:, :], in1=sg_[:, :], op=ALU.mult), reads=[xs_, sg_], writes=[hid])
                for hc in range(2):
                    S.op("tensor", lambda e: e.transpose(out=pA[:, hc, :], in_=hid[:, hc * 128:(hc + 1) * 128], identity=ident[:, :]),
                         reads=[hid, ident], writes=[pA])
                S.op("vector", lambda e: e.tensor_copy(out=hidT[:, :, :], in_=pA[:, 0:2, :]), reads=[pA], writes=[hidT])
                if X == 0:
                    for hc in range(2):
                        S.op("tensor", lambda e: e.matmul(pC[:, :], lhsT=w2[0][:, hc, :], rhs=hidT[:, hc, :],
                                                          start=(hc == 0), stop=(hc == 1)), reads=[w2[0], hidT], writes=[pC])
                    S.op("scalar", lambda e: e.copy(out=KcT[:, s16 * 128:(s16 + 1) * 128], in_=pC[:, :]), reads=[pC], writes=[KcT])
                else:
                    for hc in range(2):
                        S.op("tensor", lambda e: e.matmul(pC[:, :], lhsT=hidT[:, hc, :], rhs=w2[1][:, hc, :],
                                                          start=(hc == 0), stop=(hc == 1)), reads=[w2[1], hidT], writes=[pC])
                    S.op("scalar", lambda e: e.copy(out=Vc[:, s16, :], in_=pC[:, :]), reads=[pC], writes=[Vc])
                S.op("vector", lambda e: e.tensor_copy(out=cb[X][:, 0:16], in_=cb[X][:, 2048:2064]), reads=[cb[X]], writes=[cb[X]])
    S.pop()

    if upto == 2:
        S.pop()
        return dbg_out(H1, bH1)
    S.push()
    ng = S.sb("ng", [128, 8], F32)
    Wn = S.sb("Wn", [128, 8, 1036], BF16)
    Wo2 = S.sb("Wo2", [128, 4, 1024], BF16)
    S.dma("sync", ng[:, :], n_g[:, :], writes=[ng])
    load_w_bf16(S, "sync", Wn, lambda k, c0, cw: Wn[:, k, c0:c0 + cw],
                n_w_in.rearrange("(k p) n -> p k n", p=128), stage, 8, 1036, gain=ng)
    load_w_bf16(S, "sync", Wo2, lambda k, c0, cw: Wo2[:, k, c0:c0 + cw],
                n_w_out.rearrange("(k p) n -> p k n", p=128), stage, 4, 1024)
    Ex = S.sb("Ex", [128, 64, 128], BF16)
    for c in range(4):
        load_const_bf16(S, "sync", Ex, Ex[:, c * 16:(c + 1) * 16, :].rearrange("p a b -> p (a b)"),
                        ex_d[:, c * 2048:(c + 1) * 2048], stage, 2048)
    Maug = S.sb("Maug", [128, 8, 257], BF16)
    load_const_bf16(S, "sync", Maug, Maug[:, 0:4, :].rearrange("p a b -> p (a b)"), maug_d[:, 0:1028], stage, 1028)
    load_const_bf16(S, "sync", Maug, Maug[:, 4:8, :].rearrange("p a b -> p (a b)"), maug_d[:, 1028:2056], stage, 1028)
    cmpm = S.sb("cmpm", [128, 33, 128], BF16)
    load_const_bf16(S, "sync", cmpm, cmpm[:, 0:16, :].rearrange("p a b -> p (a b)"), cmpm_d[:, 0:2048], stage, 2048)
    load_const_bf16(S, "sync", cmpm, cmpm[:, 16:32, :].rearrange("p a b -> p (a b)"), cmpm_d[:, 2048:4096], stage, 2048)
    load_const_bf16(S, "sync", cmpm, cmpm[:, 32, :], cmpm_d[:, 4096:4224], stage, 128)
    onesel = S.sb("onesel", [128, 3, 3], BF16)
    load_const_bf16(S, "sync", onesel, onesel[:, :, :].rearrange("p a b -> p (a b)"), onesel_d[:, :], stage, 9)
    causT = S.sb("causT", [128, 128], BF16)
    upT = S.sb("upT", [128, 128], BF16)
    load_const_bf16(S, "sync", causT, causT[:, :], causT_d[:, :], stage, 128)
    load_const_bf16(S, "sync", upT, upT[:, :], upT_d[:, :], stage, 128)

    hTs = [S.sb("hT", [128, 8, 128], BF16) for i in range(2)]
    h1s = [S.sb("h1", [128, 1024], F32) for i in range(2)]
    kwr = S.sb("kwr", [128, 6, 128], BF16)
    vwr = S.sb("vwr", [128, 6, 128], BF16)
    kwb = [Buf("kwb%d" % i, kwr.t) for i in range(6)]
    vwb = [Buf("vwb%d" % i, vwr.t) for i in range(6)]
    QTs = [S.sb("QT", [128, 512], BF16) for i in range(2)]
    gsils = [S.sb("gsil", [128, 512], F32) for i in range(2)]
    bgs = [S.sb("bg", [128, 12], F32) for i in range(2)]
    EmC = S.sb("EmC", [128, 8, 512], BF16)
    NBUF = 4
    Eb = [S.sb("Eb", [128, 512], BF16) for i in range(NBUF)]
    Emb = [S.sb("Emb", [128, 512], BF16) for i in range(NBUF)]
    imp = S.sb("imp", [128, 256], F32)
    score = S.sb("score", [128, 256], F32)
    sc2 = S.sb("sc2", [128, 256], F32)
    selF = S.sb("selF", [128, 256], F32)
    selT = S.sb("selT", [128, 2, 128], BF16)
    m8 = S.sb("m8", [128, 16], F32)
    rcols = [S.sb("rcol", [128, 1], F32) for i in range(2)]
    sumsbs = [S.sb("sumsb", [3, 512], F32) for i in range(2)]
    coef = S.sb("coef", [128, 12], F32)
    o_ = S.sb("o_", [128, 512], F32)
    ogf = S.sb("ogf", [128, 512], F32)
    ogT2 = S.sb("ogT2", [128, 4, 128], BF16)
    yts = [S.sb("yt", [128, 1024], F32) for i in range(2)]
    ObTs = [[S.sb("ObT", [128, 512], F32) for i in range(3)] for p in range(2)]
    pSc = [S.ps("pSc", [128, 512], F32) for i in range(2)]
    pMs = [S.ps("pM", [128, 512], F32) for i in range(2)]
    pO2 = [S.ps("pOb", [128, 512], F32) for i in range(2)]
    pOb = [pO2[0], pO2[1], pO2[0]]
    pSum1 = S.ps("pSum", [3, 512], F32)
    pSums = [pSum1, pSum1]
    pX = S.ps("pX", [128, 512], F32)

    S.op("gpsimd", lambda e: e.memset(selF[:, :], 0.0), writes=[selF])
    ctr = {"e": 0, "m": 0, "s": 0, "pm": 0}

    def stageA(u):
        rows = u["rows"]
        QT = u["QT"]
        ps = pSc[ctr["s"] % 2]
        ctr["s"] += 1
        S.op("tensor", lambda e: e.matmul(ps[0:rows, :], lhsT=u["ksrc"], rhs=QT[:, :], start=True, stop=True),
             reads=[u["ktrack"], QT], writes=[ps])
        E = Eb[ctr["e"] % NBUF]
        ctr["e"] += 1
        S.op("scalar", lambda e: e.activation(out=E[0:rows, :], in_=ps[0:rows, :], func=AF.Exp, scale=SCALE),
             reads=[ps], writes=[E])
        mask = u["mask"]
        if mask is None:
            Em = E
        else:
            Em = Emb[ctr["m"] % NBUF]
            ctr["m"] += 1
            if mask[0] == "sb":
                S.op("vector", lambda e: e.tensor_tensor(
                    out=Em[0:rows, :].rearrange("p (r q) -> p r q", r=4), in0=E[0:rows, :].rearrange("p (r q) -> p r q", r=4),
                    in1=mask[1].unsqueeze(1).broadcast_to([rows, 4, 128]), op=ALU.mult),
                     reads=[E] + mask[2], writes=[Em])
            else:
                t = mask[1]
                pM = pMs[ctr["pm"] % 2]
                ctr["pm"] += 1
                S.op("tensor", lambda e: e.matmul(pM[:, 0:128], lhsT=Ex[:, t % 64, :], rhs=selT[:, t // 64, :],
                                                  start=True, stop=True), reads=[Ex, selT], writes=[pM])
                S.op("vector", lambda e: e.tensor_tensor(
                    out=Em[0:rows, :].rearrange("p (r q) -> p r q", r=4), in0=E[0:rows, :].rearrange("p (r q) -> p r q", r=4),
                    in1=pM[:, 0:128].unsqueeze(1).broadcast_to([128, 4, 128]), op=ALU.mult),
                     reads=[E, pM], writes=[Em])
        u["Em"] = Em
        if u.get("emc") is not None:
            c = u["emc"]
            S.op("gpsimd", lambda e: e.tensor_copy(out=EmC[0:rows, c, :], in_=Em[0:rows, :]), reads=[Em], writes=[EmC])

    def stageB(u):
        rows = u["rows"]
        Em = u["Em"]
        b_idx = u["b"]
        q = u["q"]
        pSum = pSums[q["par"]]
        S.op("tensor", lambda e: e.matmul(pOb[b_idx][:, :], lhsT=u["vsrc"], rhs=Em[0:rows, :], start=u["first"], stop=u["last"]),
             reads=[u["vtrack"], Em], writes=[pOb[b_idx]])
        S.op("tensor", lambda e: e.matmul(pSum[:, :], lhsT=onesel[0:rows, b_idx, :], rhs=Em[0:rows, :],
                                          start=(q["nsum"] == 0), stop=(q["nsum"] == q["total_sum"] - 1)),
             reads=[onesel, Em], writes=[pSum])
        q["nsum"] += 1
        for f in u.get("post", ()):
            f()

    def rs_chunk(c):
        tl = list(range(c * 8, c * 8 + 8))
        S.collective("ReduceScatter", ALU.add, groups, Y2[c * 1024:(c + 1) * 1024, :], Y2s[c * 256:(c + 1) * 256, :],
                     reads=[bY2[t] for t in tl], writes=[bY2s[c]])

    def prologue(i):
        par = i % 2
        rows = slice(i * 128, (i + 1) * 128)
        hT, h1, QT, gsil, bg = hTs[par], h1s[par], QTs[par], gsils[par], bgs[par]
        S.dma("sync", hT[:, :, :], HT[i].rearrange("p (k t) -> p k t", k=8), reads=[bHT[i]], writes=[hT])
        S.dma("sync", h1[:, :], H1[rows, :], reads=[bH1[i]], writes=[h1])
        S.dma("gpsimd", kwr[:, i % 6, :], KW[i], reads=[bKW[i]], writes=[kwb[i % 6]])
        S.dma("gpsimd", vwr[:, i % 6, :], VW[i], reads=[bVW[i]], writes=[vwb[i % 6]])
        for r in range(4):
            for k in range(8):
                S.op("tensor", lambda e: e.matmul(pX[:, r * 128:(r + 1) * 128], lhsT=Wn[:, k, r * 128:(r + 1) * 128],
                                                  rhs=hT[:, k, :], start=(k == 0), stop=(k == 7)), reads=[Wn, hT], writes=[pX])
        S.op("scalar", lambda e: e.copy(out=QT[:, :], in_=pX[:, :]), reads=[pX], writes=[QT])
        for k in range(8):
            S.op("tensor", lambda e: e.matmul(pX[:, :], lhsT=hT[:, k, :], rhs=Wn[:, k, 512:1024],
                                              start=(k == 0), stop=(k == 7)), reads=[Wn, hT], writes=[pX])
        S.op("scalar", lambda e: e.activation(out=gsil[:, :], in_=pX[:, :], func=AF.Silu), reads=[pX], writes=[gsil])
        for k in range(8):
            S.op("tensor", lambda e: e.matmul(pX[:, 0:12], lhsT=hT[:, k, :], rhs=Wn[:, k, 1024:1036],
                                              start=(k == 0), stop=(k == 7)), reads=[Wn, hT], writes=[pX])
        S.op("scalar", lambda e: e.activation(out=bg[:, :], in_=pX[:, 0:12], func=AF.Sigmoid), reads=[pX], writes=[bg])

    def make_units(i):
        par = i % 2
        QT = QTs[par]
        Wp = 8 * (i + 1)
        nch = (Wp + 127) // 128
        wt = [t for t in range(i - 4, i + 1) if t >= 0]
        q = dict(i=i, par=par, nsum=0, total_sum=nch + (i + 1) + len(wt), topk_done=(i < 8))
        OT = ObTs[par]

        def evac(b):
            return lambda: S.op("scalar", lambda e: e.copy(out=OT[b][:, :], in_=pOb[b][:, :]), reads=[pOb[b]], writes=[OT[b]])

        def topk():
            ncol = 2 * i
            for r in range(4):
                pI = pMs[ctr["pm"] % 2]
                ctr["pm"] += 1
                rc_ = rcols[r % 2]
                for c in range(nch):
                    rws = min(128, Wp - c * 128)
                    S.op("tensor", lambda e: e.matmul(pI[:, 0:257], lhsT=EmC[0:rws, c, r * 128:(r + 1) * 128], rhs=Maug[0:rws, c, :],
                                                      start=(c == 0), stop=(c == nch - 1)), reads=[EmC, Maug], writes=[pI])
                S.op("vector", lambda e: e.tensor_scalar(out=rc_[:, :], in0=pI[:, 256:257], scalar1=1e-30, scalar2=None,
                                                         op0=ALU.add), reads=[pI], writes=[rc_])
                S.op("vector", lambda e: e.reciprocal(out=rc_[:, :], in_=rc_[:, :]), reads=[rc_], writes=[rc_])
                if r == 0:
                    S.op("vector", lambda e: e.tensor_scalar(out=imp[:, 0:ncol], in0=pI[:, 0:ncol], scalar1=rc_[:, 0:1],
                                                             scalar2=None, op0=ALU.mult), reads=[pI, rc_], writes=[imp])
                else:
                    S.op("vector", lambda e: e.scalar_tensor_tensor(out=imp[:, 0:ncol], in0=pI[:, 0:ncol], scalar=rc_[:, 0:1],
                                                                    in1=imp[:, 0:ncol], op0=ALU.mult, op1=ALU.add),
                         reads=[pI, rc_, imp], writes=[imp])
            S.op("vector", lambda e: e.tensor_copy(out=score[:, 0:ncol], in_=imp[:, 0:ncol]), reads=[imp], writes=[score])
            S.op("vector", lambda e: e.memset(score[:, 0:1], -1.0), writes=[score])
            S.op("vector", lambda e: e.memset(score[0:64, ncol - 1:ncol], -1.0), writes=[score])
            S.op("vector", lambda e: e.max(out=m8[:, 0:8], in_=score[:, 0:ncol]), reads=[score], writes=[m8])
            S.op("vector", lambda e: e.match_replace(out=sc2[:, 0:ncol], in_to_replace=m8[:, 0:8], in_values=score[:, 0:ncol],
                                                     imm_value=-2.0), reads=[m8, score], writes=[sc2])
            S.op("vector", lambda e: e.max(out=m8[:, 8:16], in_=sc2[:, 0:ncol]), reads=[sc2, m8], writes=[m8])
            S.op("vector", lambda e: e.tensor_scalar(out=selF[:, 0:ncol], in0=score[:, 0:ncol], scalar1=m8[:, 12:13], scalar2=None,
                                                     op0=ALU.is_ge), reads=[score, m8], writes=[selF])
            S.op("vector", lambda e: e.memset(selF[:, 0:1], 1.0), writes=[selF])
            S.op("vector", lambda e: e.memset(selF[0:64, ncol - 1:ncol], 1.0), writes=[selF])
            for c in range((ncol + 127) // 128):
                S.op("tensor", lambda e: e.transpose(out=pX[:, c * 128:(c + 1) * 128], in_=selF[:, c * 128:(c + 1) * 128],
                                                     identity=idf[:, :]), reads=[selF, idf], writes=[pX])
                S.op("vector", lambda e: e.tensor_copy(out=selT[:, c, :], in_=pX[:, c * 128:(c + 1) * 128]), reads=[pX], writes=[selT])
            q["topk_done"] = True

        units = []
        for c in range(nch):
            rws = min(128, Wp - c * 128)
            if c == nch - 1:
                mk = ("sb", cmpm[0:rws, (i % 16) + (0 if i < 16 else 16), :], [cmpm])
            elif c == 0:
                mk = ("sb", cmpm[0:rws, 32, :], [cmpm])
            else:
                mk = None
            u = dict(q=q, QT=QT, ksrc=KcT[:, c * 128:c * 128 + rws], ktrack=KcT, vsrc=Vc[0:rws, c, :], vtrack=Vc, rows=rws, mask=mk,
                     b=0, first=(c == 0), last=(c == nch - 1), emc=(c if i >= 8 else None), post=[])
            if c == nch - 1:
                u["post"].append(evac(0))
                if i >= 8:
                    u["post"].append(topk)
            units.append(u)
        for n, t in enumerate(wt):
            if t == i:
                mk = ("sb", causT[:, :], [causT])
            elif t == i - 4:
                mk = ("sb", upT[:, :], [upT])
            else:
                mk = None
            u = dict(q=q, QT=QT, ksrc=kwr[:, t % 6, :], ktrack=kwb[t % 6], vsrc=vwr[:, t % 6, :], vtrack=vwb[t % 6], rows=128, mask=mk,
                     b=2, first=(n == 0), last=(n == len(wt) - 1), post=[])
            if n == len(wt) - 1:
                u["post"].append(evac(2))
            units.append(u)
        for t in range(i + 1):
            if t == i:
                mk = ("sb", causT[:, :], [causT])
            elif i >= 8:
                mk = ("sel", t)
            else:
                mk = None
            u = dict(q=q, QT=QT, ksrc=KsT[:, t * 128:(t + 1) * 128], ktrack=KsT, vsrc=Vs[:, t, :], vtrack=Vs, rows=128, mask=mk,
                     b=1, first=(t == 0), last=(t == i), post=[], needs_sel=(mk is not None and mk[0] == "sel"))
            if t == i:
                u["post"].append(evac(1))
            units.append(u)
        return units, q

    def epilogue_parts(i):
        par = i % 2
        rows = slice(i * 128, (i + 1) * 128)
        h1, gsil, bg = h1s[par], gsils[par], bgs[par]
        OT = ObTs[par]
        pSum = pSums[par]
        sumsb = sumsbs[par]
        yt = yts[par]

        def combine(b):
            for r in range(4):
                S.op("tensor", lambda e: e.transpose(out=pX[:, r * 128:(r + 1) * 128], in_=OT[b][:, r * 128:(r + 1) * 128],
                                                     identity=idf[:, :]), reads=[OT[b], idf], writes=[pX])
            for r in range(4):
                cs_ = slice(r * 128, (r + 1) * 128)
                if b == 0:
                    S.op("vector", lambda e: e.tensor_scalar(out=o_[:, cs_], in0=pX[:, cs_], scalar1=coef[:, r * 3 + b:r * 3 + b + 1],
                                                             scalar2=None, op0=ALU.mult), reads=[pX, coef], writes=[o_])
                else:
                    S.op("vector", lambda e: e.scalar_tensor_tensor(out=o_[:, cs_], in0=pX[:, cs_], scalar=coef[:, r * 3 + b:r * 3 + b + 1],
                                                                    in1=o_[:, cs_], op0=ALU.mult, op1=ALU.add),
                         reads=[pX, coef, o_], writes=[o_])

        def part1():
            S.op("scalar", lambda e: e.copy(out=sumsb[:, :], in_=pSum[:, :]), reads=[pSum], writes=[sumsb])
            for r in range(4):
                S.op("tensor", lambda e: e.matmul(pX[:, r * 3:(r + 1) * 3], lhsT=sumsb[0:3, r * 128:(r + 1) * 128], rhs=idf[0:3, 0:3],
                                                  start=True, stop=True), reads=[sumsb, idf], writes=[pX])
            S.op("vector", lambda e: e.tensor_scalar(out=coef[:, :], in0=pX[:, 0:12], scalar1=1e-30, scalar2=None, op0=ALU.add),
                 reads=[pX], writes=[coef])
            S.op("vector", lambda e: e.reciprocal(out=coef[:, :], in_=coef[:, :]), reads=[coef], writes=[coef])
            S.op("vector", lambda e: e.tensor_tensor(out=coef[:, :], in0=coef[:, :], in1=bg[:, :], op=ALU.mult), reads=[coef, bg], writes=[coef])
            combine(0)

        def part2():
            combine(2)

        def part2b():
            combine(1)
            S.op("gpsimd", lambda e: e.tensor_mul(out=ogf[:, :], in0=o_[:, :], in1=gsil[:, :]), reads=[o_, gsil], writes=[ogf])

        def part3():
            for c in range(4):
                S.op("tensor", lambda e: e.transpose(out=pX[:, c * 128:(c + 1) * 128], in_=ogf[:, c * 128:(c + 1) * 128],
                                                     identity=idf[:, :]), reads=[ogf, idf], writes=[pX])
            S.op("vector", lambda e: e.tensor_copy(out=ogT2[:, :, :].rearrange("p a b -> p (a b)"), in_=pX[:, :]), reads=[pX], writes=[ogT2])

        def part4(hf):
            if True:
                for c in range(4):
                    S.op("tensor", lambda e: e.matmul(pX[:, :], lhsT=ogT2[:, c, :], rhs=Wo2[:, c, hf * 512:(hf + 1) * 512],
                                                      start=(c == 0), stop=(c == 3)), reads=[ogT2, Wo2], writes=[pX])
                S.op("vector", lambda e: e.scalar_tensor_tensor(out=yt[:, hf * 512:(hf + 1) * 512], in0=h1[:, hf * 512:(hf + 1) * 512],
                                                                scalar=0.25, in1=pX[:, :], op0=ALU.mult, op1=ALU.add),
                     reads=[h1, pX], writes=[yt])
            if hf == 1:
                S.dma("sync", Y2[rows, :], yt[:, :], reads=[yt], writes=[bY2[i]])
                if i % 8 == 7:
                    rs_chunk(i // 8)

        return [part1, part2, part2b, part3, lambda: part4(0), lambda: part4(1)]

    LOOK = 2
    prologue(0)
    pending = []
    for i in range(NT):
        units, q = make_units(i)
        nu = len(units)
        hooks = {}
        pos = [1, 4, 7, 10, 13, 16]
        for k, f in enumerate(pending):
            hooks.setdefault(min(nu - 1, pos[k]), []).append(f)
        if i + 1 < NT:
            hooks.setdefault(min(nu - 1, 19), []).append(lambda i=i: prologue(i + 1))
        for n in range(nu + LOOK):
            if n < nu:
                if units[n].get("needs_sel"):
                    assert q["topk_done"]
                stageA(units[n])
            if n - LOOK >= 0:
                stageB(units[n - LOOK])
            for f in hooks.get(n, ()):
                f()
        assert q["nsum"] == q["total_sum"]
        pending = epilogue_parts(i)
    for f in pending:
        f()
    S.pop()
    S.pop()
    if upto == 3:
        return dbg_out(Y2, bY2)

    S.push()
    ple = PleCtx(S)
    ple.load_weights("sync", pg[1][:, :], wg[1], we[1], stage)
    fn = S.sb("fn", [128, 1024], F32)
    S.dma("sync", fn[:, :], fng[:, :], writes=[fn])
    hs = [S.sb("h", [128, 1024], F32) for i in range(2)]
    ob = [S.sb("ob", [128, 1024], F32) for i in range(2)]
    sqs = S.sb("sqs", [128, 1024], F32)
    ss = S.sb("ss", [128, 1], F32)
    rs = S.sb("rs", [128, 1], F32)
    pA = S.ps("pA", [128, 8, 128], BF16)
    pG = S.ps("pG", [128, 512], F32)
    pE = S.ps("pE", [128, 512], F32)
    for u in range(TL // 128):
        rows = slice(u * 128, (u + 1) * 128)
        h = hs[u % 2]
        S.dma("sync", h[:, :], Y2s[rows, :], reads=[bY2s[u // 2]], writes=[h])
        ple.apply(h, p1s[rows, :], ident, pA, pG, pE)
        rms_rstd(S, h, 1024, sqs, ss, rs)
        o = ob[u % 2]
        S.op("vector", lambda e: e.scalar_tensor_tensor(out=o[:, :], in0=h[:, :], scalar=rs[:, 0:1], in1=fn[:, :],
                                                        op0=ALU.mult, op1=ALU.mult), reads=[h, rs, fn], writes=[o])
        S.dma("gpsimd", out[rows, :], o[:, :], reads=[o])
    S.finish()
    return nc, S.ninstr


def _colgain(g):
    return np.ascontiguousarray(np.asarray(g, np.float32).reshape(8, 128).T)


def _consts(T):
    half = 128
    inv = (10000.0 ** (-np.arange(half, dtype=np.float32) / np.float32(half))).astype(np.float32)
    pos = np.arange(T, dtype=np.float32)
    ang = (inv[:, None] * pos[None, :]).astype(np.float32)
    c = dict(cosT=np.cos(ang).astype(np.float32), sinT=np.sin(ang).astype(np.float32))
    p = np.arange(128)
    c["causT"] = (p[:, None] <= p[None, :]).astype(np.float32)
    c["upT"] = (p[:, None] > p[None, :]).astype(np.float32)
    c["ident"] = np.eye(128, dtype=np.float32)
    ex = np.zeros((128, 64, 128), np.float32)
    for tt in range(64):
        for hb in range(2):
            ex[(2 * tt + hb) % 128, tt, hb * 64:(hb + 1) * 64] = 1.0
    c["ex"] = ex.reshape(128, 64 * 128)
    m = np.zeros((1024, 257), np.float32)
    for j in range(256):
        for (off, w) in ((0, 1.0), (1, 2.0), (2, 2.0), (3, 2.0), (4, 1.0)):
            n = 4 * j + off
            if n < 1024:
                m[n, j] = w
    m[:, 256] = 1.0
    c["maug"] = np.ascontiguousarray(m.reshape(8, 128, 257).transpose(1, 0, 2)).reshape(128, 8 * 257)
    lane = np.arange(128)[:, None]
    ql = np.arange(128)[None, :]
    cm = np.zeros((128, 33, 128), np.float32)
    for res in range(16):
        v = (ql >= 16 * lane + 15 - 128 * res).astype(np.float32)
        b = v.copy()
        a = v.copy()
        a[0, :] = 0.0
        cm[:, res, :] = a
        cm[:, 16 + res, :] = b
    fm = np.ones((128, 128), np.float32)
    fm[0, :] = 0.0
    cm[:, 32, :] = fm
    c["cmpm"] = cm.reshape(128, 33 * 128)
    os_ = np.zeros((128, 3, 3), np.float32)
    for b in range(3):
        os_[:, b, b] = 1.0
    c["onesel"] = os_.reshape(128, 9)
    return c


def _head_consts(hd):
    lg = np.log1p(-(np.float32(2.0) ** np.float32(-5.0 - hd))).astype(np.float32)
    p = np.arange(128, dtype=np.float32)
    qd = np.exp((p + 1.0) * lg).astype(np.float32)
    kdv = (np.exp(-(p + 1.0) * lg) / 16.0).astype(np.float32)
    return dict(qdec=np.ascontiguousarray(np.broadcast_to(np.tile(qd, 4)[None, :], (128, 512))).astype(np.float32),
                kdec=np.ascontiguousarray(np.broadcast_to(np.tile(kdv, 4)[None, :], (128, 512))).astype(np.float32),
                cdec=np.full((128, 1), np.exp(np.float32(128.0) * lg), np.float32))


def _inmaps(T, B, I):
    C = _consts(T)
    TL = T // 4
    maps = []
    ca = np.ascontiguousarray
    for b in range(B):
        for g in range(4):
            m = dict(C)
            m.update(_head_consts(g))
            m["x"] = ca(I["x"][b, :T])
            m["p0"] = ca(I["p"][0, b, :T])
            p1 = I["p"][1, b, :T].reshape(T // 1024, 4, 256, 256)[:, g].reshape(TL, 256)
            m["p1s"] = ca(p1)
            wi = I["ret_w_in"][0]
            m["r_w_in"] = ca(np.concatenate([wi[:, g * 256:(g + 1) * 256], wi[:, 1024 + g * 256:1024 + (g + 1) * 256],
                                             wi[:, 2048 + g * 512:2048 + (g + 1) * 512], wi[:, 4096 + g * 512:4096 + (g + 1) * 512]], axis=1))
            m["r_g_in"] = _colgain(I["ret_norm"][0])
            m["r_gn"] = ca(np.broadcast_to(I["ret_gn"][0][g * 512:(g + 1) * 512][None, :], (128, 512)))
            m["r_w_out"] = ca(I["ret_w_out"][0][g * 512:(g + 1) * 512, :])
            for l in range(2):
                m["pg%d" % l] = _colgain(I["ple_norm"][l])
                m["wg%d" % l] = ca(I["ple_w_gate"][l])
                m["we%d" % l] = ca(I["ple_w_emb"][l])
            m["fng"] = ca(np.broadcast_to(I["final_norm"][None, :], (128, 1024)))
            m["kv_g"] = _colgain(I["kv_norm"])
            kw = I["kv_w"]
            order = [0, 1, 2, 4, 3, 5]
            m["kv_w"] = ca(np.concatenate([kw[:, pt * 512 + g * 128: pt * 512 + (g + 1) * 128] for pt in order], axis=1))
            m["peT_k"] = ca(I["cmp_pe_k"].T)
            m["peT_v"] = ca(I["cmp_pe_v"].T)
            m["w1_k"] = ca(I["cmp_w1_k"])
            m["w1_v"] = ca(I["cmp_w1_v"])
            m["w2_k"] = ca(I["cmp_w2_k"])
            m["w2_v"] = ca(I["cmp_w2_v"])
            m["n_g"] = _colgain(I["nsa_norm"][0])
            nw = I["nsa_w_in"][0]
            m["n_w_in"] = ca(np.concatenate([nw[:, g * 512:(g + 1) * 512], nw[:, 2048 + g * 512:2048 + (g + 1) * 512],
                                             nw[:, 4096 + g * 12:4096 + (g + 1) * 12]], axis=1))
            m["n_w_out"] = ca(I["nsa_w_out"][0][g * 512:(g + 1) * 512, :])
            maps.append({k: np.asarray(v, np.float32) for k, v in m.items()})
    return maps


_PROG = {}


def run_module(I, T, B):
    key = (T, B)
    if key not in _PROG:
        groups = [[b * 4 + g for g in range(4)] for b in range(B)]
        _PROG[key] = build_program(T, groups)[0]
    nc = _PROG[key]
    maps = _inmaps(T, B, I)
    res = run_bass_kernel_spmd(nc, maps, core_ids=list(range(4 * B)))
    outp = np.empty((B, T, 1024), np.float32)
    for b in range(B):
        for g in range(4):
            o = res.results[b * 4 + g]["out"].reshape(T // 1024, 256, 1024)
            outp[b].reshape(T // 1024, 4, 256, 1024)[:, g] = o
    return outp


def kernel(**inputs):
    I = {k: np.asarray(v) for k, v in inputs.items()}
    return run_module(I, 16384, 2)
```

```python
import contextlib
import numpy as np
import concourse.bass as bass
import concourse.mybir as mybir
from concourse.bass_utils import run_bass_kernel_spmd

F32 = mybir.dt.float32
BF16 = mybir.dt.bfloat16
AF = mybir.ActivationFunctionType
ALU = mybir.AluOpType
AX = mybir.AxisListType
ENGS = ("tensor", "vector", "scalar", "gpsimd", "sync")
EPS = 1e-6
SCALE = 128 ** -0.5


class Buf:
    __slots__ = ("name", "t", "w", "r", "psum")

    def __init__(self, name, t=None, psum=False):
        self.name = name
        self.t = t
        self.w = None
        self.r = []
        self.psum = psum

    def __getitem__(self, idx):
        return self.t[idx]


class _Rec:
    def __init__(self):
        self.call = None

    def __getattr__(self, name):
        def f(*a, **kw):
            assert self.call is None
            self.call = (name, a, kw)
            return self
        return f


class Sched:
    def __init__(self, nc, n_dma_sems=12):
        self.nc = nc
        self.sems = {}
        self.cnt = {}
        for e in ENGS:
            self.sems[e] = nc.alloc_semaphore("s_" + e)
            self.cnt[e] = 0
        self.sems["cc"] = nc.alloc_semaphore("s_cc")
        self.cnt["cc"] = 0
        self.dq = {}
        for q in ("sync", "gpsimd", "scalar"):
            lst = []
            for i in range(n_dma_sems):
                k = "d_%s_%d" % (q, i)
                self.sems[k] = nc.alloc_semaphore(k)
                self.cnt[k] = 0
                lst.append(k)
            self.dq[q] = [lst, 0]
        self.known = {e: {} for e in ENGS}
        self.E = {e: getattr(nc, e) for e in ENGS}
        self.ninstr = 0
        self.uid = 0
        self.stacks = [contextlib.ExitStack()]

    def push(self):
        self.stacks.append(contextlib.ExitStack())

    def pop(self):
        self.barrier()
        self.stacks.pop().close()

    def _nm(self, name):
        self.uid += 1
        return "%s_%d" % (name, self.uid)

    def sb(self, name, shape, dtype):
        nm = self._nm(name)
        return Buf(nm, self.stacks[-1].enter_context(self.nc.sbuf_tensor(nm, list(shape), dtype)))

    def ps(self, name, shape, dtype=F32):
        nm = self._nm(name)
        return Buf(nm, self.stacks[-1].enter_context(self.nc.psum_tensor(nm, list(shape), dtype)), psum=True)

    def dr(self, name, shape, dtype):
        return self.nc.dram_tensor(self._nm(name), list(shape), dtype)

    def _need(self, eng, deps):
        kn = self.known[eng]
        best = {}
        for d in deps:
            if d is None:
                continue
            k, v = d
            if k == eng and eng == "tensor":
                continue
            if kn.get(k, 0) >= v:
                continue
            if best.get(k, 0) < v:
                best[k] = v
        return best

    def _emit_waits(self, eng, best):
        for k, v in best.items():
            self.E[eng].wait_ge(self.sems[k], v)
            self.known[eng][k] = v
            self.ninstr += 1

    @staticmethod
    def _deps(reads, writes):
        deps = []
        for b in reads:
            deps.append(b.w)
            if b.psum:
                deps.extend(b.r)
        for b in writes:
            deps.append(b.w)
            deps.extend(b.r)
        return deps

    @staticmethod
    def _mark(ev, reads, writes):
        for b in reads:
            b.r.append(ev)
        for b in writes:
            b.w = ev
            b.r = []

    def op(self, eng, fn, reads=(), writes=()):
        self._emit_waits(eng, self._need(eng, self._deps(reads, writes)))
        self.cnt[eng] += 1
        rec = _Rec()
        fn(rec)
        name, a, kw = rec.call
        getattr(self.E[eng], name)(*a, **kw).then_inc(self.sems[eng], 1)
        self.ninstr += 1
        ev = (eng, self.cnt[eng])
        self._mark(ev, reads, writes)
        return ev

    def dma(self, q, out, in_, reads=(), writes=(), **kw):
        lst, idx = self.dq[q]
        k = lst[idx % len(lst)]
        self.dq[q][1] = idx + 1
        deps = self._deps(reads, writes)
        if self.cnt[k] > 0:
            deps.append((k, self.cnt[k]))
        self._emit_waits(q, self._need(q, deps))
        self.cnt[k] += 16
        self.E[q].dma_start(out=out, in_=in_, **kw).then_inc(self.sems[k], 16)
        self.ninstr += 1
        ev = (k, self.cnt[k])
        self._mark(ev, reads, writes)
        return ev

    def collective(self, kind, op, groups, in_ap, out_ap, reads=(), writes=()):
        deps = self._deps(reads, writes)
        if self.cnt["cc"] > 0:
            deps.append(("cc", self.cnt["cc"]))
        self._emit_waits("gpsimd", self._need("gpsimd", deps))
        self.cnt["cc"] += 1
        self.E["gpsimd"].collective_compute(kind, op, replica_groups=groups, ins=[in_ap], outs=[out_ap]).then_inc(
            self.sems["cc"], 1)
        self.ninstr += 1
        ev = ("cc", self.cnt["cc"])
        self._mark(ev, reads, writes)
        return ev

    def _all_events(self):
        return [(k, v) for k, v in self.cnt.items() if v > 0]

    def barrier(self):
        ev = self._all_events()
        for e in ENGS:
            self._emit_waits(e, self._need(e, [d for d in ev if d[0] != e]))

    def finish(self):
        deps = [(k, v) for k, v in self.cnt.items() if v > 0 and (k.startswith("d_") or k == "cc")]
        self._emit_waits("sync", self._need("sync", deps))
        self.barrier()
        while self.stacks:
            self.stacks.pop().close()


def load_w_bf16(S, q, dst, dst_fn, src_ap, stage, kc, ncols, gain=None):
    for k in range(kc):
        c0 = 0
        while c0 < ncols:
            cw = min(2048, ncols - c0)
            S.dma(q, stage[:, 0:cw], src_ap[:, k, c0:c0 + cw], writes=[stage])
            if gain is None:
                S.op("gpsimd", lambda e: e.tensor_copy(out=dst_fn(k, c0, cw), in_=stage[:, 0:cw]),
                     reads=[stage], writes=[dst])
            else:
                S.op("gpsimd", lambda e: e.tensor_scalar(out=dst_fn(k, c0, cw), in0=stage[:, 0:cw],
                                                         scalar1=gain[:, k:k + 1], scalar2=None, op0=ALU.mult),
                     reads=[stage, gain], writes=[dst])
            c0 += cw


def load_const_bf16(S, q, dst, dst_ap, src_ap, stage, ncols):
    S.dma(q, stage[:, 0:ncols], src_ap, writes=[stage])
    S.op("gpsimd", lambda e: e.tensor_copy(out=dst_ap, in_=stage[:, 0:ncols]), reads=[stage], writes=[dst])


def rms_rstd(S, xt, D, sqs, ss, rs):
    S.op("scalar", lambda e: e.activation(out=sqs[:, :], in_=xt[:, :], func=AF.Square, accum_out=ss[:, :]),
         reads=[xt], writes=[sqs, ss])
    S.op("vector", lambda e: e.tensor_scalar(out=rs[:, :], in0=ss[:, :], scalar1=1.0 / D, scalar2=EPS,
                                             op0=ALU.mult, op1=ALU.add), reads=[ss], writes=[rs])
    S.op("scalar", lambda e: e.activation(out=rs[:, :], in_=rs[:, :], func=AF.Sqrt), reads=[rs], writes=[rs])
    S.op("vector", lambda e: e.reciprocal(out=rs[:, :], in_=rs[:, :]), reads=[rs], writes=[rs])


def rmsnorm_T(S, xt, ident, sqs, ss, rs, xn, pT, dstT, tok0, copy_eng="vector"):
    rms_rstd(S, xt, 1024, sqs, ss, rs)
    S.op("vector", lambda e: e.tensor_scalar(out=xn[:, :], in0=xt[:, :], scalar1=rs[:, 0:1], scalar2=None,
                                             op0=ALU.mult), reads=[xt, rs], writes=[xn])
    for k in range(8):
        S.op("tensor", lambda e: e.transpose(out=pT[:, k, :], in_=xn[:, k * 128:(k + 1) * 128], identity=ident[:, :]),
             reads=[xn, ident], writes=[pT])
    if copy_eng == "vector":
        S.op("vector", lambda e: e.tensor_copy(out=dstT[:, :, tok0:tok0 + 128], in_=pT[:, :, :]), reads=[pT], writes=[dstT])
    else:
        S.op("scalar", lambda e: e.copy(out=dstT[:, :, tok0:tok0 + 128], in_=pT[:, :, :]), reads=[pT], writes=[dstT])


class PleCtx:
    def __init__(self, S):
        self.S = S
        self.Wg = S.sb("pleWg", [128, 8, 1024], BF16)
        self.We = S.sb("pleWe", [128, 2, 1024], BF16)
        self.gain = S.sb("pleGain", [128, 8], F32)
        self.sqs = S.sb("ple_sqs", [128, 1024], BF16)
        self.ss = [S.sb("ple_ss", [128, 1], F32) for i in range(2)]
        self.rs = [S.sb("ple_rs", [128, 1], F32) for i in range(2)]
        _xn = S.sb("ple_xn", [128, 1024], BF16)
        self.xn = [_xn, _xn]
        self.hnT = [S.sb("ple_hnT", [128, 8, 128], BF16) for i in range(2)]
        self.pf = [S.sb("ple_pf", [128, 256], F32) for i in range(2)]
        self.pb = [S.sb("ple_pb", [128, 256], BF16) for i in range(2)]
        self.pT = [S.sb("ple_pT", [128, 2, 128], BF16) for i in range(2)]
        _sig = S.sb("ple_sig", [128, 512], F32)
        _prod = S.sb("ple_prod", [128, 512], F32)
        self.sig = [_sig, _sig]
        self.prod = [_prod, _prod]

    def load_weights(self, q, g_ap, wg_ap, we_ap, stage):
        S = self.S
        S.dma(q, self.gain[:, :], g_ap, writes=[self.gain])
        load_w_bf16(S, q, self.Wg, lambda k, c0, cw: self.Wg[:, k, c0:c0 + cw],
                    wg_ap.rearrange("(k p) n -> p k n", p=128), stage, 8, 1024, gain=self.gain)
        load_w_bf16(S, q, self.We, lambda k, c0, cw: self.We[:, k, c0:c0 + cw],
                    we_ap.rearrange("(k p) n -> p k n", p=128), stage, 2, 1024)

    def front(self, par, h, p_ap, ident, pA):
        S = self.S
        pf, pb, pT = self.pf[par], self.pb[par], self.pT[par]
        S.dma("sync", pf[:, :], p_ap, writes=[pf])
        rmsnorm_T(S, h, ident, self.sqs, self.ss[par], self.rs[par], self.xn[par], pA, self.hnT[par], 0)
        S.op("gpsimd", lambda e: e.tensor_copy(out=pb[:, :], in_=pf[:, :]), reads=[pf], writes=[pb])
        for k in range(2):
            S.op("tensor", lambda e: e.transpose(out=pA[:, k, :], in_=pb[:, k * 128:(k + 1) * 128], identity=ident[:, :]),
                 reads=[pb, ident], writes=[pA])
        S.op("vector", lambda e: e.tensor_copy(out=pT[:, :, :], in_=pA[:, 0:2, :]), reads=[pA], writes=[pT])

    def back(self, par, h, pG, pE):
        S = self.S
        hnT, pT = self.hnT[par], self.pT[par]
        for hf in range(2):
            cs = slice(hf * 512, (hf + 1) * 512)
            sig, prod = self.sig[hf], self.prod[hf]
            for k in range(8):
                S.op("tensor", lambda e: e.matmul(pG[hf][:, :], lhsT=hnT[:, k, :], rhs=self.Wg[:, k, cs],
                                                  start=(k == 0), stop=(k == 7)), reads=[hnT, self.Wg], writes=[pG[hf]])
            for k in range(2):
                S.op("tensor", lambda e: e.matmul(pE[hf][:, :], lhsT=pT[:, k, :], rhs=self.We[:, k, cs],
                                                  start=(k == 0), stop=(k == 1)), reads=[pT, self.We], writes=[pE[hf]])
            S.op("scalar", lambda e: e.activation(out=sig[:, :], in_=pG[hf][:, :], func=AF.Sigmoid), reads=[pG[hf]], writes=[sig])
            S.op("vector", lambda e: e.tensor_tensor(out=prod[:, :], in0=sig[:, :], in1=pE[hf][:, :], op=ALU.mult),
                 reads=[sig, pE[hf]], writes=[prod])
            S.op("gpsimd", lambda e: e.tensor_add(out=h[:, cs], in0=h[:, cs], in1=prod[:, :]), reads=[h, prod], writes=[h])


def build_program(T, groups, upto=4):
    nc = bass.Bass("TRN2", target_bir_lowering=False)
    NT = T // 128
    NS = T // 512
    NCH = T // 1024
    NC16 = T // 2048
    TL = T // 4

    def din(name, shape):
        return nc.dram_tensor(name, list(shape), F32, kind="ExternalInput").ap()

    x = din("x", [T, 1024])
    p0 = din("p0", [T, 256])
    p1s = din("p1s", [TL, 256])
    identd = din("ident", [128, 128])
    r_w_in = din("r_w_in", [1024, 1536])
    r_g_in = din("r_g_in", [128, 8])
    r_gn = din("r_gn", [128, 512])
    r_w_out = din("r_w_out", [512, 1024])
    cosT = din("cosT", [128, T])
    sinT = din("sinT", [128, T])
    qdec = din("qdec", [128, 512])
    kdec = din("kdec", [128, 512])
    cdec = din("cdec", [128, 1])
    causT_d = din("causT", [128, 128])
    upT_d = din("upT", [128, 128])
    pg = [din("pg%d" % l, [128, 8]) for l in range(2)]
    wg = [din("wg%d" % l, [1024, 1024]) for l in range(2)]
    we = [din("we%d" % l, [256, 1024]) for l in range(2)]
    fng = din("fng", [128, 1024])
    kv_g = din("kv_g", [128, 8])
    kv_w = din("kv_w", [1024, 768])
    peT_k = din("peT_k", [128, 32])
    peT_v = din("peT_v", [128, 32])
    w1_k = din("w1_k", [4096, 256])
    w1_v = din("w1_v", [4096, 256])
    w2_k = din("w2_k", [256, 128])
    w2_v = din("w2_v", [256, 128])
    n_g = din("n_g", [128, 8])
    n_w_in = din("n_w_in", [1024, 1036])
    n_w_out = din("n_w_out", [512, 1024])
    ex_d = din("ex", [128, 64 * 128])
    maug_d = din("maug", [128, 8 * 257])
    cmpm_d = din("cmpm", [128, 33 * 128])
    onesel_d = din("onesel", [128, 9])
    out = nc.dram_tensor("out", [TL, 1024], F32, kind="ExternalOutput").ap()
    dbg = nc.dram_tensor("dbg", [T, 1024], F32, kind="ExternalOutput").ap() if upto < 4 else None

    def dbg_out(src, bufs):
        for c in range(T // 1024):
            S.dma("sync", dbg[c * 1024:(c + 1) * 1024, :], src[c * 1024:(c + 1) * 1024, :], reads=bufs[c * 8:(c + 1) * 8])
        S.finish()
        return nc, S.ninstr

    S = Sched(nc)
    Y1 = S.dr("Y1", [T, 1024], F32)
    Y1s = S.dr("Y1s", [T, 1024], F32)
    Y2 = S.dr("Y2", [T, 1024], F32)
    Y2s = S.dr("Y2s", [TL, 1024], F32)
    H1 = S.dr("H1", [T, 1024], F32)
    HT = S.dr("HT", [NT, 128, 1024], BF16)
    KW = S.dr("KW", [NT, 128, 128], BF16)
    VW = S.dr("VW", [NT, 128, 128], BF16)
    bY1 = [Buf("bY1_%d" % i) for i in range(NT)]
    bY1s = [Buf("bY1s_%d" % i) for i in range(NT)]
    bY2 = [Buf("bY2_%d" % i) for i in range(NT)]
    bY2s = [Buf("bY2s_%d" % i) for i in range(NCH)]
    bH1 = [Buf("bH1_%d" % i) for i in range(NT)]
    bHT = [Buf("bHT_%d" % i) for i in range(NT)]
    bKW = [Buf("bKW_%d" % i) for i in range(NT)]
    bVW = [Buf("bVW_%d" % i) for i in range(NT)]

    stage = S.sb("stage", [128, 2048], F32)
    idf = S.sb("idf", [128, 128], F32)
    ident = S.sb("identb", [128, 128], BF16)
    S.dma("sync", idf[:, :], identd[:, :], writes=[idf])
    S.op("vector", lambda e: e.tensor_copy(out=ident[:, :], in_=idf[:, :]), reads=[idf], writes=[ident])

    S.push()
    W = S.sb("W", [128, 8, 1536], BF16)
    Wo = S.sb("Wo", [128, 4, 1024], BF16)
    gin = S.sb("gin", [128, 8], F32)
    gnt = S.sb("gnt", [128, 512], F32)
    qd_t = S.sb("qd_t", [128, 512], F32)
    kd_t = S.sb("kd_t", [128, 512], F32)
    cd_t = S.sb("cd_t", [128, 1], F32)
    caus = S.sb("caus", [128, 128], F32)
    xts = [S.sb("xt", [128, 1024], F32) for i in range(2)]
    sqs = S.sb("sqs", [128, 1024], F32)
    ss = S.sb("ss", [128, 1], F32)
    rs = S.sb("rs", [128, 1], F32)
    xn = S.sb("xn", [128, 1024], BF16)
    xnT2 = [S.sb("xnT", [128, 8, 512], BF16) for i in range(2)]
    cs = [S.sb("cs", [128, 512], F32) for i in range(2)]
    sn = [S.sb("sn", [128, 512], F32) for i in range(2)]
    tabs2 = [[S.sb("tab", [128, 512], F32) for i in range(4)] for p in range(2)]
    sss = [S.sb("ss", [128, 1], F32) for i in range(2)]
    rss = [S.sb("rs", [128, 1], F32) for i in range(2)]
    xns = [S.sb("xn", [128, 1024], BF16) for i in range(2)]
    raw = [S.sb("raw", [128, 512], F32) for i in range(4)]
    tmp = [S.sb("tmp", [128, 512], F32) for i in range(4)]
    qdT2 = [S.sb("qdT", [128, 2, 512], BF16) for i in range(2)]
    kTp2 = [S.sb("kTp", [128, 2, 512], BF16) for i in range(2)]
    vb2 = [S.sb("vb", [128, 4, 512], BF16) for i in range(2)]
    gs2 = [S.sb("gs", [128, 4, 512], F32) for i in range(2)]
    st_f = [S.sb("st_f", [128, 512], F32) for i in range(2)]
    st_b = [S.sb("st_b", [128, 512], BF16) for i in range(2)]
    kd = S.sb("kd", [128, 256], BF16)
    ST = S.sb("ST", [128, 128], BF16)
    osq = S.sb("osq", [128, 512], F32)
    stats = [S.sb("stat", [128, 4], F32) for i in range(2)]
    ons = [S.sb("on", [128, 512], F32) for i in range(2)]
    ogs = [S.sb("og", [128, 512], BF16) for i in range(2)]
    ogT = S.sb("ogT", [128, 4, 128], BF16)
    yo = [S.sb("yo", [128, 1024], F32) for i in range(2)]
    pA = S.ps("pA", [128, 8, 128], BF16)
    pB = [S.ps("pB", [128, 512], F32) for i in range(2)]
    pS = S.ps("pS", [128, 128], F32)
    pO = S.ps("pO", [128, 512], F32)
    pSt = [S.ps("pSt", [128, 512], F32) for i in range(2)]
    pY = S.ps("pY", [128, 512], F32)

    for (dst, src) in ((gin, r_g_in), (gnt, r_gn), (qd_t, qdec), (kd_t, kdec), (cd_t, cdec), (caus, causT_d)):
        S.dma("sync", dst[:, :], src[:, :], writes=[dst])
    load_w_bf16(S, "sync", W, lambda k, c0, cw: W[:, k, c0:c0 + cw],
                r_w_in.rearrange("(k p) n -> p k n", p=128), stage, 8, 1536, gain=gin)
    load_w_bf16(S, "sync", Wo, lambda k, c0, cw: Wo[:, k, c0:c0 + cw],
                r_w_out.rearrange("(k p) n -> p k n", p=128), stage, 4, 1024)
    for i in range(2):
        S.op("gpsimd", lambda e: e.memset(st_f[i][:, :], 0.0), writes=[st_f[i]])
        S.op("gpsimd", lambda e: e.memset(st_b[i][:, :], 0.0), writes=[st_b[i]])

    def allreduce_chunk(c):
        r0 = c * 1024
        tl = list(range(c * 8, c * 8 + 8))
        S.collective("AllReduce", ALU.add, groups, Y1[r0:r0 + 1024, :], Y1s[r0:r0 + 1024, :],
                     reads=[bY1[t] for t in tl], writes=[bY1s[t] for t in tl])

    def tabs_for(s):
        t0 = s * 512
        cst, snt = cs[s % 2], sn[s % 2]
        tb = tabs2[s % 2]
        S.dma("sync", cst[:, :], cosT[:, t0:t0 + 512], writes=[cst])
        S.dma("sync", snt[:, :], sinT[:, t0:t0 + 512], writes=[snt])
        S.op("gpsimd", lambda e: e.tensor_mul(out=tb[0][:, :], in0=cst[:, :], in1=qd_t[:, :]), reads=[cst, qd_t], writes=[tb[0]])
        S.op("gpsimd", lambda e: e.tensor_mul(out=tb[1][:, :], in0=snt[:, :], in1=qd_t[:, :]), reads=[snt, qd_t], writes=[tb[1]])
        S.op("gpsimd", lambda e: e.tensor_mul(out=tb[2][:, :], in0=cst[:, :], in1=kd_t[:, :]), reads=[cst, kd_t], writes=[tb[2]])
        S.op("gpsimd", lambda e: e.tensor_mul(out=tb[3][:, :], in0=snt[:, :], in1=kd_t[:, :]), reads=[snt, kd_t], writes=[tb[3]])

    def F(s, j):
        ti = s * 4 + j
        xt = xts[ti % 2]
        S.dma("sync", xt[:, :], x[ti * 128:(ti + 1) * 128, :], writes=[xt])
        rmsnorm_T(S, xt, ident, sqs, sss[ti % 2], rss[ti % 2], xns[ti % 2], pA, xnT2[s % 2], j * 128)

    def P1(s):
        xnT = xnT2[s % 2]
        tb = tabs2[s % 2]
        qdT, kTp, vb, gs = qdT2[s % 2], kTp2[s % 2], vb2[s % 2], gs2[s % 2]
        for dc in range(4):
            pb = pB[dc % 2]
            for k in range(8):
                S.op("tensor", lambda e: e.matmul(pb[:, :], lhsT=W[:, k, dc * 128:(dc + 1) * 128], rhs=xnT[:, k, :],
                                                  start=(k == 0), stop=(k == 7)), reads=[W, xnT], writes=[pb])
            S.op("scalar", lambda e: e.copy(out=raw[dc][:, :], in_=pb[:, :]), reads=[pb], writes=[raw[dc]])
        for (eng, x1, x2, ct, st_, dst, ta, tb_) in (("gpsimd", raw[0], raw[1], tb[0], tb[1], qdT, tmp[0], tmp[1]),
                                                     ("vector", raw[2], raw[3], tb[2], tb[3], kTp, tmp[2], tmp[3])):
            S.op(eng, lambda e: e.tensor_mul(out=ta[:, :], in0=x1[:, :], in1=ct[:, :]), reads=[x1, ct], writes=[ta])
            S.op(eng, lambda e: e.tensor_mul(out=tb_[:, :], in0=x2[:, :], in1=st_[:, :]), reads=[x2, st_], writes=[tb_])
            S.op(eng, lambda e: e.tensor_sub(out=dst[:, 0, :], in0=ta[:, :], in1=tb_[:, :]), reads=[ta, tb_], writes=[dst])
            S.op(eng, lambda e: e.tensor_mul(out=ta[:, :], in0=x1[:, :], in1=st_[:, :]), reads=[x1, st_], writes=[ta])
            S.op(eng, lambda e: e.tensor_mul(out=tb_[:, :], in0=x2[:, :], in1=ct[:, :]), reads=[x2, ct], writes=[tb_])
            S.op(eng, lambda e: e.tensor_add(out=dst[:, 1, :], in0=ta[:, :], in1=tb_[:, :]), reads=[ta, tb_], writes=[dst])
        for j in range(4):
            for (which, c0) in (("v", 512), ("g", 1024)):
                pb = pB[0] if which == "v" else pB[1]
                for k in range(8):
                    S.op("tensor", lambda e: e.matmul(pb[:, :], lhsT=xnT[:, k, j * 128:(j + 1) * 128], rhs=W[:, k, c0:c0 + 512],
                                                      start=(k == 0), stop=(k == 7)), reads=[W, xnT], writes=[pb])
                if which == "v":
                    S.op("scalar", lambda e: e.copy(out=vb[:, j, :], in_=pb[:, :]), reads=[pb], writes=[vb])
                else:
                    S.op("scalar", lambda e: e.activation(out=gs[:, j, :], in_=pb[:, :], func=AF.Silu), reads=[pb], writes=[gs])

    def CA(s, j):
        ti = s * 4 + j
        qdT, kTp, vb = qdT2[s % 2], kTp2[s % 2], vb2[s % 2]
        on, stat = ons[ti % 2], stats[ti % 2]
        tk = slice(j * 128, (j + 1) * 128)
        for dc in range(2):
            S.op("tensor", lambda e: e.transpose(out=pA[:, dc, :], in_=kTp[:, dc, tk], identity=ident[:, :]),
                 reads=[kTp, ident], writes=[pA])
        S.op("vector", lambda e: e.tensor_scalar(out=kd[:, :], in0=pA[:, 0:2, :].rearrange("p a b -> p (a b)"),
                                                 scalar1=cd_t[:, 0:1], scalar2=None, op0=ALU.mult),
             reads=[pA, cd_t], writes=[kd])
        for dc in range(2):
            S.op("tensor", lambda e: e.matmul(pS[:, :], lhsT=kTp[:, dc, tk], rhs=qdT[:, dc, tk],
                                              start=(dc == 0), stop=(dc == 1)), reads=[kTp, qdT], writes=[pS])
        S.op("vector", lambda e: e.tensor_tensor(out=ST[:, :], in0=pS[:, :], in1=caus[:, :], op=ALU.mult),
             reads=[pS, caus], writes=[ST])
        S.op("tensor", lambda e: e.matmul(pO[:, :], lhsT=ST[:, :], rhs=vb[:, j, :], start=True, stop=False),
             reads=[ST, vb], writes=[pO])
        for dc in range(2):
            S.op("tensor", lambda e: e.matmul(pO[:, :], lhsT=qdT[:, dc, tk], rhs=st_b[dc][:, :],
                                              start=False, stop=(dc == 1)), reads=[qdT, st_b[dc]], writes=[pO])
        for dc in range(2):
            S.op("tensor", lambda e: e.matmul(pSt[dc][:, :], lhsT=kd[:, dc * 128:(dc + 1) * 128], rhs=vb[:, j, :],
                                              start=True, stop=True), reads=[kd, vb], writes=[pSt[dc]])
            S.op("vector", lambda e: e.scalar_tensor_tensor(out=st_f[dc][:, :], in0=st_f[dc][:, :], scalar=cd_t[:, 0:1],
                                                            in1=pSt[dc][:, :], op0=ALU.mult, op1=ALU.add),
                 reads=[st_f[dc], cd_t, pSt[dc]], writes=[st_f[dc]])
            S.op("gpsimd", lambda e: e.tensor_copy(out=st_b[dc][:, :], in_=st_f[dc][:, :]), reads=[st_f[dc]], writes=[st_b[dc]])
        S.op("scalar", lambda e: e.activation(out=on[:, :], in_=pO[:, :], func=AF.Identity, accum_out=stat[:, 0:1]),
             reads=[pO], writes=[on, stat])
        S.op("scalar", lambda e: e.activation(out=osq[:, :], in_=pO[:, :], func=AF.Square, accum_out=stat[:, 1:2]),
             reads=[pO], writes=[osq, stat])

    def CB(s, j):
        ti = s * 4 + j
        gs = gs2[s % 2]
        on, stat, og = ons[ti % 2], stats[ti % 2], ogs[ti % 2]
        S.op("vector", lambda e: e.tensor_scalar(out=stat[:, 0:2], in0=stat[:, 0:2], scalar1=1.0 / 512, scalar2=None,
                                                 op0=ALU.mult), reads=[stat], writes=[stat])
        S.op("vector", lambda e: e.tensor_tensor(out=stat[:, 2:3], in0=stat[:, 0:1], in1=stat[:, 0:1], op=ALU.mult),
             reads=[stat], writes=[stat])
        S.op("vector", lambda e: e.tensor_tensor(out=stat[:, 2:3], in0=stat[:, 1:2], in1=stat[:, 2:3], op=ALU.subtract),
             reads=[stat], writes=[stat])
        S.op("vector", lambda e: e.tensor_scalar(out=stat[:, 2:3], in0=stat[:, 2:3], scalar1=EPS, scalar2=None,
                                                 op0=ALU.add), reads=[stat], writes=[stat])
        S.op("scalar", lambda e: e.activation(out=stat[:, 2:3], in_=stat[:, 2:3], func=AF.Sqrt), reads=[stat], writes=[stat])
        S.op("vector", lambda e: e.reciprocal(out=stat[:, 3:4], in_=stat[:, 2:3]), reads=[stat], writes=[stat])
        S.op("vector", lambda e: e.tensor_scalar(out=on[:, :], in0=on[:, :], scalar1=stat[:, 0:1], scalar2=stat[:, 3:4],
                                                 op0=ALU.subtract, op1=ALU.mult), reads=[on, stat], writes=[on])
        S.op("gpsimd", lambda e: e.tensor_mul(out=on[:, :], in0=on[:, :], in1=gnt[:, :]), reads=[on, gnt], writes=[on])
        S.op("gpsimd", lambda e: e.tensor_mul(out=og[:, :], in0=on[:, :], in1=gs[:, j, :]), reads=[on, gs], writes=[og])
        for c in range(4):
            S.op("tensor", lambda e: e.transpose(out=pA[:, 2 + c, :], in_=og[:, c * 128:(c + 1) * 128], identity=ident[:, :]),
                 reads=[og, ident], writes=[pA])
        S.op("vector", lambda e: e.tensor_copy(out=ogT[:, :, :], in_=pA[:, 2:6, :]), reads=[pA], writes=[ogT])
        yt = yo[ti % 2]
        for hf in range(2):
            for c in range(4):
                S.op("tensor", lambda e: e.matmul(pY[:, :], lhsT=ogT[:, c, :], rhs=Wo[:, c, hf * 512:(hf + 1) * 512],
                                                  start=(c == 0), stop=(c == 3)), reads=[ogT, Wo], writes=[pY])
            S.op("scalar", lambda e: e.copy(out=yt[:, hf * 512:(hf + 1) * 512], in_=pY[:, :]), reads=[pY], writes=[yt])
        S.dma("scalar", Y1[ti * 128:(ti + 1) * 128, :], yt[:, :], reads=[yt], writes=[bY1[ti]])
        if ti >= 9 and (ti - 9) % 8 == 0:
            allreduce_chunk((ti - 9) // 8)

    prev = None
    for s in range(NS + 1):
        if s < NS:
            tabs_for(s)
        for j in range(4):
            if s < NS:
                F(s, j)
            if s >= 1:
                CA(s - 1, j)
                if prev is not None:
                    CB(*prev)
                prev = (s - 1, j)
        if s < NS:
            P1(s)
    CB(*prev)
    allreduce_chunk(NCH - 1)
    if NCH >= 2 and (NT - 1) < 9 + 8 * (NCH - 2):
        pass
    issued = set([(ti - 9) // 8 for ti in range(NT) if ti >= 9 and (ti - 9) % 8 == 0] + [NCH - 1])
    for c in range(NCH):
        if c not in issued:
            allreduce_chunk(c)
    S.pop()
    if upto == 1:
        return dbg_out(Y1s, bY1s)

    S.push()
    KsT = S.sb("KsT", [128, T], BF16)
    Vs = S.sb("Vs", [128, NT, 128], BF16)
    KcT = S.sb("KcT", [128, NC16 * 128], BF16)
    Vc = S.sb("Vc", [128, NC16, 128], BF16)

    S.push()
    ple = PleCtx(S)
    ple.load_weights("sync", pg[0][:, :], wg[0], we[0], stage)
    kvg = S.sb("kvg", [128, 8], F32)
    Wkv = S.sb("Wkv", [128, 8, 768], BF16)
    S.dma("sync", kvg[:, :], kv_g[:, :], writes=[kvg])
    load_w_bf16(S, "sync", Wkv, lambda k, c0, cw: Wkv[:, k, c0:c0 + cw],
                kv_w.rearrange("(k p) n -> p k n", p=128), stage, 8, 768, gain=kvg)
    w1 = [S.sb("w1", [128, 32, 256], BF16) for i in range(2)]
    w2 = [S.sb("w2", [128, 2, 128], BF16) for i in range(2)]
    peT = [S.sb("peT", [128, 32], BF16) for i in range(2)]
    for i, (w1d, w2d, ped) in enumerate(((w1_k, w2_k, peT_k), (w1_v, w2_v, peT_v))):
        load_w_bf16(S, "sync", w1[i], lambda k, c0, cw: w1[i][:, k, c0:c0 + cw],
                    w1d.rearrange("(l d) h -> d l h", d=128), stage, 32, 256)
        load_w_bf16(S, "sync", w2[i], lambda k, c0, cw: w2[i][:, k, c0:c0 + cw],
                    w2d.rearrange("(k p) n -> p k n", p=128), stage, 2, 128)
        load_const_bf16(S, "sync", peT[i], peT[i][:, :], ped[:, :], stage, 32)
    cb = [S.sb("cb", [128, 2064], BF16) for i in range(2)]
    hs = [S.sb("h", [128, 1024], F32) for i in range(3)]
    _yb = S.sb("yb", [128, 1024], F32)
    ybs = [_yb, _yb]
    hTs = [S.sb("hT", [128, 8, 128], BF16) for i in range(2)]
    kwt = [S.sb("kwt", [128, 128], BF16) for i in range(2)]
    vwt = [S.sb("vwt", [128, 128], BF16) for i in range(2)]
    sqs = ple.sqs
    ss2 = [S.sb("ss", [128, 1], F32) for i in range(2)]
    rs2 = [S.sb("rs", [128, 1], F32) for i in range(2)]
    _xn2 = S.sb("xn", [128, 1024], BF16)
    xn2 = [_xn2, _xn2]
    ones1 = S.sb("ones1", [1, 128], BF16)
    bias_f = S.sb("bias_f", [1, 512], F32)
    bias_hi = S.sb("bias_hi", [1, 512], BF16)
    bias_hif = S.sb("bias_hif", [1, 512], F32)
    bias_lo = S.sb("bias_lo", [1, 512], BF16)
    xs_ = S.sb("xs_", [128, 256], F32)
    x2_ = S.sb("x2_", [128, 256], F32)
    sg_ = S.sb("sg_", [128, 256], F32)
    hid = S.sb("hid", [128, 256], BF16)
    hidT = S.sb("hidT", [128, 2, 128], BF16)
    pA = S.ps("pA", [128, 8, 128], BF16)
    pA2 = S.ps("pA2", [128, 8, 128], BF16)
    pG1 = S.ps("pG", [128, 512], F32)
    pE1 = S.ps("pE", [128, 512], F32)
    pGs = [pG1, pG1]
    pEs = [pE1, pE1]
    pKT = S.ps("pKT", [128, 4, 128], F32)
    pKV = S.ps("pKV", [128, 256], F32)
    pH = S.ps("pH", [128, 256], F32)
    pC = S.ps("pC", [128, 128], F32)

    S.op("gpsimd", lambda e: e.memset(ones1[:, :], 1.0), writes=[ones1])
    for i in range(2):
        S.op("gpsimd", lambda e: e.memset(cb[i][:, 0:16], 0.0), writes=[cb[i]])
    for i in range(2):
        for l in range(32):
            S.op("tensor", lambda e: e.matmul(pH[0:1, :], lhsT=peT[i][:, l:l + 1], rhs=w1[i][:, l, :],
                                              start=(l == 0), stop=(l == 31)), reads=[peT[i], w1[i]], writes=[pH])
        S.op("scalar", lambda e: e.copy(out=bias_f[:, i * 256:(i + 1) * 256], in_=pH[0:1, :]), reads=[pH], writes=[bias_f])
    S.op("vector", lambda e: e.tensor_copy(out=bias_hi[:, :], in_=bias_f[:, :]), reads=[bias_f], writes=[bias_hi])
    S.op("vector", lambda e: e.tensor_copy(out=bias_hif[:, :], in_=bias_hi[:, :]), reads=[bias_hi], writes=[bias_hif])
    S.op("vector", lambda e: e.tensor_sub(out=bias_hif[:, :], in0=bias_f[:, :], in1=bias_hif[:, :]), reads=[bias_f, bias_hif], writes=[bias_hif])
    S.op("vector", lambda e: e.tensor_copy(out=bias_lo[:, :], in_=bias_hif[:, :]), reads=[bias_hif], writes=[bias_lo])

    def TA(i):
        rows = slice(i * 128, (i + 1) * 128)
        h = hs[i % 3]
        yb = ybs[i % 2]
        S.dma("sync", h[:, :], x[rows, :], writes=[h])
        S.dma("sync", yb[:, :], Y1s[rows, :], reads=[bY1s[i]], writes=[yb])
        S.op("vector", lambda e: e.tensor_add(out=h[:, :], in0=h[:, :], in1=yb[:, :]), reads=[h, yb], writes=[h])
        ple.front(i % 2, h, p0[rows, :], ident, pA)

    def TB(i):
        rows = slice(i * 128, (i + 1) * 128)
        h = hs[i % 3]
        ple.back(i % 2, h, pGs, pEs)
        S.dma("gpsimd", H1[rows, :], h[:, :], reads=[h], writes=[bH1[i]])

    def TC(i):
        rows = slice(i * 128, (i + 1) * 128)
        h = hs[i % 3]
        hT = hTs[i % 2]
        rmsnorm_T(S, h, ident, sqs, ss2[i % 2], rs2[i % 2], xn2[i % 2], pA2, hT, 0, copy_eng="scalar")
        S.dma("scalar", HT[i].rearrange("p (k t) -> p k t", k=8), hT[:, :, :], reads=[hT], writes=[bHT[i]])
        for a_ in range(4):
            for k in range(8):
                S.op("tensor", lambda e: e.matmul(pKT[:, a_, :], lhsT=Wkv[:, k, a_ * 128:(a_ + 1) * 128], rhs=hT[:, k, :],
                                                  start=(k == 0), stop=(k == 7)), reads=[Wkv, hT], writes=[pKT])
        for k in range(8):
            S.op("tensor", lambda e: e.matmul(pKV[:, :], lhsT=hT[:, k, :], rhs=Wkv[:, k, 512:768],
                                              start=(k == 0), stop=(k == 7)), reads=[Wkv, hT], writes=[pKV])
        cc0 = 16 + (i % 16) * 128
        S.op("scalar", lambda e: e.copy(out=cb[0][:, cc0:cc0 + 128], in_=pKT[:, 0, :]), reads=[pKT], writes=[cb[0]])
        S.op("scalar", lambda e: e.copy(out=cb[1][:, cc0:cc0 + 128], in_=pKT[:, 1, :]), reads=[pKT], writes=[cb[1]])
        S.op("scalar", lambda e: e.copy(out=KsT[:, rows], in_=pKT[:, 2, :]), reads=[pKT], writes=[KsT])
        kw_, vw_ = kwt[i % 2], vwt[i % 2]
        S.op("scalar", lambda e: e.copy(out=kw_[:, :], in_=pKT[:, 3, :]), reads=[pKT], writes=[kw_])
        S.op("scalar", lambda e: e.copy(out=Vs[:, i, :], in_=pKV[:, 0:128]), reads=[pKV], writes=[Vs])
        S.op("scalar", lambda e: e.copy(out=vw_[:, :], in_=pKV[:, 128:256]), reads=[pKV], writes=[vw_])
        S.dma("scalar", KW[i], kw_[:, :], reads=[kw_], writes=[bKW[i]])
        S.dma("scalar", VW[i], vw_[:, :], reads=[vw_], writes=[bVW[i]])
        if i % 16 == 15:
            compress(i)

    def compress(i):
        if True:
            s16 = i // 16
            for X in range(2):
                bc = slice(X * 256, (X + 1) * 256)
                S.op("tensor", lambda e: e.matmul(pH[:, :], lhsT=ones1[0:1, :], rhs=bias_hi[0:1, bc], start=True, stop=False),
                     reads=[ones1, bias_hi], writes=[pH])
                S.op("tensor", lambda e: e.matmul(pH[:, :], lhsT=ones1[0:1, :], rhs=bias_lo[0:1, bc], start=False, stop=False),
                     reads=[ones1, bias_lo], writes=[pH])
                for l in range(32):
                    S.op("tensor", lambda e: e.matmul(pH[:, :], lhsT=cb[X][:, l:l + 2033:16], rhs=w1[X][:, l, :],
                                                      start=False, stop=(l == 31)), reads=[cb[X], w1[X]], writes=[pH])
                S.op("scalar", lambda e: e.copy(out=xs_[:, :], in_=pH[:, :]), reads=[pH], writes=[xs_])
                S.op("vector", lambda e: e.tensor_tensor(out=x2_[:, :], in0=xs_[:, :], in1=xs_[:, :], op=ALU.mult), reads=[xs_], writes=[x2_])
                S.op("vector", lambda e: e.tensor_scalar(out=x2_[:, :], in0=x2_[:, :], scalar1=0.044715, scalar2=1.0,
                                                         op0=ALU.mult, op1=ALU.add), reads=[x2_], writes=[x2_])
                S.op("vector", lambda e: e.tensor_tensor(out=x2_[:, :], in0=x2_[:, :], in1=xs_[:, :], op=ALU.mult), reads=[x2_, xs_], writes=[x2_])
                S.op("scalar", lambda e: e.activation(out=sg_[:, :], in_=x2_[:, :], func=AF.Sigmoid, scale=1.5957691216057308),
                     reads=[x2_], writes=[sg_])
                S.op("vector", lambda e: e.tensor_tensor(out=hid[:, :], in0=xs_[:, :], in1=sg_[:, :], op=ALU.mult), reads=[xs_, sg_], writes=[hid])
                for hc in range(2):
                    S.op("tensor", lambda e: e.transpose(out=pA2[:, hc, :], in_=hid[:, hc * 128:(hc + 1) * 128], identity=ident[:, :]),
                         reads=[hid, ident], writes=[pA2])
                S.op("vector", lambda e: e.tensor_copy(out=hidT[:, :, :], in_=pA2[:, 0:2, :]), reads=[pA2], writes=[hidT])
                if X == 0:
                    for hc in range(2):
                        S.op("tensor", lambda e: e.matmul(pC[:, :], lhsT=w2[0][:, hc, :], rhs=hidT[:, hc, :],
                                                          start=(hc == 0), stop=(hc == 1)), reads=[w2[0], hidT], writes=[pC])
                    S.op("scalar", lambda e: e.copy(out=KcT[:, s16 * 128:(s16 + 1) * 128], in_=pC[:, :]), reads=[pC], writes=[KcT])
                else:
                    for hc in range(2):
                        S.op("tensor", lambda e: e.matmul(pC[:, :], lhsT=hidT[:, hc, :], rhs=w2[1][:, hc, :],
                                                          start=(hc == 0), stop=(hc == 1)), reads=[w2[1], hidT], writes=[pC])
                    S.op("scalar", lambda e: e.copy(out=Vc[:, s16, :], in_=pC[:, :]), reads=[pC], writes=[Vc])
                S.op("vector", lambda e: e.tensor_copy(out=cb[X][:, 0:16], in_=cb[X][:, 2048:2064]), reads=[cb[X]], writes=[cb[X]])
    for step in range(NT + 2):
        if step < NT:
            TA(step)
        if 0 <= step - 1 < NT:
            TB(step - 1)
        if 0 <= step - 2 < NT:
            TC(step - 2)
    S.pop()

    if upto == 2:
        S.pop()
        return dbg_out(H1, bH1)
    S.push()
    ng = S.sb("ng", [128, 8], F32)
    Wn = S.sb("Wn", [128, 8, 1036], BF16)
    Wo2 = S.sb("Wo2", [128, 4, 1024], BF16)
    S.dma("sync", ng[:, :], n_g[:, :], writes=[ng])
    load_w_bf16(S, "sync", Wn, lambda k, c0, cw: Wn[:, k, c0:c0 + cw],
                n_w_in.rearrange("(k p) n -> p k n", p=128), stage, 8, 1036, gain=ng)
    load_w_bf16(S, "sync", Wo2, lambda k, c0, cw: Wo2[:, k, c0:c0 + cw],
                n_w_out.rearrange("(k p) n -> p k n", p=128), stage, 4, 1024)
    Ex = S.sb("Ex", [128, 64, 128], BF16)
    for c in range(4):
        load_const_bf16(S, "sync", Ex, Ex[:, c * 16:(c + 1) * 16, :].rearrange("p a b -> p (a b)"),
                        ex_d[:, c * 2048:(c + 1) * 2048], stage, 2048)
    Maug = S.sb("Maug", [128, 8, 257], BF16)
    load_const_bf16(S, "sync", Maug, Maug[:, 0:4, :].rearrange("p a b -> p (a b)"), maug_d[:, 0:1028], stage, 1028)
    load_const_bf16(S, "sync", Maug, Maug[:, 4:8, :].rearrange("p a b -> p (a b)"), maug_d[:, 1028:2056], stage, 1028)
    cmpm = S.sb("cmpm", [128, 33, 128], BF16)
    load_const_bf16(S, "sync", cmpm, cmpm[:, 0:16, :].rearrange("p a b -> p (a b)"), cmpm_d[:, 0:2048], stage, 2048)
    load_const_bf16(S, "sync", cmpm, cmpm[:, 16:32, :].rearrange("p a b -> p (a b)"), cmpm_d[:, 2048:4096], stage, 2048)
    load_const_bf16(S, "sync", cmpm, cmpm[:, 32, :], cmpm_d[:, 4096:4224], stage, 128)
    onesel = S.sb("onesel", [128, 3, 3], BF16)
    load_const_bf16(S, "sync", onesel, onesel[:, :, :].rearrange("p a b -> p (a b)"), onesel_d[:, :], stage, 9)
    causT = S.sb("causT", [128, 128], BF16)
    upT = S.sb("upT", [128, 128], BF16)
    load_const_bf16(S, "sync", causT, causT[:, :], causT_d[:, :], stage, 128)
    load_const_bf16(S, "sync", upT, upT[:, :], upT_d[:, :], stage, 128)

    hTs = [S.sb("hT", [128, 8, 128], BF16) for i in range(2)]
    h1s = [S.sb("h1", [128, 1024], F32) for i in range(2)]
    kwr = S.sb("kwr", [128, 6, 128], BF16)
    vwr = S.sb("vwr", [128, 6, 128], BF16)
    kwb = [Buf("kwb%d" % i, kwr.t) for i in range(6)]
    vwb = [Buf("vwb%d" % i, vwr.t) for i in range(6)]
    QTs = [S.sb("QT", [128, 512], BF16) for i in range(2)]
    gsils = [S.sb("gsil", [128, 512], F32) for i in range(2)]
    bgs = [S.sb("bg", [128, 12], F32) for i in range(2)]
    EmC = S.sb("EmC", [128, 8, 512], BF16)
    NBUF = 4
    Eb = [S.sb("Eb", [128, 512], BF16) for i in range(NBUF)]
    Emb = [S.sb("Emb", [128, 512], BF16) for i in range(NBUF)]
    imp = S.sb("imp", [128, 256], F32)
    score = S.sb("score", [128, 256], F32)
    sc2 = S.sb("sc2", [128, 256], F32)
    selF = S.sb("selF", [128, 256], F32)
    selT = S.sb("selT", [128, 2, 128], BF16)
    m8 = S.sb("m8", [128, 16], F32)
    rcols = [S.sb("rcol", [128, 1], F32) for i in range(2)]
    sumsbs = [S.sb("sumsb", [3, 512], F32) for i in range(2)]
    coef = S.sb("coef", [128, 12], F32)
    o_ = S.sb("o_", [128, 512], F32)
    ogf = S.sb("ogf", [128, 512], F32)
    ogT2 = S.sb("ogT2", [128, 4, 128], BF16)
    yts = [S.sb("yt", [128, 1024], F32) for i in range(2)]
    ObTs = [[S.sb("ObT", [128, 512], F32) for i in range(3)] for p in range(2)]
    pSc = [S.ps("pSc", [128, 512], F32) for i in range(2)]
    pMs = [S.ps("pM", [128, 512], F32) for i in range(2)]
    pO2 = [S.ps("pOb", [128, 512], F32) for i in range(2)]
    pOb = [pO2[0], pO2[1], pO2[0]]
    pSum1 = S.ps("pSum", [3, 512], F32)
    pSums = [pSum1, pSum1]
    pX = S.ps("pX", [128, 512], F32)

    S.op("gpsimd", lambda e: e.memset(selF[:, :], 0.0), writes=[selF])
    ctr = {"e": 0, "m": 0, "s": 0, "pm": 0}

    def stageA(u):
        rows = u["rows"]
        QT = u["QT"]
        ps = pSc[ctr["s"] % 2]
        ctr["s"] += 1
        S.op("tensor", lambda e: e.matmul(ps[0:rows, :], lhsT=u["ksrc"], rhs=QT[:, :], start=True, stop=True),
             reads=[u["ktrack"], QT], writes=[ps])
        E = Eb[ctr["e"] % NBUF]
        ctr["e"] += 1
        S.op("scalar", lambda e: e.activation(out=E[0:rows, :], in_=ps[0:rows, :], func=AF.Exp, scale=SCALE),
             reads=[ps], writes=[E])
        mask = u["mask"]
        if mask is None:
            Em = E
        else:
            Em = Emb[ctr["m"] % NBUF]
            ctr["m"] += 1
            if mask[0] == "sb":
                S.op("vector", lambda e: e.tensor_tensor(
                    out=Em[0:rows, :].rearrange("p (r q) -> p r q", r=4), in0=E[0:rows, :].rearrange("p (r q) -> p r q", r=4),
                    in1=mask[1].unsqueeze(1).broadcast_to([rows, 4, 128]), op=ALU.mult),
                     reads=[E] + mask[2], writes=[Em])
            else:
                t = mask[1]
                pM = pMs[ctr["pm"] % 2]
                ctr["pm"] += 1
                S.op("tensor", lambda e: e.matmul(pM[:, 0:128], lhsT=Ex[:, t % 64, :], rhs=selT[:, t // 64, :],
                                                  start=True, stop=True), reads=[Ex, selT], writes=[pM])
                S.op("vector", lambda e: e.tensor_tensor(
                    out=Em[0:rows, :].rearrange("p (r q) -> p r q", r=4), in0=E[0:rows, :].rearrange("p (r q) -> p r q", r=4),
                    in1=pM[:, 0:128].unsqueeze(1).broadcast_to([128, 4, 128]), op=ALU.mult),
                     reads=[E, pM], writes=[Em])
        u["Em"] = Em
        if u.get("emc") is not None:
            c = u["emc"]
            S.op("gpsimd", lambda e: e.tensor_copy(out=EmC[0:rows, c, :], in_=Em[0:rows, :]), reads=[Em], writes=[EmC])

    def stageB(u):
        rows = u["rows"]
        Em = u["Em"]
        b_idx = u["b"]
        q = u["q"]
        pSum = pSums[q["par"]]
        S.op("tensor", lambda e: e.matmul(pOb[b_idx][:, :], lhsT=u["vsrc"], rhs=Em[0:rows, :], start=u["first"], stop=u["last"]),
             reads=[u["vtrack"], Em], writes=[pOb[b_idx]])
        S.op("tensor", lambda e: e.matmul(pSum[:, :], lhsT=onesel[0:rows, b_idx, :], rhs=Em[0:rows, :],
                                          start=(q["nsum"] == 0), stop=(q["nsum"] == q["total_sum"] - 1)),
             reads=[onesel, Em], writes=[pSum])
        q["nsum"] += 1
        for f in u.get("post", ()):
            f()

    def rs_chunk(c):
        tl = list(range(c * 8, c * 8 + 8))
        S.collective("ReduceScatter", ALU.add, groups, Y2[c * 1024:(c + 1) * 1024, :], Y2s[c * 256:(c + 1) * 256, :],
                     reads=[bY2[t] for t in tl], writes=[bY2s[c]])

    def prologue(i):
        par = i % 2
        rows = slice(i * 128, (i + 1) * 128)
        hT, h1, QT, gsil, bg = hTs[par], h1s[par], QTs[par], gsils[par], bgs[par]
        S.dma("sync", hT[:, :, :], HT[i].rearrange("p (k t) -> p k t", k=8), reads=[bHT[i]], writes=[hT])
        S.dma("sync", h1[:, :], H1[rows, :], reads=[bH1[i]], writes=[h1])
        S.dma("sync", kwr[:, i % 6, :], KW[i], reads=[bKW[i]], writes=[kwb[i % 6]])
        S.dma("sync", vwr[:, i % 6, :], VW[i], reads=[bVW[i]], writes=[vwb[i % 6]])
        for r in range(4):
            for k in range(8):
                S.op("tensor", lambda e: e.matmul(pX[:, r * 128:(r + 1) * 128], lhsT=Wn[:, k, r * 128:(r + 1) * 128],
                                                  rhs=hT[:, k, :], start=(k == 0), stop=(k == 7)), reads=[Wn, hT], writes=[pX])
        S.op("scalar", lambda e: e.copy(out=QT[:, :], in_=pX[:, :]), reads=[pX], writes=[QT])
        for k in range(8):
            S.op("tensor", lambda e: e.matmul(pX[:, :], lhsT=hT[:, k, :], rhs=Wn[:, k, 512:1024],
                                              start=(k == 0), stop=(k == 7)), reads=[Wn, hT], writes=[pX])
        S.op("scalar", lambda e: e.activation(out=gsil[:, :], in_=pX[:, :], func=AF.Silu), reads=[pX], writes=[gsil])
        for k in range(8):
            S.op("tensor", lambda e: e.matmul(pX[:, 0:12], lhsT=hT[:, k, :], rhs=Wn[:, k, 1024:1036],
                                              start=(k == 0), stop=(k == 7)), reads=[Wn, hT], writes=[pX])
        S.op("scalar", lambda e: e.activation(out=bg[:, :], in_=pX[:, 0:12], func=AF.Sigmoid), reads=[pX], writes=[bg])

    def make_units(i):
        par = i % 2
        QT = QTs[par]
        Wp = 8 * (i + 1)
        nch = (Wp + 127) // 128
        wt = [t for t in range(i - 4, i + 1) if t >= 0]
        q = dict(i=i, par=par, nsum=0, total_sum=nch + (i + 1) + len(wt), topk_done=(i < 8))
        OT = ObTs[par]

        def evac(b):
            return lambda: S.op("scalar", lambda e: e.copy(out=OT[b][:, :], in_=pOb[b][:, :]), reads=[pOb[b]], writes=[OT[b]])

        def topk():
            ncol = 2 * i
            for r in range(4):
                pI = pMs[ctr["pm"] % 2]
                ctr["pm"] += 1
                rc_ = rcols[r % 2]
                for c in range(nch):
                    rws = min(128, Wp - c * 128)
                    S.op("tensor", lambda e: e.matmul(pI[:, 0:257], lhsT=EmC[0:rws, c, r * 128:(r + 1) * 128], rhs=Maug[0:rws, c, :],
                                                      start=(c == 0), stop=(c == nch - 1)), reads=[EmC, Maug], writes=[pI])
                S.op("vector", lambda e: e.tensor_scalar(out=rc_[:, :], in0=pI[:, 256:257], scalar1=1e-30, scalar2=None,
                                                         op0=ALU.add), reads=[pI], writes=[rc_])
                S.op("vector", lambda e: e.reciprocal(out=rc_[:, :], in_=rc_[:, :]), reads=[rc_], writes=[rc_])
                if r == 0:
                    S.op("vector", lambda e: e.tensor_scalar(out=imp[:, 0:ncol], in0=pI[:, 0:ncol], scalar1=rc_[:, 0:1],
                                                             scalar2=None, op0=ALU.mult), reads=[pI, rc_], writes=[imp])
                else:
                    S.op("vector", lambda e: e.scalar_tensor_tensor(out=imp[:, 0:ncol], in0=pI[:, 0:ncol], scalar=rc_[:, 0:1],
                                                                    in1=imp[:, 0:ncol], op0=ALU.mult, op1=ALU.add),
                         reads=[pI, rc_, imp], writes=[imp])
            S.op("vector", lambda e: e.tensor_copy(out=score[:, 0:ncol], in_=imp[:, 0:ncol]), reads=[imp], writes=[score])
            S.op("vector", lambda e: e.memset(score[:, 0:1], -1.0), writes=[score])
            S.op("vector", lambda e: e.memset(score[0:64, ncol - 1:ncol], -1.0), writes=[score])
            S.op("vector", lambda e: e.max(out=m8[:, 0:8], in_=score[:, 0:ncol]), reads=[score], writes=[m8])
            S.op("vector", lambda e: e.match_replace(out=sc2[:, 0:ncol], in_to_replace=m8[:, 0:8], in_values=score[:, 0:ncol],
                                                     imm_value=-2.0), reads=[m8, score], writes=[sc2])
            S.op("vector", lambda e: e.max(out=m8[:, 8:16], in_=sc2[:, 0:ncol]), reads=[sc2, m8], writes=[m8])
            S.op("vector", lambda e: e.tensor_scalar(out=selF[:, 0:ncol], in0=score[:, 0:ncol], scalar1=m8[:, 12:13], scalar2=None,
                                                     op0=ALU.is_ge), reads=[score, m8], writes=[selF])
            S.op("vector", lambda e: e.memset(selF[:, 0:1], 1.0), writes=[selF])
            S.op("vector", lambda e: e.memset(selF[0:64, ncol - 1:ncol], 1.0), writes=[selF])
            for c in range((ncol + 127) // 128):
                S.op("tensor", lambda e: e.transpose(out=pX[:, c * 128:(c + 1) * 128], in_=selF[:, c * 128:(c + 1) * 128],
                                                     identity=idf[:, :]), reads=[selF, idf], writes=[pX])
                S.op("vector", lambda e: e.tensor_copy(out=selT[:, c, :], in_=pX[:, c * 128:(c + 1) * 128]), reads=[pX], writes=[selT])
            q["topk_done"] = True

        units = []
        for c in range(nch):
            rws = min(128, Wp - c * 128)
            if c == nch - 1:
                mk = ("sb", cmpm[0:rws, (i % 16) + (0 if i < 16 else 16), :], [cmpm])
            elif c == 0:
                mk = ("sb", cmpm[0:rws, 32, :], [cmpm])
            else:
                mk = None
            u = dict(q=q, QT=QT, ksrc=KcT[:, c * 128:c * 128 + rws], ktrack=KcT, vsrc=Vc[0:rws, c, :], vtrack=Vc, rows=rws, mask=mk,
                     b=0, first=(c == 0), last=(c == nch - 1), emc=(c if i >= 8 else None), post=[])
            if c == nch - 1:
                u["post"].append(evac(0))
                if i >= 8:
                    u["post"].append(topk)
            units.append(u)
        for n, t in enumerate(wt):
            if t == i:
                mk = ("sb", causT[:, :], [causT])
            elif t == i - 4:
                mk = ("sb", upT[:, :], [upT])
            else:
                mk = None
            u = dict(q=q, QT=QT, ksrc=kwr[:, t % 6, :], ktrack=kwb[t % 6], vsrc=vwr[:, t % 6, :], vtrack=vwb[t % 6], rows=128, mask=mk,
                     b=2, first=(n == 0), last=(n == len(wt) - 1), post=[])
            if n == len(wt) - 1:
                u["post"].append(evac(2))
            units.append(u)
        for t in range(i + 1):
            if t == i:
                mk = ("sb", causT[:, :], [causT])
            elif i >= 8:
                mk = ("sel", t)
            else:
                mk = None
            u = dict(q=q, QT=QT, ksrc=KsT[:, t * 128:(t + 1) * 128], ktrack=KsT, vsrc=Vs[:, t, :], vtrack=Vs, rows=128, mask=mk,
                     b=1, first=(t == 0), last=(t == i), post=[], needs_sel=(mk is not None and mk[0] == "sel"))
            if t == i:
                u["post"].append(evac(1))
            units.append(u)
        return units, q

    def epilogue_parts(i):
        par = i % 2
        rows = slice(i * 128, (i + 1) * 128)
        h1, gsil, bg = h1s[par], gsils[par], bgs[par]
        OT = ObTs[par]
        pSum = pSums[par]
        sumsb = sumsbs[par]
        yt = yts[par]

        def combine(b):
            for r in range(4):
                S.op("tensor", lambda e: e.transpose(out=pX[:, r * 128:(r + 1) * 128], in_=OT[b][:, r * 128:(r + 1) * 128],
                                                     identity=idf[:, :]), reads=[OT[b], idf], writes=[pX])
            for r in range(4):
                cs_ = slice(r * 128, (r + 1) * 128)
                if b == 0:
                    S.op("vector", lambda e: e.tensor_scalar(out=o_[:, cs_], in0=pX[:, cs_], scalar1=coef[:, r * 3 + b:r * 3 + b + 1],
                                                             scalar2=None, op0=ALU.mult), reads=[pX, coef], writes=[o_])
                else:
                    S.op("vector", lambda e: e.scalar_tensor_tensor(out=o_[:, cs_], in0=pX[:, cs_], scalar=coef[:, r * 3 + b:r * 3 + b + 1],
                                                                    in1=o_[:, cs_], op0=ALU.mult, op1=ALU.add),
                         reads=[pX, coef, o_], writes=[o_])

        def part1():
            S.op("scalar", lambda e: e.copy(out=sumsb[:, :], in_=pSum[:, :]), reads=[pSum], writes=[sumsb])
            for r in range(4):
                S.op("tensor", lambda e: e.matmul(pX[:, r * 3:(r + 1) * 3], lhsT=sumsb[0:3, r * 128:(r + 1) * 128], rhs=idf[0:3, 0:3],
                                                  start=True, stop=True), reads=[sumsb, idf], writes=[pX])
            S.op("vector", lambda e: e.tensor_scalar(out=coef[:, :], in0=pX[:, 0:12], scalar1=1e-30, scalar2=None, op0=ALU.add),
                 reads=[pX], writes=[coef])
            S.op("vector", lambda e: e.reciprocal(out=coef[:, :], in_=coef[:, :]), reads=[coef], writes=[coef])
            S.op("vector", lambda e: e.tensor_tensor(out=coef[:, :], in0=coef[:, :], in1=bg[:, :], op=ALU.mult), reads=[coef, bg], writes=[coef])
            combine(0)

        def part2():
            combine(2)

        def part2b():
            combine(1)
            S.op("gpsimd", lambda e: e.tensor_mul(out=ogf[:, :], in0=o_[:, :], in1=gsil[:, :]), reads=[o_, gsil], writes=[ogf])

        def part3():
            for c in range(4):
                S.op("tensor", lambda e: e.transpose(out=pX[:, c * 128:(c + 1) * 128], in_=ogf[:, c * 128:(c + 1) * 128],
                                                     identity=idf[:, :]), reads=[ogf, idf], writes=[pX])
            S.op("vector", lambda e: e.tensor_copy(out=ogT2[:, :, :].rearrange("p a b -> p (a b)"), in_=pX[:, :]), reads=[pX], writes=[ogT2])

        def part4(hf):
            if True:
                for c in range(4):
                    S.op("tensor", lambda e: e.matmul(pX[:, :], lhsT=ogT2[:, c, :], rhs=Wo2[:, c, hf * 512:(hf + 1) * 512],
                                                      start=(c == 0), stop=(c == 3)), reads=[ogT2, Wo2], writes=[pX])
                S.op("vector", lambda e: e.scalar_tensor_tensor(out=yt[:, hf * 512:(hf + 1) * 512], in0=h1[:, hf * 512:(hf + 1) * 512],
                                                                scalar=0.25, in1=pX[:, :], op0=ALU.mult, op1=ALU.add),
                     reads=[h1, pX], writes=[yt])
            if hf == 1:
                S.dma("gpsimd", Y2[rows, :], yt[:, :], reads=[yt], writes=[bY2[i]])
                if i % 8 == 7:
                    rs_chunk(i // 8)

        return [part1, part2, part2b, part3, lambda: part4(0), lambda: part4(1)]

    LOOK = 2
    prologue(0)
    pending = []
    for i in range(NT):
        units, q = make_units(i)
        nu = len(units)
        hooks = {}
        pos = [1, 4, 7, 10, 13, 16]
        for k, f in enumerate(pending):
            hooks.setdefault(min(nu - 1, pos[k]), []).append(f)
        if i + 1 < NT:
            hooks.setdefault(min(nu - 1, 19), []).append(lambda i=i: prologue(i + 1))
        for n in range(nu + LOOK):
            if n < nu:
                if units[n].get("needs_sel"):
                    assert q["topk_done"]
                stageA(units[n])
            if n - LOOK >= 0:
                stageB(units[n - LOOK])
            for f in hooks.get(n, ()):
                f()
        assert q["nsum"] == q["total_sum"]
        pending = epilogue_parts(i)
    for f in pending:
        f()
    S.pop()
    S.pop()
    if upto == 3:
        return dbg_out(Y2, bY2)

    S.push()
    ple = PleCtx(S)
    ple.load_weights("sync", pg[1][:, :], wg[1], we[1], stage)
    fn = S.sb("fn", [128, 1024], F32)
    S.dma("sync", fn[:, :], fng[:, :], writes=[fn])
    hs = [S.sb("h", [128, 1024], F32) for i in range(2)]
    ob = [S.sb("ob", [128, 1024], F32) for i in range(2)]
    sqs = S.sb("sqs", [128, 1024], F32)
    ss = S.sb("ss", [128, 1], F32)
    rs = S.sb("rs", [128, 1], F32)
    pA = S.ps("pA", [128, 8, 128], BF16)
    pG = S.ps("pG", [128, 512], F32)
    pE = S.ps("pE", [128, 512], F32)
    pG2 = S.ps("pG2", [128, 512], F32)
    pE2 = S.ps("pE2", [128, 512], F32)
    NU = TL // 128

    def UA(u):
        rows = slice(u * 128, (u + 1) * 128)
        h = hs[u % 2]
        S.dma("sync", h[:, :], Y2s[rows, :], reads=[bY2s[u // 2]], writes=[h])
        ple.front(u % 2, h, p1s[rows, :], ident, pA)

    def UB(u):
        rows = slice(u * 128, (u + 1) * 128)
        h = hs[u % 2]
        ple.back(u % 2, h, [pG, pG2], [pE, pE2])
        rms_rstd(S, h, 1024, sqs, ss, rs)
        o = ob[u % 2]
        S.op("vector", lambda e: e.scalar_tensor_tensor(out=o[:, :], in0=h[:, :], scalar=rs[:, 0:1], in1=fn[:, :],
                                                        op0=ALU.mult, op1=ALU.mult), reads=[h, rs, fn], writes=[o])
        S.dma("gpsimd", out[rows, :], o[:, :], reads=[o])

    for step in range(NU + 1):
        if step < NU:
            UA(step)
        if step >= 1:
            UB(step - 1)
    S.finish()
    return nc, S.ninstr


def _colgain(g):
    return np.ascontiguousarray(np.asarray(g, np.float32).reshape(8, 128).T)


def _consts(T):
    half = 128
    inv = (10000.0 ** (-np.arange(half, dtype=np.float32) / np.float32(half))).astype(np.float32)
    pos = np.arange(T, dtype=np.float32)
    ang = (inv[:, None] * pos[None, :]).astype(np.float32)
    c = dict(cosT=np.cos(ang).astype(np.float32), sinT=np.sin(ang).astype(np.float32))
    p = np.arange(128)
    c["causT"] = (p[:, None] <= p[None, :]).astype(np.float32)
    c["upT"] = (p[:, None] > p[None, :]).astype(np.float32)
    c["ident"] = np.eye(128, dtype=np.float32)
    ex = np.zeros((128, 64, 128), np.float32)
    for tt in range(64):
        for hb in range(2):
            ex[(2 * tt + hb) % 128, tt, hb * 64:(hb + 1) * 64] = 1.0
    c["ex"] = ex.reshape(128, 64 * 128)
    m = np.zeros((1024, 257), np.float32)
    for j in range(256):
        for (off, w) in ((0, 1.0), (1, 2.0), (2, 2.0), (3, 2.0), (4, 1.0)):
            n = 4 * j + off
            if n < 1024:
                m[n, j] = w
    m[:, 256] = 1.0
    c["maug"] = np.ascontiguousarray(m.reshape(8, 128, 257).transpose(1, 0, 2)).reshape(128, 8 * 257)
    lane = np.arange(128)[:, None]
    ql = np.arange(128)[None, :]
    cm = np.zeros((128, 33, 128), np.float32)
    for res in range(16):
        v = (ql >= 16 * lane + 15 - 128 * res).astype(np.float32)
        b = v.copy()
        a = v.copy()
        a[0, :] = 0.0
        cm[:, res, :] = a
        cm[:, 16 + res, :] = b
    fm = np.ones((128, 128), np.float32)
    fm[0, :] = 0.0
    cm[:, 32, :] = fm
    c["cmpm"] = cm.reshape(128, 33 * 128)
    os_ = np.zeros((128, 3, 3), np.float32)
    for b in range(3):
        os_[:, b, b] = 1.0
    c["onesel"] = os_.reshape(128, 9)
    return c


def _head_consts(hd):
    lg = np.log1p(-(np.float32(2.0) ** np.float32(-5.0 - hd))).astype(np.float32)
    p = np.arange(128, dtype=np.float32)
    qd = np.exp((p + 1.0) * lg).astype(np.float32)
    kdv = (np.exp(-(p + 1.0) * lg) / 16.0).astype(np.float32)
    return dict(qdec=np.ascontiguousarray(np.broadcast_to(np.tile(qd, 4)[None, :], (128, 512))).astype(np.float32),
                kdec=np.ascontiguousarray(np.broadcast_to(np.tile(kdv, 4)[None, :], (128, 512))).astype(np.float32),
                cdec=np.full((128, 1), np.exp(np.float32(128.0) * lg), np.float32))


def _inmaps(T, B, I):
    C = _consts(T)
    TL = T // 4
    maps = []
    ca = np.ascontiguousarray
    for b in range(B):
        for g in range(4):
            m = dict(C)
            m.update(_head_consts(g))
            m["x"] = ca(I["x"][b, :T])
            m["p0"] = ca(I["p"][0, b, :T])
            p1 = I["p"][1, b, :T].reshape(T // 1024, 4, 256, 256)[:, g].reshape(TL, 256)
            m["p1s"] = ca(p1)
            wi = I["ret_w_in"][0]
            m["r_w_in"] = ca(np.concatenate([wi[:, g * 256:(g + 1) * 256], wi[:, 1024 + g * 256:1024 + (g + 1) * 256],
                                             wi[:, 2048 + g * 512:2048 + (g + 1) * 512], wi[:, 4096 + g * 512:4096 + (g + 1) * 512]], axis=1))
            m["r_g_in"] = _colgain(I["ret_norm"][0])
            m["r_gn"] = ca(np.broadcast_to(I["ret_gn"][0][g * 512:(g + 1) * 512][None, :], (128, 512)))
            m["r_w_out"] = ca(I["ret_w_out"][0][g * 512:(g + 1) * 512, :])
            for l in range(2):
                m["pg%d" % l] = _colgain(I["ple_norm"][l])
                m["wg%d" % l] = ca(I["ple_w_gate"][l])
                m["we%d" % l] = ca(I["ple_w_emb"][l])
            m["fng"] = ca(np.broadcast_to(I["final_norm"][None, :], (128, 1024)))
            m["kv_g"] = _colgain(I["kv_norm"])
            kw = I["kv_w"]
            order = [0, 1, 2, 4, 3, 5]
            m["kv_w"] = ca(np.concatenate([kw[:, pt * 512 + g * 128: pt * 512 + (g + 1) * 128] for pt in order], axis=1))
            m["peT_k"] = ca(I["cmp_pe_k"].T)
            m["peT_v"] = ca(I["cmp_pe_v"].T)
            m["w1_k"] = ca(I["cmp_w1_k"])
            m["w1_v"] = ca(I["cmp_w1_v"])
            m["w2_k"] = ca(I["cmp_w2_k"])
            m["w2_v"] = ca(I["cmp_w2_v"])
            m["n_g"] = _colgain(I["nsa_norm"][0])
            nw = I["nsa_w_in"][0]
            m["n_w_in"] = ca(np.concatenate([nw[:, g * 512:(g + 1) * 512], nw[:, 2048 + g * 512:2048 + (g + 1) * 512],
                                             nw[:, 4096 + g * 12:4096 + (g + 1) * 12]], axis=1))
            m["n_w_out"] = ca(I["nsa_w_out"][0][g * 512:(g + 1) * 512, :])
            maps.append({k: np.asarray(v, np.float32) for k, v in m.items()})
    return maps


_PROG = {}


def run_module(I, T, B):
    key = (T, B)
    if key not in _PROG:
        groups = [[b * 4 + g for g in range(4)] for b in range(B)]
        _PROG[key] = build_program(T, groups)[0]
    nc = _PROG[key]
    maps = _inmaps(T, B, I)
    res = run_bass_kernel_spmd(nc, maps, core_ids=list(range(4 * B)))
    outp = np.empty((B, T, 1024), np.float32)
    for b in range(B):
        for g in range(4):
            o = res.results[b * 4 + g]["out"].reshape(T // 1024, 256, 1024)
            outp[b].reshape(T // 1024, 4, 256, 1024)[:, g] = o
    return outp


def kernel(**inputs):
    I = {k: np.asarray(v) for k, v in inputs.items()}
    return run_module(I, 16384, 2)
```

```python
import contextlib
import numpy as np
import concourse.bass as bass
import concourse.mybir as mybir
from concourse.bass_utils import run_bass_kernel_spmd

F32 = mybir.dt.float32
BF16 = mybir.dt.bfloat16
AF = mybir.ActivationFunctionType
ALU = mybir.AluOpType
AX = mybir.AxisListType
ENGS = ("tensor", "vector", "scalar", "gpsimd", "sync")
EPS = 1e-6
SCALE = 128 ** -0.5


class Buf:
    __slots__ = ("name", "t", "w", "r", "psum")

    def __init__(self, name, t=None, psum=False):
        self.name = name
        self.t = t
        self.w = None
        self.r = []
        self.psum = psum

    def __getitem__(self, idx):
        return self.t[idx]


class _Rec:
    def __init__(self):
        self.call = None

    def __getattr__(self, name):
        def f(*a, **kw):
            assert self.call is None
            self.call = (name, a, kw)
            return self
        return f


class Sched:
    def __init__(self, nc, n_dma_sems=12):
        self.nc = nc
        self.sems = {}
        self.cnt = {}
        for e in ENGS:
            self.sems[e] = nc.alloc_semaphore("s_" + e)
            self.cnt[e] = 0
        self.sems["cc"] = nc.alloc_semaphore("s_cc")
        self.cnt["cc"] = 0
        self.dq = {}
        for q in ("sync", "gpsimd", "scalar"):
            lst = []
            for i in range(n_dma_sems):
                k = "d_%s_%d" % (q, i)
                self.sems[k] = nc.alloc_semaphore(k)
                self.cnt[k] = 0
                lst.append(k)
            self.dq[q] = [lst, 0]
        self.known = {e: {} for e in ENGS}
        self.E = {e: getattr(nc, e) for e in ENGS}
        self.ninstr = 0
        self.uid = 0
        self.stacks = [contextlib.ExitStack()]

    def push(self):
        self.stacks.append(contextlib.ExitStack())

    def pop(self):
        self.barrier()
        self.stacks.pop().close()

    def _nm(self, name):
        self.uid += 1
        return "%s_%d" % (name, self.uid)

    def sb(self, name, shape, dtype):
        nm = self._nm(name)
        return Buf(nm, self.stacks[-1].enter_context(self.nc.sbuf_tensor(nm, list(shape), dtype)))

    def ps(self, name, shape, dtype=F32):
        nm = self._nm(name)
        return Buf(nm, self.stacks[-1].enter_context(self.nc.psum_tensor(nm, list(shape), dtype)), psum=True)

    def dr(self, name, shape, dtype):
        return self.nc.dram_tensor(self._nm(name), list(shape), dtype)

    def _need(self, eng, deps):
        kn = self.known[eng]
        best = {}
        for d in deps:
            if d is None:
                continue
            k, v = d
            if k == eng and eng == "tensor":
                continue
            if kn.get(k, 0) >= v:
                continue
            if best.get(k, 0) < v:
                best[k] = v
        return best

    def _emit_waits(self, eng, best):
        for k, v in best.items():
            self.E[eng].wait_ge(self.sems[k], v)
            self.known[eng][k] = v
            self.ninstr += 1

    @staticmethod
    def _deps(reads, writes):
        deps = []
        for b in reads:
            deps.append(b.w)
            if b.psum:
                deps.extend(b.r)
        for b in writes:
            deps.append(b.w)
            deps.extend(b.r)
        return deps

    @staticmethod
    def _mark(ev, reads, writes):
        for b in reads:
            b.r.append(ev)
        for b in writes:
            b.w = ev
            b.r = []

    def op(self, eng, fn, reads=(), writes=()):
        self._emit_waits(eng, self._need(eng, self._deps(reads, writes)))
        self.cnt[eng] += 1
        rec = _Rec()
        fn(rec)
        name, a, kw = rec.call
        getattr(self.E[eng], name)(*a, **kw).then_inc(self.sems[eng], 1)
        self.ninstr += 1
        ev = (eng, self.cnt[eng])
        self._mark(ev, reads, writes)
        return ev

    def dma(self, q, out, in_, reads=(), writes=(), **kw):
        lst, idx = self.dq[q]
        k = lst[idx % len(lst)]
        self.dq[q][1] = idx + 1
        deps = self._deps(reads, writes)
        if self.cnt[k] > 0:
            deps.append((k, self.cnt[k]))
        self._emit_waits(q, self._need(q, deps))
        self.cnt[k] += 16
        self.E[q].dma_start(out=out, in_=in_, **kw).then_inc(self.sems[k], 16)
        self.ninstr += 1
        ev = (k, self.cnt[k])
        self._mark(ev, reads, writes)
        return ev

    def collective(self, kind, op, groups, in_ap, out_ap, reads=(), writes=()):
        deps = self._deps(reads, writes)
        if self.cnt["cc"] > 0:
            deps.append(("cc", self.cnt["cc"]))
        self._emit_waits("gpsimd", self._need("gpsimd", deps))
        self.cnt["cc"] += 1
        self.E["gpsimd"].collective_compute(kind, op, replica_groups=groups, ins=[in_ap], outs=[out_ap]).then_inc(
            self.sems["cc"], 1)
        self.ninstr += 1
        ev = ("cc", self.cnt["cc"])
        self._mark(ev, reads, writes)
        return ev

    def _all_events(self):
        return [(k, v) for k, v in self.cnt.items() if v > 0]

    def barrier(self):
        ev = self._all_events()
        for e in ENGS:
            self._emit_waits(e, self._need(e, [d for d in ev if d[0] != e]))

    def finish(self):
        deps = [(k, v) for k, v in self.cnt.items() if v > 0 and (k.startswith("d_") or k == "cc")]
        self._emit_waits("sync", self._need("sync", deps))
        self.barrier()
        while self.stacks:
            self.stacks.pop().close()


def load_w_bf16(S, q, dst, dst_fn, src_ap, stage, kc, ncols, gain=None):
    for k in range(kc):
        c0 = 0
        while c0 < ncols:
            cw = min(2048, ncols - c0)
            S.dma(q, stage[:, 0:cw], src_ap[:, k, c0:c0 + cw], writes=[stage])
            if gain is None:
                S.op("gpsimd", lambda e: e.tensor_copy(out=dst_fn(k, c0, cw), in_=stage[:, 0:cw]),
                     reads=[stage], writes=[dst])
            else:
                S.op("gpsimd", lambda e: e.tensor_scalar(out=dst_fn(k, c0, cw), in0=stage[:, 0:cw],
                                                         scalar1=gain[:, k:k + 1], scalar2=None, op0=ALU.mult),
                     reads=[stage, gain], writes=[dst])
            c0 += cw


def load_const_bf16(S, q, dst, dst_ap, src_ap, stage, ncols):
    S.dma(q, stage[:, 0:ncols], src_ap, writes=[stage])
    S.op("gpsimd", lambda e: e.tensor_copy(out=dst_ap, in_=stage[:, 0:ncols]), reads=[stage], writes=[dst])


def rms_rstd(S, xt, D, sqs, ss, rs):
    S.op("scalar", lambda e: e.activation(out=sqs[:, :], in_=xt[:, :], func=AF.Square, accum_out=ss[:, :]),
         reads=[xt], writes=[sqs, ss])
    S.op("vector", lambda e: e.tensor_scalar(out=rs[:, :], in0=ss[:, :], scalar1=1.0 / D, scalar2=EPS,
                                             op0=ALU.mult, op1=ALU.add), reads=[ss], writes=[rs])
    S.op("scalar", lambda e: e.activation(out=rs[:, :], in_=rs[:, :], func=AF.Sqrt), reads=[rs], writes=[rs])
    S.op("vector", lambda e: e.reciprocal(out=rs[:, :], in_=rs[:, :]), reads=[rs], writes=[rs])


def rmsnorm_T(S, xt, ident, sqs, ss, rs, xn, pT, dstT, tok0, copy_eng="vector"):
    rms_rstd(S, xt, 1024, sqs, ss, rs)
    S.op("vector", lambda e: e.tensor_scalar(out=xn[:, :], in0=xt[:, :], scalar1=rs[:, 0:1], scalar2=None,
                                             op0=ALU.mult), reads=[xt, rs], writes=[xn])
    for k in range(8):
        S.op("tensor", lambda e: e.transpose(out=pT[:, k, :], in_=xn[:, k * 128:(k + 1) * 128], identity=ident[:, :]),
             reads=[xn, ident], writes=[pT])
    if copy_eng == "vector":
        S.op("vector", lambda e: e.tensor_copy(out=dstT[:, :, tok0:tok0 + 128], in_=pT[:, :, :]), reads=[pT], writes=[dstT])
    else:
        S.op("scalar", lambda e: e.copy(out=dstT[:, :, tok0:tok0 + 128], in_=pT[:, :, :]), reads=[pT], writes=[dstT])


class PleCtx:
    def __init__(self, S):
        self.S = S
        self.Wg = S.sb("pleWg", [128, 8, 1024], BF16)
        self.We = S.sb("pleWe", [128, 2, 1024], BF16)
        self.gain = S.sb("pleGain", [128, 8], F32)
        self.sqs = S.sb("ple_sqs", [128, 1024], BF16)
        self.ss = [S.sb("ple_ss", [128, 1], F32) for i in range(2)]
        self.rs = [S.sb("ple_rs", [128, 1], F32) for i in range(2)]
        _xn = S.sb("ple_xn", [128, 1024], BF16)
        self.xn = [_xn, _xn]
        self.hnT = [S.sb("ple_hnT", [128, 8, 128], BF16) for i in range(2)]
        self.pf = [S.sb("ple_pf", [128, 256], F32) for i in range(2)]
        self.pb = [S.sb("ple_pb", [128, 256], BF16) for i in range(2)]
        self.pT = [S.sb("ple_pT", [128, 2, 128], BF16) for i in range(2)]
        _sig = S.sb("ple_sig", [128, 512], F32)
        _prod = S.sb("ple_prod", [128, 512], F32)
        self.sig = [_sig, _sig]
        self.prod = [_prod, _prod]

    def load_weights(self, q, g_ap, wg_ap, we_ap, stage):
        S = self.S
        S.dma(q, self.gain[:, :], g_ap, writes=[self.gain])
        load_w_bf16(S, q, self.Wg, lambda k, c0, cw: self.Wg[:, k, c0:c0 + cw],
                    wg_ap.rearrange("(k p) n -> p k n", p=128), stage, 8, 1024, gain=self.gain)
        load_w_bf16(S, q, self.We, lambda k, c0, cw: self.We[:, k, c0:c0 + cw],
                    we_ap.rearrange("(k p) n -> p k n", p=128), stage, 2, 1024)

    def front(self, par, h, p_ap, ident, pA):
        S = self.S
        pf, pb, pT = self.pf[par], self.pb[par], self.pT[par]
        S.dma("sync", pf[:, :], p_ap, writes=[pf])
        rmsnorm_T(S, h, ident, self.sqs, self.ss[par], self.rs[par], self.xn[par], pA, self.hnT[par], 0)
        S.op("gpsimd", lambda e: e.tensor_copy(out=pb[:, :], in_=pf[:, :]), reads=[pf], writes=[pb])
        for k in range(2):
            S.op("tensor", lambda e: e.transpose(out=pA[:, k, :], in_=pb[:, k * 128:(k + 1) * 128], identity=ident[:, :]),
                 reads=[pb, ident], writes=[pA])
        S.op("vector", lambda e: e.tensor_copy(out=pT[:, :, :], in_=pA[:, 0:2, :]), reads=[pA], writes=[pT])

    def back(self, par, h, pG, pE):
        S = self.S
        hnT, pT = self.hnT[par], self.pT[par]
        for hf in range(2):
            cs = slice(hf * 512, (hf + 1) * 512)
            sig, prod = self.sig[hf], self.prod[hf]
            for k in range(8):
                S.op("tensor", lambda e: e.matmul(pG[hf][:, :], lhsT=hnT[:, k, :], rhs=self.Wg[:, k, cs],
                                                  start=(k == 0), stop=(k == 7)), reads=[hnT, self.Wg], writes=[pG[hf]])
            for k in range(2):
                S.op("tensor", lambda e: e.matmul(pE[hf][:, :], lhsT=pT[:, k, :], rhs=self.We[:, k, cs],
                                                  start=(k == 0), stop=(k == 1)), reads=[pT, self.We], writes=[pE[hf]])
            S.op("scalar", lambda e: e.activation(out=sig[:, :], in_=pG[hf][:, :], func=AF.Sigmoid), reads=[pG[hf]], writes=[sig])
            S.op("vector", lambda e: e.tensor_tensor(out=prod[:, :], in0=sig[:, :], in1=pE[hf][:, :], op=ALU.mult),
                 reads=[sig, pE[hf]], writes=[prod])
            S.op("gpsimd", lambda e: e.tensor_add(out=h[:, cs], in0=h[:, cs], in1=prod[:, :]), reads=[h, prod], writes=[h])


def build_program(T, groups, upto=4):
    nc = bass.Bass("TRN2", target_bir_lowering=False)
    NT = T // 128
    NS = T // 512
    NCH = T // 1024
    NC16 = T // 2048
    TL = T // 4

    def din(name, shape):
        return nc.dram_tensor(name, list(shape), F32, kind="ExternalInput").ap()

    x = din("x", [T, 1024])
    p0 = din("p0", [T, 256])
    p1s = din("p1s", [TL, 256])
    identd = din("ident", [128, 128])
    r_w_in = din("r_w_in", [1024, 1536])
    r_g_in = din("r_g_in", [128, 8])
    r_gn = din("r_gn", [128, 512])
    r_w_out = din("r_w_out", [512, 1024])
    cosT = din("cosT", [128, T])
    sinT = din("sinT", [128, T])
    qdec = din("qdec", [128, 512])
    kdec = din("kdec", [128, 512])
    cdec = din("cdec", [128, 1])
    causT_d = din("causT", [128, 128])
    upT_d = din("upT", [128, 128])
    pg = [din("pg%d" % l, [128, 8]) for l in range(2)]
    wg = [din("wg%d" % l, [1024, 1024]) for l in range(2)]
    we = [din("we%d" % l, [256, 1024]) for l in range(2)]
    fng = din("fng", [128, 1024])
    kv_g = din("kv_g", [128, 8])
    kv_w = din("kv_w", [1024, 768])
    peT_k = din("peT_k", [128, 32])
    peT_v = din("peT_v", [128, 32])
    w1_k = din("w1_k", [4096, 256])
    w1_v = din("w1_v", [4096, 256])
    w2_k = din("w2_k", [256, 128])
    w2_v = din("w2_v", [256, 128])
    n_g = din("n_g", [128, 8])
    n_w_in = din("n_w_in", [1024, 1036])
    n_w_out = din("n_w_out", [512, 1024])
    ex_d = din("ex", [128, 64 * 128])
    maug_d = din("maug", [128, 8 * 257])
    cmpm_d = din("cmpm", [128, 33 * 128])
    onesel_d = din("onesel", [128, 9])
    out = nc.dram_tensor("out", [TL, 1024], F32, kind="ExternalOutput").ap()
    dbg = nc.dram_tensor("dbg", [T, 1024], F32, kind="ExternalOutput").ap() if upto < 4 else None

    def dbg_out(src, bufs):
        for c in range(T // 1024):
            S.dma("sync", dbg[c * 1024:(c + 1) * 1024, :], src[c * 1024:(c + 1) * 1024, :], reads=bufs[c * 8:(c + 1) * 8])
        S.finish()
        return nc, S.ninstr

    S = Sched(nc)
    Y1 = S.dr("Y1", [T, 1024], F32)
    Y1s = S.dr("Y1s", [T, 1024], F32)
    Y2 = S.dr("Y2", [T, 1024], F32)
    Y2s = S.dr("Y2s", [TL, 1024], F32)
    H1 = S.dr("H1", [T, 1024], F32)
    HT = S.dr("HT", [NT, 128, 1024], BF16)
    KW = S.dr("KW", [NT, 128, 128], BF16)
    VW = S.dr("VW", [NT, 128, 128], BF16)
    bY1 = [Buf("bY1_%d" % i) for i in range(NT)]
    bY1s = [Buf("bY1s_%d" % i) for i in range(NT)]
    bY2 = [Buf("bY2_%d" % i) for i in range(NT)]
    bY2s = [Buf("bY2s_%d" % i) for i in range(NCH)]
    bH1 = [Buf("bH1_%d" % i) for i in range(NT)]
    bHT = [Buf("bHT_%d" % i) for i in range(NT)]
    bKW = [Buf("bKW_%d" % i) for i in range(NT)]
    bVW = [Buf("bVW_%d" % i) for i in range(NT)]

    stage = S.sb("stage", [128, 2048], F32)
    idf = S.sb("idf", [128, 128], F32)
    ident = S.sb("identb", [128, 128], BF16)
    S.dma("sync", idf[:, :], identd[:, :], writes=[idf])
    S.op("vector", lambda e: e.tensor_copy(out=ident[:, :], in_=idf[:, :]), reads=[idf], writes=[ident])

    S.push()
    W = S.sb("W", [128, 8, 1536], BF16)
    Wo = S.sb("Wo", [128, 4, 1024], BF16)
    gin = S.sb("gin", [128, 8], F32)
    gnt = S.sb("gnt", [128, 512], F32)
    qd_t = S.sb("qd_t", [128, 512], F32)
    kd_t = S.sb("kd_t", [128, 512], F32)
    cd_t = S.sb("cd_t", [128, 1], F32)
    caus = S.sb("caus", [128, 128], F32)
    xts = [S.sb("xt", [128, 1024], F32) for i in range(2)]
    sqs = S.sb("sqs", [128, 1024], F32)
    ss = S.sb("ss", [128, 1], F32)
    rs = S.sb("rs", [128, 1], F32)
    xn = S.sb("xn", [128, 1024], BF16)
    xnT2 = [S.sb("xnT", [128, 8, 512], BF16) for i in range(2)]
    cs = [S.sb("cs", [128, 512], F32) for i in range(2)]
    sn = [S.sb("sn", [128, 512], F32) for i in range(2)]
    tabs2 = [[S.sb("tab", [128, 512], F32) for i in range(4)] for p in range(2)]
    sss = [S.sb("ss", [128, 1], F32) for i in range(2)]
    rss = [S.sb("rs", [128, 1], F32) for i in range(2)]
    xns = [S.sb("xn", [128, 1024], BF16) for i in range(2)]
    raw = [S.sb("raw", [128, 512], F32) for i in range(4)]
    tmp = [S.sb("tmp", [128, 512], F32) for i in range(4)]
    qdT2 = [S.sb("qdT", [128, 2, 512], BF16) for i in range(2)]
    kTp2 = [S.sb("kTp", [128, 2, 512], BF16) for i in range(2)]
    vb2 = [S.sb("vb", [128, 4, 512], BF16) for i in range(2)]
    gs2 = [S.sb("gs", [128, 4, 512], F32) for i in range(2)]
    st_f = [S.sb("st_f", [128, 512], F32) for i in range(2)]
    st_b = [S.sb("st_b", [128, 512], BF16) for i in range(2)]
    kd = S.sb("kd", [128, 256], BF16)
    ST = S.sb("ST", [128, 128], BF16)
    osq = S.sb("osq", [128, 512], F32)
    stats = [S.sb("stat", [128, 4], F32) for i in range(2)]
    ons = [S.sb("on", [128, 512], F32) for i in range(2)]
    ogs = [S.sb("og", [128, 512], BF16) for i in range(2)]
    ogT = S.sb("ogT", [128, 4, 128], BF16)
    yo = [S.sb("yo", [128, 1024], F32) for i in range(2)]
    pA = S.ps("pA", [128, 8, 128], BF16)
    pB = [S.ps("pB", [128, 512], F32) for i in range(2)]
    pS = S.ps("pS", [128, 128], F32)
    pO = S.ps("pO", [128, 512], F32)
    pSt = [S.ps("pSt", [128, 512], F32) for i in range(2)]
    pY = S.ps("pY", [128, 512], F32)

    for (dst, src) in ((gin, r_g_in), (gnt, r_gn), (qd_t, qdec), (kd_t, kdec), (cd_t, cdec), (caus, causT_d)):
        S.dma("sync", dst[:, :], src[:, :], writes=[dst])
    load_w_bf16(S, "sync", W, lambda k, c0, cw: W[:, k, c0:c0 + cw],
                r_w_in.rearrange("(k p) n -> p k n", p=128), stage, 8, 1536, gain=gin)
    load_w_bf16(S, "sync", Wo, lambda k, c0, cw: Wo[:, k, c0:c0 + cw],
                r_w_out.rearrange("(k p) n -> p k n", p=128), stage, 4, 1024)
    for i in range(2):
        S.op("gpsimd", lambda e: e.memset(st_f[i][:, :], 0.0), writes=[st_f[i]])
        S.op("gpsimd", lambda e: e.memset(st_b[i][:, :], 0.0), writes=[st_b[i]])

    def allreduce_chunk(c):
        r0 = c * 1024
        tl = list(range(c * 8, c * 8 + 8))
        S.collective("AllReduce", ALU.add, groups, Y1[r0:r0 + 1024, :], Y1s[r0:r0 + 1024, :],
                     reads=[bY1[t] for t in tl], writes=[bY1s[t] for t in tl])

    def tabs_for(s):
        t0 = s * 512
        cst, snt = cs[s % 2], sn[s % 2]
        tb = tabs2[s % 2]
        S.dma("sync", cst[:, :], cosT[:, t0:t0 + 512], writes=[cst])
        S.dma("sync", snt[:, :], sinT[:, t0:t0 + 512], writes=[snt])
        S.op("gpsimd", lambda e: e.tensor_mul(out=tb[0][:, :], in0=cst[:, :], in1=qd_t[:, :]), reads=[cst, qd_t], writes=[tb[0]])
        S.op("gpsimd", lambda e: e.tensor_mul(out=tb[1][:, :], in0=snt[:, :], in1=qd_t[:, :]), reads=[snt, qd_t], writes=[tb[1]])
        S.op("gpsimd", lambda e: e.tensor_mul(out=tb[2][:, :], in0=cst[:, :], in1=kd_t[:, :]), reads=[cst, kd_t], writes=[tb[2]])
        S.op("gpsimd", lambda e: e.tensor_mul(out=tb[3][:, :], in0=snt[:, :], in1=kd_t[:, :]), reads=[snt, kd_t], writes=[tb[3]])

    def F(s, j):
        ti = s * 4 + j
        xt = xts[ti % 2]
        S.dma("sync", xt[:, :], x[ti * 128:(ti + 1) * 128, :], writes=[xt])
        rmsnorm_T(S, xt, ident, sqs, sss[ti % 2], rss[ti % 2], xns[ti % 2], pA, xnT2[s % 2], j * 128)

    def P1(s):
        xnT = xnT2[s % 2]
        tb = tabs2[s % 2]
        qdT, kTp, vb, gs = qdT2[s % 2], kTp2[s % 2], vb2[s % 2], gs2[s % 2]
        for dc in range(4):
            pb = pB[dc % 2]
            for k in range(8):
                S.op("tensor", lambda e: e.matmul(pb[:, :], lhsT=W[:, k, dc * 128:(dc + 1) * 128], rhs=xnT[:, k, :],
                                                  start=(k == 0), stop=(k == 7)), reads=[W, xnT], writes=[pb])
            S.op("scalar", lambda e: e.copy(out=raw[dc][:, :], in_=pb[:, :]), reads=[pb], writes=[raw[dc]])
        for (eng, x1, x2, ct, st_, dst, ta, tb_) in (("gpsimd", raw[0], raw[1], tb[0], tb[1], qdT, tmp[0], tmp[1]),
                                                     ("vector", raw[2], raw[3], tb[2], tb[3], kTp, tmp[2], tmp[3])):
            S.op(eng, lambda e: e.tensor_mul(out=ta[:, :], in0=x1[:, :], in1=ct[:, :]), reads=[x1, ct], writes=[ta])
            S.op(eng, lambda e: e.tensor_mul(out=tb_[:, :], in0=x2[:, :], in1=st_[:, :]), reads=[x2, st_], writes=[tb_])
            S.op(eng, lambda e: e.tensor_sub(out=dst[:, 0, :], in0=ta[:, :], in1=tb_[:, :]), reads=[ta, tb_], writes=[dst])
            S.op(eng, lambda e: e.tensor_mul(out=ta[:, :], in0=x1[:, :], in1=st_[:, :]), reads=[x1, st_], writes=[ta])
            S.op(eng, lambda e: e.tensor_mul(out=tb_[:, :], in0=x2[:, :], in1=ct[:, :]), reads=[x2, ct], writes=[tb_])
            S.op(eng, lambda e: e.tensor_add(out=dst[:, 1, :], in0=ta[:, :], in1=tb_[:, :]), reads=[ta, tb_], writes=[dst])
        for j in range(4):
            for (which, c0) in (("v", 512), ("g", 1024)):
                pb = pB[0] if which == "v" else pB[1]
                for k in range(8):
                    S.op("tensor", lambda e: e.matmul(pb[:, :], lhsT=xnT[:, k, j * 128:(j + 1) * 128], rhs=W[:, k, c0:c0 + 512],
                                                      start=(k == 0), stop=(k == 7)), reads=[W, xnT], writes=[pb])
                if which == "v":
                    S.op("scalar", lambda e: e.copy(out=vb[:, j, :], in_=pb[:, :]), reads=[pb], writes=[vb])
                else:
                    S.op("scalar", lambda e: e.activation(out=gs[:, j, :], in_=pb[:, :], func=AF.Silu), reads=[pb], writes=[gs])

    def CA(s, j):
        ti = s * 4 + j
        qdT, kTp, vb = qdT2[s % 2], kTp2[s % 2], vb2[s % 2]
        on, stat = ons[ti % 2], stats[ti % 2]
        tk = slice(j * 128, (j + 1) * 128)
        for dc in range(2):
            S.op("tensor", lambda e: e.transpose(out=pA[:, dc, :], in_=kTp[:, dc, tk], identity=ident[:, :]),
                 reads=[kTp, ident], writes=[pA])
        S.op("vector", lambda e: e.tensor_scalar(out=kd[:, :], in0=pA[:, 0:2, :].rearrange("p a b -> p (a b)"),
                                                 scalar1=cd_t[:, 0:1], scalar2=None, op0=ALU.mult),
             reads=[pA, cd_t], writes=[kd])
        for dc in range(2):
            S.op("tensor", lambda e: e.matmul(pS[:, :], lhsT=kTp[:, dc, tk], rhs=qdT[:, dc, tk],
                                              start=(dc == 0), stop=(dc == 1)), reads=[kTp, qdT], writes=[pS])
        S.op("vector", lambda e: e.tensor_tensor(out=ST[:, :], in0=pS[:, :], in1=caus[:, :], op=ALU.mult),
             reads=[pS, caus], writes=[ST])
        S.op("tensor", lambda e: e.matmul(pO[:, :], lhsT=ST[:, :], rhs=vb[:, j, :], start=True, stop=False),
             reads=[ST, vb], writes=[pO])
        for dc in range(2):
            S.op("tensor", lambda e: e.matmul(pO[:, :], lhsT=qdT[:, dc, tk], rhs=st_b[dc][:, :],
                                              start=False, stop=(dc == 1)), reads=[qdT, st_b[dc]], writes=[pO])
        for dc in range(2):
            S.op("tensor", lambda e: e.matmul(pSt[dc][:, :], lhsT=kd[:, dc * 128:(dc + 1) * 128], rhs=vb[:, j, :],
                                              start=True, stop=True), reads=[kd, vb], writes=[pSt[dc]])
            S.op("vector", lambda e: e.scalar_tensor_tensor(out=st_f[dc][:, :], in0=st_f[dc][:, :], scalar=cd_t[:, 0:1],
                                                            in1=pSt[dc][:, :], op0=ALU.mult, op1=ALU.add),
                 reads=[st_f[dc], cd_t, pSt[dc]], writes=[st_f[dc]])
            S.op("gpsimd", lambda e: e.tensor_copy(out=st_b[dc][:, :], in_=st_f[dc][:, :]), reads=[st_f[dc]], writes=[st_b[dc]])
        S.op("scalar", lambda e: e.activation(out=on[:, :], in_=pO[:, :], func=AF.Identity, accum_out=stat[:, 0:1]),
             reads=[pO], writes=[on, stat])
        S.op("scalar", lambda e: e.activation(out=osq[:, :], in_=pO[:, :], func=AF.Square, accum_out=stat[:, 1:2]),
             reads=[pO], writes=[osq, stat])

    def CB(s, j):
        ti = s * 4 + j
        gs = gs2[s % 2]
        on, stat, og = ons[ti % 2], stats[ti % 2], ogs[ti % 2]
        S.op("vector", lambda e: e.tensor_scalar(out=stat[:, 0:2], in0=stat[:, 0:2], scalar1=1.0 / 512, scalar2=None,
                                                 op0=ALU.mult), reads=[stat], writes=[stat])
        S.op("vector", lambda e: e.tensor_tensor(out=stat[:, 2:3], in0=stat[:, 0:1], in1=stat[:, 0:1], op=ALU.mult),
             reads=[stat], writes=[stat])
        S.op("vector", lambda e: e.tensor_tensor(out=stat[:, 2:3], in0=stat[:, 1:2], in1=stat[:, 2:3], op=ALU.subtract),
             reads=[stat], writes=[stat])
        S.op("vector", lambda e: e.tensor_scalar(out=stat[:, 2:3], in0=stat[:, 2:3], scalar1=EPS, scalar2=None,
                                                 op0=ALU.add), reads=[stat], writes=[stat])
        S.op("scalar", lambda e: e.activation(out=stat[:, 2:3], in_=stat[:, 2:3], func=AF.Sqrt), reads=[stat], writes=[stat])
        S.op("vector", lambda e: e.reciprocal(out=stat[:, 3:4], in_=stat[:, 2:3]), reads=[stat], writes=[stat])
        S.op("vector", lambda e: e.tensor_scalar(out=on[:, :], in0=on[:, :], scalar1=stat[:, 0:1], scalar2=stat[:, 3:4],
                                                 op0=ALU.subtract, op1=ALU.mult), reads=[on, stat], writes=[on])
        S.op("gpsimd", lambda e: e.tensor_mul(out=on[:, :], in0=on[:, :], in1=gnt[:, :]), reads=[on, gnt], writes=[on])
        S.op("gpsimd", lambda e: e.tensor_mul(out=og[:, :], in0=on[:, :], in1=gs[:, j, :]), reads=[on, gs], writes=[og])
        for c in range(4):
            S.op("tensor", lambda e: e.transpose(out=pA[:, 2 + c, :], in_=og[:, c * 128:(c + 1) * 128], identity=ident[:, :]),
                 reads=[og, ident], writes=[pA])
        S.op("vector", lambda e: e.tensor_copy(out=ogT[:, :, :], in_=pA[:, 2:6, :]), reads=[pA], writes=[ogT])
        yt = yo[ti % 2]
        for hf in range(2):
            for c in range(4):
                S.op("tensor", lambda e: e.matmul(pY[:, :], lhsT=ogT[:, c, :], rhs=Wo[:, c, hf * 512:(hf + 1) * 512],
                                                  start=(c == 0), stop=(c == 3)), reads=[ogT, Wo], writes=[pY])
            S.op("scalar", lambda e: e.copy(out=yt[:, hf * 512:(hf + 1) * 512], in_=pY[:, :]), reads=[pY], writes=[yt])
        S.dma("scalar", Y1[ti * 128:(ti + 1) * 128, :], yt[:, :], reads=[yt], writes=[bY1[ti]])
        if ti >= 9 and (ti - 9) % 8 == 0:
            allreduce_chunk((ti - 9) // 8)

    prev = None
    for s in range(NS + 1):
        if s < NS:
            tabs_for(s)
        for j in range(4):
            if s < NS:
                F(s, j)
            if s >= 1:
                CA(s - 1, j)
                if prev is not None:
                    CB(*prev)
                prev = (s - 1, j)
        if s < NS:
            P1(s)
    CB(*prev)
    allreduce_chunk(NCH - 1)
    if NCH >= 2 and (NT - 1) < 9 + 8 * (NCH - 2):
        pass
    issued = set([(ti - 9) // 8 for ti in range(NT) if ti >= 9 and (ti - 9) % 8 == 0] + [NCH - 1])
    for c in range(NCH):
        if c not in issued:
            allreduce_chunk(c)
    S.pop()
    if upto == 1:
        return dbg_out(Y1s, bY1s)

    S.push()
    KsT = S.sb("KsT", [128, T], BF16)
    Vs = S.sb("Vs", [128, NT, 128], BF16)
    KcT = S.sb("KcT", [128, NC16 * 128], BF16)
    Vc = S.sb("Vc", [128, NC16, 128], BF16)

    S.push()
    ple = PleCtx(S)
    ple.load_weights("sync", pg[0][:, :], wg[0], we[0], stage)
    kvg = S.sb("kvg", [128, 8], F32)
    Wkv = S.sb("Wkv", [128, 8, 768], BF16)
    S.dma("sync", kvg[:, :], kv_g[:, :], writes=[kvg])
    load_w_bf16(S, "sync", Wkv, lambda k, c0, cw: Wkv[:, k, c0:c0 + cw],
                kv_w.rearrange("(k p) n -> p k n", p=128), stage, 8, 768, gain=kvg)
    w1 = [S.sb("w1", [128, 32, 256], BF16) for i in range(2)]
    w2 = [S.sb("w2", [128, 2, 128], BF16) for i in range(2)]
    peT = [S.sb("peT", [128, 32], BF16) for i in range(2)]
    for i, (w1d, w2d, ped) in enumerate(((w1_k, w2_k, peT_k), (w1_v, w2_v, peT_v))):
        load_w_bf16(S, "sync", w1[i], lambda k, c0, cw: w1[i][:, k, c0:c0 + cw],
                    w1d.rearrange("(l d) h -> d l h", d=128), stage, 32, 256)
        load_w_bf16(S, "sync", w2[i], lambda k, c0, cw: w2[i][:, k, c0:c0 + cw],
                    w2d.rearrange("(k p) n -> p k n", p=128), stage, 2, 128)
        load_const_bf16(S, "sync", peT[i], peT[i][:, :], ped[:, :], stage, 32)
    cb = [S.sb("cb", [128, 2064], BF16) for i in range(2)]
    hs = [S.sb("h", [128, 1024], F32) for i in range(3)]
    _yb = S.sb("yb", [128, 1024], F32)
    ybs = [_yb, _yb]
    hTs = [S.sb("hT", [128, 8, 128], BF16) for i in range(2)]
    kwt = [S.sb("kwt", [128, 128], BF16) for i in range(2)]
    vwt = [S.sb("vwt", [128, 128], BF16) for i in range(2)]
    sqs = ple.sqs
    ss2 = [S.sb("ss", [128, 1], F32) for i in range(2)]
    rs2 = [S.sb("rs", [128, 1], F32) for i in range(2)]
    _xn2 = S.sb("xn", [128, 1024], BF16)
    xn2 = [_xn2, _xn2]
    ones1 = S.sb("ones1", [1, 128], BF16)
    bias_f = S.sb("bias_f", [1, 512], F32)
    bias_hi = S.sb("bias_hi", [1, 512], BF16)
    bias_hif = S.sb("bias_hif", [1, 512], F32)
    bias_lo = S.sb("bias_lo", [1, 512], BF16)
    xs_ = S.sb("xs_", [128, 256], F32)
    x2_ = S.sb("x2_", [128, 256], F32)
    sg_ = S.sb("sg_", [128, 256], F32)
    hid = S.sb("hid", [128, 256], BF16)
    hidT = S.sb("hidT", [128, 2, 128], BF16)
    pA = S.ps("pA", [128, 8, 128], BF16)
    pA2 = S.ps("pA2", [128, 8, 128], BF16)
    pG1 = S.ps("pG", [128, 512], F32)
    pE1 = S.ps("pE", [128, 512], F32)
    pGs = [pG1, pG1]
    pEs = [pE1, pE1]
    pKT = S.ps("pKT", [128, 4, 128], F32)
    pKV = S.ps("pKV", [128, 256], F32)
    pH = S.ps("pH", [128, 256], F32)
    pC = S.ps("pC", [128, 128], F32)

    S.op("gpsimd", lambda e: e.memset(ones1[:, :], 1.0), writes=[ones1])
    for i in range(2):
        S.op("gpsimd", lambda e: e.memset(cb[i][:, 0:16], 0.0), writes=[cb[i]])
    for i in range(2):
        for l in range(32):
            S.op("tensor", lambda e: e.matmul(pH[0:1, :], lhsT=peT[i][:, l:l + 1], rhs=w1[i][:, l, :],
                                              start=(l == 0), stop=(l == 31)), reads=[peT[i], w1[i]], writes=[pH])
        S.op("scalar", lambda e: e.copy(out=bias_f[:, i * 256:(i + 1) * 256], in_=pH[0:1, :]), reads=[pH], writes=[bias_f])
    S.op("vector", lambda e: e.tensor_copy(out=bias_hi[:, :], in_=bias_f[:, :]), reads=[bias_f], writes=[bias_hi])
    S.op("vector", lambda e: e.tensor_copy(out=bias_hif[:, :], in_=bias_hi[:, :]), reads=[bias_hi], writes=[bias_hif])
    S.op("vector", lambda e: e.tensor_sub(out=bias_hif[:, :], in0=bias_f[:, :], in1=bias_hif[:, :]), reads=[bias_f, bias_hif], writes=[bias_hif])
    S.op("vector", lambda e: e.tensor_copy(out=bias_lo[:, :], in_=bias_hif[:, :]), reads=[bias_hif], writes=[bias_lo])

    def TA(i):
        rows = slice(i * 128, (i + 1) * 128)
        h = hs[i % 3]
        yb = ybs[i % 2]
        S.dma("sync", h[:, :], x[rows, :], writes=[h])
        S.dma("sync", yb[:, :], Y1s[rows, :], reads=[bY1s[i]], writes=[yb])
        S.op("vector", lambda e: e.tensor_add(out=h[:, :], in0=h[:, :], in1=yb[:, :]), reads=[h, yb], writes=[h])
        ple.front(i % 2, h, p0[rows, :], ident, pA)

    def TB(i):
        rows = slice(i * 128, (i + 1) * 128)
        h = hs[i % 3]
        ple.back(i % 2, h, pGs, pEs)
        S.dma("gpsimd", H1[rows, :], h[:, :], reads=[h], writes=[bH1[i]])

    def TC(i):
        rows = slice(i * 128, (i + 1) * 128)
        h = hs[i % 3]
        hT = hTs[i % 2]
        rmsnorm_T(S, h, ident, sqs, ss2[i % 2], rs2[i % 2], xn2[i % 2], pA2, hT, 0, copy_eng="scalar")
        S.dma("scalar", HT[i].rearrange("p (k t) -> p k t", k=8), hT[:, :, :], reads=[hT], writes=[bHT[i]])
        for a_ in range(4):
            for k in range(8):
                S.op("tensor", lambda e: e.matmul(pKT[:, a_, :], lhsT=Wkv[:, k, a_ * 128:(a_ + 1) * 128], rhs=hT[:, k, :],
                                                  start=(k == 0), stop=(k == 7)), reads=[Wkv, hT], writes=[pKT])
        for k in range(8):
            S.op("tensor", lambda e: e.matmul(pKV[:, :], lhsT=hT[:, k, :], rhs=Wkv[:, k, 512:768],
                                              start=(k == 0), stop=(k == 7)), reads=[Wkv, hT], writes=[pKV])
        cc0 = 16 + (i % 16) * 128
        S.op("scalar", lambda e: e.copy(out=cb[0][:, cc0:cc0 + 128], in_=pKT[:, 0, :]), reads=[pKT], writes=[cb[0]])
        S.op("scalar", lambda e: e.copy(out=cb[1][:, cc0:cc0 + 128], in_=pKT[:, 1, :]), reads=[pKT], writes=[cb[1]])
        S.op("scalar", lambda e: e.copy(out=KsT[:, rows], in_=pKT[:, 2, :]), reads=[pKT], writes=[KsT])
        kw_, vw_ = kwt[i % 2], vwt[i % 2]
        S.op("scalar", lambda e: e.copy(out=kw_[:, :], in_=pKT[:, 3, :]), reads=[pKT], writes=[kw_])
        S.op("scalar", lambda e: e.copy(out=Vs[:, i, :], in_=pKV[:, 0:128]), reads=[pKV], writes=[Vs])
        S.op("scalar", lambda e: e.copy(out=vw_[:, :], in_=pKV[:, 128:256]), reads=[pKV], writes=[vw_])
        S.dma("scalar", KW[i], kw_[:, :], reads=[kw_], writes=[bKW[i]])
        S.dma("scalar", VW[i], vw_[:, :], reads=[vw_], writes=[bVW[i]])
        if i % 16 == 15:
            compress(i)

    def compress(i):
        if True:
            s16 = i // 16
            for X in range(2):
                bc = slice(X * 256, (X + 1) * 256)
                S.op("tensor", lambda e: e.matmul(pH[:, :], lhsT=ones1[0:1, :], rhs=bias_hi[0:1, bc], start=True, stop=False),
                     reads=[ones1, bias_hi], writes=[pH])
                S.op("tensor", lambda e: e.matmul(pH[:, :], lhsT=ones1[0:1, :], rhs=bias_lo[0:1, bc], start=False, stop=False),
                     reads=[ones1, bias_lo], writes=[pH])
                for l in range(32):
                    S.op("tensor", lambda e: e.matmul(pH[:, :], lhsT=cb[X][:, l:l + 2033:16], rhs=w1[X][:, l, :],
                                                      start=False, stop=(l == 31)), reads=[cb[X], w1[X]], writes=[pH])
                S.op("scalar", lambda e: e.copy(out=xs_[:, :], in_=pH[:, :]), reads=[pH], writes=[xs_])
                S.op("vector", lambda e: e.tensor_tensor(out=x2_[:, :], in0=xs_[:, :], in1=xs_[:, :], op=ALU.mult), reads=[xs_], writes=[x2_])
                S.op("vector", lambda e: e.tensor_scalar(out=x2_[:, :], in0=x2_[:, :], scalar1=0.044715, scalar2=1.0,
                                                         op0=ALU.mult, op1=ALU.add), reads=[x2_], writes=[x2_])
                S.op("vector", lambda e: e.tensor_tensor(out=x2_[:, :], in0=x2_[:, :], in1=xs_[:, :], op=ALU.mult), reads=[x2_, xs_], writes=[x2_])
                S.op("scalar", lambda e: e.activation(out=sg_[:, :], in_=x2_[:, :], func=AF.Sigmoid, scale=1.5957691216057308),
                     reads=[x2_], writes=[sg_])
                S.op("vector", lambda e: e.tensor_tensor(out=hid[:, :], in0=xs_[:, :], in1=sg_[:, :], op=ALU.mult), reads=[xs_, sg_], writes=[hid])
                for hc in range(2):
                    S.op("tensor", lambda e: e.transpose(out=pA2[:, hc, :], in_=hid[:, hc * 128:(hc + 1) * 128], identity=ident[:, :]),
                         reads=[hid, ident], writes=[pA2])
                S.op("vector", lambda e: e.tensor_copy(out=hidT[:, :, :], in_=pA2[:, 0:2, :]), reads=[pA2], writes=[hidT])
                if X == 0:
                    for hc in range(2):
                        S.op("tensor", lambda e: e.matmul(pC[:, :], lhsT=w2[0][:, hc, :], rhs=hidT[:, hc, :],
                                                          start=(hc == 0), stop=(hc == 1)), reads=[w2[0], hidT], writes=[pC])
                    S.op("scalar", lambda e: e.copy(out=KcT[:, s16 * 128:(s16 + 1) * 128], in_=pC[:, :]), reads=[pC], writes=[KcT])
                else:
                    for hc in range(2):
                        S.op("tensor", lambda e: e.matmul(pC[:, :], lhsT=hidT[:, hc, :], rhs=w2[1][:, hc, :],
                                                          start=(hc == 0), stop=(hc == 1)), reads=[w2[1], hidT], writes=[pC])
                    S.op("scalar", lambda e: e.copy(out=Vc[:, s16, :], in_=pC[:, :]), reads=[pC], writes=[Vc])
                S.op("vector", lambda e: e.tensor_copy(out=cb[X][:, 0:16], in_=cb[X][:, 2048:2064]), reads=[cb[X]], writes=[cb[X]])
    for step in range(NT + 2):
        if step < NT:
            TA(step)
        if 0 <= step - 1 < NT:
            TB(step - 1)
        if 0 <= step - 2 < NT:
            TC(step - 2)
    S.pop()

    if upto == 2:
        S.pop()
        return dbg_out(H1, bH1)
    S.push()
    ng = S.sb("ng", [128, 8], F32)
    Wn = S.sb("Wn", [128, 8, 1036], BF16)
    Wo2 = S.sb("Wo2", [128, 4, 1024], BF16)
    S.dma("sync", ng[:, :], n_g[:, :], writes=[ng])
    load_w_bf16(S, "sync", Wn, lambda k, c0, cw: Wn[:, k, c0:c0 + cw],
                n_w_in.rearrange("(k p) n -> p k n", p=128), stage, 8, 1036, gain=ng)
    load_w_bf16(S, "sync", Wo2, lambda k, c0, cw: Wo2[:, k, c0:c0 + cw],
                n_w_out.rearrange("(k p) n -> p k n", p=128), stage, 4, 1024)
    Ex = S.sb("Ex", [128, 64, 128], BF16)
    for c in range(4):
        load_const_bf16(S, "sync", Ex, Ex[:, c * 16:(c + 1) * 16, :].rearrange("p a b -> p (a b)"),
                        ex_d[:, c * 2048:(c + 1) * 2048], stage, 2048)
    Maug = S.sb("Maug", [128, 8, 257], BF16)
    load_const_bf16(S, "sync", Maug, Maug[:, 0:4, :].rearrange("p a b -> p (a b)"), maug_d[:, 0:1028], stage, 1028)
    load_const_bf16(S, "sync", Maug, Maug[:, 4:8, :].rearrange("p a b -> p (a b)"), maug_d[:, 1028:2056], stage, 1028)
    cmpm = S.sb("cmpm", [128, 33, 128], BF16)
    load_const_bf16(S, "sync", cmpm, cmpm[:, 0:16, :].rearrange("p a b -> p (a b)"), cmpm_d[:, 0:2048], stage, 2048)
    load_const_bf16(S, "sync", cmpm, cmpm[:, 16:32, :].rearrange("p a b -> p (a b)"), cmpm_d[:, 2048:4096], stage, 2048)
    load_const_bf16(S, "sync", cmpm, cmpm[:, 32, :], cmpm_d[:, 4096:4224], stage, 128)
    onesel = S.sb("onesel", [128, 3, 3], BF16)
    load_const_bf16(S, "sync", onesel, onesel[:, :, :].rearrange("p a b -> p (a b)"), onesel_d[:, :], stage, 9)
    causT = S.sb("causT", [128, 128], BF16)
    upT = S.sb("upT", [128, 128], BF16)
    load_const_bf16(S, "sync", causT, causT[:, :], causT_d[:, :], stage, 128)
    load_const_bf16(S, "sync", upT, upT[:, :], upT_d[:, :], stage, 128)

    hTs = [S.sb("hT", [128, 8, 128], BF16) for i in range(2)]
    h1s = [S.sb("h1", [128, 1024], F32) for i in range(2)]
    kwr = S.sb("kwr", [128, 6, 128], BF16)
    vwr = S.sb("vwr", [128, 6, 128], BF16)
    kwb = [Buf("kwb%d" % i, kwr.t) for i in range(6)]
    vwb = [Buf("vwb%d" % i, vwr.t) for i in range(6)]
    QTs = [S.sb("QT", [128, 512], BF16) for i in range(2)]
    gsils = [S.sb("gsil", [128, 512], F32) for i in range(2)]
    bgs = [S.sb("bg", [128, 12], F32) for i in range(2)]
    EmC = S.sb("EmC", [128, 8, 512], BF16)
    NBUF = 4
    Eb = [S.sb("Eb", [128, 512], BF16) for i in range(NBUF)]
    Emb = [S.sb("Emb", [128, 512], BF16) for i in range(NBUF)]
    imp = S.sb("imp", [128, 256], F32)
    score = S.sb("score", [128, 256], F32)
    sc2 = S.sb("sc2", [128, 256], F32)
    selF = S.sb("selF", [128, 256], F32)
    selT = S.sb("selT", [128, 2, 128], BF16)
    m8 = S.sb("m8", [128, 16], F32)
    rcols = [S.sb("rcol", [128, 1], F32) for i in range(2)]
    sumsbs = [S.sb("sumsb", [3, 512], F32) for i in range(2)]
    coef = S.sb("coef", [128, 12], F32)
    o_ = S.sb("o_", [128, 512], F32)
    ogf = S.sb("ogf", [128, 512], BF16)
    ogT2 = S.sb("ogT2", [128, 4, 128], BF16)
    yts = [S.sb("yt", [128, 1024], F32) for i in range(2)]
    ObTs = [[S.sb("ObT", [128, 512], BF16) for i in range(3)] for p in range(2)]
    pSc = [S.ps("pSc", [128, 512], F32) for i in range(2)]
    pMs = [S.ps("pM", [128, 512], F32) for i in range(2)]
    pO2 = [S.ps("pOb", [128, 512], F32) for i in range(2)]
    pOb = [pO2[0], pO2[1], pO2[0]]
    pSum1 = S.ps("pSum", [3, 512], F32)
    pSums = [pSum1, pSum1]
    pX = S.ps("pX", [128, 512], F32)
    pXb = pX.t[:, :].bitcast(BF16)

    S.op("gpsimd", lambda e: e.memset(selF[:, :], 0.0), writes=[selF])
    ctr = {"e": 0, "m": 0, "s": 0, "pm": 0}

    def stageA(u):
        rows = u["rows"]
        QT = u["QT"]
        ps = pSc[ctr["s"] % 2]
        ctr["s"] += 1
        S.op("tensor", lambda e: e.matmul(ps[0:rows, :], lhsT=u["ksrc"], rhs=QT[:, :], start=True, stop=True),
             reads=[u["ktrack"], QT], writes=[ps])
        E = Eb[ctr["e"] % NBUF]
        ctr["e"] += 1
        S.op("scalar", lambda e: e.activation(out=E[0:rows, :], in_=ps[0:rows, :], func=AF.Exp, scale=SCALE),
             reads=[ps], writes=[E])
        u["E"] = E

    def stageA2(u):
        rows = u["rows"]
        E = u["E"]
        mask = u["mask"]
        if mask is None:
            Em = E
        else:
            Em = Emb[ctr["m"] % NBUF]
            ctr["m"] += 1
            if mask[0] == "sb":
                S.op("vector", lambda e: e.tensor_tensor(
                    out=Em[0:rows, :].rearrange("p (r q) -> p r q", r=4), in0=E[0:rows, :].rearrange("p (r q) -> p r q", r=4),
                    in1=mask[1].unsqueeze(1).broadcast_to([rows, 4, 128]), op=ALU.mult),
                     reads=[E] + mask[2], writes=[Em])
            else:
                t = mask[1]
                pM = pMs[ctr["pm"] % 2]
                ctr["pm"] += 1
                S.op("tensor", lambda e: e.matmul(pM[:, 0:128], lhsT=Ex[:, t % 64, :], rhs=selT[:, t // 64, :],
                                                  start=True, stop=True), reads=[Ex, selT], writes=[pM])
                S.op("vector", lambda e: e.tensor_tensor(
                    out=Em[0:rows, :].rearrange("p (r q) -> p r q", r=4), in0=E[0:rows, :].rearrange("p (r q) -> p r q", r=4),
                    in1=pM[:, 0:128].unsqueeze(1).broadcast_to([128, 4, 128]), op=ALU.mult),
                     reads=[E, pM], writes=[Em])
        u["Em"] = Em
        if u.get("emc") is not None:
            c = u["emc"]
            S.op("gpsimd", lambda e: e.tensor_copy(out=EmC[0:rows, c, :], in_=Em[0:rows, :]), reads=[Em], writes=[EmC])

    def stageB(u):
        rows = u["rows"]
        Em = u["Em"]
        b_idx = u["b"]
        q = u["q"]
        pSum = pSums[q["par"]]
        S.op("tensor", lambda e: e.matmul(pOb[b_idx][:, :], lhsT=u["vsrc"], rhs=Em[0:rows, :], start=u["first"], stop=u["last"]),
             reads=[u["vtrack"], Em], writes=[pOb[b_idx]])
        S.op("tensor", lambda e: e.matmul(pSum[:, :], lhsT=onesel[0:rows, b_idx, :], rhs=Em[0:rows, :],
                                          start=(q["nsum"] == 0), stop=(q["nsum"] == q["total_sum"] - 1)),
             reads=[onesel, Em], writes=[pSum])
        q["nsum"] += 1
        for f in u.get("post", ()):
            f()

    def rs_chunk(c):
        tl = list(range(c * 8, c * 8 + 8))
        S.collective("ReduceScatter", ALU.add, groups, Y2[c * 1024:(c + 1) * 1024, :], Y2s[c * 256:(c + 1) * 256, :],
                     reads=[bY2[t] for t in tl], writes=[bY2s[c]])

    def prologue(i):
        par = i % 2
        rows = slice(i * 128, (i + 1) * 128)
        hT, h1, QT, gsil, bg = hTs[par], h1s[par], QTs[par], gsils[par], bgs[par]
        S.dma("sync", hT[:, :, :], HT[i].rearrange("p (k t) -> p k t", k=8), reads=[bHT[i]], writes=[hT])
        S.dma("sync", h1[:, :], H1[rows, :], reads=[bH1[i]], writes=[h1])
        S.dma("sync", kwr[:, i % 6, :], KW[i], reads=[bKW[i]], writes=[kwb[i % 6]])
        S.dma("sync", vwr[:, i % 6, :], VW[i], reads=[bVW[i]], writes=[vwb[i % 6]])
        for r in range(4):
            for k in range(8):
                S.op("tensor", lambda e: e.matmul(pX[:, r * 128:(r + 1) * 128], lhsT=Wn[:, k, r * 128:(r + 1) * 128],
                                                  rhs=hT[:, k, :], start=(k == 0), stop=(k == 7)), reads=[Wn, hT], writes=[pX])
        S.op("scalar", lambda e: e.copy(out=QT[:, :], in_=pX[:, :]), reads=[pX], writes=[QT])
        for k in range(8):
            S.op("tensor", lambda e: e.matmul(pX[:, :], lhsT=hT[:, k, :], rhs=Wn[:, k, 512:1024],
                                              start=(k == 0), stop=(k == 7)), reads=[Wn, hT], writes=[pX])
        S.op("scalar", lambda e: e.activation(out=gsil[:, :], in_=pX[:, :], func=AF.Silu), reads=[pX], writes=[gsil])
        for k in range(8):
            S.op("tensor", lambda e: e.matmul(pX[:, 0:12], lhsT=hT[:, k, :], rhs=Wn[:, k, 1024:1036],
                                              start=(k == 0), stop=(k == 7)), reads=[Wn, hT], writes=[pX])
        S.op("scalar", lambda e: e.activation(out=bg[:, :], in_=pX[:, 0:12], func=AF.Sigmoid), reads=[pX], writes=[bg])

    def make_units(i):
        par = i % 2
        QT = QTs[par]
        Wp = 8 * (i + 1)
        nch = (Wp + 127) // 128
        wt = [t for t in range(i - 4, i + 1) if t >= 0]
        q = dict(i=i, par=par, nsum=0, total_sum=nch + (i + 1) + len(wt), topk_done=(i < 8))
        OT = ObTs[par]

        def evac(b):
            return lambda: S.op("scalar", lambda e: e.copy(out=OT[b][:, :], in_=pOb[b][:, :]), reads=[pOb[b]], writes=[OT[b]])

        def topk():
            ncol = 2 * i
            for r in range(4):
                pI = pMs[ctr["pm"] % 2]
                ctr["pm"] += 1
                rc_ = rcols[r % 2]
                for c in range(nch):
                    rws = min(128, Wp - c * 128)
                    S.op("tensor", lambda e: e.matmul(pI[:, 0:257], lhsT=EmC[0:rws, c, r * 128:(r + 1) * 128], rhs=Maug[0:rws, c, :],
                                                      start=(c == 0), stop=(c == nch - 1)), reads=[EmC, Maug], writes=[pI])
                S.op("vector", lambda e: e.tensor_scalar(out=rc_[:, :], in0=pI[:, 256:257], scalar1=1e-30, scalar2=None,
                                                         op0=ALU.add), reads=[pI], writes=[rc_])
                S.op("vector", lambda e: e.reciprocal(out=rc_[:, :], in_=rc_[:, :]), reads=[rc_], writes=[rc_])
                if r == 0:
                    S.op("vector", lambda e: e.tensor_scalar(out=imp[:, 0:ncol], in0=pI[:, 0:ncol], scalar1=rc_[:, 0:1],
                                                             scalar2=None, op0=ALU.mult), reads=[pI, rc_], writes=[imp])
                else:
                    S.op("vector", lambda e: e.scalar_tensor_tensor(out=imp[:, 0:ncol], in0=pI[:, 0:ncol], scalar=rc_[:, 0:1],
                                                                    in1=imp[:, 0:ncol], op0=ALU.mult, op1=ALU.add),
                         reads=[pI, rc_, imp], writes=[imp])
            S.op("vector", lambda e: e.tensor_copy(out=score[:, 0:ncol], in_=imp[:, 0:ncol]), reads=[imp], writes=[score])
            S.op("vector", lambda e: e.memset(score[:, 0:1], -1.0), writes=[score])
            S.op("vector", lambda e: e.memset(score[0:64, ncol - 1:ncol], -1.0), writes=[score])
            S.op("vector", lambda e: e.max(out=m8[:, 0:8], in_=score[:, 0:ncol]), reads=[score], writes=[m8])
            S.op("vector", lambda e: e.match_replace(out=sc2[:, 0:ncol], in_to_replace=m8[:, 0:8], in_values=score[:, 0:ncol],
                                                     imm_value=-2.0), reads=[m8, score], writes=[sc2])
            S.op("vector", lambda e: e.max(out=m8[:, 8:16], in_=sc2[:, 0:ncol]), reads=[sc2, m8], writes=[m8])
            S.op("vector", lambda e: e.tensor_scalar(out=selF[:, 0:ncol], in0=score[:, 0:ncol], scalar1=m8[:, 12:13], scalar2=None,
                                                     op0=ALU.is_ge), reads=[score, m8], writes=[selF])
            S.op("vector", lambda e: e.memset(selF[:, 0:1], 1.0), writes=[selF])
            S.op("vector", lambda e: e.memset(selF[0:64, ncol - 1:ncol], 1.0), writes=[selF])
            for c in range((ncol + 127) // 128):
                S.op("tensor", lambda e: e.transpose(out=pX[:, c * 128:(c + 1) * 128], in_=selF[:, c * 128:(c + 1) * 128],
                                                     identity=idf[:, :]), reads=[selF, idf], writes=[pX])
                S.op("vector", lambda e: e.tensor_copy(out=selT[:, c, :], in_=pX[:, c * 128:(c + 1) * 128]), reads=[pX], writes=[selT])
            q["topk_done"] = True

        units = []
        for c in range(nch):
            rws = min(128, Wp - c * 128)
            if c == nch - 1:
                mk = ("sb", cmpm[0:rws, (i % 16) + (0 if i < 16 else 16), :], [cmpm])
            elif c == 0:
                mk = ("sb", cmpm[0:rws, 32, :], [cmpm])
            else:
                mk = None
            u = dict(q=q, QT=QT, ksrc=KcT[:, c * 128:c * 128 + rws], ktrack=KcT, vsrc=Vc[0:rws, c, :], vtrack=Vc, rows=rws, mask=mk,
                     b=0, first=(c == 0), last=(c == nch - 1), emc=(c if i >= 8 else None), post=[])
            if c == nch - 1:
                u["post"].append(evac(0))
                if i >= 8:
                    u["post"].append(topk)
            units.append(u)
        for n, t in enumerate(wt):
            if t == i:
                mk = ("sb", causT[:, :], [causT])
            elif t == i - 4:
                mk = ("sb", upT[:, :], [upT])
            else:
                mk = None
            u = dict(q=q, QT=QT, ksrc=kwr[:, t % 6, :], ktrack=kwb[t % 6], vsrc=vwr[:, t % 6, :], vtrack=vwb[t % 6], rows=128, mask=mk,
                     b=2, first=(n == 0), last=(n == len(wt) - 1), post=[])
            if n == len(wt) - 1:
                u["post"].append(evac(2))
            units.append(u)
        for t in range(i + 1):
            if t == i:
                mk = ("sb", causT[:, :], [causT])
            elif i >= 8:
                mk = ("sel", t)
            else:
                mk = None
            u = dict(q=q, QT=QT, ksrc=KsT[:, t * 128:(t + 1) * 128], ktrack=KsT, vsrc=Vs[:, t, :], vtrack=Vs, rows=128, mask=mk,
                     b=1, first=(t == 0), last=(t == i), post=[], needs_sel=(mk is not None and mk[0] == "sel"))
            if t == i:
                u["post"].append(evac(1))
            units.append(u)
        return units, q

    def epilogue_parts(i):
        par = i % 2
        rows = slice(i * 128, (i + 1) * 128)
        h1, gsil, bg = h1s[par], gsils[par], bgs[par]
        OT = ObTs[par]
        pSum = pSums[par]
        sumsb = sumsbs[par]
        yt = yts[par]

        def combine(b):
            for r in range(4):
                S.op("tensor", lambda e: e.transpose(out=pXb[:, r * 128:(r + 1) * 128], in_=OT[b][:, r * 128:(r + 1) * 128],
                                                     identity=ident[:, :]), reads=[OT[b], ident], writes=[pX])
            for r in range(4):
                cs_ = slice(r * 128, (r + 1) * 128)
                if b == 0:
                    S.op("vector", lambda e: e.tensor_scalar(out=o_[:, cs_], in0=pXb[:, cs_], scalar1=coef[:, r * 3 + b:r * 3 + b + 1],
                                                             scalar2=None, op0=ALU.mult), reads=[pX, coef], writes=[o_])
                else:
                    S.op("vector", lambda e: e.scalar_tensor_tensor(out=o_[:, cs_], in0=pXb[:, cs_], scalar=coef[:, r * 3 + b:r * 3 + b + 1],
                                                                    in1=o_[:, cs_], op0=ALU.mult, op1=ALU.add),
                         reads=[pX, coef, o_], writes=[o_])

        def part1():
            S.op("scalar", lambda e: e.copy(out=sumsb[:, :], in_=pSum[:, :]), reads=[pSum], writes=[sumsb])
            for r in range(4):
                S.op("tensor", lambda e: e.matmul(pX[:, r * 3:(r + 1) * 3], lhsT=sumsb[0:3, r * 128:(r + 1) * 128], rhs=idf[0:3, 0:3],
                                                  start=True, stop=True), reads=[sumsb, idf], writes=[pX])
            S.op("vector", lambda e: e.tensor_scalar(out=coef[:, :], in0=pX[:, 0:12], scalar1=1e-30, scalar2=None, op0=ALU.add),
                 reads=[pX], writes=[coef])
            S.op("vector", lambda e: e.reciprocal(out=coef[:, :], in_=coef[:, :]), reads=[coef], writes=[coef])
            S.op("vector", lambda e: e.tensor_tensor(out=coef[:, :], in0=coef[:, :], in1=bg[:, :], op=ALU.mult), reads=[coef, bg], writes=[coef])
            combine(0)

        def part2():
            combine(2)

        def part2b():
            combine(1)
            S.op("gpsimd", lambda e: e.tensor_mul(out=ogf[:, :], in0=o_[:, :], in1=gsil[:, :]), reads=[o_, gsil], writes=[ogf])

        def part3():
            for c in range(4):
                S.op("tensor", lambda e: e.transpose(out=pXb[:, c * 128:(c + 1) * 128], in_=ogf[:, c * 128:(c + 1) * 128],
                                                     identity=ident[:, :]), reads=[ogf, ident], writes=[pX])
            S.op("vector", lambda e: e.tensor_copy(out=ogT2[:, :, :].rearrange("p a b -> p (a b)"), in_=pXb[:, 0:512]), reads=[pX], writes=[ogT2])

        def part4(hf):
            if True:
                for c in range(4):
                    S.op("tensor", lambda e: e.matmul(pX[:, :], lhsT=ogT2[:, c, :], rhs=Wo2[:, c, hf * 512:(hf + 1) * 512],
                                                      start=(c == 0), stop=(c == 3)), reads=[ogT2, Wo2], writes=[pX])
                S.op("vector", lambda e: e.scalar_tensor_tensor(out=yt[:, hf * 512:(hf + 1) * 512], in0=h1[:, hf * 512:(hf + 1) * 512],
                                                                scalar=0.25, in1=pX[:, :], op0=ALU.mult, op1=ALU.add),
                     reads=[h1, pX], writes=[yt])
            if hf == 1:
                S.dma("gpsimd", Y2[rows, :], yt[:, :], reads=[yt], writes=[bY2[i]])
                if i % 8 == 7:
                    rs_chunk(i // 8)

        return [part1, part2, part2b, part3, lambda: part4(0), lambda: part4(1)]

    LOOK = 3
    prologue(0)
    pending = []
    for i in range(NT):
        units, q = make_units(i)
        nu = len(units)
        hooks = {}
        pos = [1, 4, 7, 10, 13, 16]
        for k, f in enumerate(pending):
            hooks.setdefault(min(nu - 1, pos[k]), []).append(f)
        if i + 1 < NT:
            hooks.setdefault(min(nu - 1, 19), []).append(lambda i=i: prologue(i + 1))
        for n in range(nu + LOOK):
            if n < nu:
                stageA(units[n])
            if 0 <= n - 1 < nu:
                if units[n - 1].get("needs_sel"):
                    assert q["topk_done"]
                stageA2(units[n - 1])
            if n - LOOK >= 0:
                stageB(units[n - LOOK])
            for f in hooks.get(n, ()):
                f()
        assert q["nsum"] == q["total_sum"]
        pending = epilogue_parts(i)
    for f in pending:
        f()
    S.pop()
    S.pop()
    if upto == 3:
        return dbg_out(Y2, bY2)

    S.push()
    ple = PleCtx(S)
    ple.load_weights("sync", pg[1][:, :], wg[1], we[1], stage)
    fn = S.sb("fn", [128, 1024], F32)
    S.dma("sync", fn[:, :], fng[:, :], writes=[fn])
    hs = [S.sb("h", [128, 1024], F32) for i in range(2)]
    ob = [S.sb("ob", [128, 1024], F32) for i in range(2)]
    sqs = S.sb("sqs", [128, 1024], F32)
    ss = S.sb("ss", [128, 1], F32)
    rs = S.sb("rs", [128, 1], F32)
    pA = S.ps("pA", [128, 8, 128], BF16)
    pG = S.ps("pG", [128, 512], F32)
    pE = S.ps("pE", [128, 512], F32)
    pG2 = S.ps("pG2", [128, 512], F32)
    pE2 = S.ps("pE2", [128, 512], F32)
    NU = TL // 128

    def UA(u):
        rows = slice(u * 128, (u + 1) * 128)
        h = hs[u % 2]
        S.dma("sync", h[:, :], Y2s[rows, :], reads=[bY2s[u // 2]], writes=[h])
        ple.front(u % 2, h, p1s[rows, :], ident, pA)

    def UB(u):
        rows = slice(u * 128, (u + 1) * 128)
        h = hs[u % 2]
        ple.back(u % 2, h, [pG, pG2], [pE, pE2])
        rms_rstd(S, h, 1024, sqs, ss, rs)
        o = ob[u % 2]
        S.op("vector", lambda e: e.scalar_tensor_tensor(out=o[:, :], in0=h[:, :], scalar=rs[:, 0:1], in1=fn[:, :],
                                                        op0=ALU.mult, op1=ALU.mult), reads=[h, rs, fn], writes=[o])
        S.dma("gpsimd", out[rows, :], o[:, :], reads=[o])

    for step in range(NU + 1):
        if step < NU:
            UA(step)
        if step >= 1:
            UB(step - 1)
    S.finish()
    return nc, S.ninstr


def _colgain(g):
    return np.ascontiguousarray(np.asarray(g, np.float32).reshape(8, 128).T)


def _consts(T):
    half = 128
    inv = (10000.0 ** (-np.arange(half, dtype=np.float32) / np.float32(half))).astype(np.float32)
    pos = np.arange(T, dtype=np.float32)
    ang = (inv[:, None] * pos[None, :]).astype(np.float32)
    c = dict(cosT=np.cos(ang).astype(np.float32), sinT=np.sin(ang).astype(np.float32))
    p = np.arange(128)
    c["causT"] = (p[:, None] <= p[None, :]).astype(np.float32)
    c["upT"] = (p[:, None] > p[None, :]).astype(np.float32)
    c["ident"] = np.eye(128, dtype=np.float32)
    ex = np.zeros((128, 64, 128), np.float32)
    for tt in range(64):
        for hb in range(2):
            ex[(2 * tt + hb) % 128, tt, hb * 64:(hb + 1) * 64] = 1.0
    c["ex"] = ex.reshape(128, 64 * 128)
    m = np.zeros((1024, 257), np.float32)
    for j in range(256):
        for (off, w) in ((0, 1.0), (1, 2.0), (2, 2.0), (3, 2.0), (4, 1.0)):
            n = 4 * j + off
            if n < 1024:
                m[n, j] = w
    m[:, 256] = 1.0
    c["maug"] = np.ascontiguousarray(m.reshape(8, 128, 257).transpose(1, 0, 2)).reshape(128, 8 * 257)
    lane = np.arange(128)[:, None]
    ql = np.arange(128)[None, :]
    cm = np.zeros((128, 33, 128), np.float32)
    for res in range(16):
        v = (ql >= 16 * lane + 15 - 128 * res).astype(np.float32)
        b = v.copy()
        a = v.copy()
        a[0, :] = 0.0
        cm[:, res, :] = a
        cm[:, 16 + res, :] = b
    fm = np.ones((128, 128), np.float32)
    fm[0, :] = 0.0
    cm[:, 32, :] = fm
    c["cmpm"] = cm.reshape(128, 33 * 128)
    os_ = np.zeros((128, 3, 3), np.float32)
    for b in range(3):
        os_[:, b, b] = 1.0
    c["onesel"] = os_.reshape(128, 9)
    return c


def _head_consts(hd):
    lg = np.log1p(-(np.float32(2.0) ** np.float32(-5.0 - hd))).astype(np.float32)
    p = np.arange(128, dtype=np.float32)
    qd = np.exp((p + 1.0) * lg).astype(np.float32)
    kdv = (np.exp(-(p + 1.0) * lg) / 16.0).astype(np.float32)
    return dict(qdec=np.ascontiguousarray(np.broadcast_to(np.tile(qd, 4)[None, :], (128, 512))).astype(np.float32),
                kdec=np.ascontiguousarray(np.broadcast_to(np.tile(kdv, 4)[None, :], (128, 512))).astype(np.float32),
                cdec=np.full((128, 1), np.exp(np.float32(128.0) * lg), np.float32))


def _inmaps(T, B, I):
    C = _consts(T)
    TL = T // 4
    maps = []
    ca = np.ascontiguousarray
    for b in range(B):
        for g in range(4):
            m = dict(C)
            m.update(_head_consts(g))
            m["x"] = ca(I["x"][b, :T])
            m["p0"] = ca(I["p"][0, b, :T])
            p1 = I["p"][1, b, :T].reshape(T // 1024, 4, 256, 256)[:, g].reshape(TL, 256)
            m["p1s"] = ca(p1)
            wi = I["ret_w_in"][0]
            m["r_w_in"] = ca(np.concatenate([wi[:, g * 256:(g + 1) * 256], wi[:, 1024 + g * 256:1024 + (g + 1) * 256],
                                             wi[:, 2048 + g * 512:2048 + (g + 1) * 512], wi[:, 4096 + g * 512:4096 + (g + 1) * 512]], axis=1))
            m["r_g_in"] = _colgain(I["ret_norm"][0])
            m["r_gn"] = ca(np.broadcast_to(I["ret_gn"][0][g * 512:(g + 1) * 512][None, :], (128, 512)))
            m["r_w_out"] = ca(I["ret_w_out"][0][g * 512:(g + 1) * 512, :])
            for l in range(2):
                m["pg%d" % l] = _colgain(I["ple_norm"][l])
                m["wg%d" % l] = ca(I["ple_w_gate"][l])
                m["we%d" % l] = ca(I["ple_w_emb"][l])
            m["fng"] = ca(np.broadcast_to(I["final_norm"][None, :], (128, 1024)))
            m["kv_g"] = _colgain(I["kv_norm"])
            kw = I["kv_w"]
            order = [0, 1, 2, 4, 3, 5]
            m["kv_w"] = ca(np.concatenate([kw[:, pt * 512 + g * 128: pt * 512 + (g + 1) * 128] for pt in order], axis=1))
            m["peT_k"] = ca(I["cmp_pe_k"].T)
            m["peT_v"] = ca(I["cmp_pe_v"].T)
            m["w1_k"] = ca(I["cmp_w1_k"])
            m["w1_v"] = ca(I["cmp_w1_v"])
            m["w2_k"] = ca(I["cmp_w2_k"])
            m["w2_v"] = ca(I["cmp_w2_v"])
            m["n_g"] = _colgain(I["nsa_norm"][0])
            nw = I["nsa_w_in"][0]
            m["n_w_in"] = ca(np.concatenate([nw[:, g * 512:(g + 1) * 512], nw[:, 2048 + g * 512:2048 + (g + 1) * 512],
                                             nw[:, 4096 + g * 12:4096 + (g + 1) * 12]], axis=1))
            m["n_w_out"] = ca(I["nsa_w_out"][0][g * 512:(g + 1) * 512, :])
            maps.append({k: np.asarray(v, np.float32) for k, v in m.items()})
    return maps


_PROG = {}


def run_module(I, T, B):
    key = (T, B)
    if key not in _PROG:
        groups = [[b * 4 + g for g in range(4)] for b in range(B)]
        _PROG[key] = build_program(T, groups)[0]
    nc = _PROG[key]
    maps = _inmaps(T, B, I)
    res = run_bass_kernel_spmd(nc, maps, core_ids=list(range(4 * B)))
    outp = np.empty((B, T, 1024), np.float32)
    for b in range(B):
        for g in range(4):
            o = res.results[b * 4 + g]["out"].reshape(T // 1024, 256, 1024)
            outp[b].reshape(T // 1024, 4, 256, 1024)[:, g] = o
    return outp


def kernel(**inputs):
    I = {k: np.asarray(v) for k, v in inputs.items()}
    return run_module(I, 16384, 2)
```

```python
import contextlib
import numpy as np
import concourse.bass as bass
import concourse.mybir as mybir
from concourse.bass_utils import run_bass_kernel_spmd

F32 = mybir.dt.float32
BF16 = mybir.dt.bfloat16
AF = mybir.ActivationFunctionType
ALU = mybir.AluOpType
AX = mybir.AxisListType
ENGS = ("tensor", "vector", "scalar", "gpsimd", "sync")
EPS = 1e-6
SCALE = 128 ** -0.5


class Buf:
    __slots__ = ("name", "t", "w", "r", "psum")

    def __init__(self, name, t=None, psum=False):
        self.name = name
        self.t = t
        self.w = None
        self.r = []
        self.psum = psum

    def __getitem__(self, idx):
        return self.t[idx]


class _Rec:
    def __init__(self):
        self.call = None

    def __getattr__(self, name):
        def f(*a, **kw):
            assert self.call is None
            self.call = (name, a, kw)
            return self
        return f


class Sched:
    def __init__(self, nc, n_dma_sems=12):
        self.nc = nc
        self.sems = {}
        self.cnt = {}
        for e in ENGS:
            self.sems[e] = nc.alloc_semaphore("s_" + e)
            self.cnt[e] = 0
        self.sems["cc"] = nc.alloc_semaphore("s_cc")
        self.cnt["cc"] = 0
        self.dq = {}
        for q in ("sync", "gpsimd", "scalar"):
            lst = []
            for i in range(n_dma_sems):
                k = "d_%s_%d" % (q, i)
                self.sems[k] = nc.alloc_semaphore(k)
                self.cnt[k] = 0
                lst.append(k)
            self.dq[q] = [lst, 0]
        self.known = {e: {} for e in ENGS}
        self.E = {e: getattr(nc, e) for e in ENGS}
        self.ninstr = 0
        self.uid = 0
        self.stacks = [contextlib.ExitStack()]

    def push(self):
        self.stacks.append(contextlib.ExitStack())

    def pop(self):
        self.barrier()
        self.stacks.pop().close()

    def _nm(self, name):
        self.uid += 1
        return "%s_%d" % (name, self.uid)

    def sb(self, name, shape, dtype):
        nm = self._nm(name)
        return Buf(nm, self.stacks[-1].enter_context(self.nc.sbuf_tensor(nm, list(shape), dtype)))

    def ps(self, name, shape, dtype=F32):
        nm = self._nm(name)
        return Buf(nm, self.stacks[-1].enter_context(self.nc.psum_tensor(nm, list(shape), dtype)), psum=True)

    def dr(self, name, shape, dtype):
        return self.nc.dram_tensor(self._nm(name), list(shape), dtype)

    def _need(self, eng, deps):
        kn = self.known[eng]
        best = {}
        for d in deps:
            if d is None:
                continue
            k, v = d
            if k == eng and eng == "tensor":
                continue
            if kn.get(k, 0) >= v:
                continue
            if best.get(k, 0) < v:
                best[k] = v
        return best

    def _emit_waits(self, eng, best):
        for k, v in best.items():
            self.E[eng].wait_ge(self.sems[k], v)
            self.known[eng][k] = v
            self.ninstr += 1

    @staticmethod
    def _deps(reads, writes):
        deps = []
        for b in reads:
            deps.append(b.w)
            if b.psum:
                deps.extend(b.r)
        for b in writes:
            deps.append(b.w)
            deps.extend(b.r)
        return deps

    @staticmethod
    def _mark(ev, reads, writes):
        for b in reads:
            b.r.append(ev)
        for b in writes:
            b.w = ev
            b.r = []

    def op(self, eng, fn, reads=(), writes=(), lhs=None):
        attach = None
        if eng == "tensor" and lhs is None:
            self._emit_waits(eng, self._need(eng, self._deps(reads, writes)))
        else:
            if lhs:
                self._emit_waits(eng, self._need(eng, self._deps(lhs, ())))
            best = self._need(eng, self._deps(reads, writes))
            if best:
                k = next(iter(best))
                attach = (k, best.pop(k))
            self._emit_waits(eng, best)
        self.cnt[eng] += 1
        rec = _Rec()
        fn(rec)
        name, a, kw = rec.call
        ins = getattr(self.E[eng], name)(*a, **kw)
        if attach is not None:
            ins = ins._wait_ge(self.sems[attach[0]], attach[1])
            self.known[eng][attach[0]] = attach[1]
        ins.then_inc(self.sems[eng], 1)
        self.ninstr += 1
        ev = (eng, self.cnt[eng])
        self._mark(ev, reads, writes)
        return ev

    def dma(self, q, out, in_, reads=(), writes=(), **kw):
        lst, idx = self.dq[q]
        k = lst[idx % len(lst)]
        self.dq[q][1] = idx + 1
        deps = self._deps(reads, writes)
        if self.cnt[k] > 0:
            deps.append((k, self.cnt[k]))
        self._emit_waits(q, self._need(q, deps))
        self.cnt[k] += 16
        self.E[q].dma_start(out=out, in_=in_, **kw).then_inc(self.sems[k], 16)
        self.ninstr += 1
        ev = (k, self.cnt[k])
        self._mark(ev, reads, writes)
        return ev

    def collective(self, kind, op, groups, in_ap, out_ap, reads=(), writes=()):
        deps = self._deps(reads, writes)
        if self.cnt["cc"] > 0:
            deps.append(("cc", self.cnt["cc"]))
        self._emit_waits("gpsimd", self._need("gpsimd", deps))
        self.cnt["cc"] += 1
        self.E["gpsimd"].collective_compute(kind, op, replica_groups=groups, ins=[in_ap], outs=[out_ap]).then_inc(
            self.sems["cc"], 1)
        self.ninstr += 1
        ev = ("cc", self.cnt["cc"])
        self._mark(ev, reads, writes)
        return ev

    def _all_events(self):
        return [(k, v) for k, v in self.cnt.items() if v > 0]

    def barrier(self):
        ev = self._all_events()
        for e in ENGS:
            self._emit_waits(e, self._need(e, [d for d in ev if d[0] != e]))

    def finish(self):
        deps = [(k, v) for k, v in self.cnt.items() if v > 0 and (k.startswith("d_") or k == "cc")]
        self._emit_waits("sync", self._need("sync", deps))
        self.barrier()
        while self.stacks:
            self.stacks.pop().close()


def load_w_bf16(S, q, dst, dst_fn, src_ap, stage, kc, ncols, gain=None):
    for k in range(kc):
        c0 = 0
        while c0 < ncols:
            cw = min(2048, ncols - c0)
            S.dma(q, stage[:, 0:cw], src_ap[:, k, c0:c0 + cw], writes=[stage])
            if gain is None:
                S.op("gpsimd", lambda e: e.tensor_copy(out=dst_fn(k, c0, cw), in_=stage[:, 0:cw]),
                     reads=[stage], writes=[dst])
            else:
                S.op("gpsimd", lambda e: e.tensor_scalar(out=dst_fn(k, c0, cw), in0=stage[:, 0:cw],
                                                         scalar1=gain[:, k:k + 1], scalar2=None, op0=ALU.mult),
                     reads=[stage, gain], writes=[dst])
            c0 += cw


def load_const_bf16(S, q, dst, dst_ap, src_ap, stage, ncols):
    S.dma(q, stage[:, 0:ncols], src_ap, writes=[stage])
    S.op("gpsimd", lambda e: e.tensor_copy(out=dst_ap, in_=stage[:, 0:ncols]), reads=[stage], writes=[dst])


def rms_rstd(S, xt, D, sqs, ss, rs):
    S.op("scalar", lambda e: e.activation(out=sqs[:, :], in_=xt[:, :], func=AF.Square, accum_out=ss[:, :]),
         reads=[xt], writes=[sqs, ss])
    S.op("vector", lambda e: e.tensor_scalar(out=rs[:, :], in0=ss[:, :], scalar1=1.0 / D, scalar2=EPS,
                                             op0=ALU.mult, op1=ALU.add), reads=[ss], writes=[rs])
    S.op("scalar", lambda e: e.activation(out=rs[:, :], in_=rs[:, :], func=AF.Sqrt), reads=[rs], writes=[rs])
    S.op("vector", lambda e: e.reciprocal(out=rs[:, :], in_=rs[:, :]), reads=[rs], writes=[rs])


def rmsnorm_T(S, xt, ident, sqs, ss, rs, xn, pT, dstT, tok0, copy_eng="vector"):
    rms_rstd(S, xt, 1024, sqs, ss, rs)
    S.op("vector", lambda e: e.tensor_scalar(out=xn[:, :], in0=xt[:, :], scalar1=rs[:, 0:1], scalar2=None,
                                             op0=ALU.mult), reads=[xt, rs], writes=[xn])
    for k in range(8):
        S.op("tensor", lambda e: e.transpose(out=pT[:, k, :], in_=xn[:, k * 128:(k + 1) * 128], identity=ident[:, :]),
             reads=[xn, ident], writes=[pT])
    if copy_eng == "vector":
        S.op("vector", lambda e: e.tensor_copy(out=dstT[:, :, tok0:tok0 + 128], in_=pT[:, :, :]), reads=[pT], writes=[dstT])
    else:
        S.op("scalar", lambda e: e.copy(out=dstT[:, :, tok0:tok0 + 128], in_=pT[:, :, :]), reads=[pT], writes=[dstT])


class PleCtx:
    def __init__(self, S):
        self.S = S
        self.Wg = S.sb("pleWg", [128, 8, 1024], BF16)
        self.We = S.sb("pleWe", [128, 2, 1024], BF16)
        self.gain = S.sb("pleGain", [128, 8], F32)
        self.sqs = S.sb("ple_sqs", [128, 1024], BF16)
        self.ss = [S.sb("ple_ss", [128, 1], F32) for i in range(2)]
        self.rs = [S.sb("ple_rs", [128, 1], F32) for i in range(2)]
        _xn = S.sb("ple_xn", [128, 1024], BF16)
        self.xn = [_xn, _xn]
        self.hnT = [S.sb("ple_hnT", [128, 8, 128], BF16) for i in range(2)]
        self.pf = [S.sb("ple_pf", [128, 256], F32) for i in range(2)]
        self.pb = [S.sb("ple_pb", [128, 256], BF16) for i in range(2)]
        self.pT = [S.sb("ple_pT", [128, 2, 128], BF16) for i in range(2)]
        _sig = S.sb("ple_sig", [128, 512], F32)
        _prod = S.sb("ple_prod", [128, 512], F32)
        self.sig = [_sig, _sig]
        self.prod = [_prod, _prod]

    def load_weights(self, q, g_ap, wg_ap, we_ap, stage):
        S = self.S
        S.dma(q, self.gain[:, :], g_ap, writes=[self.gain])
        load_w_bf16(S, q, self.Wg, lambda k, c0, cw: self.Wg[:, k, c0:c0 + cw],
                    wg_ap.rearrange("(k p) n -> p k n", p=128), stage, 8, 1024, gain=self.gain)
        load_w_bf16(S, q, self.We, lambda k, c0, cw: self.We[:, k, c0:c0 + cw],
                    we_ap.rearrange("(k p) n -> p k n", p=128), stage, 2, 1024)

    def front(self, par, h, p_ap, ident, pA):
        S = self.S
        pf, pb, pT = self.pf[par], self.pb[par], self.pT[par]
        S.dma("sync", pf[:, :], p_ap, writes=[pf])
        rmsnorm_T(S, h, ident, self.sqs, self.ss[par], self.rs[par], self.xn[par], pA, self.hnT[par], 0)
        S.op("gpsimd", lambda e: e.tensor_copy(out=pb[:, :], in_=pf[:, :]), reads=[pf], writes=[pb])
        for k in range(2):
            S.op("tensor", lambda e: e.transpose(out=pA[:, k, :], in_=pb[:, k * 128:(k + 1) * 128], identity=ident[:, :]),
                 reads=[pb, ident], writes=[pA])
        S.op("vector", lambda e: e.tensor_copy(out=pT[:, :, :], in_=pA[:, 0:2, :]), reads=[pA], writes=[pT])

    def back(self, par, h, pG, pE):
        S = self.S
        hnT, pT = self.hnT[par], self.pT[par]
        for hf in range(2):
            cs = slice(hf * 512, (hf + 1) * 512)
            sig, prod = self.sig[hf], self.prod[hf]
            for k in range(8):
                S.op("tensor", lambda e: e.matmul(pG[hf][:, :], lhsT=hnT[:, k, :], rhs=self.Wg[:, k, cs],
                                                  start=(k == 0), stop=(k == 7)), reads=[hnT, self.Wg], writes=[pG[hf]])
            for k in range(2):
                S.op("tensor", lambda e: e.matmul(pE[hf][:, :], lhsT=pT[:, k, :], rhs=self.We[:, k, cs],
                                                  start=(k == 0), stop=(k == 1)), reads=[pT, self.We], writes=[pE[hf]])
            S.op("scalar", lambda e: e.activation(out=sig[:, :], in_=pG[hf][:, :], func=AF.Sigmoid), reads=[pG[hf]], writes=[sig])
            S.op("vector", lambda e: e.tensor_tensor(out=prod[:, :], in0=sig[:, :], in1=pE[hf][:, :], op=ALU.mult),
                 reads=[sig, pE[hf]], writes=[prod])
            S.op("gpsimd", lambda e: e.tensor_add(out=h[:, cs], in0=h[:, cs], in1=prod[:, :]), reads=[h, prod], writes=[h])


def build_program(T, groups, upto=4):
    nc = bass.Bass("TRN2", target_bir_lowering=False)
    NT = T // 128
    NS = T // 512
    NCH = T // 1024
    NC16 = T // 2048
    TL = T // 4

    def din(name, shape):
        return nc.dram_tensor(name, list(shape), F32, kind="ExternalInput").ap()

    x = din("x", [T, 1024])
    p0 = din("p0", [T, 256])
    p1s = din("p1s", [TL, 256])
    identd = din("ident", [128, 128])
    r_w_in = din("r_w_in", [1024, 1536])
    r_g_in = din("r_g_in", [128, 8])
    r_gn = din("r_gn", [128, 512])
    r_w_out = din("r_w_out", [512, 1024])
    cosT = din("cosT", [128, T])
    sinT = din("sinT", [128, T])
    qdec = din("qdec", [128, 512])
    kdec = din("kdec", [128, 512])
    cdec = din("cdec", [128, 1])
    causT_d = din("causT", [128, 128])
    upT_d = din("upT", [128, 128])
    pg = [din("pg%d" % l, [128, 8]) for l in range(2)]
    wg = [din("wg%d" % l, [1024, 1024]) for l in range(2)]
    we = [din("we%d" % l, [256, 1024]) for l in range(2)]
    fng = din("fng", [128, 1024])
    kv_g = din("kv_g", [128, 8])
    kv_w = din("kv_w", [1024, 768])
    peT_k = din("peT_k", [128, 32])
    peT_v = din("peT_v", [128, 32])
    w1_k = din("w1_k", [4096, 256])
    w1_v = din("w1_v", [4096, 256])
    w2_k = din("w2_k", [256, 128])
    w2_v = din("w2_v", [256, 128])
    n_g = din("n_g", [128, 8])
    n_w_in = din("n_w_in", [1024, 1036])
    n_w_out = din("n_w_out", [512, 1024])
    ex_d = din("ex", [128, 64 * 128])
    maug_d = din("maug", [128, 8 * 257])
    cmpm_d = din("cmpm", [128, 33 * 128])
    onesel_d = din("onesel", [128, 9])
    out = nc.dram_tensor("out", [TL, 1024], F32, kind="ExternalOutput").ap()
    dbg = nc.dram_tensor("dbg", [T, 1024], F32, kind="ExternalOutput").ap() if upto < 4 else None

    def dbg_out(src, bufs):
        for c in range(T // 1024):
            S.dma("sync", dbg[c * 1024:(c + 1) * 1024, :], src[c * 1024:(c + 1) * 1024, :], reads=bufs[c * 8:(c + 1) * 8])
        S.finish()
        return nc, S.ninstr

    S = Sched(nc)
    Y1 = S.dr("Y1", [T, 1024], F32)
    Y1s = S.dr("Y1s", [T, 1024], F32)
    Y2 = S.dr("Y2", [T, 1024], F32)
    Y2s = S.dr("Y2s", [TL, 1024], F32)
    H1 = S.dr("H1", [T, 1024], F32)
    HT = S.dr("HT", [NT, 128, 1024], BF16)
    KW = S.dr("KW", [NT, 128, 128], BF16)
    VW = S.dr("VW", [NT, 128, 128], BF16)
    bY1 = [Buf("bY1_%d" % i) for i in range(NT)]
    bY1s = [Buf("bY1s_%d" % i) for i in range(NT)]
    bY2 = [Buf("bY2_%d" % i) for i in range(NT)]
    bY2s = [Buf("bY2s_%d" % i) for i in range(NCH)]
    bH1 = [Buf("bH1_%d" % i) for i in range(NT)]
    bHT = [Buf("bHT_%d" % i) for i in range(NT)]
    bKW = [Buf("bKW_%d" % i) for i in range(NT)]
    bVW = [Buf("bVW_%d" % i) for i in range(NT)]

    stage = S.sb("stage", [128, 2048], F32)
    idf = S.sb("idf", [128, 128], F32)
    ident = S.sb("identb", [128, 128], BF16)
    S.dma("sync", idf[:, :], identd[:, :], writes=[idf])
    S.op("vector", lambda e: e.tensor_copy(out=ident[:, :], in_=idf[:, :]), reads=[idf], writes=[ident])

    S.push()
    W = S.sb("W", [128, 8, 1536], BF16)
    Wo = S.sb("Wo", [128, 4, 1024], BF16)
    gin = S.sb("gin", [128, 8], F32)
    gnt = S.sb("gnt", [128, 512], F32)
    qd_t = S.sb("qd_t", [128, 512], F32)
    kd_t = S.sb("kd_t", [128, 512], F32)
    cd_t = S.sb("cd_t", [128, 1], F32)
    caus = S.sb("caus", [128, 128], F32)
    xts = [S.sb("xt", [128, 1024], F32) for i in range(2)]
    sqs = S.sb("sqs", [128, 1024], F32)
    ss = S.sb("ss", [128, 1], F32)
    rs = S.sb("rs", [128, 1], F32)
    xn = S.sb("xn", [128, 1024], BF16)
    xnT2 = [S.sb("xnT", [128, 8, 512], BF16) for i in range(2)]
    cs = [S.sb("cs", [128, 512], F32) for i in range(2)]
    sn = [S.sb("sn", [128, 512], F32) for i in range(2)]
    tabs2 = [[S.sb("tab", [128, 512], F32) for i in range(4)] for p in range(2)]
    sss = [S.sb("ss", [128, 1], F32) for i in range(2)]
    rss = [S.sb("rs", [128, 1], F32) for i in range(2)]
    xns = [S.sb("xn", [128, 1024], BF16) for i in range(2)]
    raw = [S.sb("raw", [128, 512], F32) for i in range(4)]
    tmp = [S.sb("tmp", [128, 512], F32) for i in range(4)]
    qdT2 = [S.sb("qdT", [128, 2, 512], BF16) for i in range(2)]
    kTp2 = [S.sb("kTp", [128, 2, 512], BF16) for i in range(2)]
    vb2 = [S.sb("vb", [128, 4, 512], BF16) for i in range(2)]
    gs2 = [S.sb("gs", [128, 4, 512], F32) for i in range(2)]
    st_f = [S.sb("st_f", [128, 512], F32) for i in range(2)]
    st_b = [S.sb("st_b", [128, 512], BF16) for i in range(2)]
    kd = S.sb("kd", [128, 256], BF16)
    ST = S.sb("ST", [128, 128], BF16)
    osq = S.sb("osq", [128, 512], F32)
    stats = [S.sb("stat", [128, 4], F32) for i in range(2)]
    ons = [S.sb("on", [128, 512], F32) for i in range(2)]
    ogs = [S.sb("og", [128, 512], BF16) for i in range(2)]
    ogT = S.sb("ogT", [128, 4, 128], BF16)
    yo = [S.sb("yo", [128, 1024], F32) for i in range(2)]
    pA = S.ps("pA", [128, 8, 128], BF16)
    pB = [S.ps("pB", [128, 512], F32) for i in range(2)]
    pS = S.ps("pS", [128, 128], F32)
    pO = S.ps("pO", [128, 512], F32)
    pSt = [S.ps("pSt", [128, 512], F32) for i in range(2)]
    pY = S.ps("pY", [128, 512], F32)

    for (dst, src) in ((gin, r_g_in), (gnt, r_gn), (qd_t, qdec), (kd_t, kdec), (cd_t, cdec), (caus, causT_d)):
        S.dma("sync", dst[:, :], src[:, :], writes=[dst])
    load_w_bf16(S, "sync", W, lambda k, c0, cw: W[:, k, c0:c0 + cw],
                r_w_in.rearrange("(k p) n -> p k n", p=128), stage, 8, 1536, gain=gin)
    load_w_bf16(S, "sync", Wo, lambda k, c0, cw: Wo[:, k, c0:c0 + cw],
                r_w_out.rearrange("(k p) n -> p k n", p=128), stage, 4, 1024)
    for i in range(2):
        S.op("gpsimd", lambda e: e.memset(st_f[i][:, :], 0.0), writes=[st_f[i]])
        S.op("gpsimd", lambda e: e.memset(st_b[i][:, :], 0.0), writes=[st_b[i]])

    def allreduce_chunk(c):
        r0 = c * 1024
        tl = list(range(c * 8, c * 8 + 8))
        S.collective("AllReduce", ALU.add, groups, Y1[r0:r0 + 1024, :], Y1s[r0:r0 + 1024, :],
                     reads=[bY1[t] for t in tl], writes=[bY1s[t] for t in tl])

    def tabs_for(s):
        t0 = s * 512
        cst, snt = cs[s % 2], sn[s % 2]
        tb = tabs2[s % 2]
        S.dma("sync", cst[:, :], cosT[:, t0:t0 + 512], writes=[cst])
        S.dma("sync", snt[:, :], sinT[:, t0:t0 + 512], writes=[snt])
        S.op("gpsimd", lambda e: e.tensor_mul(out=tb[0][:, :], in0=cst[:, :], in1=qd_t[:, :]), reads=[cst, qd_t], writes=[tb[0]])
        S.op("gpsimd", lambda e: e.tensor_mul(out=tb[1][:, :], in0=snt[:, :], in1=qd_t[:, :]), reads=[snt, qd_t], writes=[tb[1]])
        S.op("gpsimd", lambda e: e.tensor_mul(out=tb[2][:, :], in0=cst[:, :], in1=kd_t[:, :]), reads=[cst, kd_t], writes=[tb[2]])
        S.op("gpsimd", lambda e: e.tensor_mul(out=tb[3][:, :], in0=snt[:, :], in1=kd_t[:, :]), reads=[snt, kd_t], writes=[tb[3]])

    def F(s, j):
        ti = s * 4 + j
        xt = xts[ti % 2]
        S.dma("sync", xt[:, :], x[ti * 128:(ti + 1) * 128, :], writes=[xt])
        rmsnorm_T(S, xt, ident, sqs, sss[ti % 2], rss[ti % 2], xns[ti % 2], pA, xnT2[s % 2], j * 128)

    def P1(s):
        xnT = xnT2[s % 2]
        tb = tabs2[s % 2]
        qdT, kTp, vb, gs = qdT2[s % 2], kTp2[s % 2], vb2[s % 2], gs2[s % 2]
        for dc in range(4):
            pb = pB[dc % 2]
            for k in range(8):
                S.op("tensor", lambda e: e.matmul(pb[:, :], lhsT=W[:, k, dc * 128:(dc + 1) * 128], rhs=xnT[:, k, :],
                                                  start=(k == 0), stop=(k == 7)), reads=[W, xnT], writes=[pb])
            S.op("scalar", lambda e: e.copy(out=raw[dc][:, :], in_=pb[:, :]), reads=[pb], writes=[raw[dc]])
        for (eng, x1, x2, ct, st_, dst, ta, tb_) in (("gpsimd", raw[0], raw[1], tb[0], tb[1], qdT, tmp[0], tmp[1]),
                                                     ("vector", raw[2], raw[3], tb[2], tb[3], kTp, tmp[2], tmp[3])):
            S.op(eng, lambda e: e.tensor_mul(out=ta[:, :], in0=x1[:, :], in1=ct[:, :]), reads=[x1, ct], writes=[ta])
            S.op(eng, lambda e: e.tensor_mul(out=tb_[:, :], in0=x2[:, :], in1=st_[:, :]), reads=[x2, st_], writes=[tb_])
            S.op(eng, lambda e: e.tensor_sub(out=dst[:, 0, :], in0=ta[:, :], in1=tb_[:, :]), reads=[ta, tb_], writes=[dst])
            S.op(eng, lambda e: e.tensor_mul(out=ta[:, :], in0=x1[:, :], in1=st_[:, :]), reads=[x1, st_], writes=[ta])
            S.op(eng, lambda e: e.tensor_mul(out=tb_[:, :], in0=x2[:, :], in1=ct[:, :]), reads=[x2, ct], writes=[tb_])
            S.op(eng, lambda e: e.tensor_add(out=dst[:, 1, :], in0=ta[:, :], in1=tb_[:, :]), reads=[ta, tb_], writes=[dst])
        for j in range(4):
            for (which, c0) in (("v", 512), ("g", 1024)):
                pb = pB[0] if which == "v" else pB[1]
                for k in range(8):
                    S.op("tensor", lambda e: e.matmul(pb[:, :], lhsT=xnT[:, k, j * 128:(j + 1) * 128], rhs=W[:, k, c0:c0 + 512],
                                                      start=(k == 0), stop=(k == 7)), reads=[W, xnT], writes=[pb])
                if which == "v":
                    S.op("scalar", lambda e: e.copy(out=vb[:, j, :], in_=pb[:, :]), reads=[pb], writes=[vb])
                else:
                    S.op("scalar", lambda e: e.activation(out=gs[:, j, :], in_=pb[:, :], func=AF.Silu), reads=[pb], writes=[gs])

    def CA(s, j):
        ti = s * 4 + j
        qdT, kTp, vb = qdT2[s % 2], kTp2[s % 2], vb2[s % 2]
        on, stat = ons[ti % 2], stats[ti % 2]
        tk = slice(j * 128, (j + 1) * 128)
        for dc in range(2):
            S.op("tensor", lambda e: e.transpose(out=pA[:, dc, :], in_=kTp[:, dc, tk], identity=ident[:, :]),
                 reads=[kTp, ident], writes=[pA])
        S.op("vector", lambda e: e.tensor_scalar(out=kd[:, :], in0=pA[:, 0:2, :].rearrange("p a b -> p (a b)"),
                                                 scalar1=cd_t[:, 0:1], scalar2=None, op0=ALU.mult),
             reads=[pA, cd_t], writes=[kd])
        for dc in range(2):
            S.op("tensor", lambda e: e.matmul(pS[:, :], lhsT=kTp[:, dc, tk], rhs=qdT[:, dc, tk],
                                              start=(dc == 0), stop=(dc == 1)), reads=[kTp, qdT], writes=[pS])
        S.op("vector", lambda e: e.tensor_tensor(out=ST[:, :], in0=pS[:, :], in1=caus[:, :], op=ALU.mult),
             reads=[pS, caus], writes=[ST])
        S.op("tensor", lambda e: e.matmul(pO[:, :], lhsT=ST[:, :], rhs=vb[:, j, :], start=True, stop=False),
             reads=[ST, vb], writes=[pO])
        for dc in range(2):
            S.op("tensor", lambda e: e.matmul(pO[:, :], lhsT=qdT[:, dc, tk], rhs=st_b[dc][:, :],
                                              start=False, stop=(dc == 1)), reads=[qdT, st_b[dc]], writes=[pO])
        for dc in range(2):
            S.op("tensor", lambda e: e.matmul(pSt[dc][:, :], lhsT=kd[:, dc * 128:(dc + 1) * 128], rhs=vb[:, j, :],
                                              start=True, stop=True), reads=[kd, vb], writes=[pSt[dc]])
            S.op("vector", lambda e: e.scalar_tensor_tensor(out=st_f[dc][:, :], in0=st_f[dc][:, :], scalar=cd_t[:, 0:1],
                                                            in1=pSt[dc][:, :], op0=ALU.mult, op1=ALU.add),
                 reads=[st_f[dc], cd_t, pSt[dc]], writes=[st_f[dc]])
            S.op("gpsimd", lambda e: e.tensor_copy(out=st_b[dc][:, :], in_=st_f[dc][:, :]), reads=[st_f[dc]], writes=[st_b[dc]])
        S.op("scalar", lambda e: e.activation(out=on[:, :], in_=pO[:, :], func=AF.Identity, accum_out=stat[:, 0:1]),
             reads=[pO], writes=[on, stat])
        S.op("scalar", lambda e: e.activation(out=osq[:, :], in_=pO[:, :], func=AF.Square, accum_out=stat[:, 1:2]),
             reads=[pO], writes=[osq, stat])

    def CB(s, j):
        ti = s * 4 + j
        gs = gs2[s % 2]
        on, stat, og = ons[ti % 2], stats[ti % 2], ogs[ti % 2]
        S.op("vector", lambda e: e.tensor_scalar(out=stat[:, 0:2], in0=stat[:, 0:2], scalar1=1.0 / 512, scalar2=None,
                                                 op0=ALU.mult), reads=[stat], writes=[stat])
        S.op("vector", lambda e: e.tensor_tensor(out=stat[:, 2:3], in0=stat[:, 0:1], in1=stat[:, 0:1], op=ALU.mult),
             reads=[stat], writes=[stat])
        S.op("vector", lambda e: e.tensor_tensor(out=stat[:, 2:3], in0=stat[:, 1:2], in1=stat[:, 2:3], op=ALU.subtract),
             reads=[stat], writes=[stat])
        S.op("vector", lambda e: e.tensor_scalar(out=stat[:, 2:3], in0=stat[:, 2:3], scalar1=EPS, scalar2=None,
                                                 op0=ALU.add), reads=[stat], writes=[stat])
        S.op("scalar", lambda e: e.activation(out=stat[:, 2:3], in_=stat[:, 2:3], func=AF.Sqrt), reads=[stat], writes=[stat])
        S.op("vector", lambda e: e.reciprocal(out=stat[:, 3:4], in_=stat[:, 2:3]), reads=[stat], writes=[stat])
        S.op("vector", lambda e: e.tensor_scalar(out=on[:, :], in0=on[:, :], scalar1=stat[:, 0:1], scalar2=stat[:, 3:4],
                                                 op0=ALU.subtract, op1=ALU.mult), reads=[on, stat], writes=[on])
        S.op("gpsimd", lambda e: e.tensor_mul(out=on[:, :], in0=on[:, :], in1=gnt[:, :]), reads=[on, gnt], writes=[on])
        S.op("gpsimd", lambda e: e.tensor_mul(out=og[:, :], in0=on[:, :], in1=gs[:, j, :]), reads=[on, gs], writes=[og])
        for c in range(4):
            S.op("tensor", lambda e: e.transpose(out=pA[:, 2 + c, :], in_=og[:, c * 128:(c + 1) * 128], identity=ident[:, :]),
                 reads=[og, ident], writes=[pA])
        S.op("vector", lambda e: e.tensor_copy(out=ogT[:, :, :], in_=pA[:, 2:6, :]), reads=[pA], writes=[ogT])
        yt = yo[ti % 2]
        for hf in range(2):
            for c in range(4):
                S.op("tensor", lambda e: e.matmul(pY[:, :], lhsT=ogT[:, c, :], rhs=Wo[:, c, hf * 512:(hf + 1) * 512],
                                                  start=(c == 0), stop=(c == 3)), reads=[ogT, Wo], writes=[pY])
            S.op("scalar", lambda e: e.copy(out=yt[:, hf * 512:(hf + 1) * 512], in_=pY[:, :]), reads=[pY], writes=[yt])
        S.dma("scalar", Y1[ti * 128:(ti + 1) * 128, :], yt[:, :], reads=[yt], writes=[bY1[ti]])
        if ti >= 9 and (ti - 9) % 8 == 0:
            allreduce_chunk((ti - 9) // 8)

    prev = None
    for s in range(NS + 1):
        if s < NS:
            tabs_for(s)
        for j in range(4):
            if s < NS:
                F(s, j)
            if s >= 1:
                CA(s - 1, j)
                if prev is not None:
                    CB(*prev)
                prev = (s - 1, j)
        if s < NS:
            P1(s)
    CB(*prev)
    allreduce_chunk(NCH - 1)
    if NCH >= 2 and (NT - 1) < 9 + 8 * (NCH - 2):
        pass
    issued = set([(ti - 9) // 8 for ti in range(NT) if ti >= 9 and (ti - 9) % 8 == 0] + [NCH - 1])
    for c in range(NCH):
        if c not in issued:
            allreduce_chunk(c)
    S.pop()
    if upto == 1:
        return dbg_out(Y1s, bY1s)

    S.push()
    KsT = S.sb("KsT", [128, T], BF16)
    Vs = S.sb("Vs", [128, NT, 128], BF16)
    KcT = S.sb("KcT", [128, NC16 * 128], BF16)
    Vc = S.sb("Vc", [128, NC16, 128], BF16)

    S.push()
    ple = PleCtx(S)
    ple.load_weights("sync", pg[0][:, :], wg[0], we[0], stage)
    kvg = S.sb("kvg", [128, 8], F32)
    Wkv = S.sb("Wkv", [128, 8, 768], BF16)
    S.dma("sync", kvg[:, :], kv_g[:, :], writes=[kvg])
    load_w_bf16(S, "sync", Wkv, lambda k, c0, cw: Wkv[:, k, c0:c0 + cw],
                kv_w.rearrange("(k p) n -> p k n", p=128), stage, 8, 768, gain=kvg)
    w1 = [S.sb("w1", [128, 32, 256], BF16) for i in range(2)]
    w2 = [S.sb("w2", [128, 2, 128], BF16) for i in range(2)]
    peT = [S.sb("peT", [128, 32], BF16) for i in range(2)]
    for i, (w1d, w2d, ped) in enumerate(((w1_k, w2_k, peT_k), (w1_v, w2_v, peT_v))):
        load_w_bf16(S, "sync", w1[i], lambda k, c0, cw: w1[i][:, k, c0:c0 + cw],
                    w1d.rearrange("(l d) h -> d l h", d=128), stage, 32, 256)
        load_w_bf16(S, "sync", w2[i], lambda k, c0, cw: w2[i][:, k, c0:c0 + cw],
                    w2d.rearrange("(k p) n -> p k n", p=128), stage, 2, 128)
        load_const_bf16(S, "sync", peT[i], peT[i][:, :], ped[:, :], stage, 32)
    cb = [S.sb("cb", [128, 2064], BF16) for i in range(2)]
    hs = [S.sb("h", [128, 1024], F32) for i in range(3)]
    _yb = S.sb("yb", [128, 1024], F32)
    ybs = [_yb, _yb]
    hTs = [S.sb("hT", [128, 8, 128], BF16) for i in range(2)]
    kwt = [S.sb("kwt", [128, 128], BF16) for i in range(2)]
    vwt = [S.sb("vwt", [128, 128], BF16) for i in range(2)]
    sqs = ple.sqs
    ss2 = [S.sb("ss", [128, 1], F32) for i in range(2)]
    rs2 = [S.sb("rs", [128, 1], F32) for i in range(2)]
    _xn2 = S.sb("xn", [128, 1024], BF16)
    xn2 = [_xn2, _xn2]
    ones1 = S.sb("ones1", [1, 128], BF16)
    bias_f = S.sb("bias_f", [1, 512], F32)
    bias_hi = S.sb("bias_hi", [1, 512], BF16)
    bias_hif = S.sb("bias_hif", [1, 512], F32)
    bias_lo = S.sb("bias_lo", [1, 512], BF16)
    xs_ = S.sb("xs_", [128, 256], F32)
    x2_ = S.sb("x2_", [128, 256], F32)
    sg_ = S.sb("sg_", [128, 256], F32)
    hid = S.sb("hid", [128, 256], BF16)
    hidT = S.sb("hidT", [128, 2, 128], BF16)
    pA = S.ps("pA", [128, 8, 128], BF16)
    pA2 = S.ps("pA2", [128, 8, 128], BF16)
    pG1 = S.ps("pG", [128, 512], F32)
    pE1 = S.ps("pE", [128, 512], F32)
    pGs = [pG1, pG1]
    pEs = [pE1, pE1]
    pKT = S.ps("pKT", [128, 4, 128], F32)
    pKV = S.ps("pKV", [128, 256], F32)
    pH = S.ps("pH", [128, 256], F32)
    pC = S.ps("pC", [128, 128], F32)

    S.op("gpsimd", lambda e: e.memset(ones1[:, :], 1.0), writes=[ones1])
    for i in range(2):
        S.op("gpsimd", lambda e: e.memset(cb[i][:, 0:16], 0.0), writes=[cb[i]])
    for i in range(2):
        for l in range(32):
            S.op("tensor", lambda e: e.matmul(pH[0:1, :], lhsT=peT[i][:, l:l + 1], rhs=w1[i][:, l, :],
                                              start=(l == 0), stop=(l == 31)), reads=[peT[i], w1[i]], writes=[pH])
        S.op("scalar", lambda e: e.copy(out=bias_f[:, i * 256:(i + 1) * 256], in_=pH[0:1, :]), reads=[pH], writes=[bias_f])
    S.op("vector", lambda e: e.tensor_copy(out=bias_hi[:, :], in_=bias_f[:, :]), reads=[bias_f], writes=[bias_hi])
    S.op("vector", lambda e: e.tensor_copy(out=bias_hif[:, :], in_=bias_hi[:, :]), reads=[bias_hi], writes=[bias_hif])
    S.op("vector", lambda e: e.tensor_sub(out=bias_hif[:, :], in0=bias_f[:, :], in1=bias_hif[:, :]), reads=[bias_f, bias_hif], writes=[bias_hif])
    S.op("vector", lambda e: e.tensor_copy(out=bias_lo[:, :], in_=bias_hif[:, :]), reads=[bias_hif], writes=[bias_lo])

    def TA(i):
        rows = slice(i * 128, (i + 1) * 128)
        h = hs[i % 3]
        yb = ybs[i % 2]
        S.dma("sync", h[:, :], x[rows, :], writes=[h])
        S.dma("sync", yb[:, :], Y1s[rows, :], reads=[bY1s[i]], writes=[yb])
        S.op("vector", lambda e: e.tensor_add(out=h[:, :], in0=h[:, :], in1=yb[:, :]), reads=[h, yb], writes=[h])
        ple.front(i % 2, h, p0[rows, :], ident, pA)

    def TB(i):
        rows = slice(i * 128, (i + 1) * 128)
        h = hs[i % 3]
        ple.back(i % 2, h, pGs, pEs)
        S.dma("gpsimd", H1[rows, :], h[:, :], reads=[h], writes=[bH1[i]])

    def TC(i):
        rows = slice(i * 128, (i + 1) * 128)
        h = hs[i % 3]
        hT = hTs[i % 2]
        rmsnorm_T(S, h, ident, sqs, ss2[i % 2], rs2[i % 2], xn2[i % 2], pA2, hT, 0, copy_eng="scalar")
        S.dma("scalar", HT[i].rearrange("p (k t) -> p k t", k=8), hT[:, :, :], reads=[hT], writes=[bHT[i]])
        for a_ in range(4):
            for k in range(8):
                S.op("tensor", lambda e: e.matmul(pKT[:, a_, :], lhsT=Wkv[:, k, a_ * 128:(a_ + 1) * 128], rhs=hT[:, k, :],
                                                  start=(k == 0), stop=(k == 7)), reads=[Wkv, hT], writes=[pKT])
        for k in range(8):
            S.op("tensor", lambda e: e.matmul(pKV[:, :], lhsT=hT[:, k, :], rhs=Wkv[:, k, 512:768],
                                              start=(k == 0), stop=(k == 7)), reads=[Wkv, hT], writes=[pKV])
        cc0 = 16 + (i % 16) * 128
        S.op("scalar", lambda e: e.copy(out=cb[0][:, cc0:cc0 + 128], in_=pKT[:, 0, :]), reads=[pKT], writes=[cb[0]])
        S.op("scalar", lambda e: e.copy(out=cb[1][:, cc0:cc0 + 128], in_=pKT[:, 1, :]), reads=[pKT], writes=[cb[1]])
        S.op("scalar", lambda e: e.copy(out=KsT[:, rows], in_=pKT[:, 2, :]), reads=[pKT], writes=[KsT])
        kw_, vw_ = kwt[i % 2], vwt[i % 2]
        S.op("scalar", lambda e: e.copy(out=kw_[:, :], in_=pKT[:, 3, :]), reads=[pKT], writes=[kw_])
        S.op("scalar", lambda e: e.copy(out=Vs[:, i, :], in_=pKV[:, 0:128]), reads=[pKV], writes=[Vs])
        S.op("scalar", lambda e: e.copy(out=vw_[:, :], in_=pKV[:, 128:256]), reads=[pKV], writes=[vw_])
        S.dma("scalar", KW[i], kw_[:, :], reads=[kw_], writes=[bKW[i]])
        S.dma("scalar", VW[i], vw_[:, :], reads=[vw_], writes=[bVW[i]])
        if i % 16 == 15:
            compress(i)

    def compress(i):
        if True:
            s16 = i // 16
            for X in range(2):
                bc = slice(X * 256, (X + 1) * 256)
                S.op("tensor", lambda e: e.matmul(pH[:, :], lhsT=ones1[0:1, :], rhs=bias_hi[0:1, bc], start=True, stop=False),
                     reads=[ones1, bias_hi], writes=[pH])
                S.op("tensor", lambda e: e.matmul(pH[:, :], lhsT=ones1[0:1, :], rhs=bias_lo[0:1, bc], start=False, stop=False),
                     reads=[ones1, bias_lo], writes=[pH])
                for l in range(32):
                    S.op("tensor", lambda e: e.matmul(pH[:, :], lhsT=cb[X][:, l:l + 2033:16], rhs=w1[X][:, l, :],
                                                      start=False, stop=(l == 31)), reads=[cb[X], w1[X]], writes=[pH])
                S.op("scalar", lambda e: e.copy(out=xs_[:, :], in_=pH[:, :]), reads=[pH], writes=[xs_])
                S.op("vector", lambda e: e.tensor_tensor(out=x2_[:, :], in0=xs_[:, :], in1=xs_[:, :], op=ALU.mult), reads=[xs_], writes=[x2_])
                S.op("vector", lambda e: e.tensor_scalar(out=x2_[:, :], in0=x2_[:, :], scalar1=0.044715, scalar2=1.0,
                                                         op0=ALU.mult, op1=ALU.add), reads=[x2_], writes=[x2_])
                S.op("vector", lambda e: e.tensor_tensor(out=x2_[:, :], in0=x2_[:, :], in1=xs_[:, :], op=ALU.mult), reads=[x2_, xs_], writes=[x2_])
                S.op("scalar", lambda e: e.activation(out=sg_[:, :], in_=x2_[:, :], func=AF.Sigmoid, scale=1.5957691216057308),
                     reads=[x2_], writes=[sg_])
                S.op("vector", lambda e: e.tensor_tensor(out=hid[:, :], in0=xs_[:, :], in1=sg_[:, :], op=ALU.mult), reads=[xs_, sg_], writes=[hid])
                for hc in range(2):
                    S.op("tensor", lambda e: e.transpose(out=pA2[:, hc, :], in_=hid[:, hc * 128:(hc + 1) * 128], identity=ident[:, :]),
                         reads=[hid, ident], writes=[pA2])
                S.op("vector", lambda e: e.tensor_copy(out=hidT[:, :, :], in_=pA2[:, 0:2, :]), reads=[pA2], writes=[hidT])
                if X == 0:
                    for hc in range(2):
                        S.op("tensor", lambda e: e.matmul(pC[:, :], lhsT=w2[0][:, hc, :], rhs=hidT[:, hc, :],
                                                          start=(hc == 0), stop=(hc == 1)), reads=[w2[0], hidT], writes=[pC])
                    S.op("scalar", lambda e: e.copy(out=KcT[:, s16 * 128:(s16 + 1) * 128], in_=pC[:, :]), reads=[pC], writes=[KcT])
                else:
                    for hc in range(2):
                        S.op("tensor", lambda e: e.matmul(pC[:, :], lhsT=hidT[:, hc, :], rhs=w2[1][:, hc, :],
                                                          start=(hc == 0), stop=(hc == 1)), reads=[w2[1], hidT], writes=[pC])
                    S.op("scalar", lambda e: e.copy(out=Vc[:, s16, :], in_=pC[:, :]), reads=[pC], writes=[Vc])
                S.op("vector", lambda e: e.tensor_copy(out=cb[X][:, 0:16], in_=cb[X][:, 2048:2064]), reads=[cb[X]], writes=[cb[X]])
    for step in range(NT + 2):
        if step < NT:
            TA(step)
        if 0 <= step - 1 < NT:
            TB(step - 1)
        if 0 <= step - 2 < NT:
            TC(step - 2)
    S.pop()

    if upto == 2:
        S.pop()
        return dbg_out(H1, bH1)
    S.push()
    ng = S.sb("ng", [128, 8], F32)
    Wn = S.sb("Wn", [128, 8, 1036], BF16)
    Wo2 = S.sb("Wo2", [128, 4, 1024], BF16)
    S.dma("sync", ng[:, :], n_g[:, :], writes=[ng])
    load_w_bf16(S, "sync", Wn, lambda k, c0, cw: Wn[:, k, c0:c0 + cw],
                n_w_in.rearrange("(k p) n -> p k n", p=128), stage, 8, 1036, gain=ng)
    load_w_bf16(S, "sync", Wo2, lambda k, c0, cw: Wo2[:, k, c0:c0 + cw],
                n_w_out.rearrange("(k p) n -> p k n", p=128), stage, 4, 1024)
    Ex = S.sb("Ex", [128, 64, 128], BF16)
    for c in range(4):
        load_const_bf16(S, "sync", Ex, Ex[:, c * 16:(c + 1) * 16, :].rearrange("p a b -> p (a b)"),
                        ex_d[:, c * 2048:(c + 1) * 2048], stage, 2048)
    Maug = S.sb("Maug", [128, 8, 257], BF16)
    load_const_bf16(S, "sync", Maug, Maug[:, 0:4, :].rearrange("p a b -> p (a b)"), maug_d[:, 0:1028], stage, 1028)
    load_const_bf16(S, "sync", Maug, Maug[:, 4:8, :].rearrange("p a b -> p (a b)"), maug_d[:, 1028:2056], stage, 1028)
    cmpm = S.sb("cmpm", [128, 33, 128], BF16)
    load_const_bf16(S, "sync", cmpm, cmpm[:, 0:16, :].rearrange("p a b -> p (a b)"), cmpm_d[:, 0:2048], stage, 2048)
    load_const_bf16(S, "sync", cmpm, cmpm[:, 16:32, :].rearrange("p a b -> p (a b)"), cmpm_d[:, 2048:4096], stage, 2048)
    load_const_bf16(S, "sync", cmpm, cmpm[:, 32, :], cmpm_d[:, 4096:4224], stage, 128)
    onesel = S.sb("onesel", [128, 3, 3], BF16)
    load_const_bf16(S, "sync", onesel, onesel[:, :, :].rearrange("p a b -> p (a b)"), onesel_d[:, :], stage, 9)
    causT = S.sb("causT", [128, 128], BF16)
    upT = S.sb("upT", [128, 128], BF16)
    load_const_bf16(S, "sync", causT, causT[:, :], causT_d[:, :], stage, 128)
    load_const_bf16(S, "sync", upT, upT[:, :], upT_d[:, :], stage, 128)

    hTs = [S.sb("hT", [128, 8, 128], BF16) for i in range(2)]
    h1s = [S.sb("h1", [128, 1024], F32) for i in range(2)]
    kwr = S.sb("kwr", [128, 6, 128], BF16)
    vwr = S.sb("vwr", [128, 6, 128], BF16)
    kwb = [Buf("kwb%d" % i, kwr.t) for i in range(6)]
    vwb = [Buf("vwb%d" % i, vwr.t) for i in range(6)]
    QTs = [S.sb("QT", [128, 512], BF16) for i in range(2)]
    gsils = [S.sb("gsil", [128, 512], F32) for i in range(2)]
    bgs = [S.sb("bg", [128, 12], F32) for i in range(2)]
    EmC = S.sb("EmC", [128, 8, 512], BF16)
    NBUF = 4
    Eb = [S.sb("Eb", [128, 512], BF16) for i in range(NBUF)]
    Emb = [S.sb("Emb", [128, 512], BF16) for i in range(NBUF)]
    imp = S.sb("imp", [128, 256], F32)
    score = S.sb("score", [128, 256], F32)
    sc2 = S.sb("sc2", [128, 256], F32)
    selF = S.sb("selF", [128, 256], F32)
    selT = S.sb("selT", [128, 2, 128], BF16)
    m8 = S.sb("m8", [128, 16], F32)
    rcols = [S.sb("rcol", [128, 1], F32) for i in range(2)]
    sumsbs = [S.sb("sumsb", [3, 512], F32) for i in range(2)]
    coef = S.sb("coef", [128, 12], F32)
    o_ = S.sb("o_", [128, 512], F32)
    ogf = S.sb("ogf", [128, 512], BF16)
    ogT2 = S.sb("ogT2", [128, 4, 128], BF16)
    yts = [S.sb("yt", [128, 1024], F32) for i in range(2)]
    ObTs = [[S.sb("ObT", [128, 512], BF16) for i in range(3)] for p in range(2)]
    pSc = [S.ps("pSc", [128, 512], F32) for i in range(2)]
    pMs = [S.ps("pM", [128, 512], F32) for i in range(2)]
    pO2 = [S.ps("pOb", [128, 512], F32) for i in range(2)]
    pOb = [pO2[0], pO2[1], pO2[0]]
    pSum1 = S.ps("pSum", [3, 512], F32)
    pSums = [pSum1, pSum1]
    pX = S.ps("pX", [128, 512], F32)
    pXb = pX.t[:, :].bitcast(BF16)

    S.op("gpsimd", lambda e: e.memset(selF[:, :], 0.0), writes=[selF])
    ctr = {"e": 0, "m": 0, "s": 0, "pm": 0}

    def stageA(u):
        rows = u["rows"]
        QT = u["QT"]
        ps = pSc[ctr["s"] % 2]
        ctr["s"] += 1
        S.op("tensor", lambda e: e.matmul(ps[0:rows, :], lhsT=u["ksrc"], rhs=QT[:, :], start=True, stop=True),
             reads=[u["ktrack"], QT], writes=[ps], lhs=[u["ktrack"]])
        E = Eb[ctr["e"] % NBUF]
        ctr["e"] += 1
        S.op("scalar", lambda e: e.activation(out=E[0:rows, :], in_=ps[0:rows, :], func=AF.Exp, scale=SCALE),
             reads=[ps], writes=[E])
        u["E"] = E

    def stageA2(u):
        rows = u["rows"]
        E = u["E"]
        mask = u["mask"]
        if mask is None:
            Em = E
        else:
            Em = Emb[ctr["m"] % NBUF]
            ctr["m"] += 1
            if mask[0] == "sb":
                S.op("vector", lambda e: e.tensor_tensor(
                    out=Em[0:rows, :].rearrange("p (r q) -> p r q", r=4), in0=E[0:rows, :].rearrange("p (r q) -> p r q", r=4),
                    in1=mask[1].unsqueeze(1).broadcast_to([rows, 4, 128]), op=ALU.mult),
                     reads=[E] + mask[2], writes=[Em])
            else:
                t = mask[1]
                pM = pMs[ctr["pm"] % 2]
                ctr["pm"] += 1
                S.op("tensor", lambda e: e.matmul(pM[:, 0:128], lhsT=Ex[:, t % 64, :], rhs=selT[:, t // 64, :],
                                                  start=True, stop=True), reads=[Ex, selT], writes=[pM], lhs=[Ex])
                S.op("vector", lambda e: e.tensor_tensor(
                    out=Em[0:rows, :].rearrange("p (r q) -> p r q", r=4), in0=E[0:rows, :].rearrange("p (r q) -> p r q", r=4),
                    in1=pM[:, 0:128].unsqueeze(1).broadcast_to([128, 4, 128]), op=ALU.mult),
                     reads=[E, pM], writes=[Em])
        u["Em"] = Em
        if u.get("emc") is not None:
            c = u["emc"]
            S.op("gpsimd", lambda e: e.tensor_copy(out=EmC[0:rows, c, :], in_=Em[0:rows, :]), reads=[Em], writes=[EmC])

    def stageB(u):
        rows = u["rows"]
        Em = u["Em"]
        b_idx = u["b"]
        q = u["q"]
        pSum = pSums[q["par"]]
        S.op("tensor", lambda e: e.matmul(pOb[b_idx][:, :], lhsT=u["vsrc"], rhs=Em[0:rows, :], start=u["first"], stop=u["last"]),
             reads=[u["vtrack"], Em], writes=[pOb[b_idx]], lhs=[u["vtrack"]])
        S.op("tensor", lambda e: e.matmul(pSum[:, :], lhsT=onesel[0:rows, b_idx, :], rhs=Em[0:rows, :],
                                          start=(q["nsum"] == 0), stop=(q["nsum"] == q["total_sum"] - 1)),
             reads=[onesel, Em], writes=[pSum], lhs=[onesel])
        q["nsum"] += 1
        for f in u.get("post", ()):
            f()

    def rs_chunk(c):
        tl = list(range(c * 8, c * 8 + 8))
        S.collective("ReduceScatter", ALU.add, groups, Y2[c * 1024:(c + 1) * 1024, :], Y2s[c * 256:(c + 1) * 256, :],
                     reads=[bY2[t] for t in tl], writes=[bY2s[c]])

    def prologue(i):
        par = i % 2
        rows = slice(i * 128, (i + 1) * 128)
        hT, h1, QT, gsil, bg = hTs[par], h1s[par], QTs[par], gsils[par], bgs[par]
        S.dma("sync", hT[:, :, :], HT[i].rearrange("p (k t) -> p k t", k=8), reads=[bHT[i]], writes=[hT])
        S.dma("sync", h1[:, :], H1[rows, :], reads=[bH1[i]], writes=[h1])
        S.dma("sync", kwr[:, i % 6, :], KW[i], reads=[bKW[i]], writes=[kwb[i % 6]])
        S.dma("sync", vwr[:, i % 6, :], VW[i], reads=[bVW[i]], writes=[vwb[i % 6]])
        for r in range(4):
            for k in range(8):
                S.op("tensor", lambda e: e.matmul(pX[:, r * 128:(r + 1) * 128], lhsT=Wn[:, k, r * 128:(r + 1) * 128],
                                                  rhs=hT[:, k, :], start=(k == 0), stop=(k == 7)), reads=[Wn, hT], writes=[pX])
        S.op("scalar", lambda e: e.copy(out=QT[:, :], in_=pX[:, :]), reads=[pX], writes=[QT])
        for k in range(8):
            S.op("tensor", lambda e: e.matmul(pX[:, :], lhsT=hT[:, k, :], rhs=Wn[:, k, 512:1024],
                                              start=(k == 0), stop=(k == 7)), reads=[Wn, hT], writes=[pX])
        S.op("scalar", lambda e: e.activation(out=gsil[:, :], in_=pX[:, :], func=AF.Silu), reads=[pX], writes=[gsil])
        for k in range(8):
            S.op("tensor", lambda e: e.matmul(pX[:, 0:12], lhsT=hT[:, k, :], rhs=Wn[:, k, 1024:1036],
                                              start=(k == 0), stop=(k == 7)), reads=[Wn, hT], writes=[pX])
        S.op("scalar", lambda e: e.activation(out=bg[:, :], in_=pX[:, 0:12], func=AF.Sigmoid), reads=[pX], writes=[bg])

    def make_units(i):
        par = i % 2
        QT = QTs[par]
        Wp = 8 * (i + 1)
        nch = (Wp + 127) // 128
        wt = [t for t in range(i - 4, i + 1) if t >= 0]
        q = dict(i=i, par=par, nsum=0, total_sum=nch + (i + 1) + len(wt), topk_done=(i < 8))
        OT = ObTs[par]

        def evac(b):
            return lambda: S.op("scalar", lambda e: e.copy(out=OT[b][:, :], in_=pOb[b][:, :]), reads=[pOb[b]], writes=[OT[b]])

        def topk():
            ncol = 2 * i
            for r in range(4):
                pI = pMs[ctr["pm"] % 2]
                ctr["pm"] += 1
                rc_ = rcols[r % 2]
                for c in range(nch):
                    rws = min(128, Wp - c * 128)
                    S.op("tensor", lambda e: e.matmul(pI[:, 0:257], lhsT=EmC[0:rws, c, r * 128:(r + 1) * 128], rhs=Maug[0:rws, c, :],
                                                      start=(c == 0), stop=(c == nch - 1)), reads=[EmC, Maug], writes=[pI])
                S.op("vector", lambda e: e.tensor_scalar(out=rc_[:, :], in0=pI[:, 256:257], scalar1=1e-30, scalar2=None,
                                                         op0=ALU.add), reads=[pI], writes=[rc_])
                S.op("vector", lambda e: e.reciprocal(out=rc_[:, :], in_=rc_[:, :]), reads=[rc_], writes=[rc_])
                if r == 0:
                    S.op("vector", lambda e: e.tensor_scalar(out=imp[:, 0:ncol], in0=pI[:, 0:ncol], scalar1=rc_[:, 0:1],
                                                             scalar2=None, op0=ALU.mult), reads=[pI, rc_], writes=[imp])
                else:
                    S.op("vector", lambda e: e.scalar_tensor_tensor(out=imp[:, 0:ncol], in0=pI[:, 0:ncol], scalar=rc_[:, 0:1],
                                                                    in1=imp[:, 0:ncol], op0=ALU.mult, op1=ALU.add),
                         reads=[pI, rc_, imp], writes=[imp])
            S.op("vector", lambda e: e.tensor_copy(out=score[:, 0:ncol], in_=imp[:, 0:ncol]), reads=[imp], writes=[score])
            S.op("vector", lambda e: e.memset(score[:, 0:1], -1.0), writes=[score])
            S.op("vector", lambda e: e.memset(score[0:64, ncol - 1:ncol], -1.0), writes=[score])
            S.op("vector", lambda e: e.max(out=m8[:, 0:8], in_=score[:, 0:ncol]), reads=[score], writes=[m8])
            S.op("vector", lambda e: e.match_replace(out=sc2[:, 0:ncol], in_to_replace=m8[:, 0:8], in_values=score[:, 0:ncol],
                                                     imm_value=-2.0), reads=[m8, score], writes=[sc2])
            S.op("vector", lambda e: e.max(out=m8[:, 8:16], in_=sc2[:, 0:ncol]), reads=[sc2, m8], writes=[m8])
            S.op("vector", lambda e: e.tensor_scalar(out=selF[:, 0:ncol], in0=score[:, 0:ncol], scalar1=m8[:, 12:13], scalar2=None,
                                                     op0=ALU.is_ge), reads=[score, m8], writes=[selF])
            S.op("vector", lambda e: e.memset(selF[:, 0:1], 1.0), writes=[selF])
            S.op("vector", lambda e: e.memset(selF[0:64, ncol - 1:ncol], 1.0), writes=[selF])
            for c in range((ncol + 127) // 128):
                S.op("tensor", lambda e: e.transpose(out=pX[:, c * 128:(c + 1) * 128], in_=selF[:, c * 128:(c + 1) * 128],
                                                     identity=idf[:, :]), reads=[selF, idf], writes=[pX])
                S.op("vector", lambda e: e.tensor_copy(out=selT[:, c, :], in_=pX[:, c * 128:(c + 1) * 128]), reads=[pX], writes=[selT])
            q["topk_done"] = True

        units = []
        for c in range(nch):
            rws = min(128, Wp - c * 128)
            if c == nch - 1:
                mk = ("sb", cmpm[0:rws, (i % 16) + (0 if i < 16 else 16), :], [cmpm])
            elif c == 0:
                mk = ("sb", cmpm[0:rws, 32, :], [cmpm])
            else:
                mk = None
            u = dict(q=q, QT=QT, ksrc=KcT[:, c * 128:c * 128 + rws], ktrack=KcT, vsrc=Vc[0:rws, c, :], vtrack=Vc, rows=rws, mask=mk,
                     b=0, first=(c == 0), last=(c == nch - 1), emc=(c if i >= 8 else None), post=[])
            if c == nch - 1:
                u["post"].append(evac(0))
                if i >= 8:
                    u["post"].append(topk)
            units.append(u)
        for n, t in enumerate(wt):
            if t == i:
                mk = ("sb", causT[:, :], [causT])
            elif t == i - 4:
                mk = ("sb", upT[:, :], [upT])
            else:
                mk = None
            u = dict(q=q, QT=QT, ksrc=kwr[:, t % 6, :], ktrack=kwb[t % 6], vsrc=vwr[:, t % 6, :], vtrack=vwb[t % 6], rows=128, mask=mk,
                     b=2, first=(n == 0), last=(n == len(wt) - 1), post=[])
            if n == len(wt) - 1:
                u["post"].append(evac(2))
            units.append(u)
        for t in range(i + 1):
            if t == i:
                mk = ("sb", causT[:, :], [causT])
            elif i >= 8:
                mk = ("sel", t)
            else:
                mk = None
            u = dict(q=q, QT=QT, ksrc=KsT[:, t * 128:(t + 1) * 128], ktrack=KsT, vsrc=Vs[:, t, :], vtrack=Vs, rows=128, mask=mk,
                     b=1, first=(t == 0), last=(t == i), post=[], needs_sel=(mk is not None and mk[0] == "sel"))
            if t == i:
                u["post"].append(evac(1))
            units.append(u)
        return units, q

    def epilogue_parts(i):
        par = i % 2
        rows = slice(i * 128, (i + 1) * 128)
        h1, gsil, bg = h1s[par], gsils[par], bgs[par]
        OT = ObTs[par]
        pSum = pSums[par]
        sumsb = sumsbs[par]
        yt = yts[par]

        def combine(b):
            for r in range(4):
                S.op("tensor", lambda e: e.transpose(out=pXb[:, r * 128:(r + 1) * 128], in_=OT[b][:, r * 128:(r + 1) * 128],
                                                     identity=ident[:, :]), reads=[OT[b], ident], writes=[pX])
            for r in range(4):
                cs_ = slice(r * 128, (r + 1) * 128)
                if b == 0:
                    S.op("vector", lambda e: e.tensor_scalar(out=o_[:, cs_], in0=pXb[:, cs_], scalar1=coef[:, r * 3 + b:r * 3 + b + 1],
                                                             scalar2=None, op0=ALU.mult), reads=[pX, coef], writes=[o_])
                else:
                    S.op("vector", lambda e: e.scalar_tensor_tensor(out=o_[:, cs_], in0=pXb[:, cs_], scalar=coef[:, r * 3 + b:r * 3 + b + 1],
                                                                    in1=o_[:, cs_], op0=ALU.mult, op1=ALU.add),
                         reads=[pX, coef, o_], writes=[o_])

        def part1():
            S.op("scalar", lambda e: e.copy(out=sumsb[:, :], in_=pSum[:, :]), reads=[pSum], writes=[sumsb])
            for r in range(4):
                S.op("tensor", lambda e: e.matmul(pX[:, r * 3:(r + 1) * 3], lhsT=sumsb[0:3, r * 128:(r + 1) * 128], rhs=idf[0:3, 0:3],
                                                  start=True, stop=True), reads=[sumsb, idf], writes=[pX])
            S.op("vector", lambda e: e.tensor_scalar(out=coef[:, :], in0=pX[:, 0:12], scalar1=1e-30, scalar2=None, op0=ALU.add),
                 reads=[pX], writes=[coef])
            S.op("vector", lambda e: e.reciprocal(out=coef[:, :], in_=coef[:, :]), reads=[coef], writes=[coef])
            S.op("vector", lambda e: e.tensor_tensor(out=coef[:, :], in0=coef[:, :], in1=bg[:, :], op=ALU.mult), reads=[coef, bg], writes=[coef])
            combine(0)

        def part2():
            combine(2)

        def part2b():
            combine(1)
            S.op("gpsimd", lambda e: e.tensor_mul(out=ogf[:, :], in0=o_[:, :], in1=gsil[:, :]), reads=[o_, gsil], writes=[ogf])

        def part3():
            for c in range(4):
                S.op("tensor", lambda e: e.transpose(out=pXb[:, c * 128:(c + 1) * 128], in_=ogf[:, c * 128:(c + 1) * 128],
                                                     identity=ident[:, :]), reads=[ogf, ident], writes=[pX])
            S.op("vector", lambda e: e.tensor_copy(out=ogT2[:, :, :].rearrange("p a b -> p (a b)"), in_=pXb[:, 0:512]), reads=[pX], writes=[ogT2])

        def part4(hf):
            if True:
                for c in range(4):
                    S.op("tensor", lambda e: e.matmul(pX[:, :], lhsT=ogT2[:, c, :], rhs=Wo2[:, c, hf * 512:(hf + 1) * 512],
                                                      start=(c == 0), stop=(c == 3)), reads=[ogT2, Wo2], writes=[pX])
                S.op("vector", lambda e: e.scalar_tensor_tensor(out=yt[:, hf * 512:(hf + 1) * 512], in0=h1[:, hf * 512:(hf + 1) * 512],
                                                                scalar=0.25, in1=pX[:, :], op0=ALU.mult, op1=ALU.add),
                     reads=[h1, pX], writes=[yt])
            if hf == 1:
                S.dma("gpsimd", Y2[rows, :], yt[:, :], reads=[yt], writes=[bY2[i]])
                if i % 8 == 7:
                    rs_chunk(i // 8)

        return [part1, part2, part2b, part3, lambda: part4(0), lambda: part4(1)]

    LOOK = 3
    prologue(0)
    pending = []
    for i in range(NT):
        units, q = make_units(i)
        nu = len(units)
        hooks = {}
        pos = [1, 4, 7, 10, 13, 16]
        for k, f in enumerate(pending):
            hooks.setdefault(min(nu - 1, pos[k]), []).append(f)
        if i + 1 < NT:
            hooks.setdefault(min(nu - 1, 19), []).append(lambda i=i: prologue(i + 1))
        for n in range(nu + LOOK):
            if n < nu:
                stageA(units[n])
            if 0 <= n - 1 < nu:
                if units[n - 1].get("needs_sel"):
                    assert q["topk_done"]
                stageA2(units[n - 1])
            if n - LOOK >= 0:
                stageB(units[n - LOOK])
            for f in hooks.get(n, ()):
                f()
        assert q["nsum"] == q["total_sum"]
        pending = epilogue_parts(i)
    for f in pending:
        f()
    S.pop()
    S.pop()
    if upto == 3:
        return dbg_out(Y2, bY2)

    S.push()
    ple = PleCtx(S)
    ple.load_weights("sync", pg[1][:, :], wg[1], we[1], stage)
    fn = S.sb("fn", [128, 1024], F32)
    S.dma("sync", fn[:, :], fng[:, :], writes=[fn])
    hs = [S.sb("h", [128, 1024], F32) for i in range(2)]
    ob = [S.sb("ob", [128, 1024], F32) for i in range(2)]
    sqs = S.sb("sqs", [128, 1024], F32)
    ss = S.sb("ss", [128, 1], F32)
    rs = S.sb("rs", [128, 1], F32)
    pA = S.ps("pA", [128, 8, 128], BF16)
    pG = S.ps("pG", [128, 512], F32)
    pE = S.ps("pE", [128, 512], F32)
    pG2 = S.ps("pG2", [128, 512], F32)
    pE2 = S.ps("pE2", [128, 512], F32)
    NU = TL // 128

    def UA(u):
        rows = slice(u * 128, (u + 1) * 128)
        h = hs[u % 2]
        S.dma("sync", h[:, :], Y2s[rows, :], reads=[bY2s[u // 2]], writes=[h])
        ple.front(u % 2, h, p1s[rows, :], ident, pA)

    def UB(u):
        rows = slice(u * 128, (u + 1) * 128)
        h = hs[u % 2]
        ple.back(u % 2, h, [pG, pG2], [pE, pE2])
        rms_rstd(S, h, 1024, sqs, ss, rs)
        o = ob[u % 2]
        S.op("vector", lambda e: e.scalar_tensor_tensor(out=o[:, :], in0=h[:, :], scalar=rs[:, 0:1], in1=fn[:, :],
                                                        op0=ALU.mult, op1=ALU.mult), reads=[h, rs, fn], writes=[o])
        S.dma("gpsimd", out[rows, :], o[:, :], reads=[o])

    for step in range(NU + 1):
        if step < NU:
            UA(step)
        if step >= 1:
            UB(step - 1)
    S.finish()
    return nc, S.ninstr


def _colgain(g):
    return np.ascontiguousarray(np.asarray(g, np.float32).reshape(8, 128).T)


def _consts(T):
    half = 128
    inv = (10000.0 ** (-np.arange(half, dtype=np.float32) / np.float32(half))).astype(np.float32)
    pos = np.arange(T, dtype=np.float32)
    ang = (inv[:, None] * pos[None, :]).astype(np.float32)
    c = dict(cosT=np.cos(ang).astype(np.float32), sinT=np.sin(ang).astype(np.float32))
    p = np.arange(128)
    c["causT"] = (p[:, None] <= p[None, :]).astype(np.float32)
    c["upT"] = (p[:, None] > p[None, :]).astype(np.float32)
    c["ident"] = np.eye(128, dtype=np.float32)
    ex = np.zeros((128, 64, 128), np.float32)
    for tt in range(64):
        for hb in range(2):
            ex[(2 * tt + hb) % 128, tt, hb * 64:(hb + 1) * 64] = 1.0
    c["ex"] = ex.reshape(128, 64 * 128)
    m = np.zeros((1024, 257), np.float32)
    for j in range(256):
        for (off, w) in ((0, 1.0), (1, 2.0), (2, 2.0), (3, 2.0), (4, 1.0)):
            n = 4 * j + off
            if n < 1024:
                m[n, j] = w
    m[:, 256] = 1.0
    c["maug"] = np.ascontiguousarray(m.reshape(8, 128, 257).transpose(1, 0, 2)).reshape(128, 8 * 257)
    lane = np.arange(128)[:, None]
    ql = np.arange(128)[None, :]
    cm = np.zeros((128, 33, 128), np.float32)
    for res in range(16):
        v = (ql >= 16 * lane + 15 - 128 * res).astype(np.float32)
        b = v.copy()
        a = v.copy()
        a[0, :] = 0.0
        cm[:, res, :] = a
        cm[:, 16 + res, :] = b
    fm = np.ones((128, 128), np.float32)
    fm[0, :] = 0.0
    cm[:, 32, :] = fm
    c["cmpm"] = cm.reshape(128, 33 * 128)
    os_ = np.zeros((128, 3, 3), np.float32)
    for b in range(3):
        os_[:, b, b] = 1.0
    c["onesel"] = os_.reshape(128, 9)
    return c


def _head_consts(hd):
    lg = np.log1p(-(np.float32(2.0) ** np.float32(-5.0 - hd))).astype(np.float32)
    p = np.arange(128, dtype=np.float32)
    qd = np.exp((p + 1.0) * lg).astype(np.float32)
    kdv = (np.exp(-(p + 1.0) * lg) / 16.0).astype(np.float32)
    return dict(qdec=np.ascontiguousarray(np.broadcast_to(np.tile(qd, 4)[None, :], (128, 512))).astype(np.float32),
                kdec=np.ascontiguousarray(np.broadcast_to(np.tile(kdv, 4)[None, :], (128, 512))).astype(np.float32),
                cdec=np.full((128, 1), np.exp(np.float32(128.0) * lg), np.float32))


def _inmaps(T, B, I):
    C = _consts(T)
    TL = T // 4
    maps = []
    ca = np.ascontiguousarray
    for b in range(B):
        for g in range(4):
            m = dict(C)
            m.update(_head_consts(g))
            m["x"] = ca(I["x"][b, :T])
            m["p0"] = ca(I["p"][0, b, :T])
            p1 = I["p"][1, b, :T].reshape(T // 1024, 4, 256, 256)[:, g].reshape(TL, 256)
            m["p1s"] = ca(p1)
            wi = I["ret_w_in"][0]
            m["r_w_in"] = ca(np.concatenate([wi[:, g * 256:(g + 1) * 256], wi[:, 1024 + g * 256:1024 + (g + 1) * 256],
                                             wi[:, 2048 + g * 512:2048 + (g + 1) * 512], wi[:, 4096 + g * 512:4096 + (g + 1) * 512]], axis=1))
            m["r_g_in"] = _colgain(I["ret_norm"][0])
            m["r_gn"] = ca(np.broadcast_to(I["ret_gn"][0][g * 512:(g + 1) * 512][None, :], (128, 512)))
            m["r_w_out"] = ca(I["ret_w_out"][0][g * 512:(g + 1) * 512, :])
            for l in range(2):
                m["pg%d" % l] = _colgain(I["ple_norm"][l])
                m["wg%d" % l] = ca(I["ple_w_gate"][l])
                m["we%d" % l] = ca(I["ple_w_emb"][l])
            m["fng"] = ca(np.broadcast_to(I["final_norm"][None, :], (128, 1024)))
            m["kv_g"] = _colgain(I["kv_norm"])
            kw = I["kv_w"]
            order = [0, 1, 2, 4, 3, 5]
            m["kv_w"] = ca(np.concatenate([kw[:, pt * 512 + g * 128: pt * 512 + (g + 1) * 128] for pt in order], axis=1))
            m["peT_k"] = ca(I["cmp_pe_k"].T)
            m["peT_v"] = ca(I["cmp_pe_v"].T)
            m["w1_k"] = ca(I["cmp_w1_k"])
            m["w1_v"] = ca(I["cmp_w1_v"])
            m["w2_k"] = ca(I["cmp_w2_k"])
            m["w2_v"] = ca(I["cmp_w2_v"])
            m["n_g"] = _colgain(I["nsa_norm"][0])
            nw = I["nsa_w_in"][0]
            m["n_w_in"] = ca(np.concatenate([nw[:, g * 512:(g + 1) * 512], nw[:, 2048 + g * 512:2048 + (g + 1) * 512],
                                             nw[:, 4096 + g * 12:4096 + (g + 1) * 12]], axis=1))
            m["n_w_out"] = ca(I["nsa_w_out"][0][g * 512:(g + 1) * 512, :])
            maps.append({k: np.asarray(v, np.float32) for k, v in m.items()})
    return maps


_PROG = {}


def run_module(I, T, B):
    key = (T, B)
    if key not in _PROG:
        groups = [[b * 4 + g for g in range(4)] for b in range(B)]
        _PROG[key] = build_program(T, groups)[0]
    nc = _PROG[key]
    maps = _inmaps(T, B, I)
    res = run_bass_kernel_spmd(nc, maps, core_ids=list(range(4 * B)))
    outp = np.empty((B, T, 1024), np.float32)
    for b in range(B):
        for g in range(4):
            o = res.results[b * 4 + g]["out"].reshape(T // 1024, 256, 1024)
            outp[b].reshape(T // 1024, 4, 256, 1024)[:, g] = o
    return outp


def kernel(**inputs):
    I = {k: np.asarray(v) for k, v in inputs.items()}
    return run_module(I, 16384, 2)
```

```python
import contextlib
import numpy as np
import concourse.bass as bass
import concourse.mybir as mybir
from concourse.bass_utils import run_bass_kernel_spmd

F32 = mybir.dt.float32
BF16 = mybir.dt.bfloat16
AF = mybir.ActivationFunctionType
ALU = mybir.AluOpType
AX = mybir.AxisListType
ENGS = ("tensor", "vector", "scalar", "gpsimd", "sync")
EPS = 1e-6
SCALE = 128 ** -0.5


class Buf:
    __slots__ = ("name", "t", "w", "r", "psum")

    def __init__(self, name, t=None, psum=False):
        self.name = name
        self.t = t
        self.w = None
        self.r = []
        self.psum = psum

    def __getitem__(self, idx):
        return self.t[idx]


class _Rec:
    def __init__(self):
        self.call = None

    def __getattr__(self, name):
        def f(*a, **kw):
            assert self.call is None
            self.call = (name, a, kw)
            return self
        return f


class Sched:
    def __init__(self, nc, n_dma_sems=12):
        self.nc = nc
        self.sems = {}
        self.cnt = {}
        for e in ENGS:
            self.sems[e] = nc.alloc_semaphore("s_" + e)
            self.cnt[e] = 0
        self.sems["cc"] = nc.alloc_semaphore("s_cc")
        self.cnt["cc"] = 0
        self.dq = {}
        for q in ("sync", "gpsimd", "scalar"):
            lst = []
            for i in range(n_dma_sems):
                k = "d_%s_%d" % (q, i)
                self.sems[k] = nc.alloc_semaphore(k)
                self.cnt[k] = 0
                lst.append(k)
            self.dq[q] = [lst, 0]
        self.known = {e: {} for e in ENGS}
        self.E = {e: getattr(nc, e) for e in ENGS}
        self.ninstr = 0
        self.uid = 0
        self.stacks = [contextlib.ExitStack()]

    def push(self):
        self.stacks.append(contextlib.ExitStack())

    def pop(self):
        self.barrier()
        self.stacks.pop().close()

    def _nm(self, name):
        self.uid += 1
        return "%s_%d" % (name, self.uid)

    def sb(self, name, shape, dtype):
        nm = self._nm(name)
        return Buf(nm, self.stacks[-1].enter_context(self.nc.sbuf_tensor(nm, list(shape), dtype)))

    def ps(self, name, shape, dtype=F32):
        nm = self._nm(name)
        return Buf(nm, self.stacks[-1].enter_context(self.nc.psum_tensor(nm, list(shape), dtype)), psum=True)

    def dr(self, name, shape, dtype):
        return self.nc.dram_tensor(self._nm(name), list(shape), dtype)

    def _need(self, eng, deps):
        kn = self.known[eng]
        best = {}
        for d in deps:
            if d is None:
                continue
            k, v = d
            if k == eng and eng == "tensor":
                continue
            if kn.get(k, 0) >= v:
                continue
            if best.get(k, 0) < v:
                best[k] = v
        return best

    def _emit_waits(self, eng, best):
        for k, v in best.items():
            self.E[eng].wait_ge(self.sems[k], v)
            self.known[eng][k] = v
            self.ninstr += 1

    @staticmethod
    def _deps(reads, writes):
        deps = []
        for b in reads:
            deps.append(b.w)
            if b.psum:
                deps.extend(b.r)
        for b in writes:
            deps.append(b.w)
            deps.extend(b.r)
        return deps

    @staticmethod
    def _mark(ev, reads, writes):
        for b in reads:
            b.r.append(ev)
        for b in writes:
            b.w = ev
            b.r = []

    def op(self, eng, fn, reads=(), writes=(), lhs=None):
        attach = None
        if eng == "tensor" and lhs is None:
            self._emit_waits(eng, self._need(eng, self._deps(reads, writes)))
        else:
            if lhs:
                self._emit_waits(eng, self._need(eng, self._deps(lhs, ())))
            best = self._need(eng, self._deps(reads, writes))
            if best:
                k = next(iter(best))
                attach = (k, best.pop(k))
            self._emit_waits(eng, best)
        self.cnt[eng] += 1
        rec = _Rec()
        fn(rec)
        name, a, kw = rec.call
        ins = getattr(self.E[eng], name)(*a, **kw)
        if attach is not None:
            ins = ins._wait_ge(self.sems[attach[0]], attach[1])
            self.known[eng][attach[0]] = attach[1]
        ins.then_inc(self.sems[eng], 1)
        self.ninstr += 1
        ev = (eng, self.cnt[eng])
        self._mark(ev, reads, writes)
        return ev

    def dma(self, q, out, in_, reads=(), writes=(), **kw):
        lst, idx = self.dq[q]
        k = lst[idx % len(lst)]
        self.dq[q][1] = idx + 1
        deps = self._deps(reads, writes)
        if self.cnt[k] > 0:
            deps.append((k, self.cnt[k]))
        self._emit_waits(q, self._need(q, deps))
        self.cnt[k] += 16
        self.E[q].dma_start(out=out, in_=in_, **kw).then_inc(self.sems[k], 16)
        self.ninstr += 1
        ev = (k, self.cnt[k])
        self._mark(ev, reads, writes)
        return ev

    def collective(self, kind, op, groups, in_ap, out_ap, reads=(), writes=()):
        deps = self._deps(reads, writes)
        if self.cnt["cc"] > 0:
            deps.append(("cc", self.cnt["cc"]))
        self._emit_waits("gpsimd", self._need("gpsimd", deps))
        self.cnt["cc"] += 1
        self.E["gpsimd"].collective_compute(kind, op, replica_groups=groups, ins=[in_ap], outs=[out_ap]).then_inc(
            self.sems["cc"], 1)
        self.ninstr += 1
        ev = ("cc", self.cnt["cc"])
        self._mark(ev, reads, writes)
        return ev

    def _all_events(self):
        return [(k, v) for k, v in self.cnt.items() if v > 0]

    def barrier(self):
        ev = self._all_events()
        for e in ENGS:
            self._emit_waits(e, self._need(e, [d for d in ev if d[0] != e]))

    def finish(self):
        deps = [(k, v) for k, v in self.cnt.items() if v > 0 and (k.startswith("d_") or k == "cc")]
        self._emit_waits("sync", self._need("sync", deps))
        self.barrier()
        while self.stacks:
            self.stacks.pop().close()


def load_w_bf16(S, q, dst, dst_fn, src_ap, stage, kc, ncols, gain=None):
    for k in range(kc):
        c0 = 0
        while c0 < ncols:
            cw = min(2048, ncols - c0)
            S.dma(q, stage[:, 0:cw], src_ap[:, k, c0:c0 + cw], writes=[stage])
            if gain is None:
                S.op("gpsimd", lambda e: e.tensor_copy(out=dst_fn(k, c0, cw), in_=stage[:, 0:cw]),
                     reads=[stage], writes=[dst])
            else:
                S.op("gpsimd", lambda e: e.tensor_scalar(out=dst_fn(k, c0, cw), in0=stage[:, 0:cw],
                                                         scalar1=gain[:, k:k + 1], scalar2=None, op0=ALU.mult),
                     reads=[stage, gain], writes=[dst])
            c0 += cw


def load_const_bf16(S, q, dst, dst_ap, src_ap, stage, ncols):
    S.dma(q, stage[:, 0:ncols], src_ap, writes=[stage])
    S.op("gpsimd", lambda e: e.tensor_copy(out=dst_ap, in_=stage[:, 0:ncols]), reads=[stage], writes=[dst])


def rms_rstd(S, xt, D, sqs, ss, rs):
    S.op("scalar", lambda e: e.activation(out=sqs[:, :], in_=xt[:, :], func=AF.Square, accum_out=ss[:, :]),
         reads=[xt], writes=[sqs, ss])
    S.op("vector", lambda e: e.tensor_scalar(out=rs[:, :], in0=ss[:, :], scalar1=1.0 / D, scalar2=EPS,
                                             op0=ALU.mult, op1=ALU.add), reads=[ss], writes=[rs])
    S.op("scalar", lambda e: e.activation(out=rs[:, :], in_=rs[:, :], func=AF.Sqrt), reads=[rs], writes=[rs])
    S.op("vector", lambda e: e.reciprocal(out=rs[:, :], in_=rs[:, :]), reads=[rs], writes=[rs])


def rmsnorm_T(S, xt, ident, sqs, ss, rs, xn, pT, dstT, tok0, copy_eng="vector"):
    rms_rstd(S, xt, 1024, sqs, ss, rs)
    S.op("vector", lambda e: e.tensor_scalar(out=xn[:, :], in0=xt[:, :], scalar1=rs[:, 0:1], scalar2=None,
                                             op0=ALU.mult), reads=[xt, rs], writes=[xn])
    for k in range(8):
        S.op("tensor", lambda e: e.transpose(out=pT[:, k, :], in_=xn[:, k * 128:(k + 1) * 128], identity=ident[:, :]),
             reads=[xn, ident], writes=[pT])
    if copy_eng == "vector":
        S.op("vector", lambda e: e.tensor_copy(out=dstT[:, :, tok0:tok0 + 128], in_=pT[:, :, :]), reads=[pT], writes=[dstT])
    else:
        S.op("scalar", lambda e: e.copy(out=dstT[:, :, tok0:tok0 + 128], in_=pT[:, :, :]), reads=[pT], writes=[dstT])


class PleCtx:
    def __init__(self, S):
        self.S = S
        self.Wg = S.sb("pleWg", [128, 8, 1024], BF16)
        self.We = S.sb("pleWe", [128, 2, 1024], BF16)
        self.gain = S.sb("pleGain", [128, 8], F32)
        self.sqs = S.sb("ple_sqs", [128, 1024], BF16)
        self.ss = [S.sb("ple_ss", [128, 1], F32) for i in range(2)]
        self.rs = [S.sb("ple_rs", [128, 1], F32) for i in range(2)]
        _xn = S.sb("ple_xn", [128, 1024], BF16)
        self.xn = [_xn, _xn]
        self.hnT = [S.sb("ple_hnT", [128, 8, 128], BF16) for i in range(2)]
        self.pf = [S.sb("ple_pf", [128, 256], F32) for i in range(2)]
        self.pb = [S.sb("ple_pb", [128, 256], BF16) for i in range(2)]
        self.pT = [S.sb("ple_pT", [128, 2, 128], BF16) for i in range(2)]
        _sig = S.sb("ple_sig", [128, 512], F32)
        _prod = S.sb("ple_prod", [128, 512], F32)
        self.sig = [_sig, _sig]
        self.prod = [_prod, _prod]

    def load_weights(self, q, g_ap, wg_ap, we_ap, stage):
        S = self.S
        S.dma(q, self.gain[:, :], g_ap, writes=[self.gain])
        load_w_bf16(S, q, self.Wg, lambda k, c0, cw: self.Wg[:, k, c0:c0 + cw],
                    wg_ap.rearrange("(k p) n -> p k n", p=128), stage, 8, 1024, gain=self.gain)
        load_w_bf16(S, q, self.We, lambda k, c0, cw: self.We[:, k, c0:c0 + cw],
                    we_ap.rearrange("(k p) n -> p k n", p=128), stage, 2, 1024)

    def front(self, par, h, p_ap, ident, pA):
        S = self.S
        pf, pb, pT = self.pf[par], self.pb[par], self.pT[par]
        S.dma("sync", pf[:, :], p_ap, writes=[pf])
        rmsnorm_T(S, h, ident, self.sqs, self.ss[par], self.rs[par], self.xn[par], pA, self.hnT[par], 0)
        S.op("gpsimd", lambda e: e.tensor_copy(out=pb[:, :], in_=pf[:, :]), reads=[pf], writes=[pb])
        for k in range(2):
            S.op("tensor", lambda e: e.transpose(out=pA[:, k, :], in_=pb[:, k * 128:(k + 1) * 128], identity=ident[:, :]),
                 reads=[pb, ident], writes=[pA])
        S.op("vector", lambda e: e.tensor_copy(out=pT[:, :, :], in_=pA[:, 0:2, :]), reads=[pA], writes=[pT])

    def back(self, par, h, pG, pE):
        S = self.S
        hnT, pT = self.hnT[par], self.pT[par]
        for hf in range(2):
            cs = slice(hf * 512, (hf + 1) * 512)
            sig, prod = self.sig[hf], self.prod[hf]
            for k in range(8):
                S.op("tensor", lambda e: e.matmul(pG[hf][:, :], lhsT=hnT[:, k, :], rhs=self.Wg[:, k, cs],
                                                  start=(k == 0), stop=(k == 7)), reads=[hnT, self.Wg], writes=[pG[hf]])
            for k in range(2):
                S.op("tensor", lambda e: e.matmul(pE[hf][:, :], lhsT=pT[:, k, :], rhs=self.We[:, k, cs],
                                                  start=(k == 0), stop=(k == 1)), reads=[pT, self.We], writes=[pE[hf]])
            S.op("scalar", lambda e: e.activation(out=sig[:, :], in_=pG[hf][:, :], func=AF.Sigmoid), reads=[pG[hf]], writes=[sig])
            S.op("vector", lambda e: e.tensor_tensor(out=prod[:, :], in0=sig[:, :], in1=pE[hf][:, :], op=ALU.mult),
                 reads=[sig, pE[hf]], writes=[prod])
            S.op("gpsimd", lambda e: e.tensor_add(out=h[:, cs], in0=h[:, cs], in1=prod[:, :]), reads=[h, prod], writes=[h])


def build_program(T, groups, upto=4):
    nc = bass.Bass("TRN2", target_bir_lowering=False)
    NT = T // 128
    NS = T // 512
    NCH = T // 1024
    NC16 = T // 2048
    TL = T // 4

    def din(name, shape):
        return nc.dram_tensor(name, list(shape), F32, kind="ExternalInput").ap()

    x = din("x", [T, 1024])
    p0 = din("p0", [T, 256])
    p1s = din("p1s", [TL, 256])
    identd = din("ident", [128, 128])
    r_w_in = din("r_w_in", [1024, 1536])
    r_g_in = din("r_g_in", [128, 8])
    r_gn = din("r_gn", [128, 512])
    r_w_out = din("r_w_out", [512, 1024])
    cosT = din("cosT", [128, T])
    sinT = din("sinT", [128, T])
    qdec = din("qdec", [128, 512])
    kdec = din("kdec", [128, 512])
    cdec = din("cdec", [128, 1])
    causT_d = din("causT", [128, 128])
    upT_d = din("upT", [128, 128])
    pg = [din("pg%d" % l, [128, 8]) for l in range(2)]
    wg = [din("wg%d" % l, [1024, 1024]) for l in range(2)]
    we = [din("we%d" % l, [256, 1024]) for l in range(2)]
    fng = din("fng", [128, 1024])
    kv_g = din("kv_g", [128, 8])
    kv_w = din("kv_w", [1024, 768])
    peT_k = din("peT_k", [128, 32])
    peT_v = din("peT_v", [128, 32])
    w1_k = din("w1_k", [4096, 256])
    w1_v = din("w1_v", [4096, 256])
    w2_k = din("w2_k", [256, 128])
    w2_v = din("w2_v", [256, 128])
    n_g = din("n_g", [128, 8])
    n_w_in = din("n_w_in", [1024, 1036])
    n_w_out = din("n_w_out", [512, 1024])
    ex_d = din("ex", [128, 64 * 128])
    maug_d = din("maug", [128, 8 * 257])
    cmpm_d = din("cmpm", [128, 33 * 128])
    onesel_d = din("onesel", [128, 9])
    out = nc.dram_tensor("out", [TL, 1024], F32, kind="ExternalOutput").ap()
    dbg = nc.dram_tensor("dbg", [T, 1024], F32, kind="ExternalOutput").ap() if upto < 4 else None

    def dbg_out(src, bufs):
        for c in range(T // 1024):
            S.dma("sync", dbg[c * 1024:(c + 1) * 1024, :], src[c * 1024:(c + 1) * 1024, :], reads=bufs[c * 8:(c + 1) * 8])
        S.finish()
        return nc, S.ninstr

    S = Sched(nc)
    Y1 = S.dr("Y1", [T, 1024], F32)
    Y1s = S.dr("Y1s", [T, 1024], F32)
    Y2 = S.dr("Y2", [T, 1024], F32)
    Y2s = S.dr("Y2s", [TL, 1024], F32)
    H1 = S.dr("H1", [T, 1024], F32)
    HT = S.dr("HT", [NT, 128, 1024], BF16)
    KW = S.dr("KW", [NT, 128, 128], BF16)
    VW = S.dr("VW", [NT, 128, 128], BF16)
    bY1 = [Buf("bY1_%d" % i) for i in range(NT)]
    bY1s = [Buf("bY1s_%d" % i) for i in range(NT)]
    bY2 = [Buf("bY2_%d" % i) for i in range(NT)]
    bY2s = [Buf("bY2s_%d" % i) for i in range(NCH)]
    bH1 = [Buf("bH1_%d" % i) for i in range(NT)]
    bHT = [Buf("bHT_%d" % i) for i in range(NT)]
    bKW = [Buf("bKW_%d" % i) for i in range(NT)]
    bVW = [Buf("bVW_%d" % i) for i in range(NT)]

    stage = S.sb("stage", [128, 2048], F32)
    idf = S.sb("idf", [128, 128], F32)
    ident = S.sb("identb", [128, 128], BF16)
    S.dma("sync", idf[:, :], identd[:, :], writes=[idf])
    S.op("vector", lambda e: e.tensor_copy(out=ident[:, :], in_=idf[:, :]), reads=[idf], writes=[ident])

    S.push()
    W = S.sb("W", [128, 8, 1536], BF16)
    Wo = S.sb("Wo", [128, 4, 1024], BF16)
    gin = S.sb("gin", [128, 8], F32)
    gnt = S.sb("gnt", [128, 512], F32)
    qd_t = S.sb("qd_t", [128, 512], F32)
    kd_t = S.sb("kd_t", [128, 512], F32)
    cd_t = S.sb("cd_t", [128, 1], F32)
    caus = S.sb("caus", [128, 128], F32)
    xts = [S.sb("xt", [128, 1024], F32) for i in range(2)]
    sqs = S.sb("sqs", [128, 1024], F32)
    ss = S.sb("ss", [128, 1], F32)
    rs = S.sb("rs", [128, 1], F32)
    xn = S.sb("xn", [128, 1024], BF16)
    xnT2 = [S.sb("xnT", [128, 8, 512], BF16) for i in range(2)]
    cs = [S.sb("cs", [128, 512], F32) for i in range(2)]
    sn = [S.sb("sn", [128, 512], F32) for i in range(2)]
    tabs2 = [[S.sb("tab", [128, 512], F32) for i in range(4)] for p in range(2)]
    sss = [S.sb("ss", [128, 1], F32) for i in range(2)]
    rss = [S.sb("rs", [128, 1], F32) for i in range(2)]
    xns = [S.sb("xn", [128, 1024], BF16) for i in range(2)]
    raw = [S.sb("raw", [128, 512], F32) for i in range(4)]
    tmp = [S.sb("tmp", [128, 512], F32) for i in range(4)]
    qdT2 = [S.sb("qdT", [128, 2, 512], BF16) for i in range(2)]
    kTp2 = [S.sb("kTp", [128, 2, 512], BF16) for i in range(2)]
    vb2 = [S.sb("vb", [128, 4, 512], BF16) for i in range(2)]
    gs2 = [S.sb("gs", [128, 4, 512], F32) for i in range(2)]
    st_f = [S.sb("st_f", [128, 512], F32) for i in range(2)]
    st_b = [S.sb("st_b", [128, 512], BF16) for i in range(2)]
    kd = S.sb("kd", [128, 256], BF16)
    ST = S.sb("ST", [128, 128], BF16)
    osq = S.sb("osq", [128, 512], F32)
    stats = [S.sb("stat", [128, 4], F32) for i in range(2)]
    ons = [S.sb("on", [128, 512], F32) for i in range(2)]
    ogs = [S.sb("og", [128, 512], BF16) for i in range(2)]
    ogT = S.sb("ogT", [128, 4, 128], BF16)
    yo = [S.sb("yo", [128, 1024], F32) for i in range(2)]
    pA = S.ps("pA", [128, 8, 128], BF16)
    pB = [S.ps("pB", [128, 512], F32) for i in range(2)]
    pS = S.ps("pS", [128, 128], F32)
    pO = S.ps("pO", [128, 512], F32)
    pSt = [S.ps("pSt", [128, 512], F32) for i in range(2)]
    pY = S.ps("pY", [128, 512], F32)

    for (dst, src) in ((gin, r_g_in), (gnt, r_gn), (qd_t, qdec), (kd_t, kdec), (cd_t, cdec), (caus, causT_d)):
        S.dma("sync", dst[:, :], src[:, :], writes=[dst])
    load_w_bf16(S, "sync", W, lambda k, c0, cw: W[:, k, c0:c0 + cw],
                r_w_in.rearrange("(k p) n -> p k n", p=128), stage, 8, 1536, gain=gin)
    load_w_bf16(S, "sync", Wo, lambda k, c0, cw: Wo[:, k, c0:c0 + cw],
                r_w_out.rearrange("(k p) n -> p k n", p=128), stage, 4, 1024)
    for i in range(2):
        S.op("gpsimd", lambda e: e.memset(st_f[i][:, :], 0.0), writes=[st_f[i]])
        S.op("gpsimd", lambda e: e.memset(st_b[i][:, :], 0.0), writes=[st_b[i]])

    def allreduce_chunk(c):
        r0 = c * 1024
        tl = list(range(c * 8, c * 8 + 8))
        S.collective("AllReduce", ALU.add, groups, Y1[r0:r0 + 1024, :], Y1s[r0:r0 + 1024, :],
                     reads=[bY1[t] for t in tl], writes=[bY1s[t] for t in tl])

    def tabs_for(s):
        t0 = s * 512
        cst, snt = cs[s % 2], sn[s % 2]
        tb = tabs2[s % 2]
        S.dma("sync", cst[:, :], cosT[:, t0:t0 + 512], writes=[cst])
        S.dma("sync", snt[:, :], sinT[:, t0:t0 + 512], writes=[snt])
        S.op("gpsimd", lambda e: e.tensor_mul(out=tb[0][:, :], in0=cst[:, :], in1=qd_t[:, :]), reads=[cst, qd_t], writes=[tb[0]])
        S.op("gpsimd", lambda e: e.tensor_mul(out=tb[1][:, :], in0=snt[:, :], in1=qd_t[:, :]), reads=[snt, qd_t], writes=[tb[1]])
        S.op("gpsimd", lambda e: e.tensor_mul(out=tb[2][:, :], in0=cst[:, :], in1=kd_t[:, :]), reads=[cst, kd_t], writes=[tb[2]])
        S.op("gpsimd", lambda e: e.tensor_mul(out=tb[3][:, :], in0=snt[:, :], in1=kd_t[:, :]), reads=[snt, kd_t], writes=[tb[3]])

    def F(s, j):
        ti = s * 4 + j
        xt = xts[ti % 2]
        S.dma("sync", xt[:, :], x[ti * 128:(ti + 1) * 128, :], writes=[xt])
        rmsnorm_T(S, xt, ident, sqs, sss[ti % 2], rss[ti % 2], xns[ti % 2], pA, xnT2[s % 2], j * 128)

    def P1(s):
        xnT = xnT2[s % 2]
        tb = tabs2[s % 2]
        qdT, kTp, vb, gs = qdT2[s % 2], kTp2[s % 2], vb2[s % 2], gs2[s % 2]
        for dc in range(4):
            pb = pB[dc % 2]
            for k in range(8):
                S.op("tensor", lambda e: e.matmul(pb[:, :], lhsT=W[:, k, dc * 128:(dc + 1) * 128], rhs=xnT[:, k, :],
                                                  start=(k == 0), stop=(k == 7)), reads=[W, xnT], writes=[pb], lhs=[W])
            S.op("scalar", lambda e: e.copy(out=raw[dc][:, :], in_=pb[:, :]), reads=[pb], writes=[raw[dc]])
        for (eng, x1, x2, ct, st_, dst, ta, tb_) in (("gpsimd", raw[0], raw[1], tb[0], tb[1], qdT, tmp[0], tmp[1]),
                                                     ("vector", raw[2], raw[3], tb[2], tb[3], kTp, tmp[2], tmp[3])):
            S.op(eng, lambda e: e.tensor_mul(out=ta[:, :], in0=x1[:, :], in1=ct[:, :]), reads=[x1, ct], writes=[ta])
            S.op(eng, lambda e: e.tensor_mul(out=tb_[:, :], in0=x2[:, :], in1=st_[:, :]), reads=[x2, st_], writes=[tb_])
            S.op(eng, lambda e: e.tensor_sub(out=dst[:, 0, :], in0=ta[:, :], in1=tb_[:, :]), reads=[ta, tb_], writes=[dst])
            S.op(eng, lambda e: e.tensor_mul(out=ta[:, :], in0=x1[:, :], in1=st_[:, :]), reads=[x1, st_], writes=[ta])
            S.op(eng, lambda e: e.tensor_mul(out=tb_[:, :], in0=x2[:, :], in1=ct[:, :]), reads=[x2, ct], writes=[tb_])
            S.op(eng, lambda e: e.tensor_add(out=dst[:, 1, :], in0=ta[:, :], in1=tb_[:, :]), reads=[ta, tb_], writes=[dst])
        for j in range(4):
            for (which, c0) in (("v", 512), ("g", 1024)):
                pb = pB[0] if which == "v" else pB[1]
                for k in range(8):
                    S.op("tensor", lambda e: e.matmul(pb[:, :], lhsT=xnT[:, k, j * 128:(j + 1) * 128], rhs=W[:, k, c0:c0 + 512],
                                                      start=(k == 0), stop=(k == 7)), reads=[W, xnT], writes=[pb])
                if which == "v":
                    S.op("scalar", lambda e: e.copy(out=vb[:, j, :], in_=pb[:, :]), reads=[pb], writes=[vb])
                else:
                    S.op("scalar", lambda e: e.activation(out=gs[:, j, :], in_=pb[:, :], func=AF.Silu), reads=[pb], writes=[gs])

    def CA(s, j):
        ti = s * 4 + j
        qdT, kTp, vb = qdT2[s % 2], kTp2[s % 2], vb2[s % 2]
        on, stat = ons[ti % 2], stats[ti % 2]
        tk = slice(j * 128, (j + 1) * 128)
        for dc in range(2):
            S.op("tensor", lambda e: e.transpose(out=pA[:, dc, :], in_=kTp[:, dc, tk], identity=ident[:, :]),
                 reads=[kTp, ident], writes=[pA])
        S.op("vector", lambda e: e.tensor_scalar(out=kd[:, :], in0=pA[:, 0:2, :].rearrange("p a b -> p (a b)"),
                                                 scalar1=cd_t[:, 0:1], scalar2=None, op0=ALU.mult),
             reads=[pA, cd_t], writes=[kd])
        for dc in range(2):
            S.op("tensor", lambda e: e.matmul(pS[:, :], lhsT=kTp[:, dc, tk], rhs=qdT[:, dc, tk],
                                              start=(dc == 0), stop=(dc == 1)), reads=[kTp, qdT], writes=[pS])
        S.op("vector", lambda e: e.tensor_tensor(out=ST[:, :], in0=pS[:, :], in1=caus[:, :], op=ALU.mult),
             reads=[pS, caus], writes=[ST])
        S.op("tensor", lambda e: e.matmul(pO[:, :], lhsT=ST[:, :], rhs=vb[:, j, :], start=True, stop=False),
             reads=[ST, vb], writes=[pO])
        for dc in range(2):
            S.op("tensor", lambda e: e.matmul(pO[:, :], lhsT=qdT[:, dc, tk], rhs=st_b[dc][:, :],
                                              start=False, stop=(dc == 1)), reads=[qdT, st_b[dc]], writes=[pO])
        for dc in range(2):
            S.op("tensor", lambda e: e.matmul(pSt[dc][:, :], lhsT=kd[:, dc * 128:(dc + 1) * 128], rhs=vb[:, j, :],
                                              start=True, stop=True), reads=[kd, vb], writes=[pSt[dc]])
            S.op("vector", lambda e: e.scalar_tensor_tensor(out=st_f[dc][:, :], in0=st_f[dc][:, :], scalar=cd_t[:, 0:1],
                                                            in1=pSt[dc][:, :], op0=ALU.mult, op1=ALU.add),
                 reads=[st_f[dc], cd_t, pSt[dc]], writes=[st_f[dc]])
            S.op("scalar", lambda e: e.copy(out=st_b[dc][:, :], in_=st_f[dc][:, :]), reads=[st_f[dc]], writes=[st_b[dc]])
        S.op("scalar", lambda e: e.activation(out=on[:, :], in_=pO[:, :], func=AF.Identity, accum_out=stat[:, 0:1]),
             reads=[pO], writes=[on, stat])
        S.op("scalar", lambda e: e.activation(out=osq[:, :], in_=pO[:, :], func=AF.Square, accum_out=stat[:, 1:2]),
             reads=[pO], writes=[osq, stat])

    def CB(s, j):
        ti = s * 4 + j
        gs = gs2[s % 2]
        on, stat, og = ons[ti % 2], stats[ti % 2], ogs[ti % 2]
        S.op("vector", lambda e: e.tensor_scalar(out=stat[:, 0:2], in0=stat[:, 0:2], scalar1=1.0 / 512, scalar2=None,
                                                 op0=ALU.mult), reads=[stat], writes=[stat])
        S.op("vector", lambda e: e.tensor_tensor(out=stat[:, 2:3], in0=stat[:, 0:1], in1=stat[:, 0:1], op=ALU.mult),
             reads=[stat], writes=[stat])
        S.op("vector", lambda e: e.tensor_tensor(out=stat[:, 2:3], in0=stat[:, 1:2], in1=stat[:, 2:3], op=ALU.subtract),
             reads=[stat], writes=[stat])
        S.op("vector", lambda e: e.tensor_scalar(out=stat[:, 2:3], in0=stat[:, 2:3], scalar1=EPS, scalar2=None,
                                                 op0=ALU.add), reads=[stat], writes=[stat])
        S.op("scalar", lambda e: e.activation(out=stat[:, 2:3], in_=stat[:, 2:3], func=AF.Sqrt), reads=[stat], writes=[stat])
        S.op("vector", lambda e: e.reciprocal(out=stat[:, 3:4], in_=stat[:, 2:3]), reads=[stat], writes=[stat])
        S.op("vector", lambda e: e.tensor_scalar(out=on[:, :], in0=on[:, :], scalar1=stat[:, 0:1], scalar2=stat[:, 3:4],
                                                 op0=ALU.subtract, op1=ALU.mult), reads=[on, stat], writes=[on])
        S.op("gpsimd", lambda e: e.tensor_mul(out=on[:, :], in0=on[:, :], in1=gnt[:, :]), reads=[on, gnt], writes=[on])
        S.op("gpsimd", lambda e: e.tensor_mul(out=og[:, :], in0=on[:, :], in1=gs[:, j, :]), reads=[on, gs], writes=[og])
        for c in range(4):
            S.op("tensor", lambda e: e.transpose(out=pA[:, 2 + c, :], in_=og[:, c * 128:(c + 1) * 128], identity=ident[:, :]),
                 reads=[og, ident], writes=[pA])
        S.op("vector", lambda e: e.tensor_copy(out=ogT[:, :, :], in_=pA[:, 2:6, :]), reads=[pA], writes=[ogT])
        yt = yo[ti % 2]
        for hf in range(2):
            for c in range(4):
                S.op("tensor", lambda e: e.matmul(pY[:, :], lhsT=ogT[:, c, :], rhs=Wo[:, c, hf * 512:(hf + 1) * 512],
                                                  start=(c == 0), stop=(c == 3)), reads=[ogT, Wo], writes=[pY])
            S.op("scalar", lambda e: e.copy(out=yt[:, hf * 512:(hf + 1) * 512], in_=pY[:, :]), reads=[pY], writes=[yt])
        S.dma("scalar", Y1[ti * 128:(ti + 1) * 128, :], yt[:, :], reads=[yt], writes=[bY1[ti]])
        if ti >= 9 and (ti - 9) % 8 == 0:
            allreduce_chunk((ti - 9) // 8)

    prev = None
    for s in range(NS + 1):
        if s < NS:
            tabs_for(s)
        for j in range(4):
            if s < NS:
                F(s, j)
            if s >= 1:
                CA(s - 1, j)
                if prev is not None:
                    CB(*prev)
                prev = (s - 1, j)
        if s < NS:
            P1(s)
    CB(*prev)
    allreduce_chunk(NCH - 1)
    if NCH >= 2 and (NT - 1) < 9 + 8 * (NCH - 2):
        pass
    issued = set([(ti - 9) // 8 for ti in range(NT) if ti >= 9 and (ti - 9) % 8 == 0] + [NCH - 1])
    for c in range(NCH):
        if c not in issued:
            allreduce_chunk(c)
    S.pop()
    if upto == 1:
        return dbg_out(Y1s, bY1s)

    S.push()
    KsT = S.sb("KsT", [128, T], BF16)
    Vs = S.sb("Vs", [128, NT, 128], BF16)
    KcT = S.sb("KcT", [128, NC16 * 128], BF16)
    Vc = S.sb("Vc", [128, NC16, 128], BF16)

    S.push()
    ple = PleCtx(S)
    ple.load_weights("sync", pg[0][:, :], wg[0], we[0], stage)
    kvg = S.sb("kvg", [128, 8], F32)
    Wkv = S.sb("Wkv", [128, 8, 768], BF16)
    S.dma("sync", kvg[:, :], kv_g[:, :], writes=[kvg])
    load_w_bf16(S, "sync", Wkv, lambda k, c0, cw: Wkv[:, k, c0:c0 + cw],
                kv_w.rearrange("(k p) n -> p k n", p=128), stage, 8, 768, gain=kvg)
    w1 = [S.sb("w1", [128, 32, 256], BF16) for i in range(2)]
    w2 = [S.sb("w2", [128, 2, 128], BF16) for i in range(2)]
    peT = [S.sb("peT", [128, 32], BF16) for i in range(2)]
    for i, (w1d, w2d, ped) in enumerate(((w1_k, w2_k, peT_k), (w1_v, w2_v, peT_v))):
        load_w_bf16(S, "sync", w1[i], lambda k, c0, cw: w1[i][:, k, c0:c0 + cw],
                    w1d.rearrange("(l d) h -> d l h", d=128), stage, 32, 256)
        load_w_bf16(S, "sync", w2[i], lambda k, c0, cw: w2[i][:, k, c0:c0 + cw],
                    w2d.rearrange("(k p) n -> p k n", p=128), stage, 2, 128)
        load_const_bf16(S, "sync", peT[i], peT[i][:, :], ped[:, :], stage, 32)
    cb = [S.sb("cb", [128, 2064], BF16) for i in range(2)]
    hs = [S.sb("h", [128, 1024], F32) for i in range(3)]
    _yb = S.sb("yb", [128, 1024], F32)
    ybs = [_yb, _yb]
    hTs = [S.sb("hT", [128, 8, 128], BF16) for i in range(2)]
    kwt = [S.sb("kwt", [128, 128], BF16) for i in range(2)]
    vwt = [S.sb("vwt", [128, 128], BF16) for i in range(2)]
    sqs = ple.sqs
    ss2 = [S.sb("ss", [128, 1], F32) for i in range(2)]
    rs2 = [S.sb("rs", [128, 1], F32) for i in range(2)]
    _xn2 = S.sb("xn", [128, 1024], BF16)
    xn2 = [_xn2, _xn2]
    ones1 = S.sb("ones1", [1, 128], BF16)
    bias_f = S.sb("bias_f", [1, 512], F32)
    bias_hi = S.sb("bias_hi", [1, 512], BF16)
    bias_hif = S.sb("bias_hif", [1, 512], F32)
    bias_lo = S.sb("bias_lo", [1, 512], BF16)
    xs_ = S.sb("xs_", [128, 256], F32)
    x2_ = S.sb("x2_", [128, 256], F32)
    sg_ = S.sb("sg_", [128, 256], F32)
    hid = S.sb("hid", [128, 256], BF16)
    hidT = S.sb("hidT", [128, 2, 128], BF16)
    pA = S.ps("pA", [128, 8, 128], BF16)
    pA2 = S.ps("pA2", [128, 8, 128], BF16)
    pG1 = S.ps("pG", [128, 512], F32)
    pE1 = S.ps("pE", [128, 512], F32)
    pGs = [pG1, pG1]
    pEs = [pE1, pE1]
    pKT = S.ps("pKT", [128, 4, 128], F32)
    pKV = S.ps("pKV", [128, 256], F32)
    pH = S.ps("pH", [128, 256], F32)
    pC = S.ps("pC", [128, 128], F32)

    S.op("gpsimd", lambda e: e.memset(ones1[:, :], 1.0), writes=[ones1])
    for i in range(2):
        S.op("gpsimd", lambda e: e.memset(cb[i][:, 0:16], 0.0), writes=[cb[i]])
    for i in range(2):
        for l in range(32):
            S.op("tensor", lambda e: e.matmul(pH[0:1, :], lhsT=peT[i][:, l:l + 1], rhs=w1[i][:, l, :],
                                              start=(l == 0), stop=(l == 31)), reads=[peT[i], w1[i]], writes=[pH])
        S.op("scalar", lambda e: e.copy(out=bias_f[:, i * 256:(i + 1) * 256], in_=pH[0:1, :]), reads=[pH], writes=[bias_f])
    S.op("vector", lambda e: e.tensor_copy(out=bias_hi[:, :], in_=bias_f[:, :]), reads=[bias_f], writes=[bias_hi])
    S.op("vector", lambda e: e.tensor_copy(out=bias_hif[:, :], in_=bias_hi[:, :]), reads=[bias_hi], writes=[bias_hif])
    S.op("vector", lambda e: e.tensor_sub(out=bias_hif[:, :], in0=bias_f[:, :], in1=bias_hif[:, :]), reads=[bias_f, bias_hif], writes=[bias_hif])
    S.op("vector", lambda e: e.tensor_copy(out=bias_lo[:, :], in_=bias_hif[:, :]), reads=[bias_hif], writes=[bias_lo])

    def TA(i):
        rows = slice(i * 128, (i + 1) * 128)
        h = hs[i % 3]
        yb = ybs[i % 2]
        S.dma("sync", h[:, :], x[rows, :], writes=[h])
        S.dma("sync", yb[:, :], Y1s[rows, :], reads=[bY1s[i]], writes=[yb])
        S.op("vector", lambda e: e.tensor_add(out=h[:, :], in0=h[:, :], in1=yb[:, :]), reads=[h, yb], writes=[h])
        ple.front(i % 2, h, p0[rows, :], ident, pA)

    def TB(i):
        rows = slice(i * 128, (i + 1) * 128)
        h = hs[i % 3]
        ple.back(i % 2, h, pGs, pEs)
        S.dma("gpsimd", H1[rows, :], h[:, :], reads=[h], writes=[bH1[i]])

    def TC(i):
        rows = slice(i * 128, (i + 1) * 128)
        h = hs[i % 3]
        hT = hTs[i % 2]
        rmsnorm_T(S, h, ident, sqs, ss2[i % 2], rs2[i % 2], xn2[i % 2], pA2, hT, 0, copy_eng="scalar")
        S.dma("scalar", HT[i].rearrange("p (k t) -> p k t", k=8), hT[:, :, :], reads=[hT], writes=[bHT[i]])
        for a_ in range(4):
            for k in range(8):
                S.op("tensor", lambda e: e.matmul(pKT[:, a_, :], lhsT=Wkv[:, k, a_ * 128:(a_ + 1) * 128], rhs=hT[:, k, :],
                                                  start=(k == 0), stop=(k == 7)), reads=[Wkv, hT], writes=[pKT], lhs=[Wkv])
        for k in range(8):
            S.op("tensor", lambda e: e.matmul(pKV[:, :], lhsT=hT[:, k, :], rhs=Wkv[:, k, 512:768],
                                              start=(k == 0), stop=(k == 7)), reads=[Wkv, hT], writes=[pKV])
        cc0 = 16 + (i % 16) * 128
        S.op("scalar", lambda e: e.copy(out=cb[0][:, cc0:cc0 + 128], in_=pKT[:, 0, :]), reads=[pKT], writes=[cb[0]])
        S.op("scalar", lambda e: e.copy(out=cb[1][:, cc0:cc0 + 128], in_=pKT[:, 1, :]), reads=[pKT], writes=[cb[1]])
        S.op("scalar", lambda e: e.copy(out=KsT[:, rows], in_=pKT[:, 2, :]), reads=[pKT], writes=[KsT])
        kw_, vw_ = kwt[i % 2], vwt[i % 2]
        S.op("scalar", lambda e: e.copy(out=kw_[:, :], in_=pKT[:, 3, :]), reads=[pKT], writes=[kw_])
        S.op("scalar", lambda e: e.copy(out=Vs[:, i, :], in_=pKV[:, 0:128]), reads=[pKV], writes=[Vs])
        S.op("scalar", lambda e: e.copy(out=vw_[:, :], in_=pKV[:, 128:256]), reads=[pKV], writes=[vw_])
        S.dma("scalar", KW[i], kw_[:, :], reads=[kw_], writes=[bKW[i]])
        S.dma("scalar", VW[i], vw_[:, :], reads=[vw_], writes=[bVW[i]])
        if i % 16 == 15:
            compress(i)

    def compress(i):
        if True:
            s16 = i // 16
            for X in range(2):
                bc = slice(X * 256, (X + 1) * 256)
                S.op("tensor", lambda e: e.matmul(pH[:, :], lhsT=ones1[0:1, :], rhs=bias_hi[0:1, bc], start=True, stop=False),
                     reads=[ones1, bias_hi], writes=[pH])
                S.op("tensor", lambda e: e.matmul(pH[:, :], lhsT=ones1[0:1, :], rhs=bias_lo[0:1, bc], start=False, stop=False),
                     reads=[ones1, bias_lo], writes=[pH])
                for l in range(32):
                    S.op("tensor", lambda e: e.matmul(pH[:, :], lhsT=cb[X][:, l:l + 2033:16], rhs=w1[X][:, l, :],
                                                      start=False, stop=(l == 31)), reads=[cb[X], w1[X]], writes=[pH])
                S.op("scalar", lambda e: e.copy(out=xs_[:, :], in_=pH[:, :]), reads=[pH], writes=[xs_])
                S.op("vector", lambda e: e.tensor_tensor(out=x2_[:, :], in0=xs_[:, :], in1=xs_[:, :], op=ALU.mult), reads=[xs_], writes=[x2_])
                S.op("vector", lambda e: e.tensor_scalar(out=x2_[:, :], in0=x2_[:, :], scalar1=0.044715, scalar2=1.0,
                                                         op0=ALU.mult, op1=ALU.add), reads=[x2_], writes=[x2_])
                S.op("vector", lambda e: e.tensor_tensor(out=x2_[:, :], in0=x2_[:, :], in1=xs_[:, :], op=ALU.mult), reads=[x2_, xs_], writes=[x2_])
                S.op("scalar", lambda e: e.activation(out=sg_[:, :], in_=x2_[:, :], func=AF.Sigmoid, scale=1.5957691216057308),
                     reads=[x2_], writes=[sg_])
                S.op("vector", lambda e: e.tensor_tensor(out=hid[:, :], in0=xs_[:, :], in1=sg_[:, :], op=ALU.mult), reads=[xs_, sg_], writes=[hid])
                for hc in range(2):
                    S.op("tensor", lambda e: e.transpose(out=pA2[:, hc, :], in_=hid[:, hc * 128:(hc + 1) * 128], identity=ident[:, :]),
                         reads=[hid, ident], writes=[pA2])
                S.op("vector", lambda e: e.tensor_copy(out=hidT[:, :, :], in_=pA2[:, 0:2, :]), reads=[pA2], writes=[hidT])
                if X == 0:
                    for hc in range(2):
                        S.op("tensor", lambda e: e.matmul(pC[:, :], lhsT=w2[0][:, hc, :], rhs=hidT[:, hc, :],
                                                          start=(hc == 0), stop=(hc == 1)), reads=[w2[0], hidT], writes=[pC])
                    S.op("scalar", lambda e: e.copy(out=KcT[:, s16 * 128:(s16 + 1) * 128], in_=pC[:, :]), reads=[pC], writes=[KcT])
                else:
                    for hc in range(2):
                        S.op("tensor", lambda e: e.matmul(pC[:, :], lhsT=hidT[:, hc, :], rhs=w2[1][:, hc, :],
                                                          start=(hc == 0), stop=(hc == 1)), reads=[w2[1], hidT], writes=[pC])
                    S.op("scalar", lambda e: e.copy(out=Vc[:, s16, :], in_=pC[:, :]), reads=[pC], writes=[Vc])
                S.op("vector", lambda e: e.tensor_copy(out=cb[X][:, 0:16], in_=cb[X][:, 2048:2064]), reads=[cb[X]], writes=[cb[X]])
    for step in range(NT + 2):
        if step < NT:
            TA(step)
        if 0 <= step - 1 < NT:
            TB(step - 1)
        if 0 <= step - 2 < NT:
            TC(step - 2)
    S.pop()

    if upto == 2:
        S.pop()
        return dbg_out(H1, bH1)
    S.push()
    ng = S.sb("ng", [128, 8], F32)
    Wn = S.sb("Wn", [128, 8, 1036], BF16)
    Wo2 = S.sb("Wo2", [128, 4, 1024], BF16)
    S.dma("sync", ng[:, :], n_g[:, :], writes=[ng])
    load_w_bf16(S, "sync", Wn, lambda k, c0, cw: Wn[:, k, c0:c0 + cw],
                n_w_in.rearrange("(k p) n -> p k n", p=128), stage, 8, 1036, gain=ng)
    load_w_bf16(S, "sync", Wo2, lambda k, c0, cw: Wo2[:, k, c0:c0 + cw],
                n_w_out.rearrange("(k p) n -> p k n", p=128), stage, 4, 1024)
    Ex = S.sb("Ex", [128, 64, 128], BF16)
    for c in range(4):
        load_const_bf16(S, "sync", Ex, Ex[:, c * 16:(c + 1) * 16, :].rearrange("p a b -> p (a b)"),
                        ex_d[:, c * 2048:(c + 1) * 2048], stage, 2048)
    Maug = S.sb("Maug", [128, 8, 257], BF16)
    load_const_bf16(S, "sync", Maug, Maug[:, 0:4, :].rearrange("p a b -> p (a b)"), maug_d[:, 0:1028], stage, 1028)
    load_const_bf16(S, "sync", Maug, Maug[:, 4:8, :].rearrange("p a b -> p (a b)"), maug_d[:, 1028:2056], stage, 1028)
    cmpm = S.sb("cmpm", [128, 33, 128], BF16)
    load_const_bf16(S, "sync", cmpm, cmpm[:, 0:16, :].rearrange("p a b -> p (a b)"), cmpm_d[:, 0:2048], stage, 2048)
    load_const_bf16(S, "sync", cmpm, cmpm[:, 16:32, :].rearrange("p a b -> p (a b)"), cmpm_d[:, 2048:4096], stage, 2048)
    load_const_bf16(S, "sync", cmpm, cmpm[:, 32, :], cmpm_d[:, 4096:4224], stage, 128)
    onesel = S.sb("onesel", [128, 3, 3], BF16)
    load_const_bf16(S, "sync", onesel, onesel[:, :, :].rearrange("p a b -> p (a b)"), onesel_d[:, :], stage, 9)
    causT = S.sb("causT", [128, 128], BF16)
    upT = S.sb("upT", [128, 128], BF16)
    load_const_bf16(S, "sync", causT, causT[:, :], causT_d[:, :], stage, 128)
    load_const_bf16(S, "sync", upT, upT[:, :], upT_d[:, :], stage, 128)

    hTs = [S.sb("hT", [128, 8, 128], BF16) for i in range(2)]
    h1s = [S.sb("h1", [128, 1024], F32) for i in range(2)]
    kwr = S.sb("kwr", [128, 6, 128], BF16)
    vwr = S.sb("vwr", [128, 6, 128], BF16)
    kwb = [Buf("kwb%d" % i, kwr.t) for i in range(6)]
    vwb = [Buf("vwb%d" % i, vwr.t) for i in range(6)]
    QTs = [S.sb("QT", [128, 512], BF16) for i in range(2)]
    gsils = [S.sb("gsil", [128, 512], F32) for i in range(2)]
    bgs = [S.sb("bg", [128, 12], F32) for i in range(2)]
    EmC = S.sb("EmC", [128, 8, 512], BF16)
    NBUF = 4
    Eb = [S.sb("Eb", [128, 512], BF16) for i in range(NBUF)]
    Emb = [S.sb("Emb", [128, 512], BF16) for i in range(NBUF)]
    imp = S.sb("imp", [128, 256], F32)
    score = S.sb("score", [128, 256], F32)
    sc2 = S.sb("sc2", [128, 256], F32)
    selF = S.sb("selF", [128, 256], F32)
    selT = S.sb("selT", [128, 2, 128], BF16)
    m8 = S.sb("m8", [128, 16], F32)
    rcols = [S.sb("rcol", [128, 1], F32) for i in range(2)]
    sumsbs = [S.sb("sumsb", [3, 512], F32) for i in range(2)]
    coef = S.sb("coef", [128, 12], F32)
    o_ = S.sb("o_", [128, 512], F32)
    ogf = S.sb("ogf", [128, 512], BF16)
    ogT2 = S.sb("ogT2", [128, 4, 128], BF16)
    yts = [S.sb("yt", [128, 1024], F32) for i in range(2)]
    ObTs = [[S.sb("ObT", [128, 512], BF16) for i in range(3)] for p in range(2)]
    pSc = [S.ps("pSc", [128, 512], F32) for i in range(2)]
    pMs = [S.ps("pM", [128, 512], F32) for i in range(2)]
    pO2 = [S.ps("pOb", [128, 512], F32) for i in range(2)]
    pOb = [pO2[0], pO2[1], pO2[0]]
    pSum1 = S.ps("pSum", [3, 512], F32)
    pSums = [pSum1, pSum1]
    pX = S.ps("pX", [128, 512], F32)
    pXb = pX.t[:, :].bitcast(BF16)

    S.op("gpsimd", lambda e: e.memset(selF[:, :], 0.0), writes=[selF])
    ctr = {"e": 0, "m": 0, "s": 0, "pm": 0}

    def stageA(u):
        rows = u["rows"]
        QT = u["QT"]
        ps = pSc[ctr["s"] % 2]
        ctr["s"] += 1
        S.op("tensor", lambda e: e.matmul(ps[0:rows, :], lhsT=u["ksrc"], rhs=QT[:, :], start=True, stop=True),
             reads=[u["ktrack"], QT], writes=[ps], lhs=[u["ktrack"]])
        E = Eb[ctr["e"] % NBUF]
        ctr["e"] += 1
        S.op("scalar", lambda e: e.activation(out=E[0:rows, :], in_=ps[0:rows, :], func=AF.Exp, scale=SCALE),
             reads=[ps], writes=[E])
        u["E"] = E

    def stageA2(u):
        rows = u["rows"]
        E = u["E"]
        mask = u["mask"]
        if mask is None:
            Em = E
        else:
            Em = Emb[ctr["m"] % NBUF]
            ctr["m"] += 1
            if mask[0] == "sb":
                S.op("vector", lambda e: e.tensor_tensor(
                    out=Em[0:rows, :].rearrange("p (r q) -> p r q", r=4), in0=E[0:rows, :].rearrange("p (r q) -> p r q", r=4),
                    in1=mask[1].unsqueeze(1).broadcast_to([rows, 4, 128]), op=ALU.mult),
                     reads=[E] + mask[2], writes=[Em])
            else:
                t = mask[1]
                pM = pMs[ctr["pm"] % 2]
                ctr["pm"] += 1
                S.op("tensor", lambda e: e.matmul(pM[:, 0:128], lhsT=Ex[:, t % 64, :], rhs=selT[:, t // 64, :],
                                                  start=True, stop=True), reads=[Ex, selT], writes=[pM], lhs=[Ex])
                S.op("vector", lambda e: e.tensor_tensor(
                    out=Em[0:rows, :].rearrange("p (r q) -> p r q", r=4), in0=E[0:rows, :].rearrange("p (r q) -> p r q", r=4),
                    in1=pM[:, 0:128].unsqueeze(1).broadcast_to([128, 4, 128]), op=ALU.mult),
                     reads=[E, pM], writes=[Em])
        u["Em"] = Em
        if u.get("emc") is not None:
            c = u["emc"]
            S.op("gpsimd", lambda e: e.tensor_copy(out=EmC[0:rows, c, :], in_=Em[0:rows, :]), reads=[Em], writes=[EmC])

    def stageB(u):
        rows = u["rows"]
        Em = u["Em"]
        b_idx = u["b"]
        q = u["q"]
        pSum = pSums[q["par"]]
        S.op("tensor", lambda e: e.matmul(pOb[b_idx][:, :], lhsT=u["vsrc"], rhs=Em[0:rows, :], start=u["first"], stop=u["last"]),
             reads=[u["vtrack"], Em], writes=[pOb[b_idx]], lhs=[u["vtrack"]])
        S.op("tensor", lambda e: e.matmul(pSum[:, :], lhsT=onesel[0:rows, b_idx, :], rhs=Em[0:rows, :],
                                          start=(q["nsum"] == 0), stop=(q["nsum"] == q["total_sum"] - 1)),
             reads=[onesel, Em], writes=[pSum], lhs=[onesel])
        q["nsum"] += 1
        for f in u.get("post", ()):
            f()

    def rs_chunk(c):
        tl = list(range(c * 8, c * 8 + 8))
        S.collective("ReduceScatter", ALU.add, groups, Y2[c * 1024:(c + 1) * 1024, :], Y2s[c * 256:(c + 1) * 256, :],
                     reads=[bY2[t] for t in tl], writes=[bY2s[c]])

    def prologue(i):
        par = i % 2
        rows = slice(i * 128, (i + 1) * 128)
        hT, h1, QT, gsil, bg = hTs[par], h1s[par], QTs[par], gsils[par], bgs[par]
        S.dma("sync", hT[:, :, :], HT[i].rearrange("p (k t) -> p k t", k=8), reads=[bHT[i]], writes=[hT])
        S.dma("sync", h1[:, :], H1[rows, :], reads=[bH1[i]], writes=[h1])
        S.dma("sync", kwr[:, i % 6, :], KW[i], reads=[bKW[i]], writes=[kwb[i % 6]])
        S.dma("sync", vwr[:, i % 6, :], VW[i], reads=[bVW[i]], writes=[vwb[i % 6]])
        for r in range(4):
            for k in range(8):
                S.op("tensor", lambda e: e.matmul(pX[:, r * 128:(r + 1) * 128], lhsT=Wn[:, k, r * 128:(r + 1) * 128],
                                                  rhs=hT[:, k, :], start=(k == 0), stop=(k == 7)), reads=[Wn, hT], writes=[pX], lhs=[Wn])
        S.op("scalar", lambda e: e.copy(out=QT[:, :], in_=pX[:, :]), reads=[pX], writes=[QT])
        for k in range(8):
            S.op("tensor", lambda e: e.matmul(pX[:, :], lhsT=hT[:, k, :], rhs=Wn[:, k, 512:1024],
                                              start=(k == 0), stop=(k == 7)), reads=[Wn, hT], writes=[pX])
        S.op("scalar", lambda e: e.activation(out=gsil[:, :], in_=pX[:, :], func=AF.Silu), reads=[pX], writes=[gsil])
        for k in range(8):
            S.op("tensor", lambda e: e.matmul(pX[:, 0:12], lhsT=hT[:, k, :], rhs=Wn[:, k, 1024:1036],
                                              start=(k == 0), stop=(k == 7)), reads=[Wn, hT], writes=[pX])
        S.op("scalar", lambda e: e.activation(out=bg[:, :], in_=pX[:, 0:12], func=AF.Sigmoid), reads=[pX], writes=[bg])

    def make_units(i):
        par = i % 2
        QT = QTs[par]
        Wp = 8 * (i + 1)
        nch = (Wp + 127) // 128
        wt = [t for t in range(i - 4, i + 1) if t >= 0]
        q = dict(i=i, par=par, nsum=0, total_sum=nch + (i + 1) + len(wt), topk_done=(i < 8))
        OT = ObTs[par]

        def evac(b):
            return lambda: S.op("scalar", lambda e: e.copy(out=OT[b][:, :], in_=pOb[b][:, :]), reads=[pOb[b]], writes=[OT[b]])

        def topk():
            ncol = 2 * i
            for r in range(4):
                pI = pMs[ctr["pm"] % 2]
                ctr["pm"] += 1
                rc_ = rcols[r % 2]
                for c in range(nch):
                    rws = min(128, Wp - c * 128)
                    S.op("tensor", lambda e: e.matmul(pI[:, 0:257], lhsT=EmC[0:rws, c, r * 128:(r + 1) * 128], rhs=Maug[0:rws, c, :],
                                                      start=(c == 0), stop=(c == nch - 1)), reads=[EmC, Maug], writes=[pI])
                S.op("vector", lambda e: e.tensor_scalar(out=rc_[:, :], in0=pI[:, 256:257], scalar1=1e-30, scalar2=None,
                                                         op0=ALU.add), reads=[pI], writes=[rc_])
                S.op("vector", lambda e: e.reciprocal(out=rc_[:, :], in_=rc_[:, :]), reads=[rc_], writes=[rc_])
                if r == 0:
                    S.op("vector", lambda e: e.tensor_scalar(out=imp[:, 0:ncol], in0=pI[:, 0:ncol], scalar1=rc_[:, 0:1],
                                                             scalar2=None, op0=ALU.mult), reads=[pI, rc_], writes=[imp])
                else:
                    S.op("vector", lambda e: e.scalar_tensor_tensor(out=imp[:, 0:ncol], in0=pI[:, 0:ncol], scalar=rc_[:, 0:1],
                                                                    in1=imp[:, 0:ncol], op0=ALU.mult, op1=ALU.add),
                         reads=[pI, rc_, imp], writes=[imp])
            S.op("vector", lambda e: e.tensor_copy(out=score[:, 0:ncol], in_=imp[:, 0:ncol]), reads=[imp], writes=[score])
            S.op("vector", lambda e: e.memset(score[:, 0:1], -1.0), writes=[score])
            S.op("vector", lambda e: e.memset(score[0:64, ncol - 1:ncol], -1.0), writes=[score])
            S.op("vector", lambda e: e.max(out=m8[:, 0:8], in_=score[:, 0:ncol]), reads=[score], writes=[m8])
            S.op("vector", lambda e: e.match_replace(out=sc2[:, 0:ncol], in_to_replace=m8[:, 0:8], in_values=score[:, 0:ncol],
                                                     imm_value=-2.0), reads=[m8, score], writes=[sc2])
            S.op("vector", lambda e: e.max(out=m8[:, 8:16], in_=sc2[:, 0:ncol]), reads=[sc2, m8], writes=[m8])
            S.op("vector", lambda e: e.tensor_scalar(out=selF[:, 0:ncol], in0=score[:, 0:ncol], scalar1=m8[:, 12:13], scalar2=None,
                                                     op0=ALU.is_ge), reads=[score, m8], writes=[selF])
            S.op("vector", lambda e: e.memset(selF[:, 0:1], 1.0), writes=[selF])
            S.op("vector", lambda e: e.memset(selF[0:64, ncol - 1:ncol], 1.0), writes=[selF])
            for c in range((ncol + 127) // 128):
                S.op("tensor", lambda e: e.transpose(out=pX[:, c * 128:(c + 1) * 128], in_=selF[:, c * 128:(c + 1) * 128],
                                                     identity=idf[:, :]), reads=[selF, idf], writes=[pX])
                S.op("vector", lambda e: e.tensor_copy(out=selT[:, c, :], in_=pX[:, c * 128:(c + 1) * 128]), reads=[pX], writes=[selT])
            q["topk_done"] = True

        units = []
        for c in range(nch):
            rws = min(128, Wp - c * 128)
            if c == nch - 1:
                mk = ("sb", cmpm[0:rws, (i % 16) + (0 if i < 16 else 16), :], [cmpm])
            elif c == 0:
                mk = ("sb", cmpm[0:rws, 32, :], [cmpm])
            else:
                mk = None
            u = dict(q=q, QT=QT, ksrc=KcT[:, c * 128:c * 128 + rws], ktrack=KcT, vsrc=Vc[0:rws, c, :], vtrack=Vc, rows=rws, mask=mk,
                     b=0, first=(c == 0), last=(c == nch - 1), emc=(c if i >= 8 else None), post=[])
            if c == nch - 1:
                u["post"].append(evac(0))
                if i >= 8:
                    u["post"].append(topk)
            units.append(u)
        for n, t in enumerate(wt):
            if t == i:
                mk = ("sb", causT[:, :], [causT])
            elif t == i - 4:
                mk = ("sb", upT[:, :], [upT])
            else:
                mk = None
            u = dict(q=q, QT=QT, ksrc=kwr[:, t % 6, :], ktrack=kwb[t % 6], vsrc=vwr[:, t % 6, :], vtrack=vwb[t % 6], rows=128, mask=mk,
                     b=2, first=(n == 0), last=(n == len(wt) - 1), post=[])
            if n == len(wt) - 1:
                u["post"].append(evac(2))
            units.append(u)
        for t in range(i + 1):
            if t == i:
                mk = ("sb", causT[:, :], [causT])
            elif i >= 8:
                mk = ("sel", t)
            else:
                mk = None
            u = dict(q=q, QT=QT, ksrc=KsT[:, t * 128:(t + 1) * 128], ktrack=KsT, vsrc=Vs[:, t, :], vtrack=Vs, rows=128, mask=mk,
                     b=1, first=(t == 0), last=(t == i), post=[], needs_sel=(mk is not None and mk[0] == "sel"))
            if t == i:
                u["post"].append(evac(1))
            units.append(u)
        return units, q

    def epilogue_parts(i):
        par = i % 2
        rows = slice(i * 128, (i + 1) * 128)
        h1, gsil, bg = h1s[par], gsils[par], bgs[par]
        OT = ObTs[par]
        pSum = pSums[par]
        sumsb = sumsbs[par]
        yt = yts[par]

        def combine(b):
            for r in range(4):
                S.op("tensor", lambda e: e.transpose(out=pXb[:, r * 128:(r + 1) * 128], in_=OT[b][:, r * 128:(r + 1) * 128],
                                                     identity=ident[:, :]), reads=[OT[b], ident], writes=[pX])
            for r in range(4):
                cs_ = slice(r * 128, (r + 1) * 128)
                if b == 0:
                    S.op("vector", lambda e: e.tensor_scalar(out=o_[:, cs_], in0=pXb[:, cs_], scalar1=coef[:, r * 3 + b:r * 3 + b + 1],
                                                             scalar2=None, op0=ALU.mult), reads=[pX, coef], writes=[o_])
                else:
                    S.op("vector", lambda e: e.scalar_tensor_tensor(out=o_[:, cs_], in0=pXb[:, cs_], scalar=coef[:, r * 3 + b:r * 3 + b + 1],
                                                                    in1=o_[:, cs_], op0=ALU.mult, op1=ALU.add),
                         reads=[pX, coef, o_], writes=[o_])

        def part1():
            S.op("scalar", lambda e: e.copy(out=sumsb[:, :], in_=pSum[:, :]), reads=[pSum], writes=[sumsb])
            for r in range(4):
                S.op("tensor", lambda e: e.matmul(pX[:, r * 3:(r + 1) * 3], lhsT=sumsb[0:3, r * 128:(r + 1) * 128], rhs=idf[0:3, 0:3],
                                                  start=True, stop=True), reads=[sumsb, idf], writes=[pX])
            S.op("vector", lambda e: e.tensor_scalar(out=coef[:, :], in0=pX[:, 0:12], scalar1=1e-30, scalar2=None, op0=ALU.add),
                 reads=[pX], writes=[coef])
            S.op("vector", lambda e: e.reciprocal(out=coef[:, :], in_=coef[:, :]), reads=[coef], writes=[coef])
            S.op("vector", lambda e: e.tensor_tensor(out=coef[:, :], in0=coef[:, :], in1=bg[:, :], op=ALU.mult), reads=[coef, bg], writes=[coef])
            combine(0)

        def part2():
            combine(2)

        def part2b():
            combine(1)
            S.op("gpsimd", lambda e: e.tensor_mul(out=ogf[:, :], in0=o_[:, :], in1=gsil[:, :]), reads=[o_, gsil], writes=[ogf])

        def part3():
            for c in range(4):
                S.op("tensor", lambda e: e.transpose(out=pXb[:, c * 128:(c + 1) * 128], in_=ogf[:, c * 128:(c + 1) * 128],
                                                     identity=ident[:, :]), reads=[ogf, ident], writes=[pX])
            S.op("vector", lambda e: e.tensor_copy(out=ogT2[:, :, :].rearrange("p a b -> p (a b)"), in_=pXb[:, 0:512]), reads=[pX], writes=[ogT2])

        def part4(hf):
            if True:
                for c in range(4):
                    S.op("tensor", lambda e: e.matmul(pX[:, :], lhsT=ogT2[:, c, :], rhs=Wo2[:, c, hf * 512:(hf + 1) * 512],
                                                      start=(c == 0), stop=(c == 3)), reads=[ogT2, Wo2], writes=[pX])
                S.op("vector", lambda e: e.scalar_tensor_tensor(out=yt[:, hf * 512:(hf + 1) * 512], in0=h1[:, hf * 512:(hf + 1) * 512],
                                                                scalar=0.25, in1=pX[:, :], op0=ALU.mult, op1=ALU.add),
                     reads=[h1, pX], writes=[yt])
            if hf == 1:
                S.dma("gpsimd", Y2[rows, :], yt[:, :], reads=[yt], writes=[bY2[i]])
                if i % 8 == 7:
                    rs_chunk(i // 8)

        return [part1, part2, part2b, part3, lambda: part4(0), lambda: part4(1)]

    LOOK = 3
    prologue(0)
    pending = []
    for i in range(NT):
        units, q = make_units(i)
        nu = len(units)
        hooks = {}
        pos = [1, 4, 7, 10, 13, 16]
        for k, f in enumerate(pending):
            hooks.setdefault(min(nu - 1, pos[k]), []).append(f)
        if i + 1 < NT:
            hooks.setdefault(min(nu - 1, 19), []).append(lambda i=i: prologue(i + 1))
        for n in range(nu + LOOK):
            if n < nu:
                stageA(units[n])
            if 0 <= n - 1 < nu:
                if units[n - 1].get("needs_sel"):
                    assert q["topk_done"]
                stageA2(units[n - 1])
            if n - LOOK >= 0:
                stageB(units[n - LOOK])
            for f in hooks.get(n, ()):
                f()
        assert q["nsum"] == q["total_sum"]
        pending = epilogue_parts(i)
    for f in pending:
        f()
    S.pop()
    S.pop()
    if upto == 3:
        return dbg_out(Y2, bY2)

    S.push()
    ple = PleCtx(S)
    ple.load_weights("sync", pg[1][:, :], wg[1], we[1], stage)
    fn = S.sb("fn", [128, 1024], F32)
    S.dma("sync", fn[:, :], fng[:, :], writes=[fn])
    hs = [S.sb("h", [128, 1024], F32) for i in range(2)]
    ob = [S.sb("ob", [128, 1024], F32) for i in range(2)]
    sqs = S.sb("sqs", [128, 1024], F32)
    ss = S.sb("ss", [128, 1], F32)
    rs = S.sb("rs", [128, 1], F32)
    pA = S.ps("pA", [128, 8, 128], BF16)
    pG = S.ps("pG", [128, 512], F32)
    pE = S.ps("pE", [128, 512], F32)
    pG2 = S.ps("pG2", [128, 512], F32)
    pE2 = S.ps("pE2", [128, 512], F32)
    NU = TL // 128

    def UA(u):
        rows = slice(u * 128, (u + 1) * 128)
        h = hs[u % 2]
        S.dma("sync", h[:, :], Y2s[rows, :], reads=[bY2s[u // 2]], writes=[h])
        ple.front(u % 2, h, p1s[rows, :], ident, pA)

    def UB(u):
        rows = slice(u * 128, (u + 1) * 128)
        h = hs[u % 2]
        ple.back(u % 2, h, [pG, pG2], [pE, pE2])
        rms_rstd(S, h, 1024, sqs, ss, rs)
        o = ob[u % 2]
        S.op("vector", lambda e: e.scalar_tensor_tensor(out=o[:, :], in0=h[:, :], scalar=rs[:, 0:1], in1=fn[:, :],
                                                        op0=ALU.mult, op1=ALU.mult), reads=[h, rs, fn], writes=[o])
        S.dma("gpsimd", out[rows, :], o[:, :], reads=[o])

    for step in range(NU + 1):
        if step < NU:
            UA(step)
        if step >= 1:
            UB(step - 1)
    S.finish()
    return nc, S.ninstr


def _colgain(g):
    return np.ascontiguousarray(np.asarray(g, np.float32).reshape(8, 128).T)


def _consts(T):
    half = 128
    inv = (10000.0 ** (-np.arange(half, dtype=np.float32) / np.float32(half))).astype(np.float32)
    pos = np.arange(T, dtype=np.float32)
    ang = (inv[:, None] * pos[None, :]).astype(np.float32)
    c = dict(cosT=np.cos(ang).astype(np.float32), sinT=np.sin(ang).astype(np.float32))
    p = np.arange(128)
    c["causT"] = (p[:, None] <= p[None, :]).astype(np.float32)
    c["upT"] = (p[:, None] > p[None, :]).astype(np.float32)
    c["ident"] = np.eye(128, dtype=np.float32)
    ex = np.zeros((128, 64, 128), np.float32)
    for tt in range(64):
        for hb in range(2):
            ex[(2 * tt + hb) % 128, tt, hb * 64:(hb + 1) * 64] = 1.0
    c["ex"] = ex.reshape(128, 64 * 128)
    m = np.zeros((1024, 257), np.float32)
    for j in range(256):
        for (off, w) in ((0, 1.0), (1, 2.0), (2, 2.0), (3, 2.0), (4, 1.0)):
            n = 4 * j + off
            if n < 1024:
                m[n, j] = w
    m[:, 256] = 1.0
    c["maug"] = np.ascontiguousarray(m.reshape(8, 128, 257).transpose(1, 0, 2)).reshape(128, 8 * 257)
    lane = np.arange(128)[:, None]
    ql = np.arange(128)[None, :]
    cm = np.zeros((128, 33, 128), np.float32)
    for res in range(16):
        v = (ql >= 16 * lane + 15 - 128 * res).astype(np.float32)
        b = v.copy()
        a = v.copy()
        a[0, :] = 0.0
        cm[:, res, :] = a
        cm[:, 16 + res, :] = b
    fm = np.ones((128, 128), np.float32)
    fm[0, :] = 0.0
    cm[:, 32, :] = fm
    c["cmpm"] = cm.reshape(128, 33 * 128)
    os_ = np.zeros((128, 3, 3), np.float32)
    for b in range(3):
        os_[:, b, b] = 1.0
    c["onesel"] = os_.reshape(128, 9)
    return c


def _head_consts(hd):
    lg = np.log1p(-(np.float32(2.0) ** np.float32(-5.0 - hd))).astype(np.float32)
    p = np.arange(128, dtype=np.float32)
    qd = np.exp((p + 1.0) * lg).astype(np.float32)
    kdv = (np.exp(-(p + 1.0) * lg) / 16.0).astype(np.float32)
    return dict(qdec=np.ascontiguousarray(np.broadcast_to(np.tile(qd, 4)[None, :], (128, 512))).astype(np.float32),
                kdec=np.ascontiguousarray(np.broadcast_to(np.tile(kdv, 4)[None, :], (128, 512))).astype(np.float32),
                cdec=np.full((128, 1), np.exp(np.float32(128.0) * lg), np.float32))


def _inmaps(T, B, I):
    C = _consts(T)
    TL = T // 4
    maps = []
    ca = np.ascontiguousarray
    for b in range(B):
        for g in range(4):
            m = dict(C)
            m.update(_head_consts(g))
            m["x"] = ca(I["x"][b, :T])
            m["p0"] = ca(I["p"][0, b, :T])
            p1 = I["p"][1, b, :T].reshape(T // 1024, 4, 256, 256)[:, g].reshape(TL, 256)
            m["p1s"] = ca(p1)
            wi = I["ret_w_in"][0]
            m["r_w_in"] = ca(np.concatenate([wi[:, g * 256:(g + 1) * 256], wi[:, 1024 + g * 256:1024 + (g + 1) * 256],
                                             wi[:, 2048 + g * 512:2048 + (g + 1) * 512], wi[:, 4096 + g * 512:4096 + (g + 1) * 512]], axis=1))
            m["r_g_in"] = _colgain(I["ret_norm"][0])
            m["r_gn"] = ca(np.broadcast_to(I["ret_gn"][0][g * 512:(g + 1) * 512][None, :], (128, 512)))
            m["r_w_out"] = ca(I["ret_w_out"][0][g * 512:(g + 1) * 512, :])
            for l in range(2):
                m["pg%d" % l] = _colgain(I["ple_norm"][l])
                m["wg%d" % l] = ca(I["ple_w_gate"][l])
                m["we%d" % l] = ca(I["ple_w_emb"][l])
            m["fng"] = ca(np.broadcast_to(I["final_norm"][None, :], (128, 1024)))
            m["kv_g"] = _colgain(I["kv_norm"])
            kw = I["kv_w"]
            order = [0, 1, 2, 4, 3, 5]
            m["kv_w"] = ca(np.concatenate([kw[:, pt * 512 + g * 128: pt * 512 + (g + 1) * 128] for pt in order], axis=1))
            m["peT_k"] = ca(I["cmp_pe_k"].T)
            m["peT_v"] = ca(I["cmp_pe_v"].T)
            m["w1_k"] = ca(I["cmp_w1_k"])
            m["w1_v"] = ca(I["cmp_w1_v"])
            m["w2_k"] = ca(I["cmp_w2_k"])
            m["w2_v"] = ca(I["cmp_w2_v"])
            m["n_g"] = _colgain(I["nsa_norm"][0])
            nw = I["nsa_w_in"][0]
            m["n_w_in"] = ca(np.concatenate([nw[:, g * 512:(g + 1) * 512], nw[:, 2048 + g * 512:2048 + (g + 1) * 512],
                                             nw[:, 4096 + g * 12:4096 + (g + 1) * 12]], axis=1))
            m["n_w_out"] = ca(I["nsa_w_out"][0][g * 512:(g + 1) * 512, :])
            maps.append({k: np.asarray(v, np.float32) for k, v in m.items()})
    return maps


_PROG = {}


def run_module(I, T, B):
    key = (T, B)
    if key not in _PROG:
        groups = [[b * 4 + g for g in range(4)] for b in range(B)]
        _PROG[key] = build_program(T, groups)[0]
    nc = _PROG[key]
    maps = _inmaps(T, B, I)
    res = run_bass_kernel_spmd(nc, maps, core_ids=list(range(4 * B)))
    outp = np.empty((B, T, 1024), np.float32)
    for b in range(B):
        for g in range(4):
            o = res.results[b * 4 + g]["out"].reshape(T // 1024, 256, 1024)
            outp[b].reshape(T // 1024, 4, 256, 1024)[:, g] = o
    return outp


def kernel(**inputs):
    I = {k: np.asarray(v) for k, v in inputs.items()}
    return run_module(I, 16384, 2)
```

```python
import contextlib
import numpy as np
import concourse.bass as bass
import concourse.mybir as mybir
from concourse.bass_utils import run_bass_kernel_spmd

F32 = mybir.dt.float32
BF16 = mybir.dt.bfloat16
AF = mybir.ActivationFunctionType
ALU = mybir.AluOpType
AX = mybir.AxisListType
ENGS = ("tensor", "vector", "scalar", "gpsimd", "sync")
EPS = 1e-6
SCALE = 128 ** -0.5


class Buf:
    __slots__ = ("name", "t", "w", "r", "psum")

    def __init__(self, name, t=None, psum=False):
        self.name = name
        self.t = t
        self.w = None
        self.r = []
        self.psum = psum

    def __getitem__(self, idx):
        return self.t[idx]


class _Rec:
    def __init__(self):
        self.call = None

    def __getattr__(self, name):
        def f(*a, **kw):
            assert self.call is None
            self.call = (name, a, kw)
            return self
        return f


class Sched:
    def __init__(self, nc, n_dma_sems=12):
        self.nc = nc
        self.sems = {}
        self.cnt = {}
        for e in ENGS:
            self.sems[e] = nc.alloc_semaphore("s_" + e)
            self.cnt[e] = 0
        self.sems["cc"] = nc.alloc_semaphore("s_cc")
        self.cnt["cc"] = 0
        self.dq = {}
        for q in ("sync", "gpsimd", "scalar"):
            lst = []
            for i in range(n_dma_sems):
                k = "d_%s_%d" % (q, i)
                self.sems[k] = nc.alloc_semaphore(k)
                self.cnt[k] = 0
                lst.append(k)
            self.dq[q] = [lst, 0]
        self.known = {e: {} for e in ENGS}
        self.E = {e: getattr(nc, e) for e in ENGS}
        self.ninstr = 0
        self.uid = 0
        self.stacks = [contextlib.ExitStack()]

    def push(self):
        self.stacks.append(contextlib.ExitStack())

    def pop(self):
        self.barrier()
        self.stacks.pop().close()

    def _nm(self, name):
        self.uid += 1
        return "%s_%d" % (name, self.uid)

    def sb(self, name, shape, dtype):
        nm = self._nm(name)
        return Buf(nm, self.stacks[-1].enter_context(self.nc.sbuf_tensor(nm, list(shape), dtype)))

    def ps(self, name, shape, dtype=F32):
        nm = self._nm(name)
        return Buf(nm, self.stacks[-1].enter_context(self.nc.psum_tensor(nm, list(shape), dtype)), psum=True)

    def dr(self, name, shape, dtype):
        return self.nc.dram_tensor(self._nm(name), list(shape), dtype)

    def _need(self, eng, deps):
        kn = self.known[eng]
        best = {}
        for d in deps:
            if d is None:
                continue
            k, v = d
            if k == eng and eng == "tensor":
                continue
            if kn.get(k, 0) >= v:
                continue
            if best.get(k, 0) < v:
                best[k] = v
        return best

    def _emit_waits(self, eng, best):
        for k, v in best.items():
            self.E[eng].wait_ge(self.sems[k], v)
            self.known[eng][k] = v
            self.ninstr += 1

    @staticmethod
    def _deps(reads, writes):
        deps = []
        for b in reads:
            deps.append(b.w)
            if b.psum:
                deps.extend(b.r)
        for b in writes:
            deps.append(b.w)
            deps.extend(b.r)
        return deps

    @staticmethod
    def _mark(ev, reads, writes):
        for b in reads:
            b.r.append(ev)
        for b in writes:
            b.w = ev
            b.r = []

    def op(self, eng, fn, reads=(), writes=(), lhs=None):
        attach = None
        if eng == "tensor" and lhs is None:
            self._emit_waits(eng, self._need(eng, self._deps(reads, writes)))
        else:
            if lhs:
                self._emit_waits(eng, self._need(eng, self._deps(lhs, ())))
            best = self._need(eng, self._deps(reads, writes))
            if best:
                k = next(iter(best))
                attach = (k, best.pop(k))
            self._emit_waits(eng, best)
        self.cnt[eng] += 1
        rec = _Rec()
        fn(rec)
        name, a, kw = rec.call
        ins = getattr(self.E[eng], name)(*a, **kw)
        if attach is not None:
            ins = ins._wait_ge(self.sems[attach[0]], attach[1])
            self.known[eng][attach[0]] = attach[1]
        ins.then_inc(self.sems[eng], 1)
        self.ninstr += 1
        ev = (eng, self.cnt[eng])
        self._mark(ev, reads, writes)
        return ev

    def dma(self, q, out, in_, reads=(), writes=(), **kw):
        lst, idx = self.dq[q]
        k = lst[idx % len(lst)]
        self.dq[q][1] = idx + 1
        deps = self._deps(reads, writes)
        if self.cnt[k] > 0:
            deps.append((k, self.cnt[k]))
        self._emit_waits(q, self._need(q, deps))
        self.cnt[k] += 16
        self.E[q].dma_start(out=out, in_=in_, **kw).then_inc(self.sems[k], 16)
        self.ninstr += 1
        ev = (k, self.cnt[k])
        self._mark(ev, reads, writes)
        return ev

    def collective(self, kind, op, groups, in_ap, out_ap, reads=(), writes=()):
        deps = self._deps(reads, writes)
        if self.cnt["cc"] > 0:
            deps.append(("cc", self.cnt["cc"]))
        self._emit_waits("gpsimd", self._need("gpsimd", deps))
        self.cnt["cc"] += 1
        self.E["gpsimd"].collective_compute(kind, op, replica_groups=groups, ins=[in_ap], outs=[out_ap]).then_inc(
            self.sems["cc"], 1)
        self.ninstr += 1
        ev = ("cc", self.cnt["cc"])
        self._mark(ev, reads, writes)
        return ev

    def _all_events(self):
        return [(k, v) for k, v in self.cnt.items() if v > 0]

    def barrier(self):
        ev = self._all_events()
        for e in ENGS:
            self._emit_waits(e, self._need(e, [d for d in ev if d[0] != e]))

    def finish(self):
        deps = [(k, v) for k, v in self.cnt.items() if v > 0 and (k.startswith("d_") or k == "cc")]
        self._emit_waits("sync", self._need("sync", deps))
        self.barrier()
        while self.stacks:
            self.stacks.pop().close()


def load_w_bf16(S, q, dst, dst_fn, src_ap, stage, kc, ncols, gain=None):
    for k in range(kc):
        c0 = 0
        while c0 < ncols:
            cw = min(2048, ncols - c0)
            S.dma(q, stage[:, 0:cw], src_ap[:, k, c0:c0 + cw], writes=[stage])
            if gain is None:
                S.op("gpsimd", lambda e: e.tensor_copy(out=dst_fn(k, c0, cw), in_=stage[:, 0:cw]),
                     reads=[stage], writes=[dst])
            else:
                S.op("gpsimd", lambda e: e.tensor_scalar(out=dst_fn(k, c0, cw), in0=stage[:, 0:cw],
                                                         scalar1=gain[:, k:k + 1], scalar2=None, op0=ALU.mult),
                     reads=[stage, gain], writes=[dst])
            c0 += cw


def load_const_bf16(S, q, dst, dst_ap, src_ap, stage, ncols):
    S.dma(q, stage[:, 0:ncols], src_ap, writes=[stage])
    S.op("gpsimd", lambda e: e.tensor_copy(out=dst_ap, in_=stage[:, 0:ncols]), reads=[stage], writes=[dst])


def rms_rstd(S, xt, D, sqs, ss, rs):
    S.op("scalar", lambda e: e.activation(out=sqs[:, :], in_=xt[:, :], func=AF.Square, accum_out=ss[:, :]),
         reads=[xt], writes=[sqs, ss])
    S.op("vector", lambda e: e.tensor_scalar(out=rs[:, :], in0=ss[:, :], scalar1=1.0 / D, scalar2=EPS,
                                             op0=ALU.mult, op1=ALU.add), reads=[ss], writes=[rs])
    S.op("scalar", lambda e: e.activation(out=rs[:, :], in_=rs[:, :], func=AF.Sqrt), reads=[rs], writes=[rs])
    S.op("vector", lambda e: e.reciprocal(out=rs[:, :], in_=rs[:, :]), reads=[rs], writes=[rs])


def rmsnorm_T(S, xt, ident, sqs, ss, rs, xn, pT, dstT, tok0, copy_eng="vector"):
    rms_rstd(S, xt, 1024, sqs, ss, rs)
    S.op("vector", lambda e: e.tensor_scalar(out=xn[:, :], in0=xt[:, :], scalar1=rs[:, 0:1], scalar2=None,
                                             op0=ALU.mult), reads=[xt, rs], writes=[xn])
    for k in range(8):
        S.op("tensor", lambda e: e.transpose(out=pT[:, k, :], in_=xn[:, k * 128:(k + 1) * 128], identity=ident[:, :]),
             reads=[xn, ident], writes=[pT])
    if copy_eng == "vector":
        S.op("vector", lambda e: e.tensor_copy(out=dstT[:, :, tok0:tok0 + 128], in_=pT[:, :, :]), reads=[pT], writes=[dstT])
    else:
        S.op("scalar", lambda e: e.copy(out=dstT[:, :, tok0:tok0 + 128], in_=pT[:, :, :]), reads=[pT], writes=[dstT])


class PleCtx:
    def __init__(self, S):
        self.S = S
        self.Wg = S.sb("pleWg", [128, 8, 1024], BF16)
        self.We = S.sb("pleWe", [128, 2, 1024], BF16)
        self.gain = S.sb("pleGain", [128, 8], F32)
        self.sqs = S.sb("ple_sqs", [128, 1024], BF16)
        self.ss = [S.sb("ple_ss", [128, 1], F32) for i in range(2)]
        self.rs = [S.sb("ple_rs", [128, 1], F32) for i in range(2)]
        _xn = S.sb("ple_xn", [128, 1024], BF16)
        self.xn = [_xn, _xn]
        self.hnT = [S.sb("ple_hnT", [128, 8, 128], BF16) for i in range(2)]
        self.pf = [S.sb("ple_pf", [128, 256], F32) for i in range(2)]
        self.pb = [S.sb("ple_pb", [128, 256], BF16) for i in range(2)]
        self.pT = [S.sb("ple_pT", [128, 2, 128], BF16) for i in range(2)]
        _sig = S.sb("ple_sig", [128, 512], F32)
        _prod = S.sb("ple_prod", [128, 512], F32)
        self.sig = [_sig, _sig]
        self.prod = [_prod, _prod]

    def load_weights(self, q, g_ap, wg_ap, we_ap, stage):
        S = self.S
        S.dma(q, self.gain[:, :], g_ap, writes=[self.gain])
        load_w_bf16(S, q, self.Wg, lambda k, c0, cw: self.Wg[:, k, c0:c0 + cw],
                    wg_ap.rearrange("(k p) n -> p k n", p=128), stage, 8, 1024, gain=self.gain)
        load_w_bf16(S, q, self.We, lambda k, c0, cw: self.We[:, k, c0:c0 + cw],
                    we_ap.rearrange("(k p) n -> p k n", p=128), stage, 2, 1024)

    def front(self, par, h, p_ap, ident, pA):
        S = self.S
        pf, pb, pT = self.pf[par], self.pb[par], self.pT[par]
        S.dma("sync", pf[:, :], p_ap, writes=[pf])
        rmsnorm_T(S, h, ident, self.sqs, self.ss[par], self.rs[par], self.xn[par], pA, self.hnT[par], 0)
        S.op("gpsimd", lambda e: e.tensor_copy(out=pb[:, :], in_=pf[:, :]), reads=[pf], writes=[pb])
        for k in range(2):
            S.op("tensor", lambda e: e.transpose(out=pA[:, k, :], in_=pb[:, k * 128:(k + 1) * 128], identity=ident[:, :]),
                 reads=[pb, ident], writes=[pA])
        S.op("vector", lambda e: e.tensor_copy(out=pT[:, :, :], in_=pA[:, 0:2, :]), reads=[pA], writes=[pT])

    def back(self, par, h, pG, pE):
        S = self.S
        hnT, pT = self.hnT[par], self.pT[par]
        for hf in range(2):
            cs = slice(hf * 512, (hf + 1) * 512)
            sig, prod = self.sig[hf], self.prod[hf]
            for k in range(8):
                S.op("tensor", lambda e: e.matmul(pG[hf][:, :], lhsT=hnT[:, k, :], rhs=self.Wg[:, k, cs],
                                                  start=(k == 0), stop=(k == 7)), reads=[hnT, self.Wg], writes=[pG[hf]])
            for k in range(2):
                S.op("tensor", lambda e: e.matmul(pE[hf][:, :], lhsT=pT[:, k, :], rhs=self.We[:, k, cs],
                                                  start=(k == 0), stop=(k == 1)), reads=[pT, self.We], writes=[pE[hf]])
            S.op("scalar", lambda e: e.activation(out=sig[:, :], in_=pG[hf][:, :], func=AF.Sigmoid), reads=[pG[hf]], writes=[sig])
            S.op("vector", lambda e: e.tensor_tensor(out=prod[:, :], in0=sig[:, :], in1=pE[hf][:, :], op=ALU.mult),
                 reads=[sig, pE[hf]], writes=[prod])
            S.op("vector", lambda e: e.tensor_add(out=h[:, cs], in0=h[:, cs], in1=prod[:, :]), reads=[h, prod], writes=[h])


def build_program(T, groups, upto=4):
    nc = bass.Bass("TRN2", target_bir_lowering=False)
    NT = T // 128
    NS = T // 512
    NCH = T // 1024
    NC16 = T // 2048
    TL = T // 4

    def din(name, shape):
        return nc.dram_tensor(name, list(shape), F32, kind="ExternalInput").ap()

    x = din("x", [T, 1024])
    p0 = din("p0", [T, 256])
    p1s = din("p1s", [TL, 256])
    identd = din("ident", [128, 128])
    r_w_in = din("r_w_in", [1024, 1536])
    r_g_in = din("r_g_in", [128, 8])
    r_gn = din("r_gn", [128, 512])
    r_w_out = din("r_w_out", [512, 1024])
    cosT = din("cosT", [128, T])
    sinT = din("sinT", [128, T])
    qdec = din("qdec", [128, 512])
    kdec = din("kdec", [128, 512])
    cdec = din("cdec", [128, 1])
    causT_d = din("causT", [128, 128])
    upT_d = din("upT", [128, 128])
    pg = [din("pg%d" % l, [128, 8]) for l in range(2)]
    wg = [din("wg%d" % l, [1024, 1024]) for l in range(2)]
    we = [din("we%d" % l, [256, 1024]) for l in range(2)]
    fng = din("fng", [128, 1024])
    kv_g = din("kv_g", [128, 8])
    kv_w = din("kv_w", [1024, 768])
    peT_k = din("peT_k", [128, 32])
    peT_v = din("peT_v", [128, 32])
    w1_k = din("w1_k", [4096, 256])
    w1_v = din("w1_v", [4096, 256])
    w2_k = din("w2_k", [256, 128])
    w2_v = din("w2_v", [256, 128])
    n_g = din("n_g", [128, 8])
    n_w_in = din("n_w_in", [1024, 1036])
    n_w_out = din("n_w_out", [512, 1024])
    ex_d = din("ex", [128, 64 * 128])
    maug_d = din("maug", [128, 8 * 257])
    cmpm_d = din("cmpm", [128, 33 * 128])
    onesel_d = din("onesel", [128, 9])
    out = nc.dram_tensor("out", [TL, 1024], F32, kind="ExternalOutput").ap()
    dbg = nc.dram_tensor("dbg", [T, 1024], F32, kind="ExternalOutput").ap() if upto < 4 else None

    def dbg_out(src, bufs):
        for c in range(T // 1024):
            S.dma("sync", dbg[c * 1024:(c + 1) * 1024, :], src[c * 1024:(c + 1) * 1024, :], reads=bufs[c * 8:(c + 1) * 8])
        S.finish()
        return nc, S.ninstr

    S = Sched(nc)
    Y1 = S.dr("Y1", [T, 1024], F32)
    Y1s = S.dr("Y1s", [T, 1024], F32)
    Y2 = S.dr("Y2", [T, 1024], F32)
    Y2s = S.dr("Y2s", [TL, 1024], F32)
    H1 = S.dr("H1", [T, 1024], F32)
    HT = S.dr("HT", [NT, 128, 1024], BF16)
    KW = S.dr("KW", [NT, 128, 128], BF16)
    VW = S.dr("VW", [NT, 128, 128], BF16)
    bY1 = [Buf("bY1_%d" % i) for i in range(NT)]
    bY1s = [Buf("bY1s_%d" % i) for i in range(NT)]
    bY2 = [Buf("bY2_%d" % i) for i in range(NT)]
    bY2s = [Buf("bY2s_%d" % i) for i in range(NCH)]
    bH1 = [Buf("bH1_%d" % i) for i in range(NT)]
    bHT = [Buf("bHT_%d" % i) for i in range(NT)]
    bKW = [Buf("bKW_%d" % i) for i in range(NT)]
    bVW = [Buf("bVW_%d" % i) for i in range(NT)]

    stage = S.sb("stage", [128, 2048], F32)
    idf = S.sb("idf", [128, 128], F32)
    ident = S.sb("identb", [128, 128], BF16)
    S.dma("sync", idf[:, :], identd[:, :], writes=[idf])
    S.op("vector", lambda e: e.tensor_copy(out=ident[:, :], in_=idf[:, :]), reads=[idf], writes=[ident])

    S.push()
    W = S.sb("W", [128, 8, 1536], BF16)
    Wo = S.sb("Wo", [128, 4, 1024], BF16)
    gin = S.sb("gin", [128, 8], F32)
    gnt = S.sb("gnt", [128, 512], F32)
    qd_t = S.sb("qd_t", [128, 512], F32)
    kd_t = S.sb("kd_t", [128, 512], F32)
    cd_t = S.sb("cd_t", [128, 1], F32)
    caus = S.sb("caus", [128, 128], F32)
    xts = [S.sb("xt", [128, 1024], F32) for i in range(2)]
    sqs = S.sb("sqs", [128, 1024], F32)
    ss = S.sb("ss", [128, 1], F32)
    rs = S.sb("rs", [128, 1], F32)
    xn = S.sb("xn", [128, 1024], BF16)
    xnT2 = [S.sb("xnT", [128, 8, 512], BF16) for i in range(2)]
    cs = [S.sb("cs", [128, 512], F32) for i in range(2)]
    sn = [S.sb("sn", [128, 512], F32) for i in range(2)]
    tabs2 = [[S.sb("tab", [128, 512], F32) for i in range(4)] for p in range(2)]
    sss = [S.sb("ss", [128, 1], F32) for i in range(2)]
    rss = [S.sb("rs", [128, 1], F32) for i in range(2)]
    xns = [S.sb("xn", [128, 1024], BF16) for i in range(2)]
    raw = [S.sb("raw", [128, 512], F32) for i in range(4)]
    tmp = [S.sb("tmp", [128, 512], F32) for i in range(4)]
    qdT2 = [S.sb("qdT", [128, 2, 512], BF16) for i in range(2)]
    kTp2 = [S.sb("kTp", [128, 2, 512], BF16) for i in range(2)]
    vb2 = [S.sb("vb", [128, 4, 512], BF16) for i in range(2)]
    gs2 = [S.sb("gs", [128, 4, 512], F32) for i in range(2)]
    st_f = [S.sb("st_f", [128, 512], F32) for i in range(2)]
    st_b = [S.sb("st_b", [128, 512], BF16) for i in range(2)]
    kd = S.sb("kd", [128, 256], BF16)
    ST = S.sb("ST", [128, 128], BF16)
    osq = S.sb("osq", [128, 512], F32)
    stats = [S.sb("stat", [128, 4], F32) for i in range(2)]
    ons = [S.sb("on", [128, 512], F32) for i in range(2)]
    ogs = [S.sb("og", [128, 512], BF16) for i in range(2)]
    ogT = S.sb("ogT", [128, 4, 128], BF16)
    yo = [S.sb("yo", [128, 1024], F32) for i in range(2)]
    pA = S.ps("pA", [128, 8, 128], BF16)
    pB = [S.ps("pB", [128, 512], F32) for i in range(2)]
    pS = S.ps("pS", [128, 128], F32)
    pO = S.ps("pO", [128, 512], F32)
    pSt = [S.ps("pSt", [128, 512], F32) for i in range(2)]
    pY = S.ps("pY", [128, 512], F32)

    for (dst, src) in ((gin, r_g_in), (gnt, r_gn), (qd_t, qdec), (kd_t, kdec), (cd_t, cdec), (caus, causT_d)):
        S.dma("sync", dst[:, :], src[:, :], writes=[dst])
    load_w_bf16(S, "sync", W, lambda k, c0, cw: W[:, k, c0:c0 + cw],
                r_w_in.rearrange("(k p) n -> p k n", p=128), stage, 8, 1536, gain=gin)
    load_w_bf16(S, "sync", Wo, lambda k, c0, cw: Wo[:, k, c0:c0 + cw],
                r_w_out.rearrange("(k p) n -> p k n", p=128), stage, 4, 1024)
    for i in range(2):
        S.op("gpsimd", lambda e: e.memset(st_f[i][:, :], 0.0), writes=[st_f[i]])
        S.op("gpsimd", lambda e: e.memset(st_b[i][:, :], 0.0), writes=[st_b[i]])

    def allreduce_chunk(c):
        r0 = c * 1024
        tl = list(range(c * 8, c * 8 + 8))
        S.collective("AllReduce", ALU.add, groups, Y1[r0:r0 + 1024, :], Y1s[r0:r0 + 1024, :],
                     reads=[bY1[t] for t in tl], writes=[bY1s[t] for t in tl])

    def tabs_for(s):
        t0 = s * 512
        cst, snt = cs[s % 2], sn[s % 2]
        tb = tabs2[s % 2]
        S.dma("sync", cst[:, :], cosT[:, t0:t0 + 512], writes=[cst])
        S.dma("sync", snt[:, :], sinT[:, t0:t0 + 512], writes=[snt])
        S.op("gpsimd", lambda e: e.tensor_mul(out=tb[0][:, :], in0=cst[:, :], in1=qd_t[:, :]), reads=[cst, qd_t], writes=[tb[0]])
        S.op("gpsimd", lambda e: e.tensor_mul(out=tb[1][:, :], in0=snt[:, :], in1=qd_t[:, :]), reads=[snt, qd_t], writes=[tb[1]])
        S.op("gpsimd", lambda e: e.tensor_mul(out=tb[2][:, :], in0=cst[:, :], in1=kd_t[:, :]), reads=[cst, kd_t], writes=[tb[2]])
        S.op("gpsimd", lambda e: e.tensor_mul(out=tb[3][:, :], in0=snt[:, :], in1=kd_t[:, :]), reads=[snt, kd_t], writes=[tb[3]])

    def F(s, j):
        ti = s * 4 + j
        xt = xts[ti % 2]
        S.dma("sync", xt[:, :], x[ti * 128:(ti + 1) * 128, :], writes=[xt])
        rmsnorm_T(S, xt, ident, sqs, sss[ti % 2], rss[ti % 2], xns[ti % 2], pA, xnT2[s % 2], j * 128)

    def P1(s):
        xnT = xnT2[s % 2]
        tb = tabs2[s % 2]
        qdT, kTp, vb, gs = qdT2[s % 2], kTp2[s % 2], vb2[s % 2], gs2[s % 2]
        for dc in range(4):
            pb = pB[dc % 2]
            for k in range(8):
                S.op("tensor", lambda e: e.matmul(pb[:, :], lhsT=W[:, k, dc * 128:(dc + 1) * 128], rhs=xnT[:, k, :],
                                                  start=(k == 0), stop=(k == 7)), reads=[W, xnT], writes=[pb], lhs=[W])
            S.op("scalar", lambda e: e.copy(out=raw[dc][:, :], in_=pb[:, :]), reads=[pb], writes=[raw[dc]])
        for (eng, x1, x2, ct, st_, dst, ta, tb_) in (("gpsimd", raw[0], raw[1], tb[0], tb[1], qdT, tmp[0], tmp[1]),
                                                     ("vector", raw[2], raw[3], tb[2], tb[3], kTp, tmp[2], tmp[3])):
            S.op(eng, lambda e: e.tensor_mul(out=ta[:, :], in0=x1[:, :], in1=ct[:, :]), reads=[x1, ct], writes=[ta])
            S.op(eng, lambda e: e.tensor_mul(out=tb_[:, :], in0=x2[:, :], in1=st_[:, :]), reads=[x2, st_], writes=[tb_])
            S.op(eng, lambda e: e.tensor_sub(out=dst[:, 0, :], in0=ta[:, :], in1=tb_[:, :]), reads=[ta, tb_], writes=[dst])
            S.op(eng, lambda e: e.tensor_mul(out=ta[:, :], in0=x1[:, :], in1=st_[:, :]), reads=[x1, st_], writes=[ta])
            S.op(eng, lambda e: e.tensor_mul(out=tb_[:, :], in0=x2[:, :], in1=ct[:, :]), reads=[x2, ct], writes=[tb_])
            S.op(eng, lambda e: e.tensor_add(out=dst[:, 1, :], in0=ta[:, :], in1=tb_[:, :]), reads=[ta, tb_], writes=[dst])
        for j in range(4):
            for (which, c0) in (("v", 512), ("g", 1024)):
                pb = pB[0] if which == "v" else pB[1]
                for k in range(8):
                    S.op("tensor", lambda e: e.matmul(pb[:, :], lhsT=xnT[:, k, j * 128:(j + 1) * 128], rhs=W[:, k, c0:c0 + 512],
                                                      start=(k == 0), stop=(k == 7)), reads=[W, xnT], writes=[pb])
                if which == "v":
                    S.op("scalar", lambda e: e.copy(out=vb[:, j, :], in_=pb[:, :]), reads=[pb], writes=[vb])
                else:
                    S.op("scalar", lambda e: e.activation(out=gs[:, j, :], in_=pb[:, :], func=AF.Silu), reads=[pb], writes=[gs])

    def CA(s, j):
        ti = s * 4 + j
        qdT, kTp, vb = qdT2[s % 2], kTp2[s % 2], vb2[s % 2]
        on, stat = ons[ti % 2], stats[ti % 2]
        tk = slice(j * 128, (j + 1) * 128)
        for dc in range(2):
            S.op("tensor", lambda e: e.transpose(out=pA[:, dc, :], in_=kTp[:, dc, tk], identity=ident[:, :]),
                 reads=[kTp, ident], writes=[pA])
        S.op("vector", lambda e: e.tensor_scalar(out=kd[:, :], in0=pA[:, 0:2, :].rearrange("p a b -> p (a b)"),
                                                 scalar1=cd_t[:, 0:1], scalar2=None, op0=ALU.mult),
             reads=[pA, cd_t], writes=[kd])
        for dc in range(2):
            S.op("tensor", lambda e: e.matmul(pS[:, :], lhsT=kTp[:, dc, tk], rhs=qdT[:, dc, tk],
                                              start=(dc == 0), stop=(dc == 1)), reads=[kTp, qdT], writes=[pS])
        S.op("vector", lambda e: e.tensor_tensor(out=ST[:, :], in0=pS[:, :], in1=caus[:, :], op=ALU.mult),
             reads=[pS, caus], writes=[ST])
        S.op("tensor", lambda e: e.matmul(pO[:, :], lhsT=ST[:, :], rhs=vb[:, j, :], start=True, stop=False),
             reads=[ST, vb], writes=[pO])
        for dc in range(2):
            S.op("tensor", lambda e: e.matmul(pO[:, :], lhsT=qdT[:, dc, tk], rhs=st_b[dc][:, :],
                                              start=False, stop=(dc == 1)), reads=[qdT, st_b[dc]], writes=[pO])
        for dc in range(2):
            S.op("tensor", lambda e: e.matmul(pSt[dc][:, :], lhsT=kd[:, dc * 128:(dc + 1) * 128], rhs=vb[:, j, :],
                                              start=True, stop=True), reads=[kd, vb], writes=[pSt[dc]])
            S.op("vector", lambda e: e.scalar_tensor_tensor(out=st_f[dc][:, :], in0=st_f[dc][:, :], scalar=cd_t[:, 0:1],
                                                            in1=pSt[dc][:, :], op0=ALU.mult, op1=ALU.add),
                 reads=[st_f[dc], cd_t, pSt[dc]], writes=[st_f[dc]])
            S.op("scalar", lambda e: e.copy(out=st_b[dc][:, :], in_=st_f[dc][:, :]), reads=[st_f[dc]], writes=[st_b[dc]])
        S.op("scalar", lambda e: e.activation(out=on[:, :], in_=pO[:, :], func=AF.Identity, accum_out=stat[:, 0:1]),
             reads=[pO], writes=[on, stat])
        S.op("scalar", lambda e: e.activation(out=osq[:, :], in_=pO[:, :], func=AF.Square, accum_out=stat[:, 1:2]),
             reads=[pO], writes=[osq, stat])

    def CB(s, j):
        ti = s * 4 + j
        gs = gs2[s % 2]
        on, stat, og = ons[ti % 2], stats[ti % 2], ogs[ti % 2]
        S.op("vector", lambda e: e.tensor_scalar(out=stat[:, 0:2], in0=stat[:, 0:2], scalar1=1.0 / 512, scalar2=None,
                                                 op0=ALU.mult), reads=[stat], writes=[stat])
        S.op("vector", lambda e: e.tensor_tensor(out=stat[:, 2:3], in0=stat[:, 0:1], in1=stat[:, 0:1], op=ALU.mult),
             reads=[stat], writes=[stat])
        S.op("vector", lambda e: e.tensor_tensor(out=stat[:, 2:3], in0=stat[:, 1:2], in1=stat[:, 2:3], op=ALU.subtract),
             reads=[stat], writes=[stat])
        S.op("vector", lambda e: e.tensor_scalar(out=stat[:, 2:3], in0=stat[:, 2:3], scalar1=EPS, scalar2=None,
                                                 op0=ALU.add), reads=[stat], writes=[stat])
        S.op("scalar", lambda e: e.activation(out=stat[:, 2:3], in_=stat[:, 2:3], func=AF.Sqrt), reads=[stat], writes=[stat])
        S.op("vector", lambda e: e.reciprocal(out=stat[:, 3:4], in_=stat[:, 2:3]), reads=[stat], writes=[stat])
        S.op("vector", lambda e: e.tensor_scalar(out=on[:, :], in0=on[:, :], scalar1=stat[:, 0:1], scalar2=stat[:, 3:4],
                                                 op0=ALU.subtract, op1=ALU.mult), reads=[on, stat], writes=[on])
        S.op("vector", lambda e: e.tensor_mul(out=on[:, :], in0=on[:, :], in1=gnt[:, :]), reads=[on, gnt], writes=[on])
        S.op("vector", lambda e: e.tensor_mul(out=og[:, :], in0=on[:, :], in1=gs[:, j, :]), reads=[on, gs], writes=[og])
        for c in range(4):
            S.op("tensor", lambda e: e.transpose(out=pA[:, 2 + c, :], in_=og[:, c * 128:(c + 1) * 128], identity=ident[:, :]),
                 reads=[og, ident], writes=[pA])
        S.op("vector", lambda e: e.tensor_copy(out=ogT[:, :, :], in_=pA[:, 2:6, :]), reads=[pA], writes=[ogT])
        yt = yo[ti % 2]
        for hf in range(2):
            for c in range(4):
                S.op("tensor", lambda e: e.matmul(pY[:, :], lhsT=ogT[:, c, :], rhs=Wo[:, c, hf * 512:(hf + 1) * 512],
                                                  start=(c == 0), stop=(c == 3)), reads=[ogT, Wo], writes=[pY])
            S.op("scalar", lambda e: e.copy(out=yt[:, hf * 512:(hf + 1) * 512], in_=pY[:, :]), reads=[pY], writes=[yt])
        S.dma("scalar", Y1[ti * 128:(ti + 1) * 128, :], yt[:, :], reads=[yt], writes=[bY1[ti]])
        if ti >= 9 and (ti - 9) % 8 == 0:
            allreduce_chunk((ti - 9) // 8)

    prev = None
    for s in range(NS + 1):
        if s < NS:
            tabs_for(s)
        for j in range(4):
            if s < NS:
                F(s, j)
            if s >= 1:
                CA(s - 1, j)
                if prev is not None:
                    CB(*prev)
                prev = (s - 1, j)
        if s < NS:
            P1(s)
    CB(*prev)
    allreduce_chunk(NCH - 1)
    if NCH >= 2 and (NT - 1) < 9 + 8 * (NCH - 2):
        pass
    issued = set([(ti - 9) // 8 for ti in range(NT) if ti >= 9 and (ti - 9) % 8 == 0] + [NCH - 1])
    for c in range(NCH):
        if c not in issued:
            allreduce_chunk(c)
    S.pop()
    if upto == 1:
        return dbg_out(Y1s, bY1s)

    S.push()
    KsT = S.sb("KsT", [128, T], BF16)
    Vs = S.sb("Vs", [128, NT, 128], BF16)
    KcT = S.sb("KcT", [128, NC16 * 128], BF16)
    Vc = S.sb("Vc", [128, NC16, 128], BF16)

    S.push()
    ple = PleCtx(S)
    ple.load_weights("sync", pg[0][:, :], wg[0], we[0], stage)
    kvg = S.sb("kvg", [128, 8], F32)
    Wkv = S.sb("Wkv", [128, 8, 768], BF16)
    S.dma("sync", kvg[:, :], kv_g[:, :], writes=[kvg])
    load_w_bf16(S, "sync", Wkv, lambda k, c0, cw: Wkv[:, k, c0:c0 + cw],
                kv_w.rearrange("(k p) n -> p k n", p=128), stage, 8, 768, gain=kvg)
    w1 = [S.sb("w1", [128, 32, 256], BF16) for i in range(2)]
    w2 = [S.sb("w2", [128, 2, 128], BF16) for i in range(2)]
    peT = [S.sb("peT", [128, 32], BF16) for i in range(2)]
    for i, (w1d, w2d, ped) in enumerate(((w1_k, w2_k, peT_k), (w1_v, w2_v, peT_v))):
        load_w_bf16(S, "sync", w1[i], lambda k, c0, cw: w1[i][:, k, c0:c0 + cw],
                    w1d.rearrange("(l d) h -> d l h", d=128), stage, 32, 256)
        load_w_bf16(S, "sync", w2[i], lambda k, c0, cw: w2[i][:, k, c0:c0 + cw],
                    w2d.rearrange("(k p) n -> p k n", p=128), stage, 2, 128)
        load_const_bf16(S, "sync", peT[i], peT[i][:, :], ped[:, :], stage, 32)
    cb = [S.sb("cb", [128, 2064], BF16) for i in range(2)]
    hs = [S.sb("h", [128, 1024], F32) for i in range(3)]
    _yb = S.sb("yb", [128, 1024], F32)
    ybs = [_yb, _yb]
    hTs = [S.sb("hT", [128, 8, 128], BF16) for i in range(2)]
    kwt = [S.sb("kwt", [128, 128], BF16) for i in range(2)]
    vwt = [S.sb("vwt", [128, 128], BF16) for i in range(2)]
    sqs = ple.sqs
    ss2 = [S.sb("ss", [128, 1], F32) for i in range(2)]
    rs2 = [S.sb("rs", [128, 1], F32) for i in range(2)]
    _xn2 = S.sb("xn", [128, 1024], BF16)
    xn2 = [_xn2, _xn2]
    ones1 = S.sb("ones1", [1, 128], BF16)
    bias_f = S.sb("bias_f", [1, 512], F32)
    bias_hi = S.sb("bias_hi", [1, 512], BF16)
    bias_hif = S.sb("bias_hif", [1, 512], F32)
    bias_lo = S.sb("bias_lo", [1, 512], BF16)
    xs_ = S.sb("xs_", [128, 256], F32)
    x2_ = S.sb("x2_", [128, 256], F32)
    sg_ = S.sb("sg_", [128, 256], F32)
    hid = S.sb("hid", [128, 256], BF16)
    hidT = S.sb("hidT", [128, 2, 128], BF16)
    pA = S.ps("pA", [128, 8, 128], BF16)
    pA2 = S.ps("pA2", [128, 8, 128], BF16)
    pG1 = S.ps("pG", [128, 512], F32)
    pE1 = S.ps("pE", [128, 512], F32)
    pGs = [pG1, pG1]
    pEs = [pE1, pE1]
    pKT = S.ps("pKT", [128, 4, 128], F32)
    pKV = S.ps("pKV", [128, 256], F32)
    pH = S.ps("pH", [128, 256], F32)
    pC = S.ps("pC", [128, 128], F32)

    S.op("gpsimd", lambda e: e.memset(ones1[:, :], 1.0), writes=[ones1])
    for i in range(2):
        S.op("gpsimd", lambda e: e.memset(cb[i][:, 0:16], 0.0), writes=[cb[i]])
    for i in range(2):
        for l in range(32):
            S.op("tensor", lambda e: e.matmul(pH[0:1, :], lhsT=peT[i][:, l:l + 1], rhs=w1[i][:, l, :],
                                              start=(l == 0), stop=(l == 31)), reads=[peT[i], w1[i]], writes=[pH])
        S.op("scalar", lambda e: e.copy(out=bias_f[:, i * 256:(i + 1) * 256], in_=pH[0:1, :]), reads=[pH], writes=[bias_f])
    S.op("vector", lambda e: e.tensor_copy(out=bias_hi[:, :], in_=bias_f[:, :]), reads=[bias_f], writes=[bias_hi])
    S.op("vector", lambda e: e.tensor_copy(out=bias_hif[:, :], in_=bias_hi[:, :]), reads=[bias_hi], writes=[bias_hif])
    S.op("vector", lambda e: e.tensor_sub(out=bias_hif[:, :], in0=bias_f[:, :], in1=bias_hif[:, :]), reads=[bias_f, bias_hif], writes=[bias_hif])
    S.op("vector", lambda e: e.tensor_copy(out=bias_lo[:, :], in_=bias_hif[:, :]), reads=[bias_hif], writes=[bias_lo])

    def TA(i):
        rows = slice(i * 128, (i + 1) * 128)
        h = hs[i % 3]
        yb = ybs[i % 2]
        S.dma("sync", h[:, :], x[rows, :], writes=[h])
        S.dma("sync", yb[:, :], Y1s[rows, :], reads=[bY1s[i]], writes=[yb])
        S.op("vector", lambda e: e.tensor_add(out=h[:, :], in0=h[:, :], in1=yb[:, :]), reads=[h, yb], writes=[h])
        ple.front(i % 2, h, p0[rows, :], ident, pA)

    def TB(i):
        rows = slice(i * 128, (i + 1) * 128)
        h = hs[i % 3]
        ple.back(i % 2, h, pGs, pEs)
        S.dma("gpsimd", H1[rows, :], h[:, :], reads=[h], writes=[bH1[i]])

    def TC(i):
        rows = slice(i * 128, (i + 1) * 128)
        h = hs[i % 3]
        hT = hTs[i % 2]
        rmsnorm_T(S, h, ident, sqs, ss2[i % 2], rs2[i % 2], xn2[i % 2], pA2, hT, 0, copy_eng="scalar")
        S.dma("scalar", HT[i].rearrange("p (k t) -> p k t", k=8), hT[:, :, :], reads=[hT], writes=[bHT[i]])
        for a_ in range(4):
            for k in range(8):
                S.op("tensor", lambda e: e.matmul(pKT[:, a_, :], lhsT=Wkv[:, k, a_ * 128:(a_ + 1) * 128], rhs=hT[:, k, :],
                                                  start=(k == 0), stop=(k == 7)), reads=[Wkv, hT], writes=[pKT], lhs=[Wkv])
        for k in range(8):
            S.op("tensor", lambda e: e.matmul(pKV[:, :], lhsT=hT[:, k, :], rhs=Wkv[:, k, 512:768],
                                              start=(k == 0), stop=(k == 7)), reads=[Wkv, hT], writes=[pKV])
        cc0 = 16 + (i % 16) * 128
        S.op("scalar", lambda e: e.copy(out=cb[0][:, cc0:cc0 + 128], in_=pKT[:, 0, :]), reads=[pKT], writes=[cb[0]])
        S.op("scalar", lambda e: e.copy(out=cb[1][:, cc0:cc0 + 128], in_=pKT[:, 1, :]), reads=[pKT], writes=[cb[1]])
        S.op("scalar", lambda e: e.copy(out=KsT[:, rows], in_=pKT[:, 2, :]), reads=[pKT], writes=[KsT])
        kw_, vw_ = kwt[i % 2], vwt[i % 2]
        S.op("scalar", lambda e: e.copy(out=kw_[:, :], in_=pKT[:, 3, :]), reads=[pKT], writes=[kw_])
        S.op("scalar", lambda e: e.copy(out=Vs[:, i, :], in_=pKV[:, 0:128]), reads=[pKV], writes=[Vs])
        S.op("scalar", lambda e: e.copy(out=vw_[:, :], in_=pKV[:, 128:256]), reads=[pKV], writes=[vw_])
        S.dma("scalar", KW[i], kw_[:, :], reads=[kw_], writes=[bKW[i]])
        S.dma("scalar", VW[i], vw_[:, :], reads=[vw_], writes=[bVW[i]])
        if i % 16 == 15:
            compress(i)

    def compress(i):
        if True:
            s16 = i // 16
            for X in range(2):
                bc = slice(X * 256, (X + 1) * 256)
                S.op("tensor", lambda e: e.matmul(pH[:, :], lhsT=ones1[0:1, :], rhs=bias_hi[0:1, bc], start=True, stop=False),
                     reads=[ones1, bias_hi], writes=[pH])
                S.op("tensor", lambda e: e.matmul(pH[:, :], lhsT=ones1[0:1, :], rhs=bias_lo[0:1, bc], start=False, stop=False),
                     reads=[ones1, bias_lo], writes=[pH])
                for l in range(32):
                    S.op("tensor", lambda e: e.matmul(pH[:, :], lhsT=cb[X][:, l:l + 2033:16], rhs=w1[X][:, l, :],
                                                      start=False, stop=(l == 31)), reads=[cb[X], w1[X]], writes=[pH])
                S.op("scalar", lambda e: e.copy(out=xs_[:, :], in_=pH[:, :]), reads=[pH], writes=[xs_])
                S.op("vector", lambda e: e.tensor_tensor(out=x2_[:, :], in0=xs_[:, :], in1=xs_[:, :], op=ALU.mult), reads=[xs_], writes=[x2_])
                S.op("vector", lambda e: e.tensor_scalar(out=x2_[:, :], in0=x2_[:, :], scalar1=0.044715, scalar2=1.0,
                                                         op0=ALU.mult, op1=ALU.add), reads=[x2_], writes=[x2_])
                S.op("vector", lambda e: e.tensor_tensor(out=x2_[:, :], in0=x2_[:, :], in1=xs_[:, :], op=ALU.mult), reads=[x2_, xs_], writes=[x2_])
                S.op("scalar", lambda e: e.activation(out=sg_[:, :], in_=x2_[:, :], func=AF.Sigmoid, scale=1.5957691216057308),
                     reads=[x2_], writes=[sg_])
                S.op("vector", lambda e: e.tensor_tensor(out=hid[:, :], in0=xs_[:, :], in1=sg_[:, :], op=ALU.mult), reads=[xs_, sg_], writes=[hid])
                for hc in range(2):
                    S.op("tensor", lambda e: e.transpose(out=pA2[:, hc, :], in_=hid[:, hc * 128:(hc + 1) * 128], identity=ident[:, :]),
                         reads=[hid, ident], writes=[pA2])
                S.op("vector", lambda e: e.tensor_copy(out=hidT[:, :, :], in_=pA2[:, 0:2, :]), reads=[pA2], writes=[hidT])
                if X == 0:
                    for hc in range(2):
                        S.op("tensor", lambda e: e.matmul(pC[:, :], lhsT=w2[0][:, hc, :], rhs=hidT[:, hc, :],
                                                          start=(hc == 0), stop=(hc == 1)), reads=[w2[0], hidT], writes=[pC])
                    S.op("scalar", lambda e: e.copy(out=KcT[:, s16 * 128:(s16 + 1) * 128], in_=pC[:, :]), reads=[pC], writes=[KcT])
                else:
                    for hc in range(2):
                        S.op("tensor", lambda e: e.matmul(pC[:, :], lhsT=hidT[:, hc, :], rhs=w2[1][:, hc, :],
                                                          start=(hc == 0), stop=(hc == 1)), reads=[w2[1], hidT], writes=[pC])
                    S.op("scalar", lambda e: e.copy(out=Vc[:, s16, :], in_=pC[:, :]), reads=[pC], writes=[Vc])
                S.op("vector", lambda e: e.tensor_copy(out=cb[X][:, 0:16], in_=cb[X][:, 2048:2064]), reads=[cb[X]], writes=[cb[X]])
    for step in range(NT + 2):
        if step < NT:
            TA(step)
        if 0 <= step - 1 < NT:
            TB(step - 1)
        if 0 <= step - 2 < NT:
            TC(step - 2)
    S.pop()

    if upto == 2:
        S.pop()
        return dbg_out(H1, bH1)
    S.push()
    ng = S.sb("ng", [128, 8], F32)
    Wn = S.sb("Wn", [128, 8, 1036], BF16)
    Wo2 = S.sb("Wo2", [128, 4, 1024], BF16)
    S.dma("sync", ng[:, :], n_g[:, :], writes=[ng])
    load_w_bf16(S, "sync", Wn, lambda k, c0, cw: Wn[:, k, c0:c0 + cw],
                n_w_in.rearrange("(k p) n -> p k n", p=128), stage, 8, 1036, gain=ng)
    load_w_bf16(S, "sync", Wo2, lambda k, c0, cw: Wo2[:, k, c0:c0 + cw],
                n_w_out.rearrange("(k p) n -> p k n", p=128), stage, 4, 1024)
    Ex = S.sb("Ex", [128, 64, 128], BF16)
    for c in range(4):
        load_const_bf16(S, "sync", Ex, Ex[:, c * 16:(c + 1) * 16, :].rearrange("p a b -> p (a b)"),
                        ex_d[:, c * 2048:(c + 1) * 2048], stage, 2048)
    Maug = S.sb("Maug", [128, 8, 257], BF16)
    load_const_bf16(S, "sync", Maug, Maug[:, 0:4, :].rearrange("p a b -> p (a b)"), maug_d[:, 0:1028], stage, 1028)
    load_const_bf16(S, "sync", Maug, Maug[:, 4:8, :].rearrange("p a b -> p (a b)"), maug_d[:, 1028:2056], stage, 1028)
    cmpm = S.sb("cmpm", [128, 33, 128], BF16)
    load_const_bf16(S, "sync", cmpm, cmpm[:, 0:16, :].rearrange("p a b -> p (a b)"), cmpm_d[:, 0:2048], stage, 2048)
    load_const_bf16(S, "sync", cmpm, cmpm[:, 16:32, :].rearrange("p a b -> p (a b)"), cmpm_d[:, 2048:4096], stage, 2048)
    load_const_bf16(S, "sync", cmpm, cmpm[:, 32, :], cmpm_d[:, 4096:4224], stage, 128)
    onesel = S.sb("onesel", [128, 3, 3], BF16)
    load_const_bf16(S, "sync", onesel, onesel[:, :, :].rearrange("p a b -> p (a b)"), onesel_d[:, :], stage, 9)
    causT = S.sb("causT", [128, 128], BF16)
    upT = S.sb("upT", [128, 128], BF16)
    load_const_bf16(S, "sync", causT, causT[:, :], causT_d[:, :], stage, 128)
    load_const_bf16(S, "sync", upT, upT[:, :], upT_d[:, :], stage, 128)

    hTs = [S.sb("hT", [128, 8, 128], BF16) for i in range(2)]
    h1s = [S.sb("h1", [128, 1024], F32) for i in range(2)]
    kwr = S.sb("kwr", [128, 6, 128], BF16)
    vwr = S.sb("vwr", [128, 6, 128], BF16)
    kwb = [Buf("kwb%d" % i, kwr.t) for i in range(6)]
    vwb = [Buf("vwb%d" % i, vwr.t) for i in range(6)]
    QTs = [S.sb("QT", [128, 512], BF16) for i in range(2)]
    gsils = [S.sb("gsil", [128, 512], F32) for i in range(2)]
    bgs = [S.sb("bg", [128, 12], F32) for i in range(2)]
    EmC = S.sb("EmC", [128, 8, 512], BF16)
    NBUF = 4
    Eb = [S.sb("Eb", [128, 512], BF16) for i in range(NBUF)]
    Emb = [S.sb("Emb", [128, 512], BF16) for i in range(NBUF)]
    imp = S.sb("imp", [128, 256], F32)
    score = S.sb("score", [128, 256], F32)
    sc2 = S.sb("sc2", [128, 256], F32)
    selF = S.sb("selF", [128, 256], F32)
    selT = S.sb("selT", [128, 2, 128], BF16)
    m8 = S.sb("m8", [128, 16], F32)
    rcols = [S.sb("rcol", [128, 1], F32) for i in range(2)]
    sumsbs = [S.sb("sumsb", [3, 512], F32) for i in range(2)]
    coef = S.sb("coef", [128, 12], F32)
    o_ = S.sb("o_", [128, 512], F32)
    ogf = S.sb("ogf", [128, 512], BF16)
    ogT2 = S.sb("ogT2", [128, 4, 128], BF16)
    yts = [S.sb("yt", [128, 1024], F32) for i in range(2)]
    ObTs = [[S.sb("ObT", [128, 512], BF16) for i in range(3)] for p in range(2)]
    pSc = [S.ps("pSc", [128, 512], F32) for i in range(2)]
    pMs = [S.ps("pM", [128, 512], F32) for i in range(2)]
    pO2 = [S.ps("pOb", [128, 512], F32) for i in range(2)]
    pOb = [pO2[0], pO2[1], pO2[0]]
    pSum1 = S.ps("pSum", [3, 512], F32)
    pSums = [pSum1, pSum1]
    pX = S.ps("pX", [128, 512], F32)
    pXb = pX.t[:, :].bitcast(BF16)

    S.op("gpsimd", lambda e: e.memset(selF[:, :], 0.0), writes=[selF])
    ctr = {"e": 0, "m": 0, "s": 0, "pm": 0}

    def stageA(u):
        rows = u["rows"]
        QT = u["QT"]
        ps = pSc[ctr["s"] % 2]
        ctr["s"] += 1
        S.op("tensor", lambda e: e.matmul(ps[0:rows, :], lhsT=u["ksrc"], rhs=QT[:, :], start=True, stop=True),
             reads=[u["ktrack"], QT], writes=[ps], lhs=[u["ktrack"]])
        E = Eb[ctr["e"] % NBUF]
        ctr["e"] += 1
        S.op("scalar", lambda e: e.activation(out=E[0:rows, :], in_=ps[0:rows, :], func=AF.Exp, scale=SCALE),
             reads=[ps], writes=[E])
        u["E"] = E

    def stageA2(u):
        rows = u["rows"]
        E = u["E"]
        mask = u["mask"]
        if mask is None:
            Em = E
        else:
            Em = Emb[ctr["m"] % NBUF]
            ctr["m"] += 1
            if mask[0] == "sb":
                S.op("vector", lambda e: e.tensor_tensor(
                    out=Em[0:rows, :].rearrange("p (r q) -> p r q", r=4), in0=E[0:rows, :].rearrange("p (r q) -> p r q", r=4),
                    in1=mask[1].unsqueeze(1).broadcast_to([rows, 4, 128]), op=ALU.mult),
                     reads=[E] + mask[2], writes=[Em])
            else:
                t = mask[1]
                pM = pMs[ctr["pm"] % 2]
                ctr["pm"] += 1
                S.op("tensor", lambda e: e.matmul(pM[:, 0:128], lhsT=Ex[:, t % 64, :], rhs=selT[:, t // 64, :],
                                                  start=True, stop=True), reads=[Ex, selT], writes=[pM], lhs=[Ex])
                S.op("vector", lambda e: e.tensor_tensor(
                    out=Em[0:rows, :].rearrange("p (r q) -> p r q", r=4), in0=E[0:rows, :].rearrange("p (r q) -> p r q", r=4),
                    in1=pM[:, 0:128].unsqueeze(1).broadcast_to([128, 4, 128]), op=ALU.mult),
                     reads=[E, pM], writes=[Em])
        u["Em"] = Em
        if u.get("emc") is not None:
            c = u["emc"]
            S.op("gpsimd", lambda e: e.tensor_copy(out=EmC[0:rows, c, :], in_=Em[0:rows, :]), reads=[Em], writes=[EmC])

    def stageB(u):
        rows = u["rows"]
        Em = u["Em"]
        b_idx = u["b"]
        q = u["q"]
        pSum = pSums[q["par"]]
        S.op("tensor", lambda e: e.matmul(pOb[b_idx][:, :], lhsT=u["vsrc"], rhs=Em[0:rows, :], start=u["first"], stop=u["last"]),
             reads=[u["vtrack"], Em], writes=[pOb[b_idx]], lhs=[u["vtrack"]])
        S.op("tensor", lambda e: e.matmul(pSum[:, :], lhsT=onesel[0:rows, b_idx, :], rhs=Em[0:rows, :],
                                          start=(q["nsum"] == 0), stop=(q["nsum"] == q["total_sum"] - 1)),
             reads=[onesel, Em], writes=[pSum], lhs=[onesel])
        q["nsum"] += 1
        for f in u.get("post", ()):
            f()

    def rs_chunk(c):
        tl = list(range(c * 8, c * 8 + 8))
        S.collective("ReduceScatter", ALU.add, groups, Y2[c * 1024:(c + 1) * 1024, :], Y2s[c * 256:(c + 1) * 256, :],
                     reads=[bY2[t] for t in tl], writes=[bY2s[c]])

    def prologue(i):
        par = i % 2
        rows = slice(i * 128, (i + 1) * 128)
        hT, h1, QT, gsil, bg = hTs[par], h1s[par], QTs[par], gsils[par], bgs[par]
        S.dma("sync", hT[:, :, :], HT[i].rearrange("p (k t) -> p k t", k=8), reads=[bHT[i]], writes=[hT])
        S.dma("sync", h1[:, :], H1[rows, :], reads=[bH1[i]], writes=[h1])
        S.dma("sync", kwr[:, i % 6, :], KW[i], reads=[bKW[i]], writes=[kwb[i % 6]])
        S.dma("sync", vwr[:, i % 6, :], VW[i], reads=[bVW[i]], writes=[vwb[i % 6]])
        for r in range(4):
            for k in range(8):
                S.op("tensor", lambda e: e.matmul(pX[:, r * 128:(r + 1) * 128], lhsT=Wn[:, k, r * 128:(r + 1) * 128],
                                                  rhs=hT[:, k, :], start=(k == 0), stop=(k == 7)), reads=[Wn, hT], writes=[pX], lhs=[Wn])
        S.op("scalar", lambda e: e.copy(out=QT[:, :], in_=pX[:, :]), reads=[pX], writes=[QT])
        for k in range(8):
            S.op("tensor", lambda e: e.matmul(pX[:, :], lhsT=hT[:, k, :], rhs=Wn[:, k, 512:1024],
                                              start=(k == 0), stop=(k == 7)), reads=[Wn, hT], writes=[pX])
        S.op("scalar", lambda e: e.activation(out=gsil[:, :], in_=pX[:, :], func=AF.Silu), reads=[pX], writes=[gsil])
        for k in range(8):
            S.op("tensor", lambda e: e.matmul(pX[:, 0:12], lhsT=hT[:, k, :], rhs=Wn[:, k, 1024:1036],
                                              start=(k == 0), stop=(k == 7)), reads=[Wn, hT], writes=[pX])
        S.op("scalar", lambda e: e.activation(out=bg[:, :], in_=pX[:, 0:12], func=AF.Sigmoid), reads=[pX], writes=[bg])

    def make_units(i):
        par = i % 2
        QT = QTs[par]
        Wp = 8 * (i + 1)
        nch = (Wp + 127) // 128
        wt = [t for t in range(i - 4, i + 1) if t >= 0]
        q = dict(i=i, par=par, nsum=0, total_sum=nch + (i + 1) + len(wt), topk_done=(i < 8))
        OT = ObTs[par]

        def evac(b):
            return lambda: S.op("scalar", lambda e: e.copy(out=OT[b][:, :], in_=pOb[b][:, :]), reads=[pOb[b]], writes=[OT[b]])

        def topk():
            ncol = 2 * i
            for r in range(4):
                pI = pMs[ctr["pm"] % 2]
                ctr["pm"] += 1
                rc_ = rcols[r % 2]
                for c in range(nch):
                    rws = min(128, Wp - c * 128)
                    S.op("tensor", lambda e: e.matmul(pI[:, 0:257], lhsT=EmC[0:rws, c, r * 128:(r + 1) * 128], rhs=Maug[0:rws, c, :],
                                                      start=(c == 0), stop=(c == nch - 1)), reads=[EmC, Maug], writes=[pI])
                S.op("vector", lambda e: e.tensor_scalar(out=rc_[:, :], in0=pI[:, 256:257], scalar1=1e-30, scalar2=None,
                                                         op0=ALU.add), reads=[pI], writes=[rc_])
                S.op("vector", lambda e: e.reciprocal(out=rc_[:, :], in_=rc_[:, :]), reads=[rc_], writes=[rc_])
                if r == 0:
                    S.op("vector", lambda e: e.tensor_scalar(out=imp[:, 0:ncol], in0=pI[:, 0:ncol], scalar1=rc_[:, 0:1],
                                                             scalar2=None, op0=ALU.mult), reads=[pI, rc_], writes=[imp])
                else:
                    S.op("vector", lambda e: e.scalar_tensor_tensor(out=imp[:, 0:ncol], in0=pI[:, 0:ncol], scalar=rc_[:, 0:1],
                                                                    in1=imp[:, 0:ncol], op0=ALU.mult, op1=ALU.add),
                         reads=[pI, rc_, imp], writes=[imp])
            S.op("vector", lambda e: e.tensor_copy(out=score[:, 0:ncol], in_=imp[:, 0:ncol]), reads=[imp], writes=[score])
            S.op("vector", lambda e: e.memset(score[:, 0:1], -1.0), writes=[score])
            S.op("vector", lambda e: e.memset(score[0:64, ncol - 1:ncol], -1.0), writes=[score])
            S.op("vector", lambda e: e.max(out=m8[:, 0:8], in_=score[:, 0:ncol]), reads=[score], writes=[m8])
            S.op("vector", lambda e: e.match_replace(out=sc2[:, 0:ncol], in_to_replace=m8[:, 0:8], in_values=score[:, 0:ncol],
                                                     imm_value=-2.0), reads=[m8, score], writes=[sc2])
            S.op("vector", lambda e: e.max(out=m8[:, 8:16], in_=sc2[:, 0:ncol]), reads=[sc2, m8], writes=[m8])
            S.op("vector", lambda e: e.tensor_scalar(out=selF[:, 0:ncol], in0=score[:, 0:ncol], scalar1=m8[:, 12:13], scalar2=None,
                                                     op0=ALU.is_ge), reads=[score, m8], writes=[selF])
            S.op("vector", lambda e: e.memset(selF[:, 0:1], 1.0), writes=[selF])
            S.op("vector", lambda e: e.memset(selF[0:64, ncol - 1:ncol], 1.0), writes=[selF])
            for c in range((ncol + 127) // 128):
                S.op("tensor", lambda e: e.transpose(out=pX[:, c * 128:(c + 1) * 128], in_=selF[:, c * 128:(c + 1) * 128],
                                                     identity=idf[:, :]), reads=[selF, idf], writes=[pX])
                S.op("vector", lambda e: e.tensor_copy(out=selT[:, c, :], in_=pX[:, c * 128:(c + 1) * 128]), reads=[pX], writes=[selT])
            q["topk_done"] = True

        units = []
        for c in range(nch):
            rws = min(128, Wp - c * 128)
            if c == nch - 1:
                mk = ("sb", cmpm[0:rws, (i % 16) + (0 if i < 16 else 16), :], [cmpm])
            elif c == 0:
                mk = ("sb", cmpm[0:rws, 32, :], [cmpm])
            else:
                mk = None
            u = dict(q=q, QT=QT, ksrc=KcT[:, c * 128:c * 128 + rws], ktrack=KcT, vsrc=Vc[0:rws, c, :], vtrack=Vc, rows=rws, mask=mk,
                     b=0, first=(c == 0), last=(c == nch - 1), emc=(c if i >= 8 else None), post=[])
            if c == nch - 1:
                u["post"].append(evac(0))
                if i >= 8:
                    u["post"].append(topk)
            units.append(u)
        for n, t in enumerate(wt):
            if t == i:
                mk = ("sb", causT[:, :], [causT])
            elif t == i - 4:
                mk = ("sb", upT[:, :], [upT])
            else:
                mk = None
            u = dict(q=q, QT=QT, ksrc=kwr[:, t % 6, :], ktrack=kwb[t % 6], vsrc=vwr[:, t % 6, :], vtrack=vwb[t % 6], rows=128, mask=mk,
                     b=2, first=(n == 0), last=(n == len(wt) - 1), post=[])
            if n == len(wt) - 1:
                u["post"].append(evac(2))
            units.append(u)
        for t in range(i + 1):
            if t == i:
                mk = ("sb", causT[:, :], [causT])
            elif i >= 8:
                mk = ("sel", t)
            else:
                mk = None
            u = dict(q=q, QT=QT, ksrc=KsT[:, t * 128:(t + 1) * 128], ktrack=KsT, vsrc=Vs[:, t, :], vtrack=Vs, rows=128, mask=mk,
                     b=1, first=(t == 0), last=(t == i), post=[], needs_sel=(mk is not None and mk[0] == "sel"))
            if t == i:
                u["post"].append(evac(1))
            units.append(u)
        return units, q

    def epilogue_parts(i):
        par = i % 2
        rows = slice(i * 128, (i + 1) * 128)
        h1, gsil, bg = h1s[par], gsils[par], bgs[par]
        OT = ObTs[par]
        pSum = pSums[par]
        sumsb = sumsbs[par]
        yt = yts[par]

        def combine(b):
            for r in range(4):
                S.op("tensor", lambda e: e.transpose(out=pXb[:, r * 128:(r + 1) * 128], in_=OT[b][:, r * 128:(r + 1) * 128],
                                                     identity=ident[:, :]), reads=[OT[b], ident], writes=[pX])
            for r in range(4):
                cs_ = slice(r * 128, (r + 1) * 128)
                if b == 0:
                    S.op("vector", lambda e: e.tensor_scalar(out=o_[:, cs_], in0=pXb[:, cs_], scalar1=coef[:, r * 3 + b:r * 3 + b + 1],
                                                             scalar2=None, op0=ALU.mult), reads=[pX, coef], writes=[o_])
                else:
                    S.op("vector", lambda e: e.scalar_tensor_tensor(out=o_[:, cs_], in0=pXb[:, cs_], scalar=coef[:, r * 3 + b:r * 3 + b + 1],
                                                                    in1=o_[:, cs_], op0=ALU.mult, op1=ALU.add),
                         reads=[pX, coef, o_], writes=[o_])

        def part1():
            S.op("scalar", lambda e: e.copy(out=sumsb[:, :], in_=pSum[:, :]), reads=[pSum], writes=[sumsb])
            for r in range(4):
                S.op("tensor", lambda e: e.matmul(pX[:, r * 3:(r + 1) * 3], lhsT=sumsb[0:3, r * 128:(r + 1) * 128], rhs=idf[0:3, 0:3],
                                                  start=True, stop=True), reads=[sumsb, idf], writes=[pX])
            S.op("vector", lambda e: e.tensor_scalar(out=coef[:, :], in0=pX[:, 0:12], scalar1=1e-30, scalar2=None, op0=ALU.add),
                 reads=[pX], writes=[coef])
            S.op("vector", lambda e: e.reciprocal(out=coef[:, :], in_=coef[:, :]), reads=[coef], writes=[coef])
            S.op("vector", lambda e: e.tensor_tensor(out=coef[:, :], in0=coef[:, :], in1=bg[:, :], op=ALU.mult), reads=[coef, bg], writes=[coef])
            combine(0)

        def part2():
            combine(2)

        def part2b():
            combine(1)
            S.op("gpsimd", lambda e: e.tensor_mul(out=ogf[:, :], in0=o_[:, :], in1=gsil[:, :]), reads=[o_, gsil], writes=[ogf])

        def part3():
            for c in range(4):
                S.op("tensor", lambda e: e.transpose(out=pXb[:, c * 128:(c + 1) * 128], in_=ogf[:, c * 128:(c + 1) * 128],
                                                     identity=ident[:, :]), reads=[ogf, ident], writes=[pX])
            S.op("vector", lambda e: e.tensor_copy(out=ogT2[:, :, :].rearrange("p a b -> p (a b)"), in_=pXb[:, 0:512]), reads=[pX], writes=[ogT2])

        def part4(hf):
            if True:
                for c in range(4):
                    S.op("tensor", lambda e: e.matmul(pX[:, :], lhsT=ogT2[:, c, :], rhs=Wo2[:, c, hf * 512:(hf + 1) * 512],
                                                      start=(c == 0), stop=(c == 3)), reads=[ogT2, Wo2], writes=[pX])
                S.op("vector", lambda e: e.scalar_tensor_tensor(out=yt[:, hf * 512:(hf + 1) * 512], in0=h1[:, hf * 512:(hf + 1) * 512],
                                                                scalar=0.25, in1=pX[:, :], op0=ALU.mult, op1=ALU.add),
                     reads=[h1, pX], writes=[yt])
            if hf == 1:
                S.dma("gpsimd", Y2[rows, :], yt[:, :], reads=[yt], writes=[bY2[i]])
                if i % 8 == 7:
                    rs_chunk(i // 8)

        return [part1, part2, part2b, part3, lambda: part4(0), lambda: part4(1)]

    LOOK = 3
    prologue(0)
    pending = []
    for i in range(NT):
        units, q = make_units(i)
        nu = len(units)
        hooks = {}
        pos = [1, 4, 7, 10, 13, 16]
        for k, f in enumerate(pending):
            hooks.setdefault(min(nu - 1, pos[k]), []).append(f)
        if i + 1 < NT:
            hooks.setdefault(min(nu - 1, 19), []).append(lambda i=i: prologue(i + 1))
        for n in range(nu + LOOK):
            if n < nu:
                stageA(units[n])
            if 0 <= n - 1 < nu:
                if units[n - 1].get("needs_sel"):
                    assert q["topk_done"]
                stageA2(units[n - 1])
            if n - LOOK >= 0:
                stageB(units[n - LOOK])
            for f in hooks.get(n, ()):
                f()
        assert q["nsum"] == q["total_sum"]
        pending = epilogue_parts(i)
    for f in pending:
        f()
    S.pop()
    S.pop()
    if upto == 3:
        return dbg_out(Y2, bY2)

    S.push()
    ple = PleCtx(S)
    ple.load_weights("sync", pg[1][:, :], wg[1], we[1], stage)
    fn = S.sb("fn", [128, 1024], F32)
    S.dma("sync", fn[:, :], fng[:, :], writes=[fn])
    hs = [S.sb("h", [128, 1024], F32) for i in range(2)]
    ob = [S.sb("ob", [128, 1024], F32) for i in range(2)]
    sqs = S.sb("sqs", [128, 1024], F32)
    ss = S.sb("ss", [128, 1], F32)
    rs = S.sb("rs", [128, 1], F32)
    pA = S.ps("pA", [128, 8, 128], BF16)
    pG = S.ps("pG", [128, 512], F32)
    pE = S.ps("pE", [128, 512], F32)
    pG2 = S.ps("pG2", [128, 512], F32)
    pE2 = S.ps("pE2", [128, 512], F32)
    NU = TL // 128

    def UA(u):
        rows = slice(u * 128, (u + 1) * 128)
        h = hs[u % 2]
        S.dma("sync", h[:, :], Y2s[rows, :], reads=[bY2s[u // 2]], writes=[h])
        ple.front(u % 2, h, p1s[rows, :], ident, pA)

    def UB(u):
        rows = slice(u * 128, (u + 1) * 128)
        h = hs[u % 2]
        ple.back(u % 2, h, [pG, pG2], [pE, pE2])
        rms_rstd(S, h, 1024, sqs, ss, rs)
        o = ob[u % 2]
        S.op("vector", lambda e: e.scalar_tensor_tensor(out=o[:, :], in0=h[:, :], scalar=rs[:, 0:1], in1=fn[:, :],
                                                        op0=ALU.mult, op1=ALU.mult), reads=[h, rs, fn], writes=[o])
        S.dma("gpsimd", out[rows, :], o[:, :], reads=[o])

    for step in range(NU + 1):
        if step < NU:
            UA(step)
        if step >= 1:
            UB(step - 1)
    S.finish()
    return nc, S.ninstr


def _colgain(g):
    return np.ascontiguousarray(np.asarray(g, np.float32).reshape(8, 128).T)


def _consts(T):
    half = 128
    inv = (10000.0 ** (-np.arange(half, dtype=np.float32) / np.float32(half))).astype(np.float32)
    pos = np.arange(T, dtype=np.float32)
    ang = (inv[:, None] * pos[None, :]).astype(np.float32)
    c = dict(cosT=np.cos(ang).astype(np.float32), sinT=np.sin(ang).astype(np.float32))
    p = np.arange(128)
    c["causT"] = (p[:, None] <= p[None, :]).astype(np.float32)
    c["upT"] = (p[:, None] > p[None, :]).astype(np.float32)
    c["ident"] = np.eye(128, dtype=np.float32)
    ex = np.zeros((128, 64, 128), np.float32)
    for tt in range(64):
        for hb in range(2):
            ex[(2 * tt + hb) % 128, tt, hb * 64:(hb + 1) * 64] = 1.0
    c["ex"] = ex.reshape(128, 64 * 128)
    m = np.zeros((1024, 257), np.float32)
    for j in range(256):
        for (off, w) in ((0, 1.0), (1, 2.0), (2, 2.0), (3, 2.0), (4, 1.0)):
            n = 4 * j + off
            if n < 1024:
                m[n, j] = w
    m[:, 256] = 1.0
    c["maug"] = np.ascontiguousarray(m.reshape(8, 128, 257).transpose(1, 0, 2)).reshape(128, 8 * 257)
    lane = np.arange(128)[:, None]
    ql = np.arange(128)[None, :]
    cm = np.zeros((128, 33, 128), np.float32)
    for res in range(16):
        v = (ql >= 16 * lane + 15 - 128 * res).astype(np.float32)
        b = v.copy()
        a = v.copy()
        a[0, :] = 0.0
        cm[:, res, :] = a
        cm[:, 16 + res, :] = b
    fm = np.ones((128, 128), np.float32)
    fm[0, :] = 0.0
    cm[:, 32, :] = fm
    c["cmpm"] = cm.reshape(128, 33 * 128)
    os_ = np.zeros((128, 3, 3), np.float32)
    for b in range(3):
        os_[:, b, b] = 1.0
    c["onesel"] = os_.reshape(128, 9)
    return c


def _head_consts(hd):
    lg = np.log1p(-(np.float32(2.0) ** np.float32(-5.0 - hd))).astype(np.float32)
    p = np.arange(128, dtype=np.float32)
    qd = np.exp((p + 1.0) * lg).astype(np.float32)
    kdv = (np.exp(-(p + 1.0) * lg) / 16.0).astype(np.float32)
    return dict(qdec=np.ascontiguousarray(np.broadcast_to(np.tile(qd, 4)[None, :], (128, 512))).astype(np.float32),
                kdec=np.ascontiguousarray(np.broadcast_to(np.tile(kdv, 4)[None, :], (128, 512))).astype(np.float32),
                cdec=np.full((128, 1), np.exp(np.float32(128.0) * lg), np.float32))


def _inmaps(T, B, I):
    C = _consts(T)
    TL = T // 4
    maps = []
    ca = np.ascontiguousarray
    for b in range(B):
        for g in range(4):
            m = dict(C)
            m.update(_head_consts(g))
            m["x"] = ca(I["x"][b, :T])
            m["p0"] = ca(I["p"][0, b, :T])
            p1 = I["p"][1, b, :T].reshape(T // 1024, 4, 256, 256)[:, g].reshape(TL, 256)
            m["p1s"] = ca(p1)
            wi = I["ret_w_in"][0]
            m["r_w_in"] = ca(np.concatenate([wi[:, g * 256:(g + 1) * 256], wi[:, 1024 + g * 256:1024 + (g + 1) * 256],
                                             wi[:, 2048 + g * 512:2048 + (g + 1) * 512], wi[:, 4096 + g * 512:4096 + (g + 1) * 512]], axis=1))
            m["r_g_in"] = _colgain(I["ret_norm"][0])
            m["r_gn"] = ca(np.broadcast_to(I["ret_gn"][0][g * 512:(g + 1) * 512][None, :], (128, 512)))
            m["r_w_out"] = ca(I["ret_w_out"][0][g * 512:(g + 1) * 512, :])
            for l in range(2):
                m["pg%d" % l] = _colgain(I["ple_norm"][l])
                m["wg%d" % l] = ca(I["ple_w_gate"][l])
                m["we%d" % l] = ca(I["ple_w_emb"][l])
            m["fng"] = ca(np.broadcast_to(I["final_norm"][None, :], (128, 1024)))
            m["kv_g"] = _colgain(I["kv_norm"])
            kw = I["kv_w"]
            order = [0, 1, 2, 4, 3, 5]
            m["kv_w"] = ca(np.concatenate([kw[:, pt * 512 + g * 128: pt * 512 + (g + 1) * 128] for pt in order], axis=1))
            m["peT_k"] = ca(I["cmp_pe_k"].T)
            m["peT_v"] = ca(I["cmp_pe_v"].T)
            m["w1_k"] = ca(I["cmp_w1_k"])
            m["w1_v"] = ca(I["cmp_w1_v"])
            m["w2_k"] = ca(I["cmp_w2_k"])
            m["w2_v"] = ca(I["cmp_w2_v"])
            m["n_g"] = _colgain(I["nsa_norm"][0])
            nw = I["nsa_w_in"][0]
            m["n_w_in"] = ca(np.concatenate([nw[:, g * 512:(g + 1) * 512], nw[:, 2048 + g * 512:2048 + (g + 1) * 512],
                                             nw[:, 4096 + g * 12:4096 + (g + 1) * 12]], axis=1))
            m["n_w_out"] = ca(I["nsa_w_out"][0][g * 512:(g + 1) * 512, :])
            maps.append({k: np.asarray(v, np.float32) for k, v in m.items()})
    return maps


_PROG = {}


def run_module(I, T, B):
    key = (T, B)
    if key not in _PROG:
        groups = [[b * 4 + g for g in range(4)] for b in range(B)]
        _PROG[key] = build_program(T, groups)[0]
    nc = _PROG[key]
    maps = _inmaps(T, B, I)
    res = run_bass_kernel_spmd(nc, maps, core_ids=list(range(4 * B)))
    outp = np.empty((B, T, 1024), np.float32)
    for b in range(B):
        for g in range(4):
            o = res.results[b * 4 + g]["out"].reshape(T // 1024, 256, 1024)
            outp[b].reshape(T // 1024, 4, 256, 1024)[:, g] = o
    return outp


def kernel(**inputs):
    I = {k: np.asarray(v) for k, v in inputs.items()}
    return run_module(I, 16384, 2)
```

```python
import contextlib
import numpy as np
import concourse.bass as bass
import concourse.mybir as mybir
from concourse.bass_utils import run_bass_kernel_spmd

F32 = mybir.dt.float32
BF16 = mybir.dt.bfloat16
AF = mybir.ActivationFunctionType
ALU = mybir.AluOpType
AX = mybir.AxisListType
ENGS = ("tensor", "vector", "scalar", "gpsimd", "sync")
EPS = 1e-6
SCALE = 128 ** -0.5


class Buf:
    __slots__ = ("name", "t", "w", "r", "psum")

    def __init__(self, name, t=None, psum=False):
        self.name = name
        self.t = t
        self.w = None
        self.r = []
        self.psum = psum

    def __getitem__(self, idx):
        return self.t[idx]


class Slot(Buf):
    __slots__ = ("c",)

    def __init__(self, name, t, c):
        Buf.__init__(self, name, t)
        self.c = c

    def __getitem__(self, idx):
        return self.t[idx[0], self.c, idx[1]]


class _Rec:
    def __init__(self):
        self.call = None

    def __getattr__(self, name):
        def f(*a, **kw):
            assert self.call is None
            self.call = (name, a, kw)
            return self
        return f


class Sched:
    def __init__(self, nc, n_dma_sems=12):
        self.nc = nc
        self.sems = {}
        self.cnt = {}
        for e in ENGS:
            self.sems[e] = nc.alloc_semaphore("s_" + e)
            self.cnt[e] = 0
        self.sems["cc"] = nc.alloc_semaphore("s_cc")
        self.cnt["cc"] = 0
        self.dq = {}
        for q in ("sync", "gpsimd", "scalar"):
            lst = []
            for i in range(n_dma_sems):
                k = "d_%s_%d" % (q, i)
                self.sems[k] = nc.alloc_semaphore(k)
                self.cnt[k] = 0
                lst.append(k)
            self.dq[q] = [lst, 0]
        self.known = {e: {} for e in ENGS}
        self.E = {e: getattr(nc, e) for e in ENGS}
        self.ninstr = 0
        self.uid = 0
        self.stacks = [contextlib.ExitStack()]

    def push(self):
        self.stacks.append(contextlib.ExitStack())

    def pop(self):
        self.barrier()
        self.stacks.pop().close()

    def _nm(self, name):
        self.uid += 1
        return "%s_%d" % (name, self.uid)

    def sb(self, name, shape, dtype):
        nm = self._nm(name)
        return Buf(nm, self.stacks[-1].enter_context(self.nc.sbuf_tensor(nm, list(shape), dtype)))

    def ps(self, name, shape, dtype=F32):
        nm = self._nm(name)
        return Buf(nm, self.stacks[-1].enter_context(self.nc.psum_tensor(nm, list(shape), dtype)), psum=True)

    def dr(self, name, shape, dtype):
        return self.nc.dram_tensor(self._nm(name), list(shape), dtype)

    def _need(self, eng, deps):
        kn = self.known[eng]
        best = {}
        for d in deps:
            if d is None:
                continue
            k, v = d
            if k == eng and eng == "tensor":
                continue
            if kn.get(k, 0) >= v:
                continue
            if best.get(k, 0) < v:
                best[k] = v
        return best

    def _emit_waits(self, eng, best):
        for k, v in best.items():
            self.E[eng].wait_ge(self.sems[k], v)
            self.known[eng][k] = v
            self.ninstr += 1

    @staticmethod
    def _deps(reads, writes):
        deps = []
        for b in reads:
            deps.append(b.w)
            if b.psum:
                deps.extend(b.r)
        for b in writes:
            deps.append(b.w)
            deps.extend(b.r)
        return deps

    @staticmethod
    def _mark(ev, reads, writes):
        for b in reads:
            b.r.append(ev)
        for b in writes:
            b.w = ev
            b.r = []

    def op(self, eng, fn, reads=(), writes=(), lhs=None):
        attach = None
        if eng == "tensor" and lhs is None:
            self._emit_waits(eng, self._need(eng, self._deps(reads, writes)))
        else:
            if lhs:
                self._emit_waits(eng, self._need(eng, self._deps(lhs, ())))
            best = self._need(eng, self._deps(reads, writes))
            if best:
                k = next(iter(best))
                attach = (k, best.pop(k))
            self._emit_waits(eng, best)
        self.cnt[eng] += 1
        rec = _Rec()
        fn(rec)
        name, a, kw = rec.call
        ins = getattr(self.E[eng], name)(*a, **kw)
        if attach is not None:
            ins = ins._wait_ge(self.sems[attach[0]], attach[1])
            self.known[eng][attach[0]] = attach[1]
        ins.then_inc(self.sems[eng], 1)
        self.ninstr += 1
        ev = (eng, self.cnt[eng])
        self._mark(ev, reads, writes)
        return ev

    def dma(self, q, out, in_, reads=(), writes=(), **kw):
        lst, idx = self.dq[q]
        k = lst[idx % len(lst)]
        self.dq[q][1] = idx + 1
        deps = self._deps(reads, writes)
        if self.cnt[k] > 0:
            deps.append((k, self.cnt[k]))
        self._emit_waits(q, self._need(q, deps))
        self.cnt[k] += 16
        self.E[q].dma_start(out=out, in_=in_, **kw).then_inc(self.sems[k], 16)
        self.ninstr += 1
        ev = (k, self.cnt[k])
        self._mark(ev, reads, writes)
        return ev

    def collective(self, kind, op, groups, in_ap, out_ap, reads=(), writes=()):
        deps = self._deps(reads, writes)
        if self.cnt["cc"] > 0:
            deps.append(("cc", self.cnt["cc"]))
        self._emit_waits("gpsimd", self._need("gpsimd", deps))
        self.cnt["cc"] += 1
        self.E["gpsimd"].collective_compute(kind, op, replica_groups=groups, ins=[in_ap], outs=[out_ap]).then_inc(
            self.sems["cc"], 1)
        self.ninstr += 1
        ev = ("cc", self.cnt["cc"])
        self._mark(ev, reads, writes)
        return ev

    def _all_events(self):
        return [(k, v) for k, v in self.cnt.items() if v > 0]

    def barrier(self):
        ev = self._all_events()
        for e in ENGS:
            self._emit_waits(e, self._need(e, [d for d in ev if d[0] != e]))

    def finish(self):
        deps = [(k, v) for k, v in self.cnt.items() if v > 0 and (k.startswith("d_") or k == "cc")]
        self._emit_waits("sync", self._need("sync", deps))
        self.barrier()
        while self.stacks:
            self.stacks.pop().close()


def load_w_bf16(S, q, dst, dst_fn, src_ap, stage, kc, ncols, gain=None):
    for k in range(kc):
        c0 = 0
        while c0 < ncols:
            cw = min(2048, ncols - c0)
            S.dma(q, stage[:, 0:cw], src_ap[:, k, c0:c0 + cw], writes=[stage])
            if gain is None:
                S.op("gpsimd", lambda e: e.tensor_copy(out=dst_fn(k, c0, cw), in_=stage[:, 0:cw]),
                     reads=[stage], writes=[dst])
            else:
                S.op("gpsimd", lambda e: e.tensor_scalar(out=dst_fn(k, c0, cw), in0=stage[:, 0:cw],
                                                         scalar1=gain[:, k:k + 1], scalar2=None, op0=ALU.mult),
                     reads=[stage, gain], writes=[dst])
            c0 += cw


def load_const_bf16(S, q, dst, dst_ap, src_ap, stage, ncols):
    S.dma(q, stage[:, 0:ncols], src_ap, writes=[stage])
    S.op("gpsimd", lambda e: e.tensor_copy(out=dst_ap, in_=stage[:, 0:ncols]), reads=[stage], writes=[dst])


def rms_rstd(S, xt, D, sqs, ss, rs):
    S.op("scalar", lambda e: e.activation(out=sqs[:, :], in_=xt[:, :], func=AF.Square, accum_out=ss[:, :]),
         reads=[xt], writes=[sqs, ss])
    S.op("vector", lambda e: e.tensor_scalar(out=rs[:, :], in0=ss[:, :], scalar1=1.0 / D, scalar2=EPS,
                                             op0=ALU.mult, op1=ALU.add), reads=[ss], writes=[rs])
    S.op("scalar", lambda e: e.activation(out=rs[:, :], in_=rs[:, :], func=AF.Sqrt), reads=[rs], writes=[rs])
    S.op("vector", lambda e: e.reciprocal(out=rs[:, :], in_=rs[:, :]), reads=[rs], writes=[rs])


def rmsnorm_T(S, xt, ident, sqs, ss, rs, xn, pT, dstT, tok0, copy_eng="vector"):
    rms_rstd(S, xt, 1024, sqs, ss, rs)
    S.op("vector", lambda e: e.tensor_scalar(out=xn[:, :], in0=xt[:, :], scalar1=rs[:, 0:1], scalar2=None,
                                             op0=ALU.mult), reads=[xt, rs], writes=[xn])
    for k in range(8):
        S.op("tensor", lambda e: e.transpose(out=pT[:, k, :], in_=xn[:, k * 128:(k + 1) * 128], identity=ident[:, :]),
             reads=[xn, ident], writes=[pT])
    if copy_eng == "vector":
        S.op("vector", lambda e: e.tensor_copy(out=dstT[:, :, tok0:tok0 + 128], in_=pT[:, :, :]), reads=[pT], writes=[dstT])
    else:
        S.op("scalar", lambda e: e.copy(out=dstT[:, :, tok0:tok0 + 128], in_=pT[:, :, :]), reads=[pT], writes=[dstT])


class PleCtx:
    def __init__(self, S):
        self.S = S
        self.Wg = S.sb("pleWg", [128, 8, 1024], BF16)
        self.We = S.sb("pleWe", [128, 2, 1024], BF16)
        self.gain = S.sb("pleGain", [128, 8], F32)
        self.sqs = S.sb("ple_sqs", [128, 1024], BF16)
        self.ss = [S.sb("ple_ss", [128, 1], F32) for i in range(2)]
        self.rs = [S.sb("ple_rs", [128, 1], F32) for i in range(2)]
        _xn = S.sb("ple_xn", [128, 1024], BF16)
        self.xn = [_xn, _xn]
        self.hnT = [S.sb("ple_hnT", [128, 8, 128], BF16) for i in range(2)]
        self.pf = [S.sb("ple_pf", [128, 256], F32) for i in range(2)]
        self.pb = [S.sb("ple_pb", [128, 256], BF16) for i in range(2)]
        self.pT = [S.sb("ple_pT", [128, 2, 128], BF16) for i in range(2)]
        _sig = S.sb("ple_sig", [128, 512], F32)
        _prod = S.sb("ple_prod", [128, 512], F32)
        self.sig = [_sig, _sig]
        self.prod = [_prod, _prod]

    def load_weights(self, q, g_ap, wg_ap, we_ap, stage):
        S = self.S
        S.dma(q, self.gain[:, :], g_ap, writes=[self.gain])
        load_w_bf16(S, q, self.Wg, lambda k, c0, cw: self.Wg[:, k, c0:c0 + cw],
                    wg_ap.rearrange("(k p) n -> p k n", p=128), stage, 8, 1024, gain=self.gain)
        load_w_bf16(S, q, self.We, lambda k, c0, cw: self.We[:, k, c0:c0 + cw],
                    we_ap.rearrange("(k p) n -> p k n", p=128), stage, 2, 1024)

    def front(self, par, h, p_ap, ident, pA):
        S = self.S
        pf, pb, pT = self.pf[par], self.pb[par], self.pT[par]
        S.dma("sync", pf[:, :], p_ap, writes=[pf])
        rmsnorm_T(S, h, ident, self.sqs, self.ss[par], self.rs[par], self.xn[par], pA, self.hnT[par], 0)
        S.op("gpsimd", lambda e: e.tensor_copy(out=pb[:, :], in_=pf[:, :]), reads=[pf], writes=[pb])
        for k in range(2):
            S.op("tensor", lambda e: e.transpose(out=pA[:, k, :], in_=pb[:, k * 128:(k + 1) * 128], identity=ident[:, :]),
                 reads=[pb, ident], writes=[pA])
        S.op("vector", lambda e: e.tensor_copy(out=pT[:, :, :], in_=pA[:, 0:2, :]), reads=[pA], writes=[pT])

    def back(self, par, h, pG, pE):
        S = self.S
        hnT, pT = self.hnT[par], self.pT[par]
        for hf in range(2):
            cs = slice(hf * 512, (hf + 1) * 512)
            sig, prod = self.sig[hf], self.prod[hf]
            for k in range(8):
                S.op("tensor", lambda e: e.matmul(pG[hf][:, :], lhsT=hnT[:, k, :], rhs=self.Wg[:, k, cs],
                                                  start=(k == 0), stop=(k == 7)), reads=[hnT, self.Wg], writes=[pG[hf]])
            for k in range(2):
                S.op("tensor", lambda e: e.matmul(pE[hf][:, :], lhsT=pT[:, k, :], rhs=self.We[:, k, cs],
                                                  start=(k == 0), stop=(k == 1)), reads=[pT, self.We], writes=[pE[hf]])
            S.op("scalar", lambda e: e.activation(out=sig[:, :], in_=pG[hf][:, :], func=AF.Sigmoid), reads=[pG[hf]], writes=[sig])
            S.op("vector", lambda e: e.tensor_tensor(out=prod[:, :], in0=sig[:, :], in1=pE[hf][:, :], op=ALU.mult),
                 reads=[sig, pE[hf]], writes=[prod])
            S.op("vector", lambda e: e.tensor_add(out=h[:, cs], in0=h[:, cs], in1=prod[:, :]), reads=[h, prod], writes=[h])


def build_program(T, groups, upto=4):
    nc = bass.Bass("TRN2", target_bir_lowering=False)
    NT = T // 128
    NS = T // 512
    NCH = T // 1024
    NC16 = T // 2048
    TL = T // 4

    def din(name, shape):
        return nc.dram_tensor(name, list(shape), F32, kind="ExternalInput").ap()

    x = din("x", [T, 1024])
    p0 = din("p0", [T, 256])
    p1s = din("p1s", [TL, 256])
    identd = din("ident", [128, 128])
    r_w_in = din("r_w_in", [1024, 1536])
    r_g_in = din("r_g_in", [128, 8])
    r_gn = din("r_gn", [128, 512])
    r_w_out = din("r_w_out", [512, 1024])
    cosT = din("cosT", [128, T])
    sinT = din("sinT", [128, T])
    qdec = din("qdec", [128, 512])
    kdec = din("kdec", [128, 512])
    cdec = din("cdec", [128, 1])
    causT_d = din("causT", [128, 128])
    upT_d = din("upT", [128, 128])
    pg = [din("pg%d" % l, [128, 8]) for l in range(2)]
    wg = [din("wg%d" % l, [1024, 1024]) for l in range(2)]
    we = [din("we%d" % l, [256, 1024]) for l in range(2)]
    fng = din("fng", [128, 1024])
    kv_g = din("kv_g", [128, 8])
    kv_w = din("kv_w", [1024, 768])
    peT_k = din("peT_k", [128, 32])
    peT_v = din("peT_v", [128, 32])
    w1_k = din("w1_k", [4096, 256])
    w1_v = din("w1_v", [4096, 256])
    w2_k = din("w2_k", [256, 128])
    w2_v = din("w2_v", [256, 128])
    n_g = din("n_g", [128, 8])
    n_w_in = din("n_w_in", [1024, 1036])
    n_w_out = din("n_w_out", [512, 1024])
    ex_d = din("ex", [128, 64 * 128])
    maug_d = din("maug", [128, 8 * 257])
    cmpm_d = din("cmpm", [128, 33 * 128])
    onesel_d = din("onesel", [128, 9])
    out = nc.dram_tensor("out", [TL, 1024], F32, kind="ExternalOutput").ap()
    dbg = nc.dram_tensor("dbg", [T, 1024], F32, kind="ExternalOutput").ap() if upto < 4 else None

    def dbg_out(src, bufs):
        for c in range(T // 1024):
            S.dma("sync", dbg[c * 1024:(c + 1) * 1024, :], src[c * 1024:(c + 1) * 1024, :], reads=bufs[c * 8:(c + 1) * 8])
        S.finish()
        return nc, S.ninstr

    S = Sched(nc)
    Y1 = S.dr("Y1", [T, 1024], F32)
    Y1s = S.dr("Y1s", [T, 1024], F32)
    Y2 = S.dr("Y2", [T, 1024], F32)
    Y2s = S.dr("Y2s", [TL, 1024], F32)
    H1 = S.dr("H1", [T, 1024], F32)
    HT = S.dr("HT", [NT, 128, 1024], BF16)
    KW = S.dr("KW", [NT, 128, 128], BF16)
    VW = S.dr("VW", [NT, 128, 128], BF16)
    bY1 = [Buf("bY1_%d" % i) for i in range(NT)]
    bY1s = [Buf("bY1s_%d" % i) for i in range(NT)]
    bY2 = [Buf("bY2_%d" % i) for i in range(NT)]
    bY2s = [Buf("bY2s_%d" % i) for i in range(NCH)]
    bH1 = [Buf("bH1_%d" % i) for i in range(NT)]
    bHT = [Buf("bHT_%d" % i) for i in range(NT)]
    bKW = [Buf("bKW_%d" % i) for i in range(NT)]
    bVW = [Buf("bVW_%d" % i) for i in range(NT)]

    stage = S.sb("stage", [128, 2048], F32)
    idf = S.sb("idf", [128, 128], F32)
    ident = S.sb("identb", [128, 128], BF16)
    S.dma("sync", idf[:, :], identd[:, :], writes=[idf])
    S.op("vector", lambda e: e.tensor_copy(out=ident[:, :], in_=idf[:, :]), reads=[idf], writes=[ident])

    S.push()
    W = S.sb("W", [128, 8, 1536], BF16)
    Wo = S.sb("Wo", [128, 4, 1024], BF16)
    gin = S.sb("gin", [128, 8], F32)
    gnt = S.sb("gnt", [128, 512], F32)
    qd_t = S.sb("qd_t", [128, 512], F32)
    kd_t = S.sb("kd_t", [128, 512], F32)
    cd_t = S.sb("cd_t", [128, 1], F32)
    caus = S.sb("caus", [128, 128], F32)
    xts = [S.sb("xt", [128, 1024], F32) for i in range(2)]
    sqs = S.sb("sqs", [128, 1024], F32)
    ss = S.sb("ss", [128, 1], F32)
    rs = S.sb("rs", [128, 1], F32)
    xn = S.sb("xn", [128, 1024], BF16)
    xnT2 = [S.sb("xnT", [128, 8, 512], BF16) for i in range(2)]
    cs = [S.sb("cs", [128, 512], F32) for i in range(2)]
    sn = [S.sb("sn", [128, 512], F32) for i in range(2)]
    tabs2 = [[S.sb("tab", [128, 512], F32) for i in range(4)] for p in range(2)]
    sss = [S.sb("ss", [128, 1], F32) for i in range(2)]
    rss = [S.sb("rs", [128, 1], F32) for i in range(2)]
    xns = [S.sb("xn", [128, 1024], BF16) for i in range(2)]
    raw = [S.sb("raw", [128, 512], F32) for i in range(4)]
    tmp = [S.sb("tmp", [128, 512], F32) for i in range(4)]
    qdT2 = [S.sb("qdT", [128, 2, 512], BF16) for i in range(2)]
    kTp2 = [S.sb("kTp", [128, 2, 512], BF16) for i in range(2)]
    vb2 = [S.sb("vb", [128, 4, 512], BF16) for i in range(2)]
    gs2 = [S.sb("gs", [128, 4, 512], F32) for i in range(2)]
    st_f = [S.sb("st_f", [128, 512], F32) for i in range(2)]
    st_b = [S.sb("st_b", [128, 512], BF16) for i in range(2)]
    kd = S.sb("kd", [128, 256], BF16)
    ST = S.sb("ST", [128, 128], BF16)
    osq = S.sb("osq", [128, 512], F32)
    stats = [S.sb("stat", [128, 4], F32) for i in range(2)]
    ons = [S.sb("on", [128, 512], F32) for i in range(2)]
    ogs = [S.sb("og", [128, 512], BF16) for i in range(2)]
    ogT = S.sb("ogT", [128, 4, 128], BF16)
    yo = [S.sb("yo", [128, 1024], F32) for i in range(2)]
    pA = S.ps("pA", [128, 8, 128], BF16)
    pB = [S.ps("pB", [128, 512], F32) for i in range(2)]
    pS = S.ps("pS", [128, 128], F32)
    pO = S.ps("pO", [128, 512], F32)
    pSt = [S.ps("pSt", [128, 512], F32) for i in range(2)]
    pY = S.ps("pY", [128, 512], F32)

    for (dst, src) in ((gin, r_g_in), (gnt, r_gn), (qd_t, qdec), (kd_t, kdec), (cd_t, cdec), (caus, causT_d)):
        S.dma("sync", dst[:, :], src[:, :], writes=[dst])
    load_w_bf16(S, "sync", W, lambda k, c0, cw: W[:, k, c0:c0 + cw],
                r_w_in.rearrange("(k p) n -> p k n", p=128), stage, 8, 1536, gain=gin)
    load_w_bf16(S, "sync", Wo, lambda k, c0, cw: Wo[:, k, c0:c0 + cw],
                r_w_out.rearrange("(k p) n -> p k n", p=128), stage, 4, 1024)
    for i in range(2):
        S.op("gpsimd", lambda e: e.memset(st_f[i][:, :], 0.0), writes=[st_f[i]])
        S.op("gpsimd", lambda e: e.memset(st_b[i][:, :], 0.0), writes=[st_b[i]])

    def allreduce_chunk(c):
        r0 = c * 1024
        tl = list(range(c * 8, c * 8 + 8))
        S.collective("AllReduce", ALU.add, groups, Y1[r0:r0 + 1024, :], Y1s[r0:r0 + 1024, :],
                     reads=[bY1[t] for t in tl], writes=[bY1s[t] for t in tl])

    def tabs_for(s):
        t0 = s * 512
        cst, snt = cs[s % 2], sn[s % 2]
        tb = tabs2[s % 2]
        S.dma("sync", cst[:, :], cosT[:, t0:t0 + 512], writes=[cst])
        S.dma("sync", snt[:, :], sinT[:, t0:t0 + 512], writes=[snt])
        S.op("gpsimd", lambda e: e.tensor_mul(out=tb[0][:, :], in0=cst[:, :], in1=qd_t[:, :]), reads=[cst, qd_t], writes=[tb[0]])
        S.op("gpsimd", lambda e: e.tensor_mul(out=tb[1][:, :], in0=snt[:, :], in1=qd_t[:, :]), reads=[snt, qd_t], writes=[tb[1]])
        S.op("gpsimd", lambda e: e.tensor_mul(out=tb[2][:, :], in0=cst[:, :], in1=kd_t[:, :]), reads=[cst, kd_t], writes=[tb[2]])
        S.op("gpsimd", lambda e: e.tensor_mul(out=tb[3][:, :], in0=snt[:, :], in1=kd_t[:, :]), reads=[snt, kd_t], writes=[tb[3]])

    def F(s, j):
        ti = s * 4 + j
        xt = xts[ti % 2]
        S.dma("sync", xt[:, :], x[ti * 128:(ti + 1) * 128, :], writes=[xt])
        rmsnorm_T(S, xt, ident, sqs, sss[ti % 2], rss[ti % 2], xns[ti % 2], pA, xnT2[s % 2], j * 128)

    def P1(s):
        xnT = xnT2[s % 2]
        tb = tabs2[s % 2]
        qdT, kTp, vb, gs = qdT2[s % 2], kTp2[s % 2], vb2[s % 2], gs2[s % 2]
        for dc in range(4):
            pb = pB[dc % 2]
            for k in range(8):
                S.op("tensor", lambda e: e.matmul(pb[:, :], lhsT=W[:, k, dc * 128:(dc + 1) * 128], rhs=xnT[:, k, :],
                                                  start=(k == 0), stop=(k == 7)), reads=[W, xnT], writes=[pb], lhs=[W])
            S.op("scalar", lambda e: e.copy(out=raw[dc][:, :], in_=pb[:, :]), reads=[pb], writes=[raw[dc]])
        for (eng, x1, x2, ct, st_, dst, ta, tb_) in (("gpsimd", raw[0], raw[1], tb[0], tb[1], qdT, tmp[0], tmp[1]),
                                                     ("vector", raw[2], raw[3], tb[2], tb[3], kTp, tmp[2], tmp[3])):
            S.op(eng, lambda e: e.tensor_mul(out=ta[:, :], in0=x1[:, :], in1=ct[:, :]), reads=[x1, ct], writes=[ta])
            S.op(eng, lambda e: e.tensor_mul(out=tb_[:, :], in0=x2[:, :], in1=st_[:, :]), reads=[x2, st_], writes=[tb_])
            S.op(eng, lambda e: e.tensor_sub(out=dst[:, 0, :], in0=ta[:, :], in1=tb_[:, :]), reads=[ta, tb_], writes=[dst])
            S.op(eng, lambda e: e.tensor_mul(out=ta[:, :], in0=x1[:, :], in1=st_[:, :]), reads=[x1, st_], writes=[ta])
            S.op(eng, lambda e: e.tensor_mul(out=tb_[:, :], in0=x2[:, :], in1=ct[:, :]), reads=[x2, ct], writes=[tb_])
            S.op(eng, lambda e: e.tensor_add(out=dst[:, 1, :], in0=ta[:, :], in1=tb_[:, :]), reads=[ta, tb_], writes=[dst])
        for j in range(4):
            for (which, c0) in (("v", 512), ("g", 1024)):
                pb = pB[0] if which == "v" else pB[1]
                for k in range(8):
                    S.op("tensor", lambda e: e.matmul(pb[:, :], lhsT=xnT[:, k, j * 128:(j + 1) * 128], rhs=W[:, k, c0:c0 + 512],
                                                      start=(k == 0), stop=(k == 7)), reads=[W, xnT], writes=[pb])
                if which == "v":
                    S.op("scalar", lambda e: e.copy(out=vb[:, j, :], in_=pb[:, :]), reads=[pb], writes=[vb])
                else:
                    S.op("scalar", lambda e: e.activation(out=gs[:, j, :], in_=pb[:, :], func=AF.Silu), reads=[pb], writes=[gs])

    def CA(s, j):
        ti = s * 4 + j
        qdT, kTp, vb = qdT2[s % 2], kTp2[s % 2], vb2[s % 2]
        on, stat = ons[ti % 2], stats[ti % 2]
        tk = slice(j * 128, (j + 1) * 128)
        for dc in range(2):
            S.op("tensor", lambda e: e.transpose(out=pA[:, dc, :], in_=kTp[:, dc, tk], identity=ident[:, :]),
                 reads=[kTp, ident], writes=[pA])
        S.op("vector", lambda e: e.tensor_scalar(out=kd[:, :], in0=pA[:, 0:2, :].rearrange("p a b -> p (a b)"),
                                                 scalar1=cd_t[:, 0:1], scalar2=None, op0=ALU.mult),
             reads=[pA, cd_t], writes=[kd])
        for dc in range(2):
            S.op("tensor", lambda e: e.matmul(pS[:, :], lhsT=kTp[:, dc, tk], rhs=qdT[:, dc, tk],
                                              start=(dc == 0), stop=(dc == 1)), reads=[kTp, qdT], writes=[pS])
        S.op("vector", lambda e: e.tensor_tensor(out=ST[:, :], in0=pS[:, :], in1=caus[:, :], op=ALU.mult),
             reads=[pS, caus], writes=[ST])
        S.op("tensor", lambda e: e.matmul(pO[:, :], lhsT=ST[:, :], rhs=vb[:, j, :], start=True, stop=False),
             reads=[ST, vb], writes=[pO])
        for dc in range(2):
            S.op("tensor", lambda e: e.matmul(pO[:, :], lhsT=qdT[:, dc, tk], rhs=st_b[dc][:, :],
                                              start=False, stop=(dc == 1)), reads=[qdT, st_b[dc]], writes=[pO])
        for dc in range(2):
            S.op("tensor", lambda e: e.matmul(pSt[dc][:, :], lhsT=kd[:, dc * 128:(dc + 1) * 128], rhs=vb[:, j, :],
                                              start=True, stop=True), reads=[kd, vb], writes=[pSt[dc]])
            S.op("vector", lambda e: e.scalar_tensor_tensor(out=st_f[dc][:, :], in0=st_f[dc][:, :], scalar=cd_t[:, 0:1],
                                                            in1=pSt[dc][:, :], op0=ALU.mult, op1=ALU.add),
                 reads=[st_f[dc], cd_t, pSt[dc]], writes=[st_f[dc]])
            S.op("scalar", lambda e: e.copy(out=st_b[dc][:, :], in_=st_f[dc][:, :]), reads=[st_f[dc]], writes=[st_b[dc]])
        S.op("scalar", lambda e: e.activation(out=on[:, :], in_=pO[:, :], func=AF.Identity, accum_out=stat[:, 0:1]),
             reads=[pO], writes=[on, stat])
        S.op("scalar", lambda e: e.activation(out=osq[:, :], in_=pO[:, :], func=AF.Square, accum_out=stat[:, 1:2]),
             reads=[pO], writes=[osq, stat])

    def CB(s, j):
        ti = s * 4 + j
        gs = gs2[s % 2]
        on, stat, og = ons[ti % 2], stats[ti % 2], ogs[ti % 2]
        S.op("vector", lambda e: e.tensor_scalar(out=stat[:, 0:2], in0=stat[:, 0:2], scalar1=1.0 / 512, scalar2=None,
                                                 op0=ALU.mult), reads=[stat], writes=[stat])
        S.op("vector", lambda e: e.tensor_tensor(out=stat[:, 2:3], in0=stat[:, 0:1], in1=stat[:, 0:1], op=ALU.mult),
             reads=[stat], writes=[stat])
        S.op("vector", lambda e: e.tensor_tensor(out=stat[:, 2:3], in0=stat[:, 1:2], in1=stat[:, 2:3], op=ALU.subtract),
             reads=[stat], writes=[stat])
        S.op("vector", lambda e: e.tensor_scalar(out=stat[:, 2:3], in0=stat[:, 2:3], scalar1=EPS, scalar2=None,
                                                 op0=ALU.add), reads=[stat], writes=[stat])
        S.op("scalar", lambda e: e.activation(out=stat[:, 2:3], in_=stat[:, 2:3], func=AF.Sqrt), reads=[stat], writes=[stat])
        S.op("vector", lambda e: e.reciprocal(out=stat[:, 3:4], in_=stat[:, 2:3]), reads=[stat], writes=[stat])
        S.op("vector", lambda e: e.tensor_scalar(out=on[:, :], in0=on[:, :], scalar1=stat[:, 0:1], scalar2=stat[:, 3:4],
                                                 op0=ALU.subtract, op1=ALU.mult), reads=[on, stat], writes=[on])
        S.op("vector", lambda e: e.tensor_mul(out=on[:, :], in0=on[:, :], in1=gnt[:, :]), reads=[on, gnt], writes=[on])
        S.op("vector", lambda e: e.tensor_mul(out=og[:, :], in0=on[:, :], in1=gs[:, j, :]), reads=[on, gs], writes=[og])
        for c in range(4):
            S.op("tensor", lambda e: e.transpose(out=pA[:, 2 + c, :], in_=og[:, c * 128:(c + 1) * 128], identity=ident[:, :]),
                 reads=[og, ident], writes=[pA])
        S.op("vector", lambda e: e.tensor_copy(out=ogT[:, :, :], in_=pA[:, 2:6, :]), reads=[pA], writes=[ogT])
        yt = yo[ti % 2]
        for hf in range(2):
            for c in range(4):
                S.op("tensor", lambda e: e.matmul(pY[:, :], lhsT=ogT[:, c, :], rhs=Wo[:, c, hf * 512:(hf + 1) * 512],
                                                  start=(c == 0), stop=(c == 3)), reads=[ogT, Wo], writes=[pY])
            S.op("scalar", lambda e: e.copy(out=yt[:, hf * 512:(hf + 1) * 512], in_=pY[:, :]), reads=[pY], writes=[yt])
        S.dma("scalar", Y1[ti * 128:(ti + 1) * 128, :], yt[:, :], reads=[yt], writes=[bY1[ti]])
        if ti >= 9 and (ti - 9) % 8 == 0:
            allreduce_chunk((ti - 9) // 8)

    prev = None
    for s in range(NS + 1):
        if s < NS:
            tabs_for(s)
        for j in range(4):
            if s < NS:
                F(s, j)
            if s >= 1:
                CA(s - 1, j)
                if prev is not None:
                    CB(*prev)
                prev = (s - 1, j)
        if s < NS:
            P1(s)
    CB(*prev)
    allreduce_chunk(NCH - 1)
    if NCH >= 2 and (NT - 1) < 9 + 8 * (NCH - 2):
        pass
    issued = set([(ti - 9) // 8 for ti in range(NT) if ti >= 9 and (ti - 9) % 8 == 0] + [NCH - 1])
    for c in range(NCH):
        if c not in issued:
            allreduce_chunk(c)
    S.pop()
    if upto == 1:
        return dbg_out(Y1s, bY1s)

    S.push()
    KsT = S.sb("KsT", [128, T], BF16)
    Vs = S.sb("Vs", [128, NT, 128], BF16)
    KcT = S.sb("KcT", [128, NC16 * 128], BF16)
    Vc = S.sb("Vc", [128, NC16, 128], BF16)

    S.push()
    ple = PleCtx(S)
    ple.load_weights("sync", pg[0][:, :], wg[0], we[0], stage)
    kvg = S.sb("kvg", [128, 8], F32)
    Wkv = S.sb("Wkv", [128, 8, 768], BF16)
    S.dma("sync", kvg[:, :], kv_g[:, :], writes=[kvg])
    load_w_bf16(S, "sync", Wkv, lambda k, c0, cw: Wkv[:, k, c0:c0 + cw],
                kv_w.rearrange("(k p) n -> p k n", p=128), stage, 8, 768, gain=kvg)
    w1 = [S.sb("w1", [128, 32, 256], BF16) for i in range(2)]
    w2 = [S.sb("w2", [128, 2, 128], BF16) for i in range(2)]
    peT = [S.sb("peT", [128, 32], BF16) for i in range(2)]
    for i, (w1d, w2d, ped) in enumerate(((w1_k, w2_k, peT_k), (w1_v, w2_v, peT_v))):
        load_w_bf16(S, "sync", w1[i], lambda k, c0, cw: w1[i][:, k, c0:c0 + cw],
                    w1d.rearrange("(l d) h -> d l h", d=128), stage, 32, 256)
        load_w_bf16(S, "sync", w2[i], lambda k, c0, cw: w2[i][:, k, c0:c0 + cw],
                    w2d.rearrange("(k p) n -> p k n", p=128), stage, 2, 128)
        load_const_bf16(S, "sync", peT[i], peT[i][:, :], ped[:, :], stage, 32)
    cb = [S.sb("cb", [128, 2064], BF16) for i in range(2)]
    hs = [S.sb("h", [128, 1024], F32) for i in range(3)]
    _yb = S.sb("yb", [128, 1024], F32)
    ybs = [_yb, _yb]
    hTs = [S.sb("hT", [128, 8, 128], BF16) for i in range(2)]
    kwt = [S.sb("kwt", [128, 128], BF16) for i in range(2)]
    vwt = [S.sb("vwt", [128, 128], BF16) for i in range(2)]
    sqs = ple.sqs
    ss2 = [S.sb("ss", [128, 1], F32) for i in range(2)]
    rs2 = [S.sb("rs", [128, 1], F32) for i in range(2)]
    _xn2 = S.sb("xn", [128, 1024], BF16)
    xn2 = [_xn2, _xn2]
    ones1 = S.sb("ones1", [1, 128], BF16)
    bias_f = S.sb("bias_f", [1, 512], F32)
    bias_hi = S.sb("bias_hi", [1, 512], BF16)
    bias_hif = S.sb("bias_hif", [1, 512], F32)
    bias_lo = S.sb("bias_lo", [1, 512], BF16)
    xs_ = S.sb("xs_", [128, 256], F32)
    x2_ = S.sb("x2_", [128, 256], F32)
    sg_ = S.sb("sg_", [128, 256], F32)
    hid = S.sb("hid", [128, 256], BF16)
    hidT = S.sb("hidT", [128, 2, 128], BF16)
    pA = S.ps("pA", [128, 8, 128], BF16)
    pA2 = S.ps("pA2", [128, 8, 128], BF16)
    pG1 = S.ps("pG", [128, 512], F32)
    pE1 = S.ps("pE", [128, 512], F32)
    pGs = [pG1, pG1]
    pEs = [pE1, pE1]
    pKT = S.ps("pKT", [128, 4, 128], F32)
    pKV = S.ps("pKV", [128, 256], F32)
    pH = S.ps("pH", [128, 256], F32)
    pC = S.ps("pC", [128, 128], F32)

    S.op("gpsimd", lambda e: e.memset(ones1[:, :], 1.0), writes=[ones1])
    for i in range(2):
        S.op("gpsimd", lambda e: e.memset(cb[i][:, 0:16], 0.0), writes=[cb[i]])
    for i in range(2):
        for l in range(32):
            S.op("tensor", lambda e: e.matmul(pH[0:1, :], lhsT=peT[i][:, l:l + 1], rhs=w1[i][:, l, :],
                                              start=(l == 0), stop=(l == 31)), reads=[peT[i], w1[i]], writes=[pH])
        S.op("scalar", lambda e: e.copy(out=bias_f[:, i * 256:(i + 1) * 256], in_=pH[0:1, :]), reads=[pH], writes=[bias_f])
    S.op("vector", lambda e: e.tensor_copy(out=bias_hi[:, :], in_=bias_f[:, :]), reads=[bias_f], writes=[bias_hi])
    S.op("vector", lambda e: e.tensor_copy(out=bias_hif[:, :], in_=bias_hi[:, :]), reads=[bias_hi], writes=[bias_hif])
    S.op("vector", lambda e: e.tensor_sub(out=bias_hif[:, :], in0=bias_f[:, :], in1=bias_hif[:, :]), reads=[bias_f, bias_hif], writes=[bias_hif])
    S.op("vector", lambda e: e.tensor_copy(out=bias_lo[:, :], in_=bias_hif[:, :]), reads=[bias_hif], writes=[bias_lo])

    def TA(i):
        rows = slice(i * 128, (i + 1) * 128)
        h = hs[i % 3]
        yb = ybs[i % 2]
        S.dma("sync", h[:, :], x[rows, :], writes=[h])
        S.dma("sync", yb[:, :], Y1s[rows, :], reads=[bY1s[i]], writes=[yb])
        S.op("vector", lambda e: e.tensor_add(out=h[:, :], in0=h[:, :], in1=yb[:, :]), reads=[h, yb], writes=[h])
        ple.front(i % 2, h, p0[rows, :], ident, pA)

    def TB(i):
        rows = slice(i * 128, (i + 1) * 128)
        h = hs[i % 3]
        ple.back(i % 2, h, pGs, pEs)
        S.dma("gpsimd", H1[rows, :], h[:, :], reads=[h], writes=[bH1[i]])

    def TC(i):
        rows = slice(i * 128, (i + 1) * 128)
        h = hs[i % 3]
        hT = hTs[i % 2]
        rmsnorm_T(S, h, ident, sqs, ss2[i % 2], rs2[i % 2], xn2[i % 2], pA2, hT, 0, copy_eng="scalar")
        S.dma("scalar", HT[i].rearrange("p (k t) -> p k t", k=8), hT[:, :, :], reads=[hT], writes=[bHT[i]])
        for a_ in range(4):
            for k in range(8):
                S.op("tensor", lambda e: e.matmul(pKT[:, a_, :], lhsT=Wkv[:, k, a_ * 128:(a_ + 1) * 128], rhs=hT[:, k, :],
                                                  start=(k == 0), stop=(k == 7)), reads=[Wkv, hT], writes=[pKT], lhs=[Wkv])
        for k in range(8):
            S.op("tensor", lambda e: e.matmul(pKV[:, :], lhsT=hT[:, k, :], rhs=Wkv[:, k, 512:768],
                                              start=(k == 0), stop=(k == 7)), reads=[Wkv, hT], writes=[pKV])
        cc0 = 16 + (i % 16) * 128
        S.op("scalar", lambda e: e.copy(out=cb[0][:, cc0:cc0 + 128], in_=pKT[:, 0, :]), reads=[pKT], writes=[cb[0]])
        S.op("scalar", lambda e: e.copy(out=cb[1][:, cc0:cc0 + 128], in_=pKT[:, 1, :]), reads=[pKT], writes=[cb[1]])
        S.op("scalar", lambda e: e.copy(out=KsT[:, rows], in_=pKT[:, 2, :]), reads=[pKT], writes=[KsT])
        kw_, vw_ = kwt[i % 2], vwt[i % 2]
        S.op("scalar", lambda e: e.copy(out=kw_[:, :], in_=pKT[:, 3, :]), reads=[pKT], writes=[kw_])
        S.op("scalar", lambda e: e.copy(out=Vs[:, i, :], in_=pKV[:, 0:128]), reads=[pKV], writes=[Vs])
        S.op("scalar", lambda e: e.copy(out=vw_[:, :], in_=pKV[:, 128:256]), reads=[pKV], writes=[vw_])
        S.dma("scalar", KW[i], kw_[:, :], reads=[kw_], writes=[bKW[i]])
        S.dma("scalar", VW[i], vw_[:, :], reads=[vw_], writes=[bVW[i]])
        if i % 16 == 15:
            compress(i)

    def compress(i):
        if True:
            s16 = i // 16
            for X in range(2):
                bc = slice(X * 256, (X + 1) * 256)
                S.op("tensor", lambda e: e.matmul(pH[:, :], lhsT=ones1[0:1, :], rhs=bias_hi[0:1, bc], start=True, stop=False),
                     reads=[ones1, bias_hi], writes=[pH])
                S.op("tensor", lambda e: e.matmul(pH[:, :], lhsT=ones1[0:1, :], rhs=bias_lo[0:1, bc], start=False, stop=False),
                     reads=[ones1, bias_lo], writes=[pH])
                for l in range(32):
                    S.op("tensor", lambda e: e.matmul(pH[:, :], lhsT=cb[X][:, l:l + 2033:16], rhs=w1[X][:, l, :],
                                                      start=False, stop=(l == 31)), reads=[cb[X], w1[X]], writes=[pH])
                S.op("scalar", lambda e: e.copy(out=xs_[:, :], in_=pH[:, :]), reads=[pH], writes=[xs_])
                S.op("vector", lambda e: e.tensor_tensor(out=x2_[:, :], in0=xs_[:, :], in1=xs_[:, :], op=ALU.mult), reads=[xs_], writes=[x2_])
                S.op("vector", lambda e: e.tensor_scalar(out=x2_[:, :], in0=x2_[:, :], scalar1=0.044715, scalar2=1.0,
                                                         op0=ALU.mult, op1=ALU.add), reads=[x2_], writes=[x2_])
                S.op("vector", lambda e: e.tensor_tensor(out=x2_[:, :], in0=x2_[:, :], in1=xs_[:, :], op=ALU.mult), reads=[x2_, xs_], writes=[x2_])
                S.op("scalar", lambda e: e.activation(out=sg_[:, :], in_=x2_[:, :], func=AF.Sigmoid, scale=1.5957691216057308),
                     reads=[x2_], writes=[sg_])
                S.op("vector", lambda e: e.tensor_tensor(out=hid[:, :], in0=xs_[:, :], in1=sg_[:, :], op=ALU.mult), reads=[xs_, sg_], writes=[hid])
                for hc in range(2):
                    S.op("tensor", lambda e: e.transpose(out=pA2[:, hc, :], in_=hid[:, hc * 128:(hc + 1) * 128], identity=ident[:, :]),
                         reads=[hid, ident], writes=[pA2])
                S.op("vector", lambda e: e.tensor_copy(out=hidT[:, :, :], in_=pA2[:, 0:2, :]), reads=[pA2], writes=[hidT])
                if X == 0:
                    for hc in range(2):
                        S.op("tensor", lambda e: e.matmul(pC[:, :], lhsT=w2[0][:, hc, :], rhs=hidT[:, hc, :],
                                                          start=(hc == 0), stop=(hc == 1)), reads=[w2[0], hidT], writes=[pC])
                    S.op("scalar", lambda e: e.copy(out=KcT[:, s16 * 128:(s16 + 1) * 128], in_=pC[:, :]), reads=[pC], writes=[KcT])
                else:
                    for hc in range(2):
                        S.op("tensor", lambda e: e.matmul(pC[:, :], lhsT=hidT[:, hc, :], rhs=w2[1][:, hc, :],
                                                          start=(hc == 0), stop=(hc == 1)), reads=[w2[1], hidT], writes=[pC])
                    S.op("scalar", lambda e: e.copy(out=Vc[:, s16, :], in_=pC[:, :]), reads=[pC], writes=[Vc])
                S.op("vector", lambda e: e.tensor_copy(out=cb[X][:, 0:16], in_=cb[X][:, 2048:2064]), reads=[cb[X]], writes=[cb[X]])
    for step in range(NT + 2):
        if step < NT:
            TA(step)
        if 0 <= step - 1 < NT:
            TB(step - 1)
        if 0 <= step - 2 < NT:
            TC(step - 2)
    S.pop()

    if upto == 2:
        S.pop()
        return dbg_out(H1, bH1)
    S.push()
    ng = S.sb("ng", [128, 8], F32)
    Wn = S.sb("Wn", [128, 8, 1036], BF16)
    Wo2 = S.sb("Wo2", [128, 4, 1024], BF16)
    S.dma("sync", ng[:, :], n_g[:, :], writes=[ng])
    load_w_bf16(S, "sync", Wn, lambda k, c0, cw: Wn[:, k, c0:c0 + cw],
                n_w_in.rearrange("(k p) n -> p k n", p=128), stage, 8, 1036, gain=ng)
    load_w_bf16(S, "sync", Wo2, lambda k, c0, cw: Wo2[:, k, c0:c0 + cw],
                n_w_out.rearrange("(k p) n -> p k n", p=128), stage, 4, 1024)
    Ex = S.sb("Ex", [128, 64, 128], BF16)
    for c in range(4):
        load_const_bf16(S, "sync", Ex, Ex[:, c * 16:(c + 1) * 16, :].rearrange("p a b -> p (a b)"),
                        ex_d[:, c * 2048:(c + 1) * 2048], stage, 2048)
    Maug = S.sb("Maug", [128, 8, 257], BF16)
    load_const_bf16(S, "sync", Maug, Maug[:, 0:4, :].rearrange("p a b -> p (a b)"), maug_d[:, 0:1028], stage, 1028)
    load_const_bf16(S, "sync", Maug, Maug[:, 4:8, :].rearrange("p a b -> p (a b)"), maug_d[:, 1028:2056], stage, 1028)
    cmpm = S.sb("cmpm", [128, 33, 128], BF16)
    load_const_bf16(S, "sync", cmpm, cmpm[:, 0:16, :].rearrange("p a b -> p (a b)"), cmpm_d[:, 0:2048], stage, 2048)
    load_const_bf16(S, "sync", cmpm, cmpm[:, 16:32, :].rearrange("p a b -> p (a b)"), cmpm_d[:, 2048:4096], stage, 2048)
    load_const_bf16(S, "sync", cmpm, cmpm[:, 32, :], cmpm_d[:, 4096:4224], stage, 128)
    onesel = S.sb("onesel", [128, 3, 3], BF16)
    load_const_bf16(S, "sync", onesel, onesel[:, :, :].rearrange("p a b -> p (a b)"), onesel_d[:, :], stage, 9)
    causT = S.sb("causT", [128, 128], BF16)
    upT = S.sb("upT", [128, 128], BF16)
    load_const_bf16(S, "sync", causT, causT[:, :], causT_d[:, :], stage, 128)
    load_const_bf16(S, "sync", upT, upT[:, :], upT_d[:, :], stage, 128)

    hTs = [S.sb("hT", [128, 8, 128], BF16) for i in range(2)]
    h1s = [S.sb("h1", [128, 1024], F32) for i in range(2)]
    kwr = S.sb("kwr", [128, 6, 128], BF16)
    vwr = S.sb("vwr", [128, 6, 128], BF16)
    kwb = [Buf("kwb%d" % i, kwr.t) for i in range(6)]
    vwb = [Buf("vwb%d" % i, vwr.t) for i in range(6)]
    QTs = [S.sb("QT", [128, 512], BF16) for i in range(2)]
    gsils = [S.sb("gsil", [128, 512], F32) for i in range(2)]
    bgs = [S.sb("bg", [128, 12], F32) for i in range(2)]
    EmC = S.sb("EmC", [128, 8, 512], BF16)
    EmCb = [Slot("EmCb%d" % c, EmC.t, c) for c in range(8)]
    NBUF = 4
    Eb = [S.sb("Eb", [128, 512], BF16) for i in range(NBUF)]
    Emb = [S.sb("Emb", [128, 512], BF16) for i in range(NBUF)]
    imp = S.sb("imp", [128, 256], F32)
    score = S.sb("score", [128, 256], F32)
    sc2 = S.sb("sc2", [128, 256], F32)
    selF = S.sb("selF", [128, 256], F32)
    selT = S.sb("selT", [128, 2, 128], BF16)
    m8 = S.sb("m8", [128, 16], F32)
    rcols = [S.sb("rcol", [128, 1], F32) for i in range(2)]
    sumsbs = [S.sb("sumsb", [3, 512], F32) for i in range(2)]
    coef = S.sb("coef", [128, 12], F32)
    o_ = S.sb("o_", [128, 512], F32)
    ogf = S.sb("ogf", [128, 512], BF16)
    ogT2 = S.sb("ogT2", [128, 4, 128], BF16)
    yts = [S.sb("yt", [128, 1024], F32) for i in range(2)]
    ObTs = [[S.sb("ObT", [128, 512], BF16) for i in range(3)] for p in range(2)]
    pSc = [S.ps("pSc", [128, 512], F32) for i in range(2)]
    pMs = [S.ps("pM", [128, 512], F32) for i in range(2)]
    pO2 = [S.ps("pOb", [128, 512], F32) for i in range(2)]
    pOb = [pO2[0], pO2[1], pO2[0]]
    pSum1 = S.ps("pSum", [3, 512], F32)
    pSums = [pSum1, pSum1]
    pX = S.ps("pX", [128, 512], F32)
    pXb = pX.t[:, :].bitcast(BF16)

    S.op("gpsimd", lambda e: e.memset(selF[:, :], 0.0), writes=[selF])
    ctr = {"e": 0, "m": 0, "s": 0, "pm": 0}

    def stageA(u):
        rows = u["rows"]
        QT = u["QT"]
        ps = pSc[ctr["s"] % 2]
        ctr["s"] += 1
        S.op("tensor", lambda e: e.matmul(ps[0:rows, :], lhsT=u["ksrc"], rhs=QT[:, :], start=True, stop=True),
             reads=[u["ktrack"], QT], writes=[ps], lhs=[u["ktrack"]])
        if u.get("emc") is not None and u["mask"] is None:
            E = EmCb[u["emc"]]
        else:
            E = Eb[ctr["e"] % NBUF]
            ctr["e"] += 1
        S.op("scalar", lambda e: e.activation(out=E[0:rows, :], in_=ps[0:rows, :], func=AF.Exp, scale=SCALE),
             reads=[ps], writes=[E])
        u["E"] = E

    def stageA2(u):
        rows = u["rows"]
        E = u["E"]
        mask = u["mask"]
        if mask is None:
            Em = E
        else:
            if u.get("emc") is not None:
                Em = EmCb[u["emc"]]
            else:
                Em = Emb[ctr["m"] % NBUF]
                ctr["m"] += 1
            if mask[0] == "sb":
                S.op("vector", lambda e: e.tensor_tensor(
                    out=Em[0:rows, :].rearrange("p (r q) -> p r q", r=4), in0=E[0:rows, :].rearrange("p (r q) -> p r q", r=4),
                    in1=mask[1].unsqueeze(1).broadcast_to([rows, 4, 128]), op=ALU.mult),
                     reads=[E] + mask[2], writes=[Em])
            else:
                t = mask[1]
                pM = pMs[ctr["pm"] % 2]
                ctr["pm"] += 1
                S.op("tensor", lambda e: e.matmul(pM[:, 0:128], lhsT=Ex[:, t % 64, :], rhs=selT[:, t // 64, :],
                                                  start=True, stop=True), reads=[Ex, selT], writes=[pM], lhs=[Ex])
                S.op("vector", lambda e: e.tensor_tensor(
                    out=Em[0:rows, :].rearrange("p (r q) -> p r q", r=4), in0=E[0:rows, :].rearrange("p (r q) -> p r q", r=4),
                    in1=pM[:, 0:128].unsqueeze(1).broadcast_to([128, 4, 128]), op=ALU.mult),
                     reads=[E, pM], writes=[Em])
        u["Em"] = Em

    def stageB(u):
        rows = u["rows"]
        Em = u["Em"]
        b_idx = u["b"]
        q = u["q"]
        pSum = pSums[q["par"]]
        S.op("tensor", lambda e: e.matmul(pOb[b_idx][:, :], lhsT=u["vsrc"], rhs=Em[0:rows, :], start=u["first"], stop=u["last"]),
             reads=[u["vtrack"], Em], writes=[pOb[b_idx]], lhs=[u["vtrack"]])
        S.op("tensor", lambda e: e.matmul(pSum[:, :], lhsT=onesel[0:rows, b_idx, :], rhs=Em[0:rows, :],
                                          start=(q["nsum"] == 0), stop=(q["nsum"] == q["total_sum"] - 1)),
             reads=[onesel, Em], writes=[pSum], lhs=[onesel])
        q["nsum"] += 1
        for f in u.get("post", ()):
            f()

    def rs_chunk(c):
        tl = list(range(c * 8, c * 8 + 8))
        S.collective("ReduceScatter", ALU.add, groups, Y2[c * 1024:(c + 1) * 1024, :], Y2s[c * 256:(c + 1) * 256, :],
                     reads=[bY2[t] for t in tl], writes=[bY2s[c]])

    def prologue(i):
        par = i % 2
        rows = slice(i * 128, (i + 1) * 128)
        hT, h1, QT, gsil, bg = hTs[par], h1s[par], QTs[par], gsils[par], bgs[par]
        S.dma("sync", hT[:, :, :], HT[i].rearrange("p (k t) -> p k t", k=8), reads=[bHT[i]], writes=[hT])
        S.dma("sync", h1[:, :], H1[rows, :], reads=[bH1[i]], writes=[h1])
        S.dma("sync", kwr[:, i % 6, :], KW[i], reads=[bKW[i]], writes=[kwb[i % 6]])
        S.dma("sync", vwr[:, i % 6, :], VW[i], reads=[bVW[i]], writes=[vwb[i % 6]])
        for r in range(4):
            for k in range(8):
                S.op("tensor", lambda e: e.matmul(pX[:, r * 128:(r + 1) * 128], lhsT=Wn[:, k, r * 128:(r + 1) * 128],
                                                  rhs=hT[:, k, :], start=(k == 0), stop=(k == 7)), reads=[Wn, hT], writes=[pX], lhs=[Wn])
        S.op("scalar", lambda e: e.copy(out=QT[:, :], in_=pX[:, :]), reads=[pX], writes=[QT])
        for k in range(8):
            S.op("tensor", lambda e: e.matmul(pX[:, :], lhsT=hT[:, k, :], rhs=Wn[:, k, 512:1024],
                                              start=(k == 0), stop=(k == 7)), reads=[Wn, hT], writes=[pX])
        S.op("scalar", lambda e: e.activation(out=gsil[:, :], in_=pX[:, :], func=AF.Silu), reads=[pX], writes=[gsil])
        for k in range(8):
            S.op("tensor", lambda e: e.matmul(pX[:, 0:12], lhsT=hT[:, k, :], rhs=Wn[:, k, 1024:1036],
                                              start=(k == 0), stop=(k == 7)), reads=[Wn, hT], writes=[pX])
        S.op("scalar", lambda e: e.activation(out=bg[:, :], in_=pX[:, 0:12], func=AF.Sigmoid), reads=[pX], writes=[bg])

    def make_units(i):
        par = i % 2
        QT = QTs[par]
        Wp = 8 * (i + 1)
        nch = (Wp + 127) // 128
        wt = [t for t in range(i - 4, i + 1) if t >= 0]
        q = dict(i=i, par=par, nsum=0, total_sum=nch + (i + 1) + len(wt), topk_done=(i < 8))
        OT = ObTs[par]

        def evac(b):
            return lambda: S.op("scalar", lambda e: e.copy(out=OT[b][:, :], in_=pOb[b][:, :]), reads=[pOb[b]], writes=[OT[b]])

        def topk():
            ncol = 2 * i
            for r in range(4):
                pI = pMs[ctr["pm"] % 2]
                ctr["pm"] += 1
                rc_ = rcols[r % 2]
                for c in range(nch):
                    rws = min(128, Wp - c * 128)
                    S.op("tensor", lambda e: e.matmul(pI[:, 0:257], lhsT=EmC[0:rws, c, r * 128:(r + 1) * 128], rhs=Maug[0:rws, c, :],
                                                      start=(c == 0), stop=(c == nch - 1)), reads=[EmCb[c], Maug], writes=[pI])
                S.op("vector", lambda e: e.tensor_scalar(out=rc_[:, :], in0=pI[:, 256:257], scalar1=1e-30, scalar2=None,
                                                         op0=ALU.add), reads=[pI], writes=[rc_])
                S.op("vector", lambda e: e.reciprocal(out=rc_[:, :], in_=rc_[:, :]), reads=[rc_], writes=[rc_])
                if r == 0:
                    S.op("vector", lambda e: e.tensor_scalar(out=imp[:, 0:ncol], in0=pI[:, 0:ncol], scalar1=rc_[:, 0:1],
                                                             scalar2=None, op0=ALU.mult), reads=[pI, rc_], writes=[imp])
                else:
                    S.op("vector", lambda e: e.scalar_tensor_tensor(out=imp[:, 0:ncol], in0=pI[:, 0:ncol], scalar=rc_[:, 0:1],
                                                                    in1=imp[:, 0:ncol], op0=ALU.mult, op1=ALU.add),
                         reads=[pI, rc_, imp], writes=[imp])
            S.op("vector", lambda e: e.tensor_copy(out=score[:, 0:ncol], in_=imp[:, 0:ncol]), reads=[imp], writes=[score])
            S.op("vector", lambda e: e.memset(score[:, 0:1], -1.0), writes=[score])
            S.op("vector", lambda e: e.memset(score[0:64, ncol - 1:ncol], -1.0), writes=[score])
            S.op("vector", lambda e: e.max(out=m8[:, 0:8], in_=score[:, 0:ncol]), reads=[score], writes=[m8])
            S.op("vector", lambda e: e.match_replace(out=sc2[:, 0:ncol], in_to_replace=m8[:, 0:8], in_values=score[:, 0:ncol],
                                                     imm_value=-2.0), reads=[m8, score], writes=[sc2])
            S.op("vector", lambda e: e.max(out=m8[:, 8:16], in_=sc2[:, 0:ncol]), reads=[sc2, m8], writes=[m8])
            S.op("vector", lambda e: e.tensor_scalar(out=selF[:, 0:ncol], in0=score[:, 0:ncol], scalar1=m8[:, 12:13], scalar2=None,
                                                     op0=ALU.is_ge), reads=[score, m8], writes=[selF])
            S.op("vector", lambda e: e.memset(selF[:, 0:1], 1.0), writes=[selF])
            S.op("vector", lambda e: e.memset(selF[0:64, ncol - 1:ncol], 1.0), writes=[selF])
            for c in range((ncol + 127) // 128):
                S.op("tensor", lambda e: e.transpose(out=pX[:, c * 128:(c + 1) * 128], in_=selF[:, c * 128:(c + 1) * 128],
                                                     identity=idf[:, :]), reads=[selF, idf], writes=[pX])
                S.op("vector", lambda e: e.tensor_copy(out=selT[:, c, :], in_=pX[:, c * 128:(c + 1) * 128]), reads=[pX], writes=[selT])
            q["topk_done"] = True

        units = []
        for c in range(nch):
            rws = min(128, Wp - c * 128)
            if c == nch - 1:
                mk = ("sb", cmpm[0:rws, (i % 16) + (0 if i < 16 else 16), :], [cmpm])
            elif c == 0:
                mk = ("sb", cmpm[0:rws, 32, :], [cmpm])
            else:
                mk = None
            u = dict(q=q, QT=QT, ksrc=KcT[:, c * 128:c * 128 + rws], ktrack=KcT, vsrc=Vc[0:rws, c, :], vtrack=Vc, rows=rws, mask=mk,
                     b=0, first=(c == 0), last=(c == nch - 1), emc=(c if i >= 8 else None), post=[])
            if c == nch - 1:
                u["post"].append(evac(0))
                if i >= 8:
                    u["post"].append(topk)
            units.append(u)
        for n, t in enumerate(wt):
            if t == i:
                mk = ("sb", causT[:, :], [causT])
            elif t == i - 4:
                mk = ("sb", upT[:, :], [upT])
            else:
                mk = None
            u = dict(q=q, QT=QT, ksrc=kwr[:, t % 6, :], ktrack=kwb[t % 6], vsrc=vwr[:, t % 6, :], vtrack=vwb[t % 6], rows=128, mask=mk,
                     b=2, first=(n == 0), last=(n == len(wt) - 1), post=[])
            if n == len(wt) - 1:
                u["post"].append(evac(2))
            units.append(u)
        for t in range(i + 1):
            if t == i:
                mk = ("sb", causT[:, :], [causT])
            elif i >= 8:
                mk = ("sel", t)
            else:
                mk = None
            u = dict(q=q, QT=QT, ksrc=KsT[:, t * 128:(t + 1) * 128], ktrack=KsT, vsrc=Vs[:, t, :], vtrack=Vs, rows=128, mask=mk,
                     b=1, first=(t == 0), last=(t == i), post=[], needs_sel=(mk is not None and mk[0] == "sel"))
            if t == i:
                u["post"].append(evac(1))
            units.append(u)
        return units, q

    def epilogue_parts(i):
        par = i % 2
        rows = slice(i * 128, (i + 1) * 128)
        h1, gsil, bg = h1s[par], gsils[par], bgs[par]
        OT = ObTs[par]
        pSum = pSums[par]
        sumsb = sumsbs[par]
        yt = yts[par]

        def combine(b):
            for r in range(4):
                S.op("tensor", lambda e: e.transpose(out=pXb[:, r * 128:(r + 1) * 128], in_=OT[b][:, r * 128:(r + 1) * 128],
                                                     identity=ident[:, :]), reads=[OT[b], ident], writes=[pX])
            for r in range(4):
                cs_ = slice(r * 128, (r + 1) * 128)
                if b == 0:
                    S.op("vector", lambda e: e.tensor_scalar(out=o_[:, cs_], in0=pXb[:, cs_], scalar1=coef[:, r * 3 + b:r * 3 + b + 1],
                                                             scalar2=None, op0=ALU.mult), reads=[pX, coef], writes=[o_])
                else:
                    S.op("vector", lambda e: e.scalar_tensor_tensor(out=o_[:, cs_], in0=pXb[:, cs_], scalar=coef[:, r * 3 + b:r * 3 + b + 1],
                                                                    in1=o_[:, cs_], op0=ALU.mult, op1=ALU.add),
                         reads=[pX, coef, o_], writes=[o_])

        def part1():
            S.op("scalar", lambda e: e.copy(out=sumsb[:, :], in_=pSum[:, :]), reads=[pSum], writes=[sumsb])
            for r in range(4):
                S.op("tensor", lambda e: e.matmul(pX[:, r * 3:(r + 1) * 3], lhsT=sumsb[0:3, r * 128:(r + 1) * 128], rhs=idf[0:3, 0:3],
                                                  start=True, stop=True), reads=[sumsb, idf], writes=[pX])
            S.op("vector", lambda e: e.tensor_scalar(out=coef[:, :], in0=pX[:, 0:12], scalar1=1e-30, scalar2=None, op0=ALU.add),
                 reads=[pX], writes=[coef])
            S.op("vector", lambda e: e.reciprocal(out=coef[:, :], in_=coef[:, :]), reads=[coef], writes=[coef])
            S.op("vector", lambda e: e.tensor_tensor(out=coef[:, :], in0=coef[:, :], in1=bg[:, :], op=ALU.mult), reads=[coef, bg], writes=[coef])
            combine(0)

        def part2():
            combine(2)

        def part2b():
            combine(1)
            S.op("gpsimd", lambda e: e.tensor_mul(out=ogf[:, :], in0=o_[:, :], in1=gsil[:, :]), reads=[o_, gsil], writes=[ogf])

        def part3():
            for c in range(4):
                S.op("tensor", lambda e: e.transpose(out=pXb[:, c * 128:(c + 1) * 128], in_=ogf[:, c * 128:(c + 1) * 128],
                                                     identity=ident[:, :]), reads=[ogf, ident], writes=[pX])
            S.op("vector", lambda e: e.tensor_copy(out=ogT2[:, :, :].rearrange("p a b -> p (a b)"), in_=pXb[:, 0:512]), reads=[pX], writes=[ogT2])

        def part4(hf):
            if True:
                for c in range(4):
                    S.op("tensor", lambda e: e.matmul(pX[:, :], lhsT=ogT2[:, c, :], rhs=Wo2[:, c, hf * 512:(hf + 1) * 512],
                                                      start=(c == 0), stop=(c == 3)), reads=[ogT2, Wo2], writes=[pX])
                S.op("vector", lambda e: e.scalar_tensor_tensor(out=yt[:, hf * 512:(hf + 1) * 512], in0=h1[:, hf * 512:(hf + 1) * 512],
                                                                scalar=0.25, in1=pX[:, :], op0=ALU.mult, op1=ALU.add),
                     reads=[h1, pX], writes=[yt])
            if hf == 1:
                S.dma("gpsimd", Y2[rows, :], yt[:, :], reads=[yt], writes=[bY2[i]])
                if i % 8 == 7:
                    rs_chunk(i // 8)

        return [part1, part2, part2b, part3, lambda: part4(0), lambda: part4(1)]

    LOOK = 3
    prologue(0)
    pending = []
    for i in range(NT):
        units, q = make_units(i)
        nu = len(units)
        hooks = {}
        pos = [1, 4, 7, 10, 13, 16]
        for k, f in enumerate(pending):
            hooks.setdefault(min(nu - 1, pos[k]), []).append(f)
        if i + 1 < NT:
            hooks.setdefault(min(nu - 1, 19), []).append(lambda i=i: prologue(i + 1))
        for n in range(nu + LOOK):
            if n < nu:
                stageA(units[n])
            if 0 <= n - 1 < nu:
                if units[n - 1].get("needs_sel"):
                    assert q["topk_done"]
                stageA2(units[n - 1])
            if n - LOOK >= 0:
                stageB(units[n - LOOK])
            for f in hooks.get(n, ()):
                f()
        assert q["nsum"] == q["total_sum"]
        pending = epilogue_parts(i)
    for f in pending:
        f()
    S.pop()
    S.pop()
    if upto == 3:
        return dbg_out(Y2, bY2)

    S.push()
    ple = PleCtx(S)
    ple.load_weights("sync", pg[1][:, :], wg[1], we[1], stage)
    fn = S.sb("fn", [128, 1024], F32)
    S.dma("sync", fn[:, :], fng[:, :], writes=[fn])
    hs = [S.sb("h", [128, 1024], F32) for i in range(2)]
    ob = [S.sb("ob", [128, 1024], F32) for i in range(2)]
    sqs = S.sb("sqs", [128, 1024], F32)
    ss = S.sb("ss", [128, 1], F32)
    rs = S.sb("rs", [128, 1], F32)
    pA = S.ps("pA", [128, 8, 128], BF16)
    pG = S.ps("pG", [128, 512], F32)
    pE = S.ps("pE", [128, 512], F32)
    pG2 = S.ps("pG2", [128, 512], F32)
    pE2 = S.ps("pE2", [128, 512], F32)
    NU = TL // 128

    def UA(u):
        rows = slice(u * 128, (u + 1) * 128)
        h = hs[u % 2]
        S.dma("sync", h[:, :], Y2s[rows, :], reads=[bY2s[u // 2]], writes=[h])
        ple.front(u % 2, h, p1s[rows, :], ident, pA)

    def UB(u):
        rows = slice(u * 128, (u + 1) * 128)
        h = hs[u % 2]
        ple.back(u % 2, h, [pG, pG2], [pE, pE2])
        rms_rstd(S, h, 1024, sqs, ss, rs)
        o = ob[u % 2]
        S.op("vector", lambda e: e.scalar_tensor_tensor(out=o[:, :], in0=h[:, :], scalar=rs[:, 0:1], in1=fn[:, :],
                                                        op0=ALU.mult, op1=ALU.mult), reads=[h, rs, fn], writes=[o])
        S.dma("gpsimd", out[rows, :], o[:, :], reads=[o])

    for step in range(NU + 1):
        if step < NU:
            UA(step)
        if step >= 1:
            UB(step - 1)
    S.finish()
    return nc, S.ninstr


def _colgain(g):
    return np.ascontiguousarray(np.asarray(g, np.float32).reshape(8, 128).T)


def _consts(T):
    half = 128
    inv = (10000.0 ** (-np.arange(half, dtype=np.float32) / np.float32(half))).astype(np.float32)
    pos = np.arange(T, dtype=np.float32)
    ang = (inv[:, None] * pos[None, :]).astype(np.float32)
    c = dict(cosT=np.cos(ang).astype(np.float32), sinT=np.sin(ang).astype(np.float32))
    p = np.arange(128)
    c["causT"] = (p[:, None] <= p[None, :]).astype(np.float32)
    c["upT"] = (p[:, None] > p[None, :]).astype(np.float32)
    c["ident"] = np.eye(128, dtype=np.float32)
    ex = np.zeros((128, 64, 128), np.float32)
    for tt in range(64):
        for hb in range(2):
            ex[(2 * tt + hb) % 128, tt, hb * 64:(hb + 1) * 64] = 1.0
    c["ex"] = ex.reshape(128, 64 * 128)
    m = np.zeros((1024, 257), np.float32)
    for j in range(256):
        for (off, w) in ((0, 1.0), (1, 2.0), (2, 2.0), (3, 2.0), (4, 1.0)):
            n = 4 * j + off
            if n < 1024:
                m[n, j] = w
    m[:, 256] = 1.0
    c["maug"] = np.ascontiguousarray(m.reshape(8, 128, 257).transpose(1, 0, 2)).reshape(128, 8 * 257)
    lane = np.arange(128)[:, None]
    ql = np.arange(128)[None, :]
    cm = np.zeros((128, 33, 128), np.float32)
    for res in range(16):
        v = (ql >= 16 * lane + 15 - 128 * res).astype(np.float32)
        b = v.copy()
        a = v.copy()
        a[0, :] = 0.0
        cm[:, res, :] = a
        cm[:, 16 + res, :] = b
    fm = np.ones((128, 128), np.float32)
    fm[0, :] = 0.0
    cm[:, 32, :] = fm
    c["cmpm"] = cm.reshape(128, 33 * 128)
    os_ = np.zeros((128, 3, 3), np.float32)
    for b in range(3):
        os_[:, b, b] = 1.0
    c["onesel"] = os_.reshape(128, 9)
    return c


def _head_consts(hd):
    lg = np.log1p(-(np.float32(2.0) ** np.float32(-5.0 - hd))).astype(np.float32)
    p = np.arange(128, dtype=np.float32)
    qd = np.exp((p + 1.0) * lg).astype(np.float32)
    kdv = (np.exp(-(p + 1.0) * lg) / 16.0).astype(np.float32)
    return dict(qdec=np.ascontiguousarray(np.broadcast_to(np.tile(qd, 4)[None, :], (128, 512))).astype(np.float32),
                kdec=np.ascontiguousarray(np.broadcast_to(np.tile(kdv, 4)[None, :], (128, 512))).astype(np.float32),
                cdec=np.full((128, 1), np.exp(np.float32(128.0) * lg), np.float32))


def _inmaps(T, B, I):
    C = _consts(T)
    TL = T // 4
    maps = []
    ca = np.ascontiguousarray
    for b in range(B):
        for g in range(4):
            m = dict(C)
            m.update(_head_consts(g))
            m["x"] = ca(I["x"][b, :T])
            m["p0"] = ca(I["p"][0, b, :T])
            p1 = I["p"][1, b, :T].reshape(T // 1024, 4, 256, 256)[:, g].reshape(TL, 256)
            m["p1s"] = ca(p1)
            wi = I["ret_w_in"][0]
            m["r_w_in"] = ca(np.concatenate([wi[:, g * 256:(g + 1) * 256], wi[:, 1024 + g * 256:1024 + (g + 1) * 256],
                                             wi[:, 2048 + g * 512:2048 + (g + 1) * 512], wi[:, 4096 + g * 512:4096 + (g + 1) * 512]], axis=1))
            m["r_g_in"] = _colgain(I["ret_norm"][0])
            m["r_gn"] = ca(np.broadcast_to(I["ret_gn"][0][g * 512:(g + 1) * 512][None, :], (128, 512)))
            m["r_w_out"] = ca(I["ret_w_out"][0][g * 512:(g + 1) * 512, :])
            for l in range(2):
                m["pg%d" % l] = _colgain(I["ple_norm"][l])
                m["wg%d" % l] = ca(I["ple_w_gate"][l])
                m["we%d" % l] = ca(I["ple_w_emb"][l])
            m["fng"] = ca(np.broadcast_to(I["final_norm"][None, :], (128, 1024)))
            m["kv_g"] = _colgain(I["kv_norm"])
            kw = I["kv_w"]
            order = [0, 1, 2, 4, 3, 5]
            m["kv_w"] = ca(np.concatenate([kw[:, pt * 512 + g * 128: pt * 512 + (g + 1) * 128] for pt in order], axis=1))
            m["peT_k"] = ca(I["cmp_pe_k"].T)
            m["peT_v"] = ca(I["cmp_pe_v"].T)
            m["w1_k"] = ca(I["cmp_w1_k"])
            m["w1_v"] = ca(I["cmp_w1_v"])
            m["w2_k"] = ca(I["cmp_w2_k"])
            m["w2_v"] = ca(I["cmp_w2_v"])
            m["n_g"] = _colgain(I["nsa_norm"][0])
            nw = I["nsa_w_in"][0]
            m["n_w_in"] = ca(np.concatenate([nw[:, g * 512:(g + 1) * 512], nw[:, 2048 + g * 512:2048 + (g + 1) * 512],
                                             nw[:, 4096 + g * 12:4096 + (g + 1) * 12]], axis=1))
            m["n_w_out"] = ca(I["nsa_w_out"][0][g * 512:(g + 1) * 512, :])
            maps.append({k: np.asarray(v, np.float32) for k, v in m.items()})
    return maps


_PROG = {}


def run_module(I, T, B):
    key = (T, B)
    if key not in _PROG:
        groups = [[b * 4 + g for g in range(4)] for b in range(B)]
        _PROG[key] = build_program(T, groups)[0]
    nc = _PROG[key]
    maps = _inmaps(T, B, I)
    res = run_bass_kernel_spmd(nc, maps, core_ids=list(range(4 * B)))
    outp = np.empty((B, T, 1024), np.float32)
    for b in range(B):
        for g in range(4):
            o = res.results[b * 4 + g]["out"].reshape(T // 1024, 256, 1024)
            outp[b].reshape(T // 1024, 4, 256, 1024)[:, g] = o
    return outp


def kernel(**inputs):
    I = {k: np.asarray(v) for k, v in inputs.items()}
    return run_module(I, 16384, 2)
```
